# Optimizing a Trainium2 kernel written in Bass

```python
import jax, jax.numpy as jnp
from jax import lax
import numpy as np

D_MODEL = 1024
BATCH = 4
SEQ = 8192
DEPTH = 2

CHUNK = 64
D_CONV = D_MODEL // 4
CONV_WIDTH = 31
SB_HEAD_DIM = 64
D_SB = D_MODEL // 2
N_SB_HEADS = D_SB // SB_HEAD_DIM
RET_HEAD_DIM = 64
D_RET = D_MODEL // 4
N_RET_HEADS = D_RET // RET_HEAD_DIM
D_MIX = D_CONV + D_SB + D_RET
D_IN_PROJ = 2 * D_CONV + 3 * D_SB + 4 * D_RET
D_FF = ((8 * D_MODEL // 3 + 127) // 128) * 128
Q_BLOCK = 128
ROPE_BASE = 10000.0
EPS = 1e-6

kernel_name = "hybrid_conv_stickbreak_retention_macaron"


def rms_norm(x, g):
    xf = x.astype(jnp.float32)
    y = xf * lax.rsqrt(jnp.mean(xf * xf, axis=-1, keepdims=True) + EPS)
    return (y * g.astype(jnp.float32)).astype(x.dtype)


def layer_norm(x, g, b):
    xf = x.astype(jnp.float32)
    mu = jnp.mean(xf, axis=-1, keepdims=True)
    xc = xf - mu
    var = jnp.mean(xc * xc, axis=-1, keepdims=True)
    return (xc * lax.rsqrt(var + EPS) * g.astype(jnp.float32) + b.astype(jnp.float32)).astype(x.dtype)


def swiglu(h, w_in, w_out):
    gate, up = jnp.split(h @ w_in, 2, axis=-1)
    return (jax.nn.silu(gate) * up) @ w_out


def conv_module(u, conv_w, conv_b, ln_g, ln_b):
    a, b = jnp.split(u, 2, axis=-1)
    v = a * jax.nn.sigmoid(b)
    v = jnp.pad(v, ((0, 0), (CONV_WIDTH - 1, 0), (0, 0)))
    y = lax.conv_general_dilated(
        v, conv_w[:, None, :].astype(v.dtype), window_strides=(1,), padding="VALID",
        dimension_numbers=("NWC", "WIO", "NWC"), feature_group_count=D_CONV)
    y = y + conv_b
    return jax.nn.silu(layer_norm(y, ln_g, ln_b))


def stick_breaking(q, k, v):
    B, S, H, Dh = q.shape
    nb = S // Q_BLOCK
    qb = q.reshape(B, nb, Q_BLOCK, H, Dh).transpose(1, 0, 2, 3, 4)
    kpos = jnp.arange(S)
    scale = Dh ** -0.5

    def block(args):
        qi, i = args
        z = jnp.einsum("bqhd,bkhd->bhqk", qi, k,
                       preferred_element_type=jnp.float32) * scale
        qpos = i * Q_BLOCK + jnp.arange(Q_BLOCK)
        mask = kpos[None, :] < qpos[:, None]
        log_beta = jax.nn.log_sigmoid(z)
        log_not = jnp.where(mask, jax.nn.log_sigmoid(-z), 0.0)
        remain = lax.cumsum(log_not, axis=3, reverse=True) - log_not
        w = jnp.where(mask, jnp.exp(log_beta + remain), 0.0)
        return jnp.einsum("bhqk,bkhd->bqhd", w.astype(v.dtype), v)

    out = lax.map(block, (qb, jnp.arange(nb)))
    return out.transpose(1, 0, 2, 3, 4).reshape(B, S, H, Dh)


def rotary(x, pos):
    half = x.shape[-1] // 2
    inv = 1.0 / (ROPE_BASE ** (jnp.arange(half, dtype=jnp.float32) / half))
    ang = pos.astype(jnp.float32)[:, None] * inv[None, :]
    cos = jnp.cos(ang)[None, :, None, :]
    sin = jnp.sin(ang)[None, :, None, :]
    x1 = x[..., :half].astype(jnp.float32)
    x2 = x[..., half:].astype(jnp.float32)
    return jnp.concatenate([x1 * cos - x2 * sin, x1 * sin + x2 * cos], axis=-1).astype(x.dtype)


def retention(q, k, v):
    B, S, H, Dh = q.shape
    nc = S // CHUNK
    log_gamma = jnp.log1p(-jnp.exp2(-5.0 - jnp.arange(H, dtype=jnp.float32)))
    qc = (q * (Dh ** -0.5)).reshape(B, nc, CHUNK, H, Dh)
    kc = k.reshape(B, nc, CHUNK, H, Dh)
    vc = v.reshape(B, nc, CHUNK, H, Dh)
    idx = jnp.arange(CHUNK, dtype=jnp.float32)
    d_intra = jnp.exp(log_gamma[:, None, None] * jnp.abs(idx[:, None] - idx[None, :]))
    scores = jnp.einsum("bnihd,bnjhd->bnhij", qc, kc,
                        preferred_element_type=jnp.float32) * d_intra
    y_intra = jnp.einsum("bnhij,bnjhe->bnihe", scores, vc.astype(jnp.float32))
    k_decay = jnp.exp(log_gamma[None, :] * (CHUNK - 1 - idx)[:, None])
    kv = jnp.einsum("bnjhd,jh,bnjhe->bnhde", kc.astype(jnp.float32), k_decay,
                    vc.astype(jnp.float32))
    chunk_decay = jnp.exp(log_gamma * CHUNK)[None, :, None, None]

    def step(state, kv_n):
        return chunk_decay * state + kv_n, state

    _, s_prev = lax.scan(step, jnp.zeros((B, H, Dh, Dh), jnp.float32),
                         kv.transpose(1, 0, 2, 3, 4))
    s_prev = s_prev.transpose(1, 0, 2, 3, 4)
    q_decay = jnp.exp(log_gamma[None, :] * (idx + 1.0)[:, None])
    y_cross = jnp.einsum("bnihd,ih,bnhde->bnihe", qc.astype(jnp.float32), q_decay, s_prev)
    return (y_intra + y_cross).reshape(B, S, H, Dh)


def head_norm(y, g):
    B, S, H, Dh = y.shape
    mu = jnp.mean(y, axis=-1, keepdims=True)
    yc = y - mu
    var = jnp.mean(yc * yc, axis=-1, keepdims=True)
    return (yc * lax.rsqrt(var + EPS)).reshape(B, S, H * Dh) * g.astype(jnp.float32)


def hybrid_mixer(h, w_in, conv_w, conv_b, conv_ln_g, conv_ln_b, ret_norm_g, w_out, pos):
    B, S, _ = h.shape
    o1 = 2 * D_CONV
    o2 = o1 + D_SB
    o3 = o2 + D_SB
    o4 = o3 + D_SB
    o5 = o4 + D_RET
    o6 = o5 + D_RET
    o7 = o6 + D_RET
    u_conv, q_sb, k_sb, v_sb, q_r, k_r, v_r, g_r = jnp.split(
        h @ w_in, [o1, o2, o3, o4, o5, o6, o7], axis=-1)
    y_conv = conv_module(u_conv, conv_w, conv_b, conv_ln_g, conv_ln_b)
    sb = lambda t: t.reshape(B, S, N_SB_HEADS, SB_HEAD_DIM)
    y_sb = stick_breaking(sb(q_sb), sb(k_sb), sb(v_sb)).reshape(B, S, D_SB)
    rt = lambda t: t.reshape(B, S, N_RET_HEADS, RET_HEAD_DIM)
    y_r = retention(rotary(rt(q_r), pos), rotary(rt(k_r), pos), rt(v_r))
    y_r = jax.nn.silu(g_r.astype(jnp.float32)) * head_norm(y_r, ret_norm_g)
    y = jnp.concatenate([y_conv, y_sb, y_r.astype(h.dtype)], axis=-1)
    return y @ w_out


def setup_inputs(seed: int = 0) -> dict:
    key = jax.random.key(seed)
    ks = jax.random.split(key, 20)
    f32 = jnp.float32
    nrm = lambda k, shape, scale: jax.random.normal(k, shape, f32) * scale
    gain = lambda k, shape: 1.0 + 0.02 * jax.random.normal(k, shape, f32)
    return {
        "x": jax.random.normal(ks[0], (BATCH, SEQ, D_MODEL), f32),
        "ffn1_norm": gain(ks[1], (DEPTH, D_MODEL)),
        "ffn1_w_in": nrm(ks[2], (DEPTH, D_MODEL, 2 * D_FF), D_MODEL ** -0.5),
        "ffn1_w_out": nrm(ks[3], (DEPTH, D_FF, D_MODEL), D_FF ** -0.5),
        "mix_norm": gain(ks[4], (DEPTH, D_MODEL)),
        "mix_w_in": nrm(ks[5], (DEPTH, D_MODEL, D_IN_PROJ), D_MODEL ** -0.5),
        "conv_w": nrm(ks[6], (DEPTH, CONV_WIDTH, D_CONV), CONV_WIDTH ** -0.5),
        "conv_b": nrm(ks[7], (DEPTH, D_CONV), 0.02),
        "conv_ln_g": gain(ks[8], (DEPTH, D_CONV)),
        "conv_ln_b": nrm(ks[9], (DEPTH, D_CONV), 0.02),
        "ret_norm_g": gain(ks[10], (DEPTH, D_RET)),
        "mix_w_out": nrm(ks[11], (DEPTH, D_MIX, D_MODEL), D_MIX ** -0.5),
        "ffn2_norm": gain(ks[12], (DEPTH, D_MODEL)),
        "ffn2_w_in": nrm(ks[13], (DEPTH, D_MODEL, 2 * D_FF), D_MODEL ** -0.5),
        "ffn2_w_out": nrm(ks[14], (DEPTH, D_FF, D_MODEL), D_FF ** -0.5),
        "final_norm": gain(ks[15], (D_MODEL,)),
    }


def reference(x, ffn1_norm, ffn1_w_in, ffn1_w_out, mix_norm, mix_w_in, conv_w, conv_b,
              conv_ln_g, conv_ln_b, ret_norm_g, mix_w_out, ffn2_norm, ffn2_w_in, ffn2_w_out,
              final_norm):
    S = x.shape[1]
    pos = jnp.arange(S)
    for l in range(DEPTH):
        x = x + 0.5 * swiglu(rms_norm(x, ffn1_norm[l]), ffn1_w_in[l], ffn1_w_out[l])
        x = x + hybrid_mixer(rms_norm(x, mix_norm[l]), mix_w_in[l], conv_w[l], conv_b[l],
                             conv_ln_g[l], conv_ln_b[l], ret_norm_g[l], mix_w_out[l], pos)
        x = x + 0.5 * swiglu(rms_norm(x, ffn2_norm[l]), ffn2_w_in[l], ffn2_w_out[l])
    return rms_norm(x, final_norm)
```

```python
import numpy as np
import ml_dtypes
from contextlib import ExitStack
import concourse.bass as bass
import concourse.mybir as mybir
from concourse.bass_utils import run_bass_kernel_spmd

F32 = mybir.dt.float32
BF16 = mybir.dt.bfloat16
AF = mybir.ActivationFunctionType
ALU = mybir.AluOpType

D = 1024
DFF = 2816
NLOC = 4096
NT = 512
NG = NLOC // NT
DEPTH = 2
EPS = 1e-6
NDS = 8


class Op:
    __slots__ = ("stream", "fn", "deps", "dma", "token", "needed", "phase", "wkeys")


class Prog:
    STREAMS = ["pe", "act", "dve", "pool", "sp"]

    def __init__(self, nc, es):
        self.nc = nc
        self.csem = {s: es.enter_context(nc.semaphore("c_" + s)) for s in ["pe", "act", "dve", "pool"]}
        self.dsem = {s: [es.enter_context(nc.semaphore(f"d_{s}{i}")) for i in range(NDS)]
                     for s in ["sp", "pool"]}
        self.ccnt = {s: 0 for s in self.csem}
        self.dcnt = {s: [0] * NDS for s in self.dsem}
        self.dnum = {s: 0 for s in self.dsem}
        self.lastw = {}
        self.readers = {}
        self.ops = []
        self.phase = 0
        self.waited = {s: {} for s in self.STREAMS}
        self.pending = {s: {} for s in self.STREAMS}
        self.out_tokens = []

    def op(self, stream, fn, reads=(), writes=(), dma=False, is_out=False):
        o = Op()
        o.stream, o.fn, o.dma, o.needed, o.phase, o.token = stream, fn, dma, False, self.phase, None
        o.wkeys = set(writes)
        deps = {}
        for k in reads:
            w = self.lastw.get(k)
            if w is not None:
                deps[id(w)] = (w, True)
        for k in writes:
            w = self.lastw.get(k)
            if w is not None and id(w) not in deps:
                deps[id(w)] = (w, False)
            for r in self.readers.get(k, {}).values():
                if id(r) not in deps:
                    deps[id(r)] = (r, False)
        o.deps = []
        for d, raw in deps.values():
            if d.phase != self.phase or d is o:
                continue
            if d.stream == stream and not d.dma and not dma:
                if stream == "pe" or not raw:
                    continue
            o.deps.append(d)
        if dma:
            i = self.dnum[stream] % NDS
            self.dnum[stream] += 1
            self.dcnt[stream][i] += 16
            o.token = (self.dsem[stream][i], self.dcnt[stream][i])
            rkey = (stream, i)
            if is_out:
                self.out_tokens.append(o.token)
        else:
            rkey = (stream, -1)
        for k in reads:
            self.readers.setdefault(k, {})[rkey] = o
        for k in writes:
            self.lastw[k] = o
            self.readers[k] = {}
        self.ops.append(o)
        return o

    def flush(self, final=False):
        ops = self.ops
        for o in ops:
            for d in o.deps:
                d.needed = True
        last = {}
        for o in ops:
            if not o.dma:
                last[o.stream] = o
        for o in last.values():
            o.needed = True
        for o in ops:
            if not o.dma and o.needed:
                self.ccnt[o.stream] += 1
                o.token = (self.csem[o.stream], self.ccnt[o.stream])
        by_stream = {s: [o for o in ops if o.stream == s] for s in self.STREAMS}
        snapshot = {}
        for s in self.csem:
            if self.ccnt[s]:
                snapshot[id(self.csem[s])] = (self.csem[s], self.ccnt[s])
        for s in self.dsem:
            for i in range(NDS):
                if self.dcnt[s][i]:
                    snapshot[id(self.dsem[s][i])] = (self.dsem[s][i], self.dcnt[s][i])

        def emit(stream, eng):
            waited = self.waited[stream]
            for o in by_stream[stream]:
                w = dict(self.pending[stream])
                self.pending[stream] = {}
                for d in o.deps:
                    sem, val = d.token
                    if id(sem) not in w or w[id(sem)][1] < val:
                        w[id(sem)] = (sem, val)
                for sid, (sem, val) in w.items():
                    if waited.get(sid, 0) < val:
                        eng.wait_ge(sem, val)
                        waited[sid] = val
                ins = o.fn(eng)
                if o.token is not None:
                    ins.then_inc(o.token[0], 16 if o.dma else 1)
            if final and stream == "sp":
                for sid, (sem, val) in snapshot.items():
                    if waited.get(sid, 0) < val:
                        eng.wait_ge(sem, val)
                        waited[sid] = val

        with self.nc.Block() as blk:
            if by_stream["pe"]:
                blk.tensor(lambda e: emit("pe", e))
            if by_stream["act"]:
                blk.scalar(lambda e: emit("act", e))
            if by_stream["dve"]:
                blk.vector(lambda e: emit("dve", e))
            if by_stream["pool"]:
                blk.gpsimd(lambda e: emit("pool", e))
            if by_stream["sp"] or final:
                blk.sync(lambda e: emit("sp", e))
        for s in self.STREAMS:
            p = self.pending[s]
            for sid, (sem, val) in snapshot.items():
                if sid not in p or p[sid][1] < val:
                    p[sid] = (sem, val)
        self.ops = []
        self.phase += 1

    def dma(self, out, in_, reads, writes, q="sp", is_out=False):
        return self.op(q, lambda e: e.dma_start(out=out, in_=in_), reads, writes, dma=True, is_out=is_out)

    def mm(self, out, lhsT, rhs, start, stop, reads, writes):
        return self.op("pe", lambda e: e.matmul(out, lhsT, rhs, start=start, stop=stop), reads, writes)

    def act(self, out, in_, func, reads, writes, bias=None, scale=None):
        kw = {}
        if bias is not None:
            kw["bias"] = bias
        if scale is not None:
            kw["scale"] = scale
        return self.op("act", lambda e: e.activation(out=out, in_=in_, func=func, **kw), reads, writes)

    def tt(self, out, in0, in1, op, reads, writes, eng="dve"):
        return self.op(eng, lambda e: e.tensor_tensor(out=out, in0=in0, in1=in1, op=op), reads, writes)

    def ts(self, out, in0, s1, s2, op0, op1, reads, writes, eng="dve"):
        if op1 is None:
            return self.op(eng, lambda e: e.tensor_scalar(out=out, in0=in0, scalar1=s1, scalar2=None, op0=op0),
                           reads, writes)
        return self.op(eng, lambda e: e.tensor_scalar(out=out, in0=in0, scalar1=s1, scalar2=s2, op0=op0, op1=op1),
                       reads, writes)

    def stt(self, out, in0, scalar, in1, op0, op1, reads, writes, eng="dve"):
        return self.op(eng, lambda e: e.scalar_tensor_tensor(out=out, in0=in0, scalar=scalar, in1=in1,
                                                             op0=op0, op1=op1), reads, writes)


class Ctx:
    pass


def rsqrt_inplace(P, t, key):
    P.act(t, t, AF.Sqrt, [key], [key])
    P.op("dve", lambda e: e.reciprocal(out=t, in_=t), [key], [key])


def rms_norm_group(P, C, pfx, xin, gam, h, sqc, rstd, g):
    ps = C.ps
    for c in range(8):
        P.act(sqc[c % 2][:], xin[:, c, :], AF.Square, [pfx + "xin"], [pfx + f"sqc{c % 2}"])
        P.mm(ps[6][:], C.ones_bf[:], sqc[c % 2][:], c == 0, c == 7, [pfx + f"sqc{c % 2}", "const"], ["ps6"])
    P.ts(rstd[:], ps[6][:], 1.0 / D, EPS, ALU.mult, ALU.add, ["ps6"], [pfx + "rstd"])
    rsqrt_inplace(P, rstd[:], pfx + "rstd")
    for c in range(8):
        P.stt(h[:, c, :], xin[:, c, :], gam[:, c:c + 1], rstd[:], ALU.mult, ALU.mult,
              [pfx + "xin", pfx + "rstd", pfx + "gam"], [pfx + "h"])


def ffn_phase(P, C, tag, w_in, w_out, gam_d, xsrc, xdst):
    nc = P.nc
    pfx = tag + "_"
    ps = C.ps
    xs = xsrc.rearrange("(c p) t -> p c t", p=128)
    xd = xdst.rearrange("(c p) t -> p c t", p=128)
    with ExitStack() as es:
        sb = lambda n, shp, dt: es.enter_context(nc.sbuf_tensor(pfx + n, shp, dt))
        win = sb("win", [128, 8, 2 * DFF], BF16)
        wout = sb("wout", [128, 22, D], BF16)
        gam = sb("gam", [128, 8], F32)
        xin = sb("xin", [128, 8, NT], F32)
        h = sb("h", [128, 8, NT], BF16)
        sqc = [sb(f"sqc{i}", [128, NT], BF16) for i in range(2)]
        rstd = sb("rstd", [128, NT], F32)
        a = sb("a", [128, 22, NT], BF16)
        sg = [sb(f"sg{i}", [128, NT], F32) for i in range(2)]
        xr = [sb(f"xr{i}", [128, NT], F32) for i in range(2)]
        xo = [sb(f"xo{i}", [128, NT], F32) for i in range(2)]

        P.dma(gam[:], gam_d, [], [pfx + "gam"])
        P.dma(xin[:], xs[:, :, 0:NT], ["xdram_" + tag + "s0"], [pfx + "xin"])
        for c in range(8):
            P.dma(win[:, c, :], w_in[c * 128:(c + 1) * 128, :], [], [pfx + f"win{c}"], q="pool")
        wo_r = w_out.rearrange("(j p) m -> p j m", p=128)
        for j0 in range(0, 22, 6):
            j1 = min(22, j0 + 6)
            P.dma(wout[:, j0:j1, :], wo_r[:, j0:j1, :], [], [pfx + f"wout{jj}" for jj in range(j0, j1)], q="pool")
        WIN = [pfx + f"win{c}" for c in range(8)]
        WOUT = [pfx + f"wout{j}" for j in range(22)]

        rms_norm_group(P, C, pfx, xin, gam, h, sqc, rstd, 0)
        for g in range(NG):
            cols = slice(g * NT, (g + 1) * NT)
            if g + 1 < NG:
                P.dma(xin[:], xs[:, :, (g + 1) * NT:(g + 2) * NT], [], [pfx + "xin"])
            for j in range(22):
                bG, bU = ps[(j % 2) * 2], ps[(j % 2) * 2 + 1]
                kG, kU = f"ps{(j % 2) * 2}", f"ps{(j % 2) * 2 + 1}"
                for c in range(8):
                    P.mm(bG[:], win[:, c, j * 128:(j + 1) * 128], h[:, c, :], c == 0, c == 7,
                         [pfx + "h", WIN[c]], [kG])
                for c in range(8):
                    P.mm(bU[:], win[:, c, DFF + j * 128:DFF + (j + 1) * 128], h[:, c, :], c == 0, c == 7,
                         [pfx + "h", WIN[c]], [kU])
                P.act(sg[j % 2][:], bG[:], AF.Silu, [kG], [pfx + f"sg{j % 2}"])
                P.tt(a[:, j, :], sg[j % 2][:], bU[:], ALU.mult, [pfx + f"sg{j % 2}", kU], [pfx + f"a{j}"])
            if g + 1 < NG:
                rms_norm_group(P, C, pfx, xin, gam, h, sqc, rstd, g + 1)
            for m in range(8):
                P.dma(xr[m % 2][:], xs[:, m, cols], [], [pfx + f"xr{m % 2}"])
                bO, kO = ps[4 + m % 2], f"ps{4 + m % 2}"
                for j in range(22):
                    P.mm(bO[:], wout[:, j, m * 128:(m + 1) * 128], a[:, j, :], j == 0, j == 21,
                         [pfx + f"a{j}", WOUT[j]], [kO])
                P.stt(xo[m % 2][:], bO[:], 0.5, xr[m % 2][:], ALU.mult, ALU.add,
                      [kO, pfx + f"xr{m % 2}"], [pfx + f"xo{m % 2}"])
                P.dma(xd[:, m, cols], xo[m % 2][:], [pfx + f"xo{m % 2}"], [])
        P.flush()


def inproj_phase(P, C, tag, w_mix, gam_d, xsrc, T):
    nc = P.nc
    pfx = tag + "_"
    ps = C.ps
    xs = xsrc.rearrange("(c p) t -> p c t", p=128)
    with ExitStack() as es:
        sb = lambda n, shp, dt: es.enter_context(nc.sbuf_tensor(pfx + n, shp, dt))
        wm = sb("wm", [128, 8, 3072], BF16)
        wsw = sb("wsw", [128, 8, 512], BF16)
        gam = sb("gam", [128, 8], F32)
        xin = sb("xin", [128, 8, NT], F32)
        h = sb("h", [128, 8, NT], BF16)
        sqc = [sb(f"sqc{i}", [128, NT], BF16) for i in range(2)]
        rstd = sb("rstd", [128, NT], F32)
        cosT = sb("cosT", [128, NLOC], F32)
        sinT = sb("sinT", [128, NLOC], F32)
        decq = sb("decq", [128, 2, NT], F32)
        deck = sb("deck", [128, 256], F32)
        t1 = [sb(f"t1{i}", [128, NT], F32) for i in range(2)]
        t2 = [sb(f"t2{i}", [128, NT], F32) for i in range(2)]
        ob = [sb(f"ob{i}", [128, NT], BF16) for i in range(4)]
        of = [sb(f"of{i}", [128, NT], F32) for i in range(2)]
        krb = [sb(f"krb{i}", [128, NT], BF16) for i in range(2)]
        kdb = [sb(f"kdb{i}", [128, 256], BF16) for i in range(2)]

        P.dma(gam[:], gam_d, [], [pfx + "gam"])
        P.dma(xin[:], xs[:, :, 0:NT], [], [pfx + "xin"])
        P.dma(cosT[:], C.rope_cos, [], [pfx + "tab"])
        P.dma(sinT[:], C.rope_sin, [], [pfx + "tab"])
        P.dma(decq[:], C.decq, [], [pfx + "tab"])
        P.dma(deck[:], C.deck, [], [pfx + "tab"])
        for c in range(8):
            P.dma(wm[:, c, :], w_mix[c * 128:(c + 1) * 128, :], [], [pfx + "wm"], q="pool")
            src = w_mix[c * 128:(c + 1) * 128, 2048:2560].rearrange("p (h two d) -> p h two d", two=2, d=32)
            dst = wsw[:, c, :].rearrange("p (h two d) -> p h two d", two=2, d=32)
            P.dma(dst[:, :, 0, :], src[:, :, 1, :], [], [pfx + "wm"], q="pool")
            P.dma(dst[:, :, 1, :], src[:, :, 0, :], [], [pfx + "wm"], q="pool")

        cnt = {"b": 0, "ob": 0, "of": 0, "t": 0, "kr": 0, "kd": 0}

        def bank():
            i = cnt["b"] % 6
            cnt["b"] += 1
            return ps[i], f"ps{i}"

        def proj(bk, kk, wt, c0, ncols):
            for c in range(8):
                P.mm(bk[:, 0:NT] if ncols == 128 else bk[:], wt[:, c, c0:c0 + 128], h[:, c, :], c == 0, c == 7,
                     [pfx + "h", pfx + "wm"], [kk])

        for g in range(NG):
            cols = slice(g * NT, (g + 1) * NT)
            rms_norm_group(P, C, pfx, xin, gam, h, sqc, rstd, g)
            if g + 1 < NG:
                P.dma(xin[:], xs[:, :, (g + 1) * NT:(g + 2) * NT], [], [pfx + "xin"])
            for cc in range(2):
                bA, kA = bank()
                proj(bA, kA, wm, cc * 128, 128)
                bB, kB = bank()
                proj(bB, kB, wm, 256 + cc * 128, 128)
                i = cnt["t"] % 2
                cnt["t"] += 1
                P.act(t1[i][:], bB[:], AF.Sigmoid, [kB], [pfx + f"t1{i}"])
                o = cnt["ob"] % 4
                cnt["ob"] += 1
                P.tt(ob[o][:], bA[:], t1[i][:], ALU.mult, [kA, pfx + f"t1{i}"], [pfx + f"ob{o}"])
                P.dma(T["vconv"][cc * 128:(cc + 1) * 128, cols], ob[o][:], [pfx + f"ob{o}"], [])
            for sec, dname, scl in ((512, "qsT", 0.125), (1024, "kT", 1.0)):
                for cc in range(4):
                    bk, kk = bank()
                    proj(bk, kk, wm, sec + cc * 128, 128)
                    o = cnt["ob"] % 4
                    cnt["ob"] += 1
                    P.ts(ob[o][:], bk[:], scl, None, ALU.mult, None, [kk], [pfx + f"ob{o}"])
                    P.dma(T[dname][cc * 128:(cc + 1) * 128, cols], ob[o][:], [pfx + f"ob{o}"], [])
            for tb in range(4):
                tsl = slice(tb * 128, (tb + 1) * 128)
                rows = slice(g * NT + tb * 128, g * NT + (tb + 1) * 128)
                bk, kk = bank()
                for c in range(8):
                    P.mm(bk[:], h[:, c, tsl], wm[:, c, 1536:2048], c == 0, c == 7, [pfx + "h", pfx + "wm"], [kk])
                o = cnt["ob"] % 4
                cnt["ob"] += 1
                P.act(ob[o][:], bk[:], AF.Copy, [kk], [pfx + f"ob{o}"])
                P.dma(T["vtok"][rows, :], ob[o][:], [pfx + f"ob{o}"], [])
                bk, kk = bank()
                for c in range(8):
                    P.mm(bk[:, 0:256], h[:, c, tsl], wm[:, c, 2560:2816], c == 0, c == 7,
                         [pfx + "h", pfx + "wm"], [kk])
                o = cnt["ob"] % 4
                cnt["ob"] += 1
                P.act(ob[o][:, 0:256], bk[:, 0:256], AF.Copy, [kk], [pfx + f"ob{o}"])
                P.dma(T["vr"][rows, :], ob[o][:, 0:256], [pfx + f"ob{o}"], [])
            for isk, sec, swc in ((0, 2048, 0), (1, 2304, 256)):
                for cc in range(2):
                    bX, kX = bank()
                    proj(bX, kX, wm, sec + cc * 128, 128)
                    bS, kS = bank()
                    proj(bS, kS, wsw, swc + cc * 128, 128)
                    i = cnt["t"] % 2
                    cnt["t"] += 1
                    P.tt(t1[i][:], bX[:], cosT[:, cols], ALU.mult, [kX, pfx + "tab"], [pfx + f"t1{i}"])
                    P.tt(t2[i][:], bS[:], sinT[:, cols], ALU.mult, [kS, pfx + "tab"], [pfx + f"t2{i}"])
                    P.tt(t1[i][:], t1[i][:], t2[i][:], ALU.add, [pfx + f"t1{i}", pfx + f"t2{i}"], [pfx + f"t1{i}"],
                         eng="pool")
                    frows = slice(cc * 128, (cc + 1) * 128)
                    if not isk:
                        o = cnt["ob"] % 4
                        cnt["ob"] += 1
                        P.ts(ob[o][:], t1[i][:], 0.125, None, ALU.mult, None, [pfx + f"t1{i}"], [pfx + f"ob{o}"],
                             eng="pool")
                        P.dma(T["qrT"][frows, cols], ob[o][:], [pfx + f"ob{o}"], [])
                        o = cnt["ob"] % 4
                        cnt["ob"] += 1
                        P.tt(ob[o][:], t1[i][:], decq[:, cc, :], ALU.mult, [pfx + f"t1{i}", pfx + "tab"],
                             [pfx + f"ob{o}"], eng="pool")
                        P.dma(T["qdT"][frows, cols], ob[o][:], [pfx + f"ob{o}"], [])
                    else:
                        r = cnt["kr"] % 2
                        cnt["kr"] += 1
                        P.act(krb[r][:], t1[i][:], AF.Copy, [pfx + f"t1{i}"], [pfx + f"krb{r}"])
                        P.dma(T["krT"][frows, cols], krb[r][:], [pfx + f"krb{r}"], [])
                        for tb in range(4):
                            rows = slice(g * NT + tb * 128, g * NT + (tb + 1) * 128)
                            P.op("pe", lambda e, r=r, tb=tb: e.transpose(C.pst[:, 0:128],
                                                                          krb[r][:, tb * 128:(tb + 1) * 128],
                                                                          C.ident[:]),
                                 [pfx + f"krb{r}", "const"], ["pst"])
                            d = cnt["kd"] % 2
                            cnt["kd"] += 1
                            P.tt(kdb[d][:, 0:128], C.pst[:, 0:128], deck[:, cc * 128:(cc + 1) * 128], ALU.mult,
                                 ["pst", pfx + "tab"], [pfx + f"kdb{d}"])
                            P.dma(T["kdtok"][rows, cc * 128:(cc + 1) * 128], kdb[d][:, 0:128],
                                  [pfx + f"kdb{d}"], [])
            for cc in range(2):
                bk, kk = bank()
                proj(bk, kk, wm, 2816 + cc * 128, 128)
                o = cnt["of"] % 2
                cnt["of"] += 1
                P.act(of[o][:], bk[:], AF.Silu, [kk], [pfx + f"of{o}"])
                P.dma(T["gsil"][cc * 128:(cc + 1) * 128, cols], of[o][:], [pfx + f"of{o}"], [])
        P.flush()


def retconv_phase(P, C, tag, T, cw_d, cb_d, lg_d, lb_d, rg_d):
    nc = P.nc
    pfx = tag + "_"
    ps = C.ps
    yT = T["yT"]
    with ExitStack() as es:
        sb = lambda n, shp, dt: es.enter_context(nc.sbuf_tensor(pfx + n, shp, dt))
        vpad = sb("vpad", [128, 2, 32 + NLOC], BF16)
        dg = sb("dg", [128, 62, 128], BF16)
        yc = [sb(f"yc{i}", [128, NT], F32) for i in range(2)]
        cw = sb("cw", [128, 2, 31], F32)
        cb = sb("cb", [128, 2], F32)
        lg = sb("lg", [128, 2], F32)
        lb = sb("lb", [128, 2], F32)
        rg = sb("rg", [64, 4], F32)
        kdq = sb("kdq", [64, 4 * NLOC], BF16)
        kd64 = kdq[:, :].rearrange("p (n c) -> p n c", c=256)
        vr64 = sb("vr64", [64, 64, 256], BF16)
        sball = sb("sball", [64, 64, 256], BF16)
        S = sb("S", [64, 256], F32)
        g64 = sb("g64", [64, 256], F32)
        dintra = sb("dintra", [64, 4, NT], F32)
        qr = [kdq[:, 0:NLOC]] * 2
        kr = [kdq[:, NLOC:2 * NLOC]] * 2
        qd = [kdq[:, 2 * NLOC:3 * NLOC]] * 2
        gs = [sb(f"gs{i}", [64, NT], F32) for i in range(2)]
        stb = [sb(f"stb{i}", [64, NT], BF16) for i in range(2)]
        osb = sb("osb", [64, NT], F32)
        osq = sb("osq", [64, NT], F32)
        mean = sb("mean", [128, NT], F32)
        msq = sb("msq", [128, NT], F32)
        var = sb("var", [128, NT], F32)
        tn = [sb(f"tn{i}", [128, NT], F32) for i in range(2)]
        yb = [sb(f"yb{i}", [128, NT], BF16) for i in range(2)]
        sqy = [sb(f"sqy{i}", [128, NT], F32) for i in range(2)]

        for t, d in ((cw, cw_d), (cb, cb_d), (lg, lg_d), (lb, lb_d), (rg, rg_d), (g64, C.g64), (dintra, C.dintra)):
            P.dma(t[:], d, [], [pfx + "tab"])
        for cc in range(2):
            P.dma(vpad[:, cc, 0:32], T["vhalo"][cc * 128:(cc + 1) * 128, :], [], [pfx + "vpad"])
            P.dma(vpad[:, cc, 32:], T["vconv"][cc * 128:(cc + 1) * 128, :], [], [pfx + "vpad"])
            for k in range(31):
                P.ts(dg[:, cc * 31 + k, :], C.ident[:], cw[:, cc, k:k + 1], None, ALU.mult, None,
                     ["const", pfx + "tab"], [pfx + "dg"])
        P.op("dve", lambda e: e.memset(S[:], 0.0), [], [pfx + "S"])
        for half in range(2):
            ksrc = T["kdtok_prev"] if half == 0 else T["kdtok"]
            vsrc = T["vr_prev"] if half == 0 else T["vr"]
            P.dma(kd64, ksrc.rearrange("(n p) c -> p n c", p=64), [], [pfx + "kd64"])
            P.dma(vr64[:], vsrc.rearrange("(n p) c -> p n c", p=64), [], [pfx + "vr64"])
            for n in range(64):
                if half == 1:
                    P.act(sball[:, n, :], S[:], AF.Copy, [pfx + "S"], [pfx + f"sball{n}"])
                for hh in range(4):
                    hs = slice(hh * 64, (hh + 1) * 64)
                    P.mm(ps[7][0:64, hs], kd64[:, n, hs], vr64[:, n, hs], True, True,
                         [pfx + "kd64", pfx + "vr64"], ["ps7"])
                P.tt(S[:], S[:], g64[:], ALU.mult, [pfx + "S", pfx + "tab"], [pfx + "S"])
                P.tt(S[:], S[:], ps[7][0:64, 0:256], ALU.add, [pfx + "S", "ps7"], [pfx + "S"])
        for hh in range(4):
            b = 0
            hrows = slice(hh * 64, (hh + 1) * 64)
            hs = slice(hh * 64, (hh + 1) * 64)
            P.dma(qr[b], T["qrT"][hrows, :], [], [pfx + f"qr{b}", pfx + "kd64"])
            P.dma(kr[b], T["krT"][hrows, :], [], [pfx + f"kr{b}", pfx + "kd64"])
            P.dma(qd[b], T["qdT"][hrows, :], [], [pfx + f"qd{b}", pfx + "kd64"])
            for g in range(NG):
                cols = slice(g * NT, (g + 1) * NT)
                gi = g % 2
                P.dma(gs[gi][:], T["gsil"][hrows, cols], [], [pfx + f"gs{gi}"])
                bS, kS = ps[gi], f"ps{gi}"
                bO, kO = ps[2 + gi], f"ps{2 + gi}"
                for c in range(8):
                    n = g * 8 + c
                    tc_ = slice(n * 64, (n + 1) * 64)
                    P.mm(bS[0:64, c * 64:(c + 1) * 64], kr[b][:, tc_], qr[b][:, tc_], True, True,
                         [pfx + f"kr{b}", pfx + f"qr{b}"], [kS])
                P.tt(stb[gi][:], bS[0:64, :], dintra[:, hh, :], ALU.mult, [kS, pfx + "tab"], [pfx + f"stb{gi}"])
                for c in range(8):
                    n = g * 8 + c
                    tc_ = slice(n * 64, (n + 1) * 64)
                    cs = slice(c * 64, (c + 1) * 64)
                    P.mm(bO[0:64, cs], vr64[:, n, hs], stb[gi][:, cs], True, False,
                         [pfx + "vr64", pfx + f"stb{gi}"], [kO])
                    P.mm(bO[0:64, cs], sball[:, n, hs], qd[b][:, tc_], False, True,
                         [pfx + f"sball{n}", pfx + f"qd{b}"], [kO])
                P.act(osb[:], bO[0:64, :], AF.Copy, [kO], [pfx + "osb"])
                P.act(osq[:], bO[0:64, :], AF.Square, [kO], [pfx + "osq"])
                P.mm(ps[4][0:64, :], C.ones_f[0:64, 0:64], osb[:], True, True, [pfx + "osb", "const"], ["ps4"])
                P.mm(ps[5][0:64, :], C.ones_f[0:64, 0:64], osq[:], True, True, [pfx + "osq", "const"], ["ps5"])
                m_, q_, v_ = mean[0:64, :], msq[0:64, :], var[0:64, :]
                P.ts(m_, ps[4][0:64, :], 1.0 / 64, None, ALU.mult, None, ["ps4"], [pfx + "mean"])
                P.tt(q_, m_, m_, ALU.mult, [pfx + "mean"], [pfx + "msq"])
                P.stt(v_, ps[5][0:64, :], 1.0 / 64, q_, ALU.mult, ALU.subtract, ["ps5", pfx + "msq"], [pfx + "var"])
                P.ts(v_, v_, EPS, None, ALU.add, None, [pfx + "var"], [pfx + "var"])
                rsqrt_inplace(P, v_, pfx + "var")
                t_ = tn[gi][0:64, :]
                P.tt(t_, osb[:], m_, ALU.subtract, [pfx + "osb", pfx + "mean"], [pfx + f"tn{gi}"])
                P.tt(t_, t_, v_, ALU.mult, [pfx + f"tn{gi}", pfx + "var"], [pfx + f"tn{gi}"])
                P.stt(yb[gi][0:64, :], t_, rg[:, hh:hh + 1], gs[gi][:], ALU.mult, ALU.mult,
                      [pfx + f"tn{gi}", pfx + "tab", pfx + f"gs{gi}"], [pfx + f"yb{gi}"])
                P.dma(yT[768 + hh * 64:768 + (hh + 1) * 64, cols], yb[gi][0:64, :], [pfx + f"yb{gi}"], [])
        for g in range(NG):
            cols = slice(g * NT, (g + 1) * NT)
            for cc in range(2):
                for k in range(31):
                    P.mm(ps[cc][:], dg[:, cc * 31 + k, :], vpad[:, cc, 2 + k + g * NT:2 + k + (g + 1) * NT],
                         k == 0, k == 30, [pfx + "dg", pfx + "vpad"], [f"ps{cc}"])
                P.ts(yc[cc][:], ps[cc][:], cb[:, cc:cc + 1], None, ALU.add, None, [f"ps{cc}", pfx + "tab"],
                     [pfx + f"yc{cc}"])
                P.act(sqy[cc][:], yc[cc][:], AF.Square, [pfx + f"yc{cc}"], [pfx + f"sqy{cc}"])
            for cc in range(2):
                P.mm(ps[4][:], C.ones_f[:], yc[cc][:], cc == 0, cc == 1, [pfx + f"yc{cc}", "const"], ["ps4"])
            for cc in range(2):
                P.mm(ps[5][:], C.ones_f[:], sqy[cc][:], cc == 0, cc == 1, [pfx + f"sqy{cc}", "const"], ["ps5"])
            P.ts(mean[:], ps[4][:], 1.0 / 256, None, ALU.mult, None, ["ps4"], [pfx + "mean"])
            P.tt(msq[:], mean[:], mean[:], ALU.mult, [pfx + "mean"], [pfx + "msq"])
            P.stt(var[:], ps[5][:], 1.0 / 256, msq[:], ALU.mult, ALU.subtract, ["ps5", pfx + "msq"], [pfx + "var"])
            P.ts(var[:], var[:], EPS, None, ALU.add, None, [pfx + "var"], [pfx + "var"])
            rsqrt_inplace(P, var[:], pfx + "var")
            for cc in range(2):
                P.tt(tn[cc][:], yc[cc][:], mean[:], ALU.subtract, [pfx + f"yc{cc}", pfx + "mean"],
                     [pfx + f"tn{cc}"])
                P.tt(tn[cc][:], tn[cc][:], var[:], ALU.mult, [pfx + f"tn{cc}", pfx + "var"], [pfx + f"tn{cc}"])
                P.act(yb[cc][:], tn[cc][:], AF.Silu, [pfx + f"tn{cc}", pfx + "tab"], [pfx + f"yb{cc}"],
                      bias=lb[:, cc:cc + 1], scale=lg[:, cc:cc + 1])
                P.dma(yT[cc * 128:(cc + 1) * 128, cols], yb[cc][:], [pfx + f"yb{cc}"], [])
        P.flush()


def sb_phase(P, C, tag, T):
    nc = P.nc
    pfx = tag + "_"
    ps = C.ps
    yT = T["yT"]
    NB = 64
    with ExitStack() as es:
        sb = lambda n, shp, dt: es.enter_context(nc.sbuf_tensor(pfx + n, shp, dt))
        VA = sb("VA", [128, NB, 512], BF16)
        KT = [sb(f"KT{i}", [64, 2 * NLOC], BF16) for i in range(2)]
        qs = [sb(f"qs{i}", [64, NLOC], BF16) for i in range(2)]
        msk = sb("msk", [128, 4, NT], BF16)
        ebuf = [sb(f"e{i}", [128, NT], F32) for i in range(2)]
        spb = [sb(f"sp{i}", [128, NT], BF16) for i in range(3)]
        wb = [sb(f"w{i}", [128, NT], BF16) for i in range(2)]
        sacc = [sb(f"sacc{i}", [128, NT], BF16) for i in range(2)]
        ob = [sb(f"ob{i}", [64, NT], BF16) for i in range(2)]

        P.dma(msk[:], C.sbmask, [], [pfx + "tab"])
        P.dma(VA[:, 0:32, :], T["vtok_prev"].rearrange("(b p) c -> p b c", p=128), [], [pfx + "VA"])
        P.dma(VA[:, 32:64, :], T["vtok"].rearrange("(b p) c -> p b c", p=128), [], [pfx + "VA"])

        def load_head(hh):
            b = hh % 2
            hrows = slice(hh * 64, (hh + 1) * 64)
            P.dma(KT[b][:, 0:NLOC], T["kT_prev"][hrows, :], [], [pfx + f"KT{b}"])
            P.dma(KT[b][:, NLOC:], T["kT"][hrows, :], [], [pfx + f"KT{b}"])
            P.dma(qs[b][:], T["qsT"][hrows, :], [], [pfx + f"qs{b}"])

        tuples = []
        for hh in range(8):
            for g in range(NG):
                blks = [32 + 4 * g + 3 - i for i in range(4 * g + 4)] + [31 - i for i in range(32)]
                for i, kb in enumerate(blks):
                    tuples.append((hh, g, kb, i == 0, i == len(blks) - 1))
        st = {}

        def stage1(idx):
            hh, g, kb, first, last = tuples[idx]
            b = hh % 2
            if g == 0 and first:
                if hh == 0:
                    load_head(0)
                if hh + 1 < 8:
                    load_head(hh + 1)
            i2, i3 = idx % 2, idx % 3
            q = qs[b][:, g * NT:(g + 1) * NT]
            kt = KT[b][:, kb * 128:(kb + 1) * 128]
            bA, kA = ps[i2], f"ps{i2}"
            P.mm(bA[:], kt, q, True, True, [pfx + f"KT{b}", pfx + f"qs{b}"], [kA])
            P.act(ebuf[i2][:], bA[:], AF.Exp, [kA], [pfx + f"e{i2}"])
            P.act(spb[i3][:], ebuf[i2][:], AF.Ln, [pfx + f"e{i2}"], [pfx + f"sp{i3}"], bias=1.0)
            dc = kb - (32 + 4 * g)
            if dc >= 0:
                P.tt(spb[i3][:], spb[i3][:], msk[:, dc, :], ALU.mult, [pfx + f"sp{i3}", pfx + "tab"],
                     [pfx + f"sp{i3}"])
            st[idx] = (i2, i3, dc)

        def stage2(idx):
            hh, g, kb, first, last = tuples[idx]
            b = hh % 2
            i2, i3, dc = st.pop(idx)
            q = qs[b][:, g * NT:(g + 1) * NT]
            kt = KT[b][:, kb * 128:(kb + 1) * 128]
            bB, kB = ps[2 + i2], f"ps{2 + i2}"
            if first:
                cur = None
                nxt = None
            P.mm(bB[:], C.trineg[:], spb[i3][:], True, False, [pfx + f"sp{i3}", "const"], [kB])
            if not first:
                sa = st["sacc"]
                P.mm(bB[:], C.negones[:], sacc[sa][:], False, False, [pfx + f"sacc{sa}", "const"], [kB])
            P.mm(bB[:], kt, q, False, True, [pfx + f"KT{b}", pfx + f"qs{b}"], [kB])
            P.act(wb[i2][:], bB[:], AF.Exp, [kB], [pfx + f"w{i2}"])
            if dc >= 0:
                P.tt(wb[i2][:], wb[i2][:], msk[:, dc, :], ALU.mult, [pfx + f"w{i2}", pfx + "tab"], [pfx + f"w{i2}"])
            gi = (hh * NG + g) % 2
            bO, kO = ps[4 + gi], f"ps{4 + gi}"
            P.mm(bO[0:64, :], VA[:, kb, hh * 64:(hh + 1) * 64], wb[i2][:], first, last,
                 [pfx + "VA", pfx + f"w{i2}"], [kO])
            if not last:
                if first:
                    st["sacc"] = 0
                    P.op("dve", lambda e: e.tensor_copy(out=sacc[0][:], in_=spb[i3][:]),
                         [pfx + f"sp{i3}"], [pfx + "sacc0"])
                else:
                    sa = st["sacc"]
                    P.tt(sacc[1 - sa][:], sacc[sa][:], spb[i3][:], ALU.add, [pfx + f"sacc{sa}", pfx + f"sp{i3}"],
                         [pfx + f"sacc{1 - sa}"])
                    st["sacc"] = 1 - sa
            else:
                P.op("dve", lambda e: e.tensor_copy(out=ob[gi][:], in_=bO[0:64, :]), [kO], [pfx + f"ob{gi}"])
                P.dma(yT[256 + hh * 64:256 + (hh + 1) * 64, g * NT:(g + 1) * NT], ob[gi][:], [pfx + f"ob{gi}"], [])

        n = len(tuples)
        for i in range(n + 1):
            if i < n:
                stage1(i)
            if i >= 1:
                stage2(i - 1)
        P.flush()


def outproj_phase(P, C, tag, w_o, T, xsrc, xdst):
    nc = P.nc
    pfx = tag + "_"
    ps = C.ps
    xs = xsrc.rearrange("(c p) t -> p c t", p=128)
    xd = xdst.rearrange("(c p) t -> p c t", p=128)
    yr = T["yT"].rearrange("(c p) t -> p c t", p=128)
    with ExitStack() as es:
        sb = lambda n, shp, dt: es.enter_context(nc.sbuf_tensor(pfx + n, shp, dt))
        wo = sb("wo", [128, 8, D], BF16)
        yin = [sb(f"yin{i}", [128, 8, NT], BF16) for i in range(2)]
        xin = [sb(f"xin{i}", [128, 8, NT], F32) for i in range(2)]
        xo = [sb(f"xo{i}", [128, 8, NT], F32) for i in range(2)]
        P.dma(wo[:], w_o.rearrange("(c p) m -> p c m", p=128), [], [pfx + "wo"], q="pool")
        for g in range(NG):
            cols = slice(g * NT, (g + 1) * NT)
            b = g % 2
            P.dma(yin[b][:], yr[:, :, cols], [], [pfx + f"yin{b}"])
            P.dma(xin[b][:], xs[:, :, cols], [], [pfx + f"xin{b}"])
            for m in range(8):
                bO, kO = ps[m % 4], f"ps{m % 4}"
                for c in range(8):
                    P.mm(bO[:], wo[:, c, m * 128:(m + 1) * 128], yin[b][:, c, :], c == 0, c == 7,
                         [pfx + "wo", pfx + f"yin{b}"], [kO])
                P.tt(xo[b][:, m, :], bO[:], xin[b][:, m, :], ALU.add, [kO, pfx + f"xin{b}"], [pfx + f"xo{b}"])
            P.dma(xd[:, :, cols], xo[b][:], [pfx + f"xo{b}"], [])
        P.flush()


def final_phase(P, C, tag, gam_d, xsrc, out_d):
    nc = P.nc
    pfx = tag + "_"
    xs = xsrc.rearrange("(c p) t -> p c t", p=128)
    od = out_d.rearrange("(c p) t -> p c t", p=128)
    with ExitStack() as es:
        sb = lambda n, shp, dt: es.enter_context(nc.sbuf_tensor(pfx + n, shp, dt))
        gam = sb("gam", [128, 8], F32)
        xin = [sb(f"xin{i}", [128, 8, NT], F32) for i in range(2)]
        sqc = [sb(f"sqc{i}", [128, NT], BF16) for i in range(2)]
        rstd = sb("rstd", [128, NT], F32)
        yo = [sb(f"yo{i}", [128, 8, NT], F32) for i in range(2)]
        P.dma(gam[:], gam_d, [], [pfx + "gam"])
        for g in range(NG):
            cols = slice(g * NT, (g + 1) * NT)
            b = g % 2
            P.dma(xin[b][:], xs[:, :, cols], [], [pfx + f"xin{b}"])
            for c in range(8):
                P.act(sqc[c % 2][:], xin[b][:, c, :], AF.Square, [pfx + f"xin{b}"], [pfx + f"sqc{c % 2}"])
                P.mm(C.ps[6][:], C.ones_bf[:], sqc[c % 2][:], c == 0, c == 7, [pfx + f"sqc{c % 2}", "const"], ["ps6"])
            P.ts(rstd[:], C.ps[6][:], 1.0 / D, EPS, ALU.mult, ALU.add, ["ps6"], [pfx + "rstd"])
            rsqrt_inplace(P, rstd[:], pfx + "rstd")
            for c in range(8):
                P.stt(yo[b][:, c, :], xin[b][:, c, :], gam[:, c:c + 1], rstd[:], ALU.mult, ALU.mult,
                      [pfx + f"xin{b}", pfx + "rstd", pfx + "gam"], [pfx + f"yo{b}"])
            P.dma(od[:, :, cols], yo[b][:], [pfx + f"yo{b}"], [], is_out=True)


MIXT = {
    "vconv": ([256, NLOC], BF16), "qsT": ([512, NLOC], BF16), "kT": ([512, NLOC], BF16),
    "vtok": ([NLOC, 512], BF16), "vr": ([NLOC, 256], BF16), "qrT": ([256, NLOC], BF16),
    "qdT": ([256, NLOC], BF16), "krT": ([256, NLOC], BF16), "kdtok": ([NLOC, 256], BF16),
    "gsil": ([256, NLOC], F32),
}
PREVT = {"vhalo": ([256, 32], BF16), "kT_prev": ([512, NLOC], BF16), "vtok_prev": ([NLOC, 512], BF16),
         "kdtok_prev": ([NLOC, 256], BF16), "vr_prev": ([NLOC, 256], BF16)}
CONSTS = {"ones_bf": ([128, 128], BF16), "ones_f": ([128, 128], F32), "ident": ([128, 128], BF16),
          "trineg": ([128, 128], BF16), "negones": ([128, 128], BF16)}
TABLES = {"rope_cos": [128, NLOC], "rope_sin": [128, NLOC], "decq": [128, 2, NT], "deck": [128, 256],
          "g64": [64, 256], "dintra": [64, 4, NT]}


def build_program(stage, debug=False):
    nc = bass.Bass("TRN2", target_bir_lowering=False)
    dt = lambda name, shape, dtype, kind: nc.dram_tensor(name, shape, dtype, kind=kind).ap()
    C = Ctx()
    W = {}
    layers = {0: [0], 1: [0, 1], 2: [1]}[stage]
    for l in layers:
        need = {0: ["f1", "mi"], 1: (["mo", "f2"] if l == 0 else ["f1", "mi"]), 2: ["mo", "f2"]}[stage]
        if "f1" in need:
            W[f"ffn1_w_in{l}"] = dt(f"ffn1_w_in{l}", [D, 2 * DFF], F32, "ExternalInput")
            W[f"ffn1_w_out{l}"] = dt(f"ffn1_w_out{l}", [DFF, D], F32, "ExternalInput")
            W[f"ffn1_norm{l}"] = dt(f"ffn1_norm{l}", [128, 8], F32, "ExternalInput")
        if "mi" in need:
            W[f"mix_w_in{l}"] = dt(f"mix_w_in{l}", [D, 3072], F32, "ExternalInput")
            W[f"mix_norm{l}"] = dt(f"mix_norm{l}", [128, 8], F32, "ExternalInput")
        if "mo" in need:
            W[f"mix_w_out{l}"] = dt(f"mix_w_out{l}", [D, D], F32, "ExternalInput")
            W[f"conv_w{l}"] = dt(f"conv_w{l}", [128, 2, 31], F32, "ExternalInput")
            for nm in ("conv_b", "conv_ln_g", "conv_ln_b"):
                W[f"{nm}{l}"] = dt(f"{nm}{l}", [128, 2], F32, "ExternalInput")
            W[f"ret_norm_g{l}"] = dt(f"ret_norm_g{l}", [64, 4], F32, "ExternalInput")
        if "f2" in need:
            W[f"ffn2_w_in{l}"] = dt(f"ffn2_w_in{l}", [D, 2 * DFF], F32, "ExternalInput")
            W[f"ffn2_w_out{l}"] = dt(f"ffn2_w_out{l}", [DFF, D], F32, "ExternalInput")
            W[f"ffn2_norm{l}"] = dt(f"ffn2_norm{l}", [128, 8], F32, "ExternalInput")
    if stage == 2:
        W["final_norm"] = dt("final_norm", [128, 8], F32, "ExternalInput")
    cd = {k: dt("c_" + k, s, d, "ExternalInput") for k, (s, d) in CONSTS.items()}
    for k, s in TABLES.items():
        setattr(C, k, dt("t_" + k, s, F32, "ExternalInput"))
    C.sbmask = dt("t_sbmask", [128, 4, NT], BF16, "ExternalInput")
    x_in = dt("x_in", [D, NLOC], F32, "ExternalInput")
    ik = "ExternalOutput" if debug else "Internal"
    xa = dt("xa", [D, NLOC], F32, ik)
    xb = dt("xb", [D, NLOC], F32, ik)
    Tin, Tout = {}, {}
    if stage >= 1:
        for k, (s, d) in list(MIXT.items()) + list(PREVT.items()):
            Tin[k] = dt("i_" + k, s, d, "ExternalInput")
        Tin["yT"] = dt("yT", [D, NLOC], BF16, ik)
    if stage <= 1:
        for k, (s, d) in MIXT.items():
            Tout[k] = dt("o_" + k, s, d, "ExternalOutput")
        x_out = dt("x_out", [D, NLOC], F32, "ExternalOutput")
    else:
        out = dt("out", [D, NLOC], F32, "ExternalOutput")

    with ExitStack() as es:
        P = Prog(nc, es)
        C.ps = [es.enter_context(nc.psum_tensor(f"ps{i}", [128, NT], F32)) for i in range(7)]
        C.ps.append(C.ps[6])
        pst = es.enter_context(nc.psum_tensor("pst", [128, 2 * NT], BF16))
        C.pst = pst
        C.ps[7] = pst.bitcast(F32)
        for k, (s, d) in CONSTS.items():
            t = es.enter_context(nc.sbuf_tensor("k_" + k, s, d))
            setattr(C, k, t)
            P.dma(t[:], cd[k], [], ["const"])
        P.flush()
        if stage == 0:
            ffn_phase(P, C, "f1a", W["ffn1_w_in0"], W["ffn1_w_out0"], W["ffn1_norm0"], x_in, x_out)
            inproj_phase(P, C, "ipa", W["mix_w_in0"], W["mix_norm0"], x_out, Tout)
            P.flush(final=True)
        elif stage == 1:
            retconv_phase(P, C, "rca", Tin, W["conv_w0"], W["conv_b0"], W["conv_ln_g0"], W["conv_ln_b0"],
                          W["ret_norm_g0"])
            sb_phase(P, C, "sba", Tin)
            outproj_phase(P, C, "opa", W["mix_w_out0"], Tin, x_in, xa)
            ffn_phase(P, C, "f2a", W["ffn2_w_in0"], W["ffn2_w_out0"], W["ffn2_norm0"], xa, xb)
            ffn_phase(P, C, "f1b", W["ffn1_w_in1"], W["ffn1_w_out1"], W["ffn1_norm1"], xb, x_out)
            inproj_phase(P, C, "ipb", W["mix_w_in1"], W["mix_norm1"], x_out, Tout)
            P.flush(final=True)
        else:
            retconv_phase(P, C, "rcb", Tin, W["conv_w1"], W["conv_b1"], W["conv_ln_g1"], W["conv_ln_b1"],
                          W["ret_norm_g1"])
            sb_phase(P, C, "sbb", Tin)
            outproj_phase(P, C, "opb", W["mix_w_out1"], Tin, x_in, xa)
            ffn_phase(P, C, "f2b", W["ffn2_w_in1"], W["ffn2_w_out1"], W["ffn2_norm1"], xa, xb)
            final_phase(P, C, "fin", W["final_norm"], xb, out)
            P.flush(final=True)
    return nc


def _consts():
    bf = ml_dtypes.bfloat16
    j = np.arange(128)
    c = {
        "c_ones_bf": np.ones((128, 128), bf), "c_ones_f": np.ones((128, 128), np.float32),
        "c_ident": np.eye(128).astype(bf),
        "c_trineg": (-(j[:, None] >= j[None, :]).astype(np.float32)).astype(bf),
        "c_negones": (-np.ones((128, 128), np.float32)).astype(bf),
    }
    gam = 1.0 - np.exp2(-5.0 - np.arange(4, dtype=np.float64))
    i512 = np.arange(NT)
    p = np.arange(128)
    decq = np.zeros((128, 2, NT))
    for cc in range(2):
        hh = 2 * cc + p // 64
        decq[:, cc, :] = 0.125 * gam[hh][:, None] ** ((i512 % 64) + 1.0)[None, :]
    col = np.arange(256)
    deck = gam[col // 64][None, :] ** (63.0 - (p % 64))[:, None]
    g64 = np.broadcast_to((gam[col // 64] ** 64.0)[None, :], (64, 256))
    jj = np.arange(64)
    dintra = np.zeros((64, 4, NT))
    for hh in range(4):
        dintra[:, hh, :] = gam[hh] ** np.abs(jj[:, None] - (i512 % 64)[None, :])
    sbmask = np.zeros((128, 4, NT), np.float32)
    for cc in range(4):
        sbmask[:, cc, :] = ((cc * 128 + p)[:, None] < i512[None, :])
    c.update({"t_decq": decq.astype(np.float32), "t_deck": deck.astype(np.float32),
              "t_g64": np.ascontiguousarray(g64).astype(np.float32), "t_dintra": dintra.astype(np.float32),
              "t_sbmask": sbmask.astype(bf)})
    return c


def _rope(half):
    pos = (half * NLOC + np.arange(NLOC)).astype(np.float32)
    inv = (1.0 / (10000.0 ** (np.arange(32, dtype=np.float32) / 32))).astype(np.float32)
    p = np.arange(128)
    ang = pos[None, :] * inv[(p % 64) % 32][:, None]
    sign = np.where((p % 64) < 32, -1.0, 1.0)[:, None]
    return np.cos(ang).astype(np.float32), (sign * np.sin(ang)).astype(np.float32)


def _vec8(v):
    return np.ascontiguousarray(v.reshape(8, 128).T)


def _vec2(v):
    return np.ascontiguousarray(v.reshape(2, 128).T)


def kernel(**inp):
    ncores = 8
    x = inp["x"]
    consts = _consts()
    ropes = [_rope(0), _rope(1)]
    progs = [build_program(s) for s in range(3)]

    def weights(stage):
        w = {}
        for l in {0: [0], 1: [0, 1], 2: [1]}[stage]:
            need = {0: ["f1", "mi"], 1: (["mo", "f2"] if l == 0 else ["f1", "mi"]), 2: ["mo", "f2"]}[stage]
            for f in ("1", "2"):
                if "f" + f in need:
                    w[f"ffn{f}_w_in{l}"] = np.ascontiguousarray(inp[f"ffn{f}_w_in"][l])
                    w[f"ffn{f}_w_out{l}"] = np.ascontiguousarray(inp[f"ffn{f}_w_out"][l])
                    w[f"ffn{f}_norm{l}"] = _vec8(inp[f"ffn{f}_norm"][l])
            if "mi" in need:
                w[f"mix_w_in{l}"] = np.ascontiguousarray(inp["mix_w_in"][l])
                w[f"mix_norm{l}"] = _vec8(inp["mix_norm"][l])
            if "mo" in need:
                w[f"mix_w_out{l}"] = np.ascontiguousarray(inp["mix_w_out"][l])
                cw = inp["conv_w"][l]
                w[f"conv_w{l}"] = np.ascontiguousarray(cw.T.reshape(2, 128, 31).transpose(1, 0, 2))
                for nm in ("conv_b", "conv_ln_g", "conv_ln_b"):
                    w[f"{nm}{l}"] = _vec2(inp[nm][l])
                w[f"ret_norm_g{l}"] = np.ascontiguousarray(inp["ret_norm_g"][l].reshape(4, 64).T)
        if stage == 2:
            w["final_norm"] = _vec8(inp["final_norm"])
        return w

    def run(stage, xs, mix):
        w = weights(stage)
        maps = []
        for core in range(ncores):
            b, half = core // 2, core % 2
            m = dict(w)
            m.update(consts)
            m["t_rope_cos"], m["t_rope_sin"] = ropes[half]
            m["x_in"] = xs[core]
            if stage >= 1:
                for k in MIXT:
                    m["i_" + k] = mix[core][k]
                if half == 1:
                    pm = mix[core - 1]
                    m["i_vhalo"] = np.ascontiguousarray(pm["vconv"][:, NLOC - 32:])
                    m["i_kT_prev"], m["i_vtok_prev"] = pm["kT"], pm["vtok"]
                    m["i_kdtok_prev"], m["i_vr_prev"] = pm["kdtok"], pm["vr"]
                else:
                    for k, (s, d) in PREVT.items():
                        m["i_" + k] = np.zeros(s, np.float32 if d == F32 else ml_dtypes.bfloat16)
            maps.append(m)
        res = run_bass_kernel_spmd(progs[stage], maps, core_ids=list(range(ncores)))
        return res.results

    xs = [np.ascontiguousarray(x[c // 2, (c % 2) * NLOC:(c % 2 + 1) * NLOC, :].T) for c in range(ncores)]
    r0 = run(0, xs, None)
    xs = [r["x_out"] for r in r0]
    mix = [{k: r["o_" + k] for k in MIXT} for r in r0]
    r1 = run(1, xs, mix)
    xs = [r["x_out"] for r in r1]
    mix = [{k: r["o_" + k] for k in MIXT} for r in r1]
    r2 = run(2, xs, mix)
    out = np.empty(x.shape, np.float32)
    for c in range(ncores):
        out[c // 2, (c % 2) * NLOC:(c % 2 + 1) * NLOC, :] = r2[c]["out"].T
    return out
```

```python
import numpy as np
import ml_dtypes
from contextlib import ExitStack
import concourse.bass as bass
import concourse.mybir as mybir
from concourse.bass_utils import run_bass_kernel_spmd

F32 = mybir.dt.float32
BF16 = mybir.dt.bfloat16
AF = mybir.ActivationFunctionType
ALU = mybir.AluOpType

D = 1024
DFF = 2816
NLOC = 4096
NSLOT = 8192
NT = 512
NG = NLOC // NT
DEPTH = 2
EPS = 1e-6
NDS = 8


class Op:
    __slots__ = ("stream", "fn", "deps", "dma", "token", "needed", "phase", "wkeys")


class Prog:
    STREAMS = ["pe", "act", "dve", "pool", "sp"]

    def __init__(self, nc, es):
        self.nc = nc
        self.csem = {s: es.enter_context(nc.semaphore("c_" + s)) for s in ["pe", "act", "dve", "pool"]}
        self.dsem = {s: [es.enter_context(nc.semaphore(f"d_{s}{i}")) for i in range(NDS)]
                     for s in ["sp", "pool"]}
        self.ccnt = {s: 0 for s in self.csem}
        self.dcnt = {s: [0] * NDS for s in self.dsem}
        self.dnum = {s: 0 for s in self.dsem}
        self.lastw = {}
        self.readers = {}
        self.ops = []
        self.phase = 0
        self.waited = {s: {} for s in self.STREAMS}
        self.pending = {s: {} for s in self.STREAMS}
        self.out_tokens = []

    def op(self, stream, fn, reads=(), writes=(), dma=False, is_out=False):
        o = Op()
        o.stream, o.fn, o.dma, o.needed, o.phase, o.token = stream, fn, dma, False, self.phase, None
        o.wkeys = set(writes)
        deps = {}
        for k in reads:
            w = self.lastw.get(k)
            if w is not None:
                deps[id(w)] = (w, True)
        for k in writes:
            w = self.lastw.get(k)
            if w is not None and id(w) not in deps:
                deps[id(w)] = (w, False)
            for r in self.readers.get(k, {}).values():
                if id(r) not in deps:
                    deps[id(r)] = (r, False)
        o.deps = []
        for d, raw in deps.values():
            if d.phase != self.phase or d is o:
                continue
            if d.stream == stream and not d.dma and not dma:
                if stream == "pe" or not raw:
                    continue
            o.deps.append(d)
        if dma:
            i = self.dnum[stream] % NDS
            self.dnum[stream] += 1
            self.dcnt[stream][i] += 16
            o.token = (self.dsem[stream][i], self.dcnt[stream][i])
            rkey = (stream, i)
            if is_out:
                self.out_tokens.append(o.token)
        else:
            rkey = (stream, -1)
        for k in reads:
            self.readers.setdefault(k, {})[rkey] = o
        for k in writes:
            self.lastw[k] = o
            self.readers[k] = {}
        self.ops.append(o)
        return o

    def flush(self, final=False):
        ops = self.ops
        for o in ops:
            for d in o.deps:
                d.needed = True
        last = {}
        for o in ops:
            if not o.dma:
                last[o.stream] = o
        for o in last.values():
            o.needed = True
        for o in ops:
            if not o.dma and o.needed:
                self.ccnt[o.stream] += 1
                o.token = (self.csem[o.stream], self.ccnt[o.stream])
        by_stream = {s: [o for o in ops if o.stream == s] for s in self.STREAMS}
        snapshot = {}
        for s in self.csem:
            if self.ccnt[s]:
                snapshot[id(self.csem[s])] = (self.csem[s], self.ccnt[s])
        for s in self.dsem:
            for i in range(NDS):
                if self.dcnt[s][i]:
                    snapshot[id(self.dsem[s][i])] = (self.dsem[s][i], self.dcnt[s][i])

        def emit(stream, eng):
            waited = self.waited[stream]
            for o in by_stream[stream]:
                w = dict(self.pending[stream])
                self.pending[stream] = {}
                for d in o.deps:
                    sem, val = d.token
                    if id(sem) not in w or w[id(sem)][1] < val:
                        w[id(sem)] = (sem, val)
                for sid, (sem, val) in w.items():
                    if waited.get(sid, 0) < val:
                        eng.wait_ge(sem, val)
                        waited[sid] = val
                ins = o.fn(eng)
                if o.token is not None:
                    ins.then_inc(o.token[0], 16 if o.dma else 1)
            if final and stream == "sp":
                for sid, (sem, val) in snapshot.items():
                    if waited.get(sid, 0) < val:
                        eng.wait_ge(sem, val)
                        waited[sid] = val

        with self.nc.Block() as blk:
            if by_stream["pe"]:
                blk.tensor(lambda e: emit("pe", e))
            if by_stream["act"]:
                blk.scalar(lambda e: emit("act", e))
            if by_stream["dve"]:
                blk.vector(lambda e: emit("dve", e))
            if by_stream["pool"]:
                blk.gpsimd(lambda e: emit("pool", e))
            if by_stream["sp"] or final:
                blk.sync(lambda e: emit("sp", e))
        for s in self.STREAMS:
            p = self.pending[s]
            for sid, (sem, val) in snapshot.items():
                if sid not in p or p[sid][1] < val:
                    p[sid] = (sem, val)
        self.ops = []
        self.phase += 1

    def dma(self, out, in_, reads, writes, q="sp", is_out=False):
        return self.op(q, lambda e: e.dma_start(out=out, in_=in_), reads, writes, dma=True, is_out=is_out)

    def mm(self, out, lhsT, rhs, start, stop, reads, writes):
        return self.op("pe", lambda e: e.matmul(out, lhsT, rhs, start=start, stop=stop), reads, writes)

    def act(self, out, in_, func, reads, writes, bias=None, scale=None):
        kw = {}
        if bias is not None:
            kw["bias"] = bias
        if scale is not None:
            kw["scale"] = scale
        return self.op("act", lambda e: e.activation(out=out, in_=in_, func=func, **kw), reads, writes)

    def tt(self, out, in0, in1, op, reads, writes, eng="dve"):
        return self.op(eng, lambda e: e.tensor_tensor(out=out, in0=in0, in1=in1, op=op), reads, writes)

    def ts(self, out, in0, s1, s2, op0, op1, reads, writes, eng="dve"):
        if op1 is None:
            return self.op(eng, lambda e: e.tensor_scalar(out=out, in0=in0, scalar1=s1, scalar2=None, op0=op0),
                           reads, writes)
        return self.op(eng, lambda e: e.tensor_scalar(out=out, in0=in0, scalar1=s1, scalar2=s2, op0=op0, op1=op1),
                       reads, writes)

    def stt(self, out, in0, scalar, in1, op0, op1, reads, writes, eng="dve"):
        return self.op(eng, lambda e: e.scalar_tensor_tensor(out=out, in0=in0, scalar=scalar, in1=in1,
                                                             op0=op0, op1=op1), reads, writes)


class Ctx:
    pass


def rsqrt_inplace(P, t, key):
    P.act(t, t, AF.Sqrt, [key], [key])
    P.op("dve", lambda e: e.reciprocal(out=t, in_=t), [key], [key])


def rms_norm_group(P, C, pfx, xin, gam, h, sqc, rstd, g):
    ps = C.ps
    for c in range(8):
        P.act(sqc[c % 2][:], xin[:, c, :], AF.Square, [pfx + "xin"], [pfx + f"sqc{c % 2}"])
        P.mm(ps[6][:], C.ones_bf[:], sqc[c % 2][:], c == 0, c == 7, [pfx + f"sqc{c % 2}", "const"], ["ps6"])
    P.ts(rstd[:], ps[6][:], 1.0 / D, EPS, ALU.mult, ALU.add, ["ps6"], [pfx + "rstd"])
    rsqrt_inplace(P, rstd[:], pfx + "rstd")
    for c in range(8):
        P.stt(h[:, c, :], xin[:, c, :], gam[:, c:c + 1], rstd[:], ALU.mult, ALU.mult,
              [pfx + "xin", pfx + "rstd", pfx + "gam"], [pfx + "h"])


def ffn_phase(P, C, tag, w_in, w_out, gam_d, xsrc, xdst, groups, flag_prefix=False):
    nc = P.nc
    pfx = tag + "_"
    ps = C.ps
    xs = xsrc.rearrange("(c p) t -> p c t", p=128)
    xd = xdst.rearrange("(c p) t -> p c t", p=128)
    with ExitStack() as es:
        sb = lambda n, shp, dt: es.enter_context(nc.sbuf_tensor(pfx + n, shp, dt))
        win = sb("win", [128, 8, 2 * DFF], BF16)
        wout = sb("wout", [128, 22, D], BF16)
        gam = sb("gam", [128, 8], F32)
        xin = sb("xin", [128, 8, NT], F32)
        h = sb("h", [128, 8, NT], BF16)
        sqc = [sb(f"sqc{i}", [128, NT], BF16) for i in range(2)]
        rstd = sb("rstd", [128, NT], F32)
        a = sb("a", [128, 22, NT], BF16)
        sg = [sb(f"sg{i}", [128, NT], F32) for i in range(2)]
        xr = [sb(f"xr{i}", [128, NT], F32) for i in range(2)]
        xo = [sb(f"xo{i}", [128, NT], F32) for i in range(2)]
        flg = sb("flg", [128, 1], F32)

        P.dma(gam[:], gam_d, [], [pfx + "gam"])
        P.dma(flg[:], C.flag, [], [pfx + "flg"])
        g0 = groups[0]
        P.dma(xin[:], xs[:, :, g0 * NT:(g0 + 1) * NT], [], [pfx + "xin"])
        for c in range(8):
            P.dma(win[:, c, :], w_in[c * 128:(c + 1) * 128, :], [], [pfx + f"win{c}"], q="pool")
        wo_r = w_out.rearrange("(j p) m -> p j m", p=128)
        for j0 in range(0, 22, 6):
            j1 = min(22, j0 + 6)
            P.dma(wout[:, j0:j1, :], wo_r[:, j0:j1, :], [], [pfx + f"wout{jj}" for jj in range(j0, j1)], q="pool")
        WIN = [pfx + f"win{c}" for c in range(8)]
        WOUT = [pfx + f"wout{j}" for j in range(22)]

        rms_norm_group(P, C, pfx, xin, gam, h, sqc, rstd, 0)
        for gi_, g in enumerate(groups):
            cols = slice(g * NT, (g + 1) * NT)
            gn = groups[gi_ + 1] if gi_ + 1 < len(groups) else None
            if gn is not None:
                P.dma(xin[:], xs[:, :, gn * NT:(gn + 1) * NT], [], [pfx + "xin"])
            for j in range(22):
                bG, bU = ps[(j % 2) * 2], ps[(j % 2) * 2 + 1]
                kG, kU = f"ps{(j % 2) * 2}", f"ps{(j % 2) * 2 + 1}"
                for c in range(8):
                    P.mm(bG[:], win[:, c, j * 128:(j + 1) * 128], h[:, c, :], c == 0, c == 7,
                         [pfx + "h", WIN[c]], [kG])
                for c in range(8):
                    P.mm(bU[:], win[:, c, DFF + j * 128:DFF + (j + 1) * 128], h[:, c, :], c == 0, c == 7,
                         [pfx + "h", WIN[c]], [kU])
                P.act(sg[j % 2][:], bG[:], AF.Silu, [kG], [pfx + f"sg{j % 2}"])
                P.tt(a[:, j, :], sg[j % 2][:], bU[:], ALU.mult, [pfx + f"sg{j % 2}", kU], [pfx + f"a{j}"])
            if gn is not None:
                rms_norm_group(P, C, pfx, xin, gam, h, sqc, rstd, gn)
            for m in range(8):
                P.dma(xr[m % 2][:], xs[:, m, cols], [], [pfx + f"xr{m % 2}"])
                bO, kO = ps[4 + m % 2], f"ps{4 + m % 2}"
                for j in range(22):
                    P.mm(bO[:], wout[:, j, m * 128:(m + 1) * 128], a[:, j, :], j == 0, j == 21,
                         [pfx + f"a{j}", WOUT[j]], [kO])
                P.stt(xo[m % 2][:], bO[:], 0.5, xr[m % 2][:], ALU.mult, ALU.add,
                      [kO, pfx + f"xr{m % 2}"], [pfx + f"xo{m % 2}"])
                if flag_prefix and g < NG:
                    P.ts(xo[m % 2][:], xo[m % 2][:], flg[:, 0:1], None, ALU.mult, None,
                         [pfx + f"xo{m % 2}", pfx + "flg"], [pfx + f"xo{m % 2}"])
                P.dma(xd[:, m, cols], xo[m % 2][:], [pfx + f"xo{m % 2}"], [])
        P.flush()


def inproj_phase(P, C, tag, w_mix, gam_d, xsrc, T, groups, kv_only=()):
    nc = P.nc
    pfx = tag + "_"
    ps = C.ps
    xs = xsrc.rearrange("(c p) t -> p c t", p=128)
    with ExitStack() as es:
        sb = lambda n, shp, dt: es.enter_context(nc.sbuf_tensor(pfx + n, shp, dt))
        wm = sb("wm", [128, 8, 3072], BF16)
        wsw = sb("wsw", [128, 8, 512], BF16)
        gam = sb("gam", [128, 8], F32)
        xin = sb("xin", [128, 8, NT], F32)
        h = sb("h", [128, 8, NT], BF16)
        sqc = [sb(f"sqc{i}", [128, NT], BF16) for i in range(2)]
        rstd = sb("rstd", [128, NT], F32)
        cosb = [sb(f"cosb{i}", [128, NT], F32) for i in range(2)]
        sinb = [sb(f"sinb{i}", [128, NT], F32) for i in range(2)]
        decq = sb("decq", [128, 2, NT], F32)
        deck = sb("deck", [128, 256], F32)
        t1 = [sb(f"t1{i}", [128, NT], F32) for i in range(2)]
        t2 = [sb(f"t2{i}", [128, NT], F32) for i in range(2)]
        ob = [sb(f"ob{i}", [128, NT], BF16) for i in range(4)]
        of = [sb(f"of{i}", [128, NT], F32) for i in range(2)]
        krb = [sb(f"krb{i}", [128, NT], BF16) for i in range(2)]
        kdb = [sb(f"kdb{i}", [128, 256], BF16) for i in range(2)]

        P.dma(gam[:], gam_d, [], [pfx + "gam"])
        P.dma(xin[:], xs[:, :, groups[0] * NT:(groups[0] + 1) * NT], [], [pfx + "xin"])
        P.dma(decq[:], C.decq, [], [pfx + "tab"])
        P.dma(deck[:], C.deck, [], [pfx + "tab"])
        for c in range(8):
            P.dma(wm[:, c, :], w_mix[c * 128:(c + 1) * 128, :], [], [pfx + "wm"], q="pool")
            src = w_mix[c * 128:(c + 1) * 128, 2048:2560].rearrange("p (h two d) -> p h two d", two=2, d=32)
            dst = wsw[:, c, :].rearrange("p (h two d) -> p h two d", two=2, d=32)
            P.dma(dst[:, :, 0, :], src[:, :, 1, :], [], [pfx + "wm"], q="pool")
            P.dma(dst[:, :, 1, :], src[:, :, 0, :], [], [pfx + "wm"], q="pool")

        cnt = {"b": 0, "ob": 0, "of": 0, "t": 0, "kr": 0, "kd": 0}

        def bank():
            i = cnt["b"] % 6
            cnt["b"] += 1
            return ps[i], f"ps{i}"

        def proj(bk, kk, wt, c0, ncols):
            for c in range(8):
                P.mm(bk[:, 0:NT] if ncols == 128 else bk[:], wt[:, c, c0:c0 + 128], h[:, c, :], c == 0, c == 7,
                     [pfx + "h", pfx + "wm"], [kk])

        for gi_, g in enumerate(groups):
            cols = slice(g * NT, (g + 1) * NT)
            kvo = g in kv_only
            rb = gi_ % 2
            cosT, sinT = cosb[rb], sinb[rb]
            P.dma(cosT[:], C.rope_cos[:, cols], [], [pfx + f"rope{rb}"])
            P.dma(sinT[:], C.rope_sin[:, cols], [], [pfx + f"rope{rb}"])
            rms_norm_group(P, C, pfx, xin, gam, h, sqc, rstd, g)
            if gi_ + 1 < len(groups):
                gn = groups[gi_ + 1]
                P.dma(xin[:], xs[:, :, gn * NT:(gn + 1) * NT], [], [pfx + "xin"])
            for cc in range(2):
                if kvo and g != NG - 1:
                    continue
                bA, kA = bank()
                proj(bA, kA, wm, cc * 128, 128)
                bB, kB = bank()
                proj(bB, kB, wm, 256 + cc * 128, 128)
                i = cnt["t"] % 2
                cnt["t"] += 1
                P.act(t1[i][:], bB[:], AF.Sigmoid, [kB], [pfx + f"t1{i}"])
                o = cnt["ob"] % 4
                cnt["ob"] += 1
                P.tt(ob[o][:], bA[:], t1[i][:], ALU.mult, [kA, pfx + f"t1{i}"], [pfx + f"ob{o}"])
                P.dma(T["vconv"][cc * 128:(cc + 1) * 128, cols], ob[o][:], [pfx + f"ob{o}"], [])
            for sec, dname, scl in ((512, "qsT", 0.125), (1024, "kT", 1.0)):
                if kvo and dname == "qsT":
                    continue
                for cc in range(4):
                    bk, kk = bank()
                    proj(bk, kk, wm, sec + cc * 128, 128)
                    o = cnt["ob"] % 4
                    cnt["ob"] += 1
                    P.ts(ob[o][:], bk[:], scl, None, ALU.mult, None, [kk], [pfx + f"ob{o}"])
                    P.dma(T[dname][cc * 128:(cc + 1) * 128, cols], ob[o][:], [pfx + f"ob{o}"], [])
            for tb in range(4):
                tsl = slice(tb * 128, (tb + 1) * 128)
                rows = slice(g * NT + tb * 128, g * NT + (tb + 1) * 128)
                bk, kk = bank()
                for c in range(8):
                    P.mm(bk[:], h[:, c, tsl], wm[:, c, 1536:2048], c == 0, c == 7, [pfx + "h", pfx + "wm"], [kk])
                o = cnt["ob"] % 4
                cnt["ob"] += 1
                P.act(ob[o][:], bk[:], AF.Copy, [kk], [pfx + f"ob{o}"])
                P.dma(T["vtok"][rows, :], ob[o][:], [pfx + f"ob{o}"], [])
                bk, kk = bank()
                for c in range(8):
                    P.mm(bk[:, 0:256], h[:, c, tsl], wm[:, c, 2560:2816], c == 0, c == 7,
                         [pfx + "h", pfx + "wm"], [kk])
                o = cnt["ob"] % 4
                cnt["ob"] += 1
                P.act(ob[o][:, 0:256], bk[:, 0:256], AF.Copy, [kk], [pfx + f"ob{o}"])
                P.dma(T["vr"][rows, :], ob[o][:, 0:256], [pfx + f"ob{o}"], [])
            for isk, sec, swc in ((0, 2048, 0), (1, 2304, 256)):
                if kvo and not isk:
                    continue
                for cc in range(2):
                    bX, kX = bank()
                    proj(bX, kX, wm, sec + cc * 128, 128)
                    bS, kS = bank()
                    proj(bS, kS, wsw, swc + cc * 128, 128)
                    i = cnt["t"] % 2
                    cnt["t"] += 1
                    P.tt(t1[i][:], bX[:], cosT[:], ALU.mult, [kX, pfx + f"rope{rb}"], [pfx + f"t1{i}"])
                    P.tt(t2[i][:], bS[:], sinT[:], ALU.mult, [kS, pfx + f"rope{rb}"], [pfx + f"t2{i}"])
                    P.tt(t1[i][:], t1[i][:], t2[i][:], ALU.add, [pfx + f"t1{i}", pfx + f"t2{i}"], [pfx + f"t1{i}"],
                         eng="pool")
                    frows = slice(cc * 128, (cc + 1) * 128)
                    if not isk:
                        o = cnt["ob"] % 4
                        cnt["ob"] += 1
                        P.ts(ob[o][:], t1[i][:], 0.125, None, ALU.mult, None, [pfx + f"t1{i}"], [pfx + f"ob{o}"],
                             eng="pool")
                        P.dma(T["qrT"][frows, cols], ob[o][:], [pfx + f"ob{o}"], [])
                        o = cnt["ob"] % 4
                        cnt["ob"] += 1
                        P.tt(ob[o][:], t1[i][:], decq[:, cc, :], ALU.mult, [pfx + f"t1{i}", pfx + "tab"],
                             [pfx + f"ob{o}"], eng="pool")
                        P.dma(T["qdT"][frows, cols], ob[o][:], [pfx + f"ob{o}"], [])
                    else:
                        r = cnt["kr"] % 2
                        cnt["kr"] += 1
                        P.act(krb[r][:], t1[i][:], AF.Copy, [pfx + f"t1{i}"], [pfx + f"krb{r}"])
                        P.dma(T["krT"][frows, cols], krb[r][:], [pfx + f"krb{r}"], [])
                        for tb in range(4):
                            rows = slice(g * NT + tb * 128, g * NT + (tb + 1) * 128)
                            P.op("pe", lambda e, r=r, tb=tb: e.transpose(C.pst[:, 0:128],
                                                                          krb[r][:, tb * 128:(tb + 1) * 128],
                                                                          C.ident[:]),
                                 [pfx + f"krb{r}", "const"], ["pst"])
                            d = cnt["kd"] % 2
                            cnt["kd"] += 1
                            P.tt(kdb[d][:, 0:128], C.pst[:, 0:128], deck[:, cc * 128:(cc + 1) * 128], ALU.mult,
                                 ["pst", pfx + "tab"], [pfx + f"kdb{d}"])
                            P.dma(T["kdtok"][rows, cc * 128:(cc + 1) * 128], kdb[d][:, 0:128],
                                  [pfx + f"kdb{d}"], [])
            for cc in range(2):
                if kvo:
                    continue
                bk, kk = bank()
                proj(bk, kk, wm, 2816 + cc * 128, 128)
                o = cnt["of"] % 2
                cnt["of"] += 1
                P.act(of[o][:], bk[:], AF.Silu, [kk], [pfx + f"of{o}"])
                P.dma(T["gsil"][cc * 128:(cc + 1) * 128, cols], of[o][:], [pfx + f"of{o}"], [])
        P.flush()


def retconv_phase(P, C, tag, T, cw_d, cb_d, lg_d, lb_d, rg_d, own0, has_prev):
    nc = P.nc
    pfx = tag + "_"
    ps = C.ps
    yT = T["yT"]
    with ExitStack() as es:
        sb = lambda n, shp, dt: es.enter_context(nc.sbuf_tensor(pfx + n, shp, dt))
        vpad = sb("vpad", [128, 2, 32 + NLOC], BF16)
        dg = sb("dg", [128, 62, 128], BF16)
        yc = [sb(f"yc{i}", [128, NT], F32) for i in range(2)]
        cw = sb("cw", [128, 2, 31], F32)
        cb = sb("cb", [128, 2], F32)
        lg = sb("lg", [128, 2], F32)
        lb = sb("lb", [128, 2], F32)
        rg = sb("rg", [64, 4], F32)
        kdq = sb("kdq", [64, 4 * NLOC], BF16)
        kd64 = kdq[:, :].rearrange("p (n c) -> p n c", c=256)
        vr64 = sb("vr64", [64, 64, 256], BF16)
        sball = sb("sball", [64, 64, 256], BF16)
        S = sb("S", [64, 256], F32)
        g64 = sb("g64", [64, 256], F32)
        dintra = sb("dintra", [64, 4, NT], F32)
        qr = [kdq[:, 0:NLOC]] * 2
        kr = [kdq[:, NLOC:2 * NLOC]] * 2
        qd = [kdq[:, 2 * NLOC:3 * NLOC]] * 2
        gs = [sb(f"gs{i}", [64, NT], F32) for i in range(2)]
        stb = [sb(f"stb{i}", [64, NT], BF16) for i in range(2)]
        osb = sb("osb", [64, NT], F32)
        osq = sb("osq", [64, NT], F32)
        mean = sb("mean", [128, NT], F32)
        msq = sb("msq", [128, NT], F32)
        var = sb("var", [128, NT], F32)
        tn = [sb(f"tn{i}", [128, NT], F32) for i in range(2)]
        yb = [sb(f"yb{i}", [128, NT], BF16) for i in range(2)]
        sqy = [sb(f"sqy{i}", [128, NT], F32) for i in range(2)]

        for t, d in ((cw, cw_d), (cb, cb_d), (lg, lg_d), (lb, lb_d), (rg, rg_d), (g64, C.g64), (dintra, C.dintra)):
            P.dma(t[:], d, [], [pfx + "tab"])
        for cc in range(2):
            if has_prev:
                P.dma(vpad[:, cc, 0:32], T["vconv"][cc * 128:(cc + 1) * 128, own0 - 32:own0], [], [pfx + "vpad"])
            else:
                P.op("dve", lambda e, cc=cc: e.memset(vpad[:, cc, 0:32], 0.0), [], [pfx + "vpad"])
            P.dma(vpad[:, cc, 32:], T["vconv"][cc * 128:(cc + 1) * 128, own0:own0 + NLOC], [], [pfx + "vpad"])
            for k in range(31):
                P.ts(dg[:, cc * 31 + k, :], C.ident[:], cw[:, cc, k:k + 1], None, ALU.mult, None,
                     ["const", pfx + "tab"], [pfx + "dg"])
        P.op("dve", lambda e: e.memset(S[:], 0.0), [], [pfx + "S"])
        for half in range(2):
            if half == 0 and not has_prev:
                continue
            r0 = own0 - NLOC if half == 0 else own0
            ksrc = T["kdtok"][r0:r0 + NLOC, :]
            vsrc = T["vr"][r0:r0 + NLOC, :]
            P.dma(kd64, ksrc.rearrange("(n p) c -> p n c", p=64), [], [pfx + "kd64"])
            P.dma(vr64[:], vsrc.rearrange("(n p) c -> p n c", p=64), [], [pfx + "vr64"])
            for n in range(64):
                if half == 1:
                    P.act(sball[:, n, :], S[:], AF.Copy, [pfx + "S"], [pfx + f"sball{n}"])
                for hh in range(4):
                    hs = slice(hh * 64, (hh + 1) * 64)
                    P.mm(ps[7][0:64, hs], kd64[:, n, hs], vr64[:, n, hs], True, True,
                         [pfx + "kd64", pfx + "vr64"], ["ps7"])
                P.tt(S[:], S[:], g64[:], ALU.mult, [pfx + "S", pfx + "tab"], [pfx + "S"])
                P.tt(S[:], S[:], ps[7][0:64, 0:256], ALU.add, [pfx + "S", "ps7"], [pfx + "S"])
        for hh in range(4):
            b = 0
            hrows = slice(hh * 64, (hh + 1) * 64)
            hs = slice(hh * 64, (hh + 1) * 64)
            P.dma(qr[b], T["qrT"][hrows, own0:own0 + NLOC], [], [pfx + f"qr{b}", pfx + "kd64"])
            P.dma(kr[b], T["krT"][hrows, own0:own0 + NLOC], [], [pfx + f"kr{b}", pfx + "kd64"])
            P.dma(qd[b], T["qdT"][hrows, own0:own0 + NLOC], [], [pfx + f"qd{b}", pfx + "kd64"])
            for g in range(NG):
                cols = slice(g * NT, (g + 1) * NT)
                gi = g % 2
                ocols = slice(own0 + g * NT, own0 + (g + 1) * NT)
                P.dma(gs[gi][:], T["gsil"][hrows, ocols], [], [pfx + f"gs{gi}"])
                bS, kS = ps[gi], f"ps{gi}"
                bO, kO = ps[2 + gi], f"ps{2 + gi}"
                for c in range(8):
                    n = g * 8 + c
                    tc_ = slice(n * 64, (n + 1) * 64)
                    P.mm(bS[0:64, c * 64:(c + 1) * 64], kr[b][:, tc_], qr[b][:, tc_], True, True,
                         [pfx + f"kr{b}", pfx + f"qr{b}"], [kS])
                P.tt(stb[gi][:], bS[0:64, :], dintra[:, hh, :], ALU.mult, [kS, pfx + "tab"], [pfx + f"stb{gi}"])
                for c in range(8):
                    n = g * 8 + c
                    tc_ = slice(n * 64, (n + 1) * 64)
                    cs = slice(c * 64, (c + 1) * 64)
                    P.mm(bO[0:64, cs], vr64[:, n, hs], stb[gi][:, cs], True, False,
                         [pfx + "vr64", pfx + f"stb{gi}"], [kO])
                    P.mm(bO[0:64, cs], sball[:, n, hs], qd[b][:, tc_], False, True,
                         [pfx + f"sball{n}", pfx + f"qd{b}"], [kO])
                P.act(osb[:], bO[0:64, :], AF.Copy, [kO], [pfx + "osb"])
                P.act(osq[:], bO[0:64, :], AF.Square, [kO], [pfx + "osq"])
                P.mm(ps[4][0:64, :], C.ones_f[0:64, 0:64], osb[:], True, True, [pfx + "osb", "const"], ["ps4"])
                P.mm(ps[5][0:64, :], C.ones_f[0:64, 0:64], osq[:], True, True, [pfx + "osq", "const"], ["ps5"])
                m_, q_, v_ = mean[0:64, :], msq[0:64, :], var[0:64, :]
                P.ts(m_, ps[4][0:64, :], 1.0 / 64, None, ALU.mult, None, ["ps4"], [pfx + "mean"])
                P.tt(q_, m_, m_, ALU.mult, [pfx + "mean"], [pfx + "msq"])
                P.stt(v_, ps[5][0:64, :], 1.0 / 64, q_, ALU.mult, ALU.subtract, ["ps5", pfx + "msq"], [pfx + "var"])
                P.ts(v_, v_, EPS, None, ALU.add, None, [pfx + "var"], [pfx + "var"])
                rsqrt_inplace(P, v_, pfx + "var")
                t_ = tn[gi][0:64, :]
                P.tt(t_, osb[:], m_, ALU.subtract, [pfx + "osb", pfx + "mean"], [pfx + f"tn{gi}"])
                P.tt(t_, t_, v_, ALU.mult, [pfx + f"tn{gi}", pfx + "var"], [pfx + f"tn{gi}"])
                P.stt(yb[gi][0:64, :], t_, rg[:, hh:hh + 1], gs[gi][:], ALU.mult, ALU.mult,
                      [pfx + f"tn{gi}", pfx + "tab", pfx + f"gs{gi}"], [pfx + f"yb{gi}"])
                P.dma(yT[768 + hh * 64:768 + (hh + 1) * 64, ocols], yb[gi][0:64, :], [pfx + f"yb{gi}"], [])
        for g in range(NG):
            cols = slice(g * NT, (g + 1) * NT)
            for cc in range(2):
                for k in range(31):
                    P.mm(ps[cc][:], dg[:, cc * 31 + k, :], vpad[:, cc, 2 + k + g * NT:2 + k + (g + 1) * NT],
                         k == 0, k == 30, [pfx + "dg", pfx + "vpad"], [f"ps{cc}"])
                P.ts(yc[cc][:], ps[cc][:], cb[:, cc:cc + 1], None, ALU.add, None, [f"ps{cc}", pfx + "tab"],
                     [pfx + f"yc{cc}"])
                P.act(sqy[cc][:], yc[cc][:], AF.Square, [pfx + f"yc{cc}"], [pfx + f"sqy{cc}"])
            for cc in range(2):
                P.mm(ps[4][:], C.ones_f[:], yc[cc][:], cc == 0, cc == 1, [pfx + f"yc{cc}", "const"], ["ps4"])
            for cc in range(2):
                P.mm(ps[5][:], C.ones_f[:], sqy[cc][:], cc == 0, cc == 1, [pfx + f"sqy{cc}", "const"], ["ps5"])
            P.ts(mean[:], ps[4][:], 1.0 / 256, None, ALU.mult, None, ["ps4"], [pfx + "mean"])
            P.tt(msq[:], mean[:], mean[:], ALU.mult, [pfx + "mean"], [pfx + "msq"])
            P.stt(var[:], ps[5][:], 1.0 / 256, msq[:], ALU.mult, ALU.subtract, ["ps5", pfx + "msq"], [pfx + "var"])
            P.ts(var[:], var[:], EPS, None, ALU.add, None, [pfx + "var"], [pfx + "var"])
            rsqrt_inplace(P, var[:], pfx + "var")
            for cc in range(2):
                P.tt(tn[cc][:], yc[cc][:], mean[:], ALU.subtract, [pfx + f"yc{cc}", pfx + "mean"],
                     [pfx + f"tn{cc}"])
                P.tt(tn[cc][:], tn[cc][:], var[:], ALU.mult, [pfx + f"tn{cc}", pfx + "var"], [pfx + f"tn{cc}"])
                P.act(yb[cc][:], tn[cc][:], AF.Silu, [pfx + f"tn{cc}", pfx + "tab"], [pfx + f"yb{cc}"],
                      bias=lb[:, cc:cc + 1], scale=lg[:, cc:cc + 1])
                P.dma(yT[cc * 128:(cc + 1) * 128, own0 + g * NT:own0 + (g + 1) * NT], yb[cc][:],
                      [pfx + f"yb{cc}"], [])
        P.flush()


def sb_phase(P, C, tag, T, own0, has_prev):
    nc = P.nc
    pfx = tag + "_"
    ps = C.ps
    yT = T["yT"]
    NB = 64 if has_prev else 32
    LB0 = NB - 32
    base = own0 - (NLOC if has_prev else 0)
    NK = NB * 128
    with ExitStack() as es:
        sb = lambda n, shp, dt: es.enter_context(nc.sbuf_tensor(pfx + n, shp, dt))
        VA = sb("VA", [128, NB, 512], BF16)
        KT = [sb(f"KT{i}", [64, NK], BF16) for i in range(2)]
        qs = [sb(f"qs{i}", [64, NLOC], BF16) for i in range(2)]
        msk = sb("msk", [128, 4, NT], BF16)
        ebuf = [sb(f"e{i}", [128, NT], F32) for i in range(2)]
        spb = [sb(f"sp{i}", [128, NT], BF16) for i in range(3)]
        wb = [sb(f"w{i}", [128, NT], BF16) for i in range(2)]
        sacc = [sb(f"sacc{i}", [128, NT], BF16) for i in range(2)]
        ob = [sb(f"ob{i}", [64, NT], BF16) for i in range(2)]

        P.dma(msk[:], C.sbmask, [], [pfx + "tab"])
        for b0 in range(0, NB, 32):
            P.dma(VA[:, b0:b0 + 32, :],
                  T["vtok"][base + b0 * 128:base + (b0 + 32) * 128, :].rearrange("(b p) c -> p b c", p=128),
                  [], [pfx + "VA"])

        def load_head(hh):
            b = hh % 2
            hrows = slice(hh * 64, (hh + 1) * 64)
            P.dma(KT[b][:], T["kT"][hrows, base:base + NK], [], [pfx + f"KT{b}"])
            P.dma(qs[b][:], T["qsT"][hrows, own0:own0 + NLOC], [], [pfx + f"qs{b}"])

        tuples = []
        for hh in range(8):
            for g in range(NG):
                blks = [LB0 + 4 * g + 3 - i for i in range(4 * g + 4)] + [LB0 - 1 - i for i in range(LB0)]
                for i, kb in enumerate(blks):
                    tuples.append((hh, g, kb, i == 0, i == len(blks) - 1))
        st = {}

        def stage1(idx):
            hh, g, kb, first, last = tuples[idx]
            b = hh % 2
            if g == 0 and first:
                if hh == 0:
                    load_head(0)
                if hh + 1 < 8:
                    load_head(hh + 1)
            i2, i3 = idx % 2, idx % 3
            q = qs[b][:, g * NT:(g + 1) * NT]
            kt = KT[b][:, kb * 128:(kb + 1) * 128]
            bA, kA = ps[i2], f"ps{i2}"
            P.mm(bA[:], kt, q, True, True, [pfx + f"KT{b}", pfx + f"qs{b}"], [kA])
            P.act(ebuf[i2][:], bA[:], AF.Exp, [kA], [pfx + f"e{i2}"])
            P.act(spb[i3][:], ebuf[i2][:], AF.Ln, [pfx + f"e{i2}"], [pfx + f"sp{i3}"], bias=1.0)
            dc = kb - (LB0 + 4 * g)
            if dc >= 0:
                P.tt(spb[i3][:], spb[i3][:], msk[:, dc, :], ALU.mult, [pfx + f"sp{i3}", pfx + "tab"],
                     [pfx + f"sp{i3}"])
            st[idx] = (i2, i3, dc)

        def stage2(idx):
            hh, g, kb, first, last = tuples[idx]
            b = hh % 2
            i2, i3, dc = st.pop(idx)
            q = qs[b][:, g * NT:(g + 1) * NT]
            kt = KT[b][:, kb * 128:(kb + 1) * 128]
            bB, kB = ps[2 + i2], f"ps{2 + i2}"
            if first:
                cur = None
                nxt = None
            P.mm(bB[:], C.trineg[:], spb[i3][:], True, False, [pfx + f"sp{i3}", "const"], [kB])
            if not first:
                sa = st["sacc"]
                P.mm(bB[:], C.negones[:], sacc[sa][:], False, False, [pfx + f"sacc{sa}", "const"], [kB])
            P.mm(bB[:], kt, q, False, True, [pfx + f"KT{b}", pfx + f"qs{b}"], [kB])
            P.act(wb[i2][:], bB[:], AF.Exp, [kB], [pfx + f"w{i2}"])
            if dc >= 0:
                P.tt(wb[i2][:], wb[i2][:], msk[:, dc, :], ALU.mult, [pfx + f"w{i2}", pfx + "tab"], [pfx + f"w{i2}"])
            gi = (hh * NG + g) % 2
            bO, kO = ps[4 + gi], f"ps{4 + gi}"
            P.mm(bO[0:64, :], VA[:, kb, hh * 64:(hh + 1) * 64], wb[i2][:], first, last,
                 [pfx + "VA", pfx + f"w{i2}"], [kO])
            if not last:
                if first:
                    st["sacc"] = 0
                    P.op("dve", lambda e: e.tensor_copy(out=sacc[0][:], in_=spb[i3][:]),
                         [pfx + f"sp{i3}"], [pfx + "sacc0"])
                else:
                    sa = st["sacc"]
                    P.tt(sacc[1 - sa][:], sacc[sa][:], spb[i3][:], ALU.add, [pfx + f"sacc{sa}", pfx + f"sp{i3}"],
                         [pfx + f"sacc{1 - sa}"])
                    st["sacc"] = 1 - sa
            else:
                P.op("dve", lambda e: e.tensor_copy(out=ob[gi][:], in_=bO[0:64, :]), [kO], [pfx + f"ob{gi}"])
                P.dma(yT[256 + hh * 64:256 + (hh + 1) * 64, own0 + g * NT:own0 + (g + 1) * NT], ob[gi][:],
                      [pfx + f"ob{gi}"], [])

        n = len(tuples)
        for i in range(n + 1):
            if i < n:
                stage1(i)
            if i >= 1:
                stage2(i - 1)
        P.flush()


def outproj_phase(P, C, tag, w_o, T, xsrc, xdst, groups):
    nc = P.nc
    pfx = tag + "_"
    ps = C.ps
    xs = xsrc.rearrange("(c p) t -> p c t", p=128)
    xd = xdst.rearrange("(c p) t -> p c t", p=128)
    yr = T["yT"].rearrange("(c p) t -> p c t", p=128)
    with ExitStack() as es:
        sb = lambda n, shp, dt: es.enter_context(nc.sbuf_tensor(pfx + n, shp, dt))
        wo = sb("wo", [128, 8, D], BF16)
        yin = [sb(f"yin{i}", [128, 8, NT], BF16) for i in range(2)]
        xin = [sb(f"xin{i}", [128, 8, NT], F32) for i in range(2)]
        xo = [sb(f"xo{i}", [128, 8, NT], F32) for i in range(2)]
        P.dma(wo[:], w_o.rearrange("(c p) m -> p c m", p=128), [], [pfx + "wo"], q="pool")
        for g in groups:
            cols = slice(g * NT, (g + 1) * NT)
            b = g % 2
            P.dma(yin[b][:], yr[:, :, cols], [], [pfx + f"yin{b}"])
            P.dma(xin[b][:], xs[:, :, cols], [], [pfx + f"xin{b}"])
            for m in range(8):
                bO, kO = ps[m % 4], f"ps{m % 4}"
                for c in range(8):
                    P.mm(bO[:], wo[:, c, m * 128:(m + 1) * 128], yin[b][:, c, :], c == 0, c == 7,
                         [pfx + "wo", pfx + f"yin{b}"], [kO])
                P.tt(xo[b][:, m, :], bO[:], xin[b][:, m, :], ALU.add, [kO, pfx + f"xin{b}"], [pfx + f"xo{b}"])
            P.dma(xd[:, :, cols], xo[b][:], [pfx + f"xo{b}"], [])
        P.flush()


def final_phase(P, C, tag, gam_d, xsrc, out_d):
    nc = P.nc
    pfx = tag + "_"
    xs = xsrc.rearrange("(c p) t -> p c t", p=128)
    od = out_d.rearrange("(c p) t -> p c t", p=128)
    with ExitStack() as es:
        sb = lambda n, shp, dt: es.enter_context(nc.sbuf_tensor(pfx + n, shp, dt))
        gam = sb("gam", [128, 8], F32)
        xin = [sb(f"xin{i}", [128, 8, NT], F32) for i in range(2)]
        sqc = [sb(f"sqc{i}", [128, NT], BF16) for i in range(2)]
        rstd = sb("rstd", [128, NT], F32)
        yo = [sb(f"yo{i}", [128, 8, NT], F32) for i in range(2)]
        P.dma(gam[:], gam_d, [], [pfx + "gam"])
        for g in range(NG):
            cols = slice(NLOC + g * NT, NLOC + (g + 1) * NT)
            ocols = slice(g * NT, (g + 1) * NT)
            b = g % 2
            P.dma(xin[b][:], xs[:, :, cols], [], [pfx + f"xin{b}"])
            for c in range(8):
                P.act(sqc[c % 2][:], xin[b][:, c, :], AF.Square, [pfx + f"xin{b}"], [pfx + f"sqc{c % 2}"])
                P.mm(C.ps[6][:], C.ones_bf[:], sqc[c % 2][:], c == 0, c == 7, [pfx + f"sqc{c % 2}", "const"], ["ps6"])
            P.ts(rstd[:], C.ps[6][:], 1.0 / D, EPS, ALU.mult, ALU.add, ["ps6"], [pfx + "rstd"])
            rsqrt_inplace(P, rstd[:], pfx + "rstd")
            for c in range(8):
                P.stt(yo[b][:, c, :], xin[b][:, c, :], gam[:, c:c + 1], rstd[:], ALU.mult, ALU.mult,
                      [pfx + f"xin{b}", pfx + "rstd", pfx + "gam"], [pfx + f"yo{b}"])
            P.dma(od[:, :, ocols], yo[b][:], [pfx + f"yo{b}"], [], is_out=True)


MIXT = {
    "vconv": ([256, NSLOT], BF16), "qsT": ([512, NSLOT], BF16), "kT": ([512, NSLOT], BF16),
    "vtok": ([NSLOT, 512], BF16), "vr": ([NSLOT, 256], BF16), "qrT": ([256, NSLOT], BF16),
    "qdT": ([256, NSLOT], BF16), "krT": ([256, NSLOT], BF16), "kdtok": ([NSLOT, 256], BF16),
    "gsil": ([256, NSLOT], F32), "yT": ([D, NSLOT], BF16),
}
CONSTS = {"ones_bf": ([128, 128], BF16), "ones_f": ([128, 128], F32), "ident": ([128, 128], BF16),
          "trineg": ([128, 128], BF16), "negones": ([128, 128], BF16)}
TABLES = {"rope_cos": [128, NSLOT], "rope_sin": [128, NSLOT], "decq": [128, 2, NT], "deck": [128, 256],
          "g64": [64, 256], "dintra": [64, 4, NT], "flag": [128, 1]}
ALLG = list(range(2 * NG))
OWNG = list(range(NG, 2 * NG))


def build_program():
    nc = bass.Bass("TRN2", target_bir_lowering=False)
    dt = lambda name, shape, dtype, kind: nc.dram_tensor(name, shape, dtype, kind=kind).ap()
    C = Ctx()
    W = {}
    for l in range(DEPTH):
        for f in ("1", "2"):
            W[f"ffn{f}_w_in{l}"] = dt(f"ffn{f}_w_in{l}", [D, 2 * DFF], F32, "ExternalInput")
            W[f"ffn{f}_w_out{l}"] = dt(f"ffn{f}_w_out{l}", [DFF, D], F32, "ExternalInput")
            W[f"ffn{f}_norm{l}"] = dt(f"ffn{f}_norm{l}", [128, 8], F32, "ExternalInput")
        W[f"mix_w_in{l}"] = dt(f"mix_w_in{l}", [D, 3072], F32, "ExternalInput")
        W[f"mix_norm{l}"] = dt(f"mix_norm{l}", [128, 8], F32, "ExternalInput")
        W[f"mix_w_out{l}"] = dt(f"mix_w_out{l}", [D, D], F32, "ExternalInput")
        W[f"conv_w{l}"] = dt(f"conv_w{l}", [128, 2, 31], F32, "ExternalInput")
        for nm in ("conv_b", "conv_ln_g", "conv_ln_b"):
            W[f"{nm}{l}"] = dt(f"{nm}{l}", [128, 2], F32, "ExternalInput")
        W[f"ret_norm_g{l}"] = dt(f"ret_norm_g{l}", [64, 4], F32, "ExternalInput")
    W["final_norm"] = dt("final_norm", [128, 8], F32, "ExternalInput")
    cd = {k: dt("c_" + k, s, d, "ExternalInput") for k, (s, d) in CONSTS.items()}
    for k, s in TABLES.items():
        setattr(C, k, dt("t_" + k, s, F32, "ExternalInput"))
    C.sbmask = dt("t_sbmask", [128, 4, NT], BF16, "ExternalInput")
    x_in = dt("x_in", [D, NSLOT], F32, "ExternalInput")
    xa = dt("xa", [D, NSLOT], F32, "Internal")
    xb = dt("xb", [D, NSLOT], F32, "Internal")
    T = {k: dt("m_" + k, s, d, "Internal") for k, (s, d) in MIXT.items()}
    out = dt("out", [D, NLOC], F32, "ExternalOutput")

    with ExitStack() as es:
        P = Prog(nc, es)
        C.ps = [es.enter_context(nc.psum_tensor(f"ps{i}", [128, NT], F32)) for i in range(7)]
        C.ps.append(C.ps[6])
        pst = es.enter_context(nc.psum_tensor("pst", [128, 2 * NT], BF16))
        C.pst = pst
        C.ps[7] = pst.bitcast(F32)
        for k, (s, d) in CONSTS.items():
            t = es.enter_context(nc.sbuf_tensor("k_" + k, s, d))
            setattr(C, k, t)
            P.dma(t[:], cd[k], [], ["const"])
        P.flush()
        mixw = lambda l: (W[f"conv_w{l}"], W[f"conv_b{l}"], W[f"conv_ln_g{l}"], W[f"conv_ln_b{l}"],
                          W[f"ret_norm_g{l}"])
        ffn_phase(P, C, "f1a", W["ffn1_w_in0"], W["ffn1_w_out0"], W["ffn1_norm0"], x_in, xa, ALLG)
        inproj_phase(P, C, "ipa", W["mix_w_in0"], W["mix_norm0"], xa, T, ALLG)
        retconv_phase(P, C, "rc0a", T, *mixw(0), own0=0, has_prev=False)
        sb_phase(P, C, "sb0a", T, own0=0, has_prev=False)
        retconv_phase(P, C, "rc0b", T, *mixw(0), own0=NLOC, has_prev=True)
        sb_phase(P, C, "sb0b", T, own0=NLOC, has_prev=True)
        outproj_phase(P, C, "opa", W["mix_w_out0"], T, xa, xb, ALLG)
        ffn_phase(P, C, "f2a", W["ffn2_w_in0"], W["ffn2_w_out0"], W["ffn2_norm0"], xb, xa, ALLG, flag_prefix=True)
        ffn_phase(P, C, "f1b", W["ffn1_w_in1"], W["ffn1_w_out1"], W["ffn1_norm1"], xa, xb, ALLG)
        inproj_phase(P, C, "ipb", W["mix_w_in1"], W["mix_norm1"], xb, T, ALLG, kv_only=tuple(range(NG)))
        retconv_phase(P, C, "rc1", T, *mixw(1), own0=NLOC, has_prev=True)
        sb_phase(P, C, "sb1", T, own0=NLOC, has_prev=True)
        outproj_phase(P, C, "opb", W["mix_w_out1"], T, xb, xa, OWNG)
        ffn_phase(P, C, "f2b", W["ffn2_w_in1"], W["ffn2_w_out1"], W["ffn2_norm1"], xa, xb, OWNG)
        final_phase(P, C, "fin", W["final_norm"], xb, out)
        P.flush(final=True)
    return nc


def _consts():
    bf = ml_dtypes.bfloat16
    j = np.arange(128)
    c = {
        "c_ones_bf": np.ones((128, 128), bf), "c_ones_f": np.ones((128, 128), np.float32),
        "c_ident": np.eye(128).astype(bf),
        "c_trineg": (-(j[:, None] >= j[None, :]).astype(np.float32)).astype(bf),
        "c_negones": (-np.ones((128, 128), np.float32)).astype(bf),
    }
    gam = 1.0 - np.exp2(-5.0 - np.arange(4, dtype=np.float64))
    i512 = np.arange(NT)
    p = np.arange(128)
    decq = np.zeros((128, 2, NT))
    for cc in range(2):
        hh = 2 * cc + p // 64
        decq[:, cc, :] = 0.125 * gam[hh][:, None] ** ((i512 % 64) + 1.0)[None, :]
    col = np.arange(256)
    deck = gam[col // 64][None, :] ** (63.0 - (p % 64))[:, None]
    g64 = np.broadcast_to((gam[col // 64] ** 64.0)[None, :], (64, 256))
    jj = np.arange(64)
    dintra = np.zeros((64, 4, NT))
    for hh in range(4):
        dintra[:, hh, :] = gam[hh] ** np.abs(jj[:, None] - (i512 % 64)[None, :])
    sbmask = np.zeros((128, 4, NT), np.float32)
    for cc in range(4):
        sbmask[:, cc, :] = ((cc * 128 + p)[:, None] < i512[None, :])
    c.update({"t_decq": decq.astype(np.float32), "t_deck": deck.astype(np.float32),
              "t_g64": np.ascontiguousarray(g64).astype(np.float32), "t_dintra": dintra.astype(np.float32),
              "t_sbmask": sbmask.astype(bf)})
    return c


def _rope(half):
    slot = np.arange(NSLOT)
    pos = (slot if half == 1 else slot % NLOC).astype(np.float32)
    inv = (1.0 / (10000.0 ** (np.arange(32, dtype=np.float32) / 32))).astype(np.float32)
    p = np.arange(128)
    ang = pos[None, :] * inv[(p % 64) % 32][:, None]
    sign = np.where((p % 64) < 32, -1.0, 1.0)[:, None]
    return np.cos(ang).astype(np.float32), (sign * np.sin(ang)).astype(np.float32)


def _vec8(v):
    return np.ascontiguousarray(v.reshape(8, 128).T)


def _vec2(v):
    return np.ascontiguousarray(v.reshape(2, 128).T)


def kernel(**inp):
    ncores = 8
    x = inp["x"]
    w = dict(_consts())
    for l in range(DEPTH):
        for f in ("1", "2"):
            w[f"ffn{f}_w_in{l}"] = np.ascontiguousarray(inp[f"ffn{f}_w_in"][l])
            w[f"ffn{f}_w_out{l}"] = np.ascontiguousarray(inp[f"ffn{f}_w_out"][l])
            w[f"ffn{f}_norm{l}"] = _vec8(inp[f"ffn{f}_norm"][l])
        w[f"mix_w_in{l}"] = np.ascontiguousarray(inp["mix_w_in"][l])
        w[f"mix_norm{l}"] = _vec8(inp["mix_norm"][l])
        w[f"mix_w_out{l}"] = np.ascontiguousarray(inp["mix_w_out"][l])
        w[f"conv_w{l}"] = np.ascontiguousarray(inp["conv_w"][l].T.reshape(2, 128, 31).transpose(1, 0, 2))
        for nm in ("conv_b", "conv_ln_g", "conv_ln_b"):
            w[f"{nm}{l}"] = _vec2(inp[nm][l])
        w[f"ret_norm_g{l}"] = np.ascontiguousarray(inp["ret_norm_g"][l].reshape(4, 64).T)
    w["final_norm"] = _vec8(inp["final_norm"])
    ropes = [_rope(0), _rope(1)]
    maps = []
    for core in range(ncores):
        b, half = core // 2, core % 2
        m = dict(w)
        m["t_rope_cos"], m["t_rope_sin"] = ropes[half]
        m["t_flag"] = np.full((128, 1), float(half), np.float32)
        xi = np.zeros((D, NSLOT), np.float32)
        if half == 1:
            xi[:, :] = x[b].T
        else:
            xi[:, NLOC:] = x[b, :NLOC, :].T
        m["x_in"] = xi
        maps.append(m)
    res = run_bass_kernel_spmd(build_program(), maps, core_ids=list(range(ncores))).results
    out = np.empty(x.shape, np.float32)
    for c in range(ncores):
        out[c // 2, (c % 2) * NLOC:(c % 2 + 1) * NLOC, :] = res[c]["out"].T
    return out
```

```python
import numpy as np
import ml_dtypes
from contextlib import ExitStack
import concourse.bass as bass
import concourse.mybir as mybir
from concourse.bass_utils import run_bass_kernel_spmd

F32 = mybir.dt.float32
BF16 = mybir.dt.bfloat16
AF = mybir.ActivationFunctionType
ALU = mybir.AluOpType

D = 1024
DFF = 2816
NLOC = 4096
NSLOT = 8192
NT = 512
NG = NLOC // NT
DEPTH = 2
EPS = 1e-6
NDS = 8


class Op:
    __slots__ = ("stream", "fn", "deps", "dma", "token", "needed", "phase", "wkeys")


class Prog:
    STREAMS = ["pe", "act", "dve", "pool", "sp"]

    def __init__(self, nc, es):
        self.nc = nc
        self.csem = {s: es.enter_context(nc.semaphore("c_" + s)) for s in ["pe", "act", "dve", "pool"]}
        self.dsem = {s: [es.enter_context(nc.semaphore(f"d_{s}{i}")) for i in range(NDS)]
                     for s in ["sp", "pool"]}
        self.ccnt = {s: 0 for s in self.csem}
        self.dcnt = {s: [0] * NDS for s in self.dsem}
        self.dnum = {s: 0 for s in self.dsem}
        self.lastw = {}
        self.readers = {}
        self.ops = []
        self.phase = 0
        self.waited = {s: {} for s in self.STREAMS}
        self.pending = {s: {} for s in self.STREAMS}
        self.out_tokens = []

    def op(self, stream, fn, reads=(), writes=(), dma=False, is_out=False):
        o = Op()
        o.stream, o.fn, o.dma, o.needed, o.phase, o.token = stream, fn, dma, False, self.phase, None
        o.wkeys = set(writes)
        deps = {}
        for k in reads:
            w = self.lastw.get(k)
            if w is not None:
                deps[id(w)] = (w, True)
        for k in writes:
            w = self.lastw.get(k)
            if w is not None and id(w) not in deps:
                deps[id(w)] = (w, False)
            for r in self.readers.get(k, {}).values():
                if id(r) not in deps:
                    deps[id(r)] = (r, False)
        o.deps = []
        for d, raw in deps.values():
            if d.phase != self.phase or d is o:
                continue
            if d.stream == stream and not d.dma and not dma:
                if stream == "pe" or not raw:
                    continue
            o.deps.append(d)
        if dma:
            i = self.dnum[stream] % NDS
            self.dnum[stream] += 1
            self.dcnt[stream][i] += 16
            o.token = (self.dsem[stream][i], self.dcnt[stream][i])
            rkey = (stream, i)
            if is_out:
                self.out_tokens.append(o.token)
        else:
            rkey = (stream, -1)
        for k in reads:
            self.readers.setdefault(k, {})[rkey] = o
        for k in writes:
            self.lastw[k] = o
            self.readers[k] = {}
        self.ops.append(o)
        return o

    def flush(self, final=False):
        ops = self.ops
        for o in ops:
            for d in o.deps:
                d.needed = True
        last = {}
        for o in ops:
            if not o.dma:
                last[o.stream] = o
        for o in last.values():
            o.needed = True
        for o in ops:
            if not o.dma and o.needed:
                self.ccnt[o.stream] += 1
                o.token = (self.csem[o.stream], self.ccnt[o.stream])
        by_stream = {s: [o for o in ops if o.stream == s] for s in self.STREAMS}
        snapshot = {}
        for s in self.csem:
            if self.ccnt[s]:
                snapshot[id(self.csem[s])] = (self.csem[s], self.ccnt[s])
        for s in self.dsem:
            for i in range(NDS):
                if self.dcnt[s][i]:
                    snapshot[id(self.dsem[s][i])] = (self.dsem[s][i], self.dcnt[s][i])

        def emit(stream, eng):
            waited = self.waited[stream]
            for o in by_stream[stream]:
                w = dict(self.pending[stream])
                self.pending[stream] = {}
                for d in o.deps:
                    sem, val = d.token
                    if id(sem) not in w or w[id(sem)][1] < val:
                        w[id(sem)] = (sem, val)
                for sid, (sem, val) in w.items():
                    if waited.get(sid, 0) < val:
                        eng.wait_ge(sem, val)
                        waited[sid] = val
                ins = o.fn(eng)
                if o.token is not None:
                    ins.then_inc(o.token[0], 16 if o.dma else 1)
            if final and stream == "sp":
                for sid, (sem, val) in snapshot.items():
                    if waited.get(sid, 0) < val:
                        eng.wait_ge(sem, val)
                        waited[sid] = val

        with self.nc.Block() as blk:
            if by_stream["pe"]:
                blk.tensor(lambda e: emit("pe", e))
            if by_stream["act"]:
                blk.scalar(lambda e: emit("act", e))
            if by_stream["dve"]:
                blk.vector(lambda e: emit("dve", e))
            if by_stream["pool"]:
                blk.gpsimd(lambda e: emit("pool", e))
            if by_stream["sp"] or final:
                blk.sync(lambda e: emit("sp", e))
        for s in self.STREAMS:
            p = self.pending[s]
            for sid, (sem, val) in snapshot.items():
                if sid not in p or p[sid][1] < val:
                    p[sid] = (sem, val)
        self.ops = []
        self.phase += 1

    def dma(self, out, in_, reads, writes, q="sp", is_out=False):
        return self.op(q, lambda e: e.dma_start(out=out, in_=in_), reads, writes, dma=True, is_out=is_out)

    def mm(self, out, lhsT, rhs, start, stop, reads, writes):
        return self.op("pe", lambda e: e.matmul(out, lhsT, rhs, start=start, stop=stop), reads, writes)

    def act(self, out, in_, func, reads, writes, bias=None, scale=None):
        kw = {}
        if bias is not None:
            kw["bias"] = bias
        if scale is not None:
            kw["scale"] = scale
        return self.op("act", lambda e: e.activation(out=out, in_=in_, func=func, **kw), reads, writes)

    def tt(self, out, in0, in1, op, reads, writes, eng="dve"):
        return self.op(eng, lambda e: e.tensor_tensor(out=out, in0=in0, in1=in1, op=op), reads, writes)

    def ts(self, out, in0, s1, s2, op0, op1, reads, writes, eng="dve"):
        if op1 is None:
            return self.op(eng, lambda e: e.tensor_scalar(out=out, in0=in0, scalar1=s1, scalar2=None, op0=op0),
                           reads, writes)
        return self.op(eng, lambda e: e.tensor_scalar(out=out, in0=in0, scalar1=s1, scalar2=s2, op0=op0, op1=op1),
                       reads, writes)

    def stt(self, out, in0, scalar, in1, op0, op1, reads, writes, eng="dve"):
        return self.op(eng, lambda e: e.scalar_tensor_tensor(out=out, in0=in0, scalar=scalar, in1=in1,
                                                             op0=op0, op1=op1), reads, writes)


class Ctx:
    pass


def rsqrt_inplace(P, t, key):
    P.act(t, t, AF.Sqrt, [key], [key])
    P.op("dve", lambda e: e.reciprocal(out=t, in_=t), [key], [key])


def rms_norm_group(P, C, pfx, xin, gam, h, sqc, rstd, g):
    ps = C.ps
    for c in range(8):
        P.act(sqc[c % 2][:], xin[:, c, :], AF.Square, [pfx + "xin"], [pfx + f"sqc{c % 2}"])
        P.mm(ps[6][:], C.ones_bf[:], sqc[c % 2][:], c == 0, c == 7, [pfx + f"sqc{c % 2}", "const"], ["ps6"])
    P.ts(rstd[:], ps[6][:], 1.0 / D, EPS, ALU.mult, ALU.add, ["ps6"], [pfx + "rstd"])
    rsqrt_inplace(P, rstd[:], pfx + "rstd")
    for c in range(8):
        P.stt(h[:, c, :], xin[:, c, :], gam[:, c:c + 1], rstd[:], ALU.mult, ALU.mult,
              [pfx + "xin", pfx + "rstd", pfx + "gam"], [pfx + "h"])


def ffn_phase(P, C, tag, w_in, w_out, gam_d, xsrc, xdst, groups, flag_prefix=False):
    nc = P.nc
    pfx = tag + "_"
    ps = C.ps
    xs = xsrc.rearrange("(c p) t -> p c t", p=128)
    xd = xdst.rearrange("(c p) t -> p c t", p=128)
    with ExitStack() as es:
        sb = lambda n, shp, dt: es.enter_context(nc.sbuf_tensor(pfx + n, shp, dt))
        win = sb("win", [128, 8, 2 * DFF], BF16)
        wout = sb("wout", [128, 22, D], BF16)
        gam = sb("gam", [128, 8], F32)
        xin = sb("xin", [128, 8, NT], F32)
        h = sb("h", [128, 8, NT], BF16)
        sqc = [sb(f"sqc{i}", [128, NT], BF16) for i in range(2)]
        rstd = sb("rstd", [128, NT], F32)
        a = sb("a", [128, 22, NT], BF16)
        sg = [sb(f"sg{i}", [128, NT], F32) for i in range(2)]
        xr = [sb(f"xr{i}", [128, NT], F32) for i in range(2)]
        xo = [sb(f"xo{i}", [128, NT], F32) for i in range(2)]
        flg = sb("flg", [128, 1], F32)

        P.dma(gam[:], gam_d, [], [pfx + "gam"])
        P.dma(flg[:], C.flag, [], [pfx + "flg"])
        g0 = groups[0]
        P.dma(xin[:], xs[:, :, g0 * NT:(g0 + 1) * NT], [], [pfx + "xin"])
        for c in range(8):
            P.dma(win[:, c, :], w_in[c * 128:(c + 1) * 128, :], [], [pfx + f"win{c}"], q="pool")
        wo_r = w_out.rearrange("(j p) m -> p j m", p=128)
        for j0 in range(0, 22, 6):
            j1 = min(22, j0 + 6)
            P.dma(wout[:, j0:j1, :], wo_r[:, j0:j1, :], [], [pfx + f"wout{jj}" for jj in range(j0, j1)], q="pool")
        WIN = [pfx + f"win{c}" for c in range(8)]
        WOUT = [pfx + f"wout{j}" for j in range(22)]

        rms_norm_group(P, C, pfx, xin, gam, h, sqc, rstd, 0)
        for gi_, g in enumerate(groups):
            cols = slice(g * NT, (g + 1) * NT)
            gn = groups[gi_ + 1] if gi_ + 1 < len(groups) else None
            if gn is not None:
                P.dma(xin[:], xs[:, :, gn * NT:(gn + 1) * NT], [], [pfx + "xin"])
            for j in range(22):
                bG, bU = ps[(j % 2) * 2], ps[(j % 2) * 2 + 1]
                kG, kU = f"ps{(j % 2) * 2}", f"ps{(j % 2) * 2 + 1}"
                for c in range(8):
                    P.mm(bG[:], win[:, c, j * 128:(j + 1) * 128], h[:, c, :], c == 0, c == 7,
                         [pfx + "h", WIN[c]], [kG])
                for c in range(8):
                    P.mm(bU[:], win[:, c, DFF + j * 128:DFF + (j + 1) * 128], h[:, c, :], c == 0, c == 7,
                         [pfx + "h", WIN[c]], [kU])
                P.act(sg[j % 2][:], bG[:], AF.Silu, [kG], [pfx + f"sg{j % 2}"])
                P.tt(a[:, j, :], sg[j % 2][:], bU[:], ALU.mult, [pfx + f"sg{j % 2}", kU], [pfx + f"a{j}"])
            if gn is not None:
                rms_norm_group(P, C, pfx, xin, gam, h, sqc, rstd, gn)
            for m in range(8):
                P.dma(xr[m % 2][:], xs[:, m, cols], [], [pfx + f"xr{m % 2}"])
                bO, kO = ps[4 + m % 2], f"ps{4 + m % 2}"
                for j in range(22):
                    P.mm(bO[:], wout[:, j, m * 128:(m + 1) * 128], a[:, j, :], j == 0, j == 21,
                         [pfx + f"a{j}", WOUT[j]], [kO])
                P.stt(xo[m % 2][:], bO[:], 0.5, xr[m % 2][:], ALU.mult, ALU.add,
                      [kO, pfx + f"xr{m % 2}"], [pfx + f"xo{m % 2}"])
                if flag_prefix and g < NG:
                    P.ts(xo[m % 2][:], xo[m % 2][:], flg[:, 0:1], None, ALU.mult, None,
                         [pfx + f"xo{m % 2}", pfx + "flg"], [pfx + f"xo{m % 2}"])
                P.dma(xd[:, m, cols], xo[m % 2][:], [pfx + f"xo{m % 2}"], [])
        P.flush()


def inproj_phase(P, C, tag, w_mix, gam_d, xsrc, T, groups, kv_only=()):
    nc = P.nc
    pfx = tag + "_"
    ps = C.ps
    xs = xsrc.rearrange("(c p) t -> p c t", p=128)
    with ExitStack() as es:
        sb = lambda n, shp, dt: es.enter_context(nc.sbuf_tensor(pfx + n, shp, dt))
        wm = sb("wm", [128, 8, 3072], BF16)
        wsw = sb("wsw", [128, 8, 512], BF16)
        gam = sb("gam", [128, 8], F32)
        xin = sb("xin", [128, 8, NT], F32)
        h = sb("h", [128, 8, NT], BF16)
        sqc = [sb(f"sqc{i}", [128, NT], BF16) for i in range(2)]
        rstd = sb("rstd", [128, NT], F32)
        cosb = [sb(f"cosb{i}", [128, NT], F32) for i in range(2)]
        sinb = [sb(f"sinb{i}", [128, NT], F32) for i in range(2)]
        decq = sb("decq", [128, 2, NT], F32)
        deck = sb("deck", [128, 256], F32)
        t1 = [sb(f"t1{i}", [128, NT], F32) for i in range(2)]
        t2 = [sb(f"t2{i}", [128, NT], F32) for i in range(2)]
        ob = [sb(f"ob{i}", [128, NT], BF16) for i in range(4)]
        of = [sb(f"of{i}", [128, NT], F32) for i in range(2)]
        krb = [sb(f"krb{i}", [128, NT], BF16) for i in range(2)]
        kdb = [sb(f"kdb{i}", [128, 256], BF16) for i in range(2)]

        P.dma(gam[:], gam_d, [], [pfx + "gam"])
        P.dma(xin[:], xs[:, :, groups[0] * NT:(groups[0] + 1) * NT], [], [pfx + "xin"])
        P.dma(decq[:], C.decq, [], [pfx + "tab"])
        P.dma(deck[:], C.deck, [], [pfx + "tab"])
        for c in range(8):
            P.dma(wm[:, c, :], w_mix[c * 128:(c + 1) * 128, :], [], [pfx + "wm"], q="pool")
            src = w_mix[c * 128:(c + 1) * 128, 2048:2560].rearrange("p (h two d) -> p h two d", two=2, d=32)
            dst = wsw[:, c, :].rearrange("p (h two d) -> p h two d", two=2, d=32)
            P.dma(dst[:, :, 0, :], src[:, :, 1, :], [], [pfx + "wm"], q="pool")
            P.dma(dst[:, :, 1, :], src[:, :, 0, :], [], [pfx + "wm"], q="pool")

        cnt = {"b": 0, "ob": 0, "of": 0, "t": 0, "kr": 0, "kd": 0}

        def bank():
            i = cnt["b"] % 6
            cnt["b"] += 1
            return ps[i], f"ps{i}"

        def proj(bk, kk, wt, c0, ncols):
            for c in range(8):
                P.mm(bk[:, 0:NT] if ncols == 128 else bk[:], wt[:, c, c0:c0 + 128], h[:, c, :], c == 0, c == 7,
                     [pfx + "h", pfx + "wm"], [kk])

        for gi_, g in enumerate(groups):
            cols = slice(g * NT, (g + 1) * NT)
            kvo = g in kv_only
            rb = gi_ % 2
            cosT, sinT = cosb[rb], sinb[rb]
            P.dma(cosT[:], C.rope_cos[:, cols], [], [pfx + f"rope{rb}"])
            P.dma(sinT[:], C.rope_sin[:, cols], [], [pfx + f"rope{rb}"])
            rms_norm_group(P, C, pfx, xin, gam, h, sqc, rstd, g)
            if gi_ + 1 < len(groups):
                gn = groups[gi_ + 1]
                P.dma(xin[:], xs[:, :, gn * NT:(gn + 1) * NT], [], [pfx + "xin"])
            for cc in range(2):
                if kvo and g != NG - 1:
                    continue
                bA, kA = bank()
                proj(bA, kA, wm, cc * 128, 128)
                bB, kB = bank()
                proj(bB, kB, wm, 256 + cc * 128, 128)
                i = cnt["t"] % 2
                cnt["t"] += 1
                P.act(t1[i][:], bB[:], AF.Sigmoid, [kB], [pfx + f"t1{i}"])
                o = cnt["ob"] % 4
                cnt["ob"] += 1
                P.tt(ob[o][:], bA[:], t1[i][:], ALU.mult, [kA, pfx + f"t1{i}"], [pfx + f"ob{o}"])
                P.dma(T["vconv"][cc * 128:(cc + 1) * 128, cols], ob[o][:], [pfx + f"ob{o}"], [])
            for sec, dname, scl in ((512, "qsT", 0.125), (1024, "kT", 1.0)):
                if kvo and dname == "qsT":
                    continue
                for cc in range(4):
                    bk, kk = bank()
                    proj(bk, kk, wm, sec + cc * 128, 128)
                    o = cnt["ob"] % 4
                    cnt["ob"] += 1
                    P.ts(ob[o][:], bk[:], scl, None, ALU.mult, None, [kk], [pfx + f"ob{o}"])
                    P.dma(T[dname][cc * 128:(cc + 1) * 128, cols], ob[o][:], [pfx + f"ob{o}"], [])
            for tb in range(4):
                tsl = slice(tb * 128, (tb + 1) * 128)
                rows = slice(g * NT + tb * 128, g * NT + (tb + 1) * 128)
                bk, kk = bank()
                for c in range(8):
                    P.mm(bk[:], h[:, c, tsl], wm[:, c, 1536:2048], c == 0, c == 7, [pfx + "h", pfx + "wm"], [kk])
                o = cnt["ob"] % 4
                cnt["ob"] += 1
                P.act(ob[o][:], bk[:], AF.Copy, [kk], [pfx + f"ob{o}"])
                P.dma(T["vtok"][rows, :], ob[o][:], [pfx + f"ob{o}"], [])
                bk, kk = bank()
                for c in range(8):
                    P.mm(bk[:, 0:256], h[:, c, tsl], wm[:, c, 2560:2816], c == 0, c == 7,
                         [pfx + "h", pfx + "wm"], [kk])
                o = cnt["ob"] % 4
                cnt["ob"] += 1
                P.act(ob[o][:, 0:256], bk[:, 0:256], AF.Copy, [kk], [pfx + f"ob{o}"])
                P.dma(T["vr"][rows, :], ob[o][:, 0:256], [pfx + f"ob{o}"], [])
            for isk, sec, swc in ((0, 2048, 0), (1, 2304, 256)):
                if kvo and not isk:
                    continue
                for cc in range(2):
                    bX, kX = bank()
                    proj(bX, kX, wm, sec + cc * 128, 128)
                    bS, kS = bank()
                    proj(bS, kS, wsw, swc + cc * 128, 128)
                    i = cnt["t"] % 2
                    cnt["t"] += 1
                    P.tt(t1[i][:], bX[:], cosT[:], ALU.mult, [kX, pfx + f"rope{rb}"], [pfx + f"t1{i}"])
                    P.tt(t2[i][:], bS[:], sinT[:], ALU.mult, [kS, pfx + f"rope{rb}"], [pfx + f"t2{i}"])
                    P.tt(t1[i][:], t1[i][:], t2[i][:], ALU.add, [pfx + f"t1{i}", pfx + f"t2{i}"], [pfx + f"t1{i}"],
                         eng="pool")
                    frows = slice(cc * 128, (cc + 1) * 128)
                    if not isk:
                        o = cnt["ob"] % 4
                        cnt["ob"] += 1
                        P.ts(ob[o][:], t1[i][:], 0.125, None, ALU.mult, None, [pfx + f"t1{i}"], [pfx + f"ob{o}"],
                             eng="pool")
                        P.dma(T["qrT"][frows, cols], ob[o][:], [pfx + f"ob{o}"], [])
                        o = cnt["ob"] % 4
                        cnt["ob"] += 1
                        P.tt(ob[o][:], t1[i][:], decq[:, cc, :], ALU.mult, [pfx + f"t1{i}", pfx + "tab"],
                             [pfx + f"ob{o}"], eng="pool")
                        P.dma(T["qdT"][frows, cols], ob[o][:], [pfx + f"ob{o}"], [])
                    else:
                        r = cnt["kr"] % 2
                        cnt["kr"] += 1
                        P.act(krb[r][:], t1[i][:], AF.Copy, [pfx + f"t1{i}"], [pfx + f"krb{r}"])
                        P.dma(T["krT"][frows, cols], krb[r][:], [pfx + f"krb{r}"], [])
                        for tb in range(4):
                            rows = slice(g * NT + tb * 128, g * NT + (tb + 1) * 128)
                            P.op("pe", lambda e, r=r, tb=tb: e.transpose(C.pst[:, 0:128],
                                                                          krb[r][:, tb * 128:(tb + 1) * 128],
                                                                          C.ident[:]),
                                 [pfx + f"krb{r}", "const"], ["pst"])
                            d = cnt["kd"] % 2
                            cnt["kd"] += 1
                            P.tt(kdb[d][:, 0:128], C.pst[:, 0:128], deck[:, cc * 128:(cc + 1) * 128], ALU.mult,
                                 ["pst", pfx + "tab"], [pfx + f"kdb{d}"])
                            P.dma(T["kdtok"][rows, cc * 128:(cc + 1) * 128], kdb[d][:, 0:128],
                                  [pfx + f"kdb{d}"], [])
            for cc in range(2):
                if kvo:
                    continue
                bk, kk = bank()
                proj(bk, kk, wm, 2816 + cc * 128, 128)
                o = cnt["of"] % 2
                cnt["of"] += 1
                P.act(of[o][:], bk[:], AF.Silu, [kk], [pfx + f"of{o}"])
                P.dma(T["gsil"][cc * 128:(cc + 1) * 128, cols], of[o][:], [pfx + f"of{o}"], [])
        P.flush()


def retconv_phase(P, C, tag, T, cw_d, cb_d, lg_d, lb_d, rg_d, own0, has_prev):
    nc = P.nc
    pfx = tag + "_"
    ps = C.ps
    yT = T["yT"]
    with ExitStack() as es:
        sb = lambda n, shp, dt: es.enter_context(nc.sbuf_tensor(pfx + n, shp, dt))
        vpad = sb("vpad", [128, 2, 32 + NLOC], BF16)
        dg = sb("dg", [128, 62, 128], BF16)
        yc = [sb(f"yc{i}", [128, NT], F32) for i in range(2)]
        cw = sb("cw", [128, 2, 31], F32)
        cb = sb("cb", [128, 2], F32)
        lg = sb("lg", [128, 2], F32)
        lb = sb("lb", [128, 2], F32)
        rg = sb("rg", [64, 4], F32)
        kdq = sb("kdq", [64, 4 * NLOC], BF16)
        kd64 = kdq[:, :].rearrange("p (n c) -> p n c", c=256)
        vr64 = sb("vr64", [64, 64, 256], BF16)
        sball = sb("sball", [64, 64, 256], BF16)
        S = sb("S", [64, 256], F32)
        g64 = sb("g64", [64, 256], F32)
        dintra = sb("dintra", [64, 4, NT], F32)
        qr = [kdq[:, 0:NLOC]] * 2
        kr = [kdq[:, NLOC:2 * NLOC]] * 2
        qd = [kdq[:, 2 * NLOC:3 * NLOC]] * 2
        gs = [sb(f"gs{i}", [64, NT], F32) for i in range(2)]
        stb = [sb(f"stb{i}", [64, NT], BF16) for i in range(2)]
        osb = sb("osb", [64, NT], F32)
        osq = sb("osq", [64, NT], F32)
        mean = sb("mean", [128, NT], F32)
        msq = sb("msq", [128, NT], F32)
        var = sb("var", [128, NT], F32)
        tn = [sb(f"tn{i}", [128, NT], F32) for i in range(2)]
        yb = [sb(f"yb{i}", [128, NT], BF16) for i in range(2)]
        sqy = [sb(f"sqy{i}", [128, NT], F32) for i in range(2)]

        for t, d in ((cw, cw_d), (cb, cb_d), (lg, lg_d), (lb, lb_d), (rg, rg_d), (g64, C.g64), (dintra, C.dintra)):
            P.dma(t[:], d, [], [pfx + "tab"])
        for cc in range(2):
            if has_prev:
                P.dma(vpad[:, cc, 0:32], T["vconv"][cc * 128:(cc + 1) * 128, own0 - 32:own0], [], [pfx + "vpad"])
            else:
                P.op("dve", lambda e, cc=cc: e.memset(vpad[:, cc, 0:32], 0.0), [], [pfx + "vpad"])
            P.dma(vpad[:, cc, 32:], T["vconv"][cc * 128:(cc + 1) * 128, own0:own0 + NLOC], [], [pfx + "vpad"])
            for k in range(31):
                P.ts(dg[:, cc * 31 + k, :], C.ident[:], cw[:, cc, k:k + 1], None, ALU.mult, None,
                     ["const", pfx + "tab"], [pfx + "dg"])
        P.op("dve", lambda e: e.memset(S[:], 0.0), [], [pfx + "S"])
        for half in range(2):
            if half == 0 and not has_prev:
                continue
            r0 = own0 - NLOC if half == 0 else own0
            ksrc = T["kdtok"][r0:r0 + NLOC, :]
            vsrc = T["vr"][r0:r0 + NLOC, :]
            P.dma(kd64, ksrc.rearrange("(n p) c -> p n c", p=64), [], [pfx + "kd64"])
            P.dma(vr64[:], vsrc.rearrange("(n p) c -> p n c", p=64), [], [pfx + "vr64"])
            for n in range(64):
                if half == 1:
                    P.act(sball[:, n, :], S[:], AF.Copy, [pfx + "S"], [pfx + f"sball{n}"])
                for hh in range(4):
                    hs = slice(hh * 64, (hh + 1) * 64)
                    P.mm(ps[7][0:64, hs], kd64[:, n, hs], vr64[:, n, hs], True, True,
                         [pfx + "kd64", pfx + "vr64"], ["ps7"])
                P.tt(S[:], S[:], g64[:], ALU.mult, [pfx + "S", pfx + "tab"], [pfx + "S"])
                P.tt(S[:], S[:], ps[7][0:64, 0:256], ALU.add, [pfx + "S", "ps7"], [pfx + "S"])
        for hh in range(4):
            b = 0
            hrows = slice(hh * 64, (hh + 1) * 64)
            hs = slice(hh * 64, (hh + 1) * 64)
            P.dma(qr[b], T["qrT"][hrows, own0:own0 + NLOC], [], [pfx + f"qr{b}", pfx + "kd64"])
            P.dma(kr[b], T["krT"][hrows, own0:own0 + NLOC], [], [pfx + f"kr{b}", pfx + "kd64"])
            P.dma(qd[b], T["qdT"][hrows, own0:own0 + NLOC], [], [pfx + f"qd{b}", pfx + "kd64"])
            for g in range(NG):
                cols = slice(g * NT, (g + 1) * NT)
                gi = g % 2
                ocols = slice(own0 + g * NT, own0 + (g + 1) * NT)
                P.dma(gs[gi][:], T["gsil"][hrows, ocols], [], [pfx + f"gs{gi}"])
                bS, kS = ps[gi], f"ps{gi}"
                bO, kO = ps[2 + gi], f"ps{2 + gi}"
                for c in range(8):
                    n = g * 8 + c
                    tc_ = slice(n * 64, (n + 1) * 64)
                    P.mm(bS[0:64, c * 64:(c + 1) * 64], kr[b][:, tc_], qr[b][:, tc_], True, True,
                         [pfx + f"kr{b}", pfx + f"qr{b}"], [kS])
                P.tt(stb[gi][:], bS[0:64, :], dintra[:, hh, :], ALU.mult, [kS, pfx + "tab"], [pfx + f"stb{gi}"])
                for c in range(8):
                    n = g * 8 + c
                    tc_ = slice(n * 64, (n + 1) * 64)
                    cs = slice(c * 64, (c + 1) * 64)
                    P.mm(bO[0:64, cs], vr64[:, n, hs], stb[gi][:, cs], True, False,
                         [pfx + "vr64", pfx + f"stb{gi}"], [kO])
                    P.mm(bO[0:64, cs], sball[:, n, hs], qd[b][:, tc_], False, True,
                         [pfx + f"sball{n}", pfx + f"qd{b}"], [kO])
                P.act(osb[:], bO[0:64, :], AF.Copy, [kO], [pfx + "osb"])
                P.act(osq[:], bO[0:64, :], AF.Square, [kO], [pfx + "osq"])
                P.mm(ps[4][0:64, :], C.ones_f[0:64, 0:64], osb[:], True, True, [pfx + "osb", "const"], ["ps4"])
                P.mm(ps[5][0:64, :], C.ones_f[0:64, 0:64], osq[:], True, True, [pfx + "osq", "const"], ["ps5"])
                m_, q_, v_ = mean[0:64, :], msq[0:64, :], var[0:64, :]
                P.ts(m_, ps[4][0:64, :], 1.0 / 64, None, ALU.mult, None, ["ps4"], [pfx + "mean"])
                P.tt(q_, m_, m_, ALU.mult, [pfx + "mean"], [pfx + "msq"])
                P.stt(v_, ps[5][0:64, :], 1.0 / 64, q_, ALU.mult, ALU.subtract, ["ps5", pfx + "msq"], [pfx + "var"])
                P.ts(v_, v_, EPS, None, ALU.add, None, [pfx + "var"], [pfx + "var"])
                rsqrt_inplace(P, v_, pfx + "var")
                t_ = tn[gi][0:64, :]
                P.tt(t_, osb[:], m_, ALU.subtract, [pfx + "osb", pfx + "mean"], [pfx + f"tn{gi}"])
                P.tt(t_, t_, v_, ALU.mult, [pfx + f"tn{gi}", pfx + "var"], [pfx + f"tn{gi}"])
                P.stt(yb[gi][0:64, :], t_, rg[:, hh:hh + 1], gs[gi][:], ALU.mult, ALU.mult,
                      [pfx + f"tn{gi}", pfx + "tab", pfx + f"gs{gi}"], [pfx + f"yb{gi}"])
                P.dma(yT[768 + hh * 64:768 + (hh + 1) * 64, ocols], yb[gi][0:64, :], [pfx + f"yb{gi}"], [])
        for g in range(NG):
            cols = slice(g * NT, (g + 1) * NT)
            for cc in range(2):
                for k in range(31):
                    P.mm(ps[cc][:], dg[:, cc * 31 + k, :], vpad[:, cc, 2 + k + g * NT:2 + k + (g + 1) * NT],
                         k == 0, k == 30, [pfx + "dg", pfx + "vpad"], [f"ps{cc}"])
                P.ts(yc[cc][:], ps[cc][:], cb[:, cc:cc + 1], None, ALU.add, None, [f"ps{cc}", pfx + "tab"],
                     [pfx + f"yc{cc}"])
                P.act(sqy[cc][:], yc[cc][:], AF.Square, [pfx + f"yc{cc}"], [pfx + f"sqy{cc}"])
            for cc in range(2):
                P.mm(ps[4][:], C.ones_f[:], yc[cc][:], cc == 0, cc == 1, [pfx + f"yc{cc}", "const"], ["ps4"])
            for cc in range(2):
                P.mm(ps[5][:], C.ones_f[:], sqy[cc][:], cc == 0, cc == 1, [pfx + f"sqy{cc}", "const"], ["ps5"])
            P.ts(mean[:], ps[4][:], 1.0 / 256, None, ALU.mult, None, ["ps4"], [pfx + "mean"])
            P.tt(msq[:], mean[:], mean[:], ALU.mult, [pfx + "mean"], [pfx + "msq"])
            P.stt(var[:], ps[5][:], 1.0 / 256, msq[:], ALU.mult, ALU.subtract, ["ps5", pfx + "msq"], [pfx + "var"])
            P.ts(var[:], var[:], EPS, None, ALU.add, None, [pfx + "var"], [pfx + "var"])
            rsqrt_inplace(P, var[:], pfx + "var")
            for cc in range(2):
                P.tt(tn[cc][:], yc[cc][:], mean[:], ALU.subtract, [pfx + f"yc{cc}", pfx + "mean"],
                     [pfx + f"tn{cc}"])
                P.tt(tn[cc][:], tn[cc][:], var[:], ALU.mult, [pfx + f"tn{cc}", pfx + "var"], [pfx + f"tn{cc}"])
                P.act(yb[cc][:], tn[cc][:], AF.Silu, [pfx + f"tn{cc}", pfx + "tab"], [pfx + f"yb{cc}"],
                      bias=lb[:, cc:cc + 1], scale=lg[:, cc:cc + 1])
                P.dma(yT[cc * 128:(cc + 1) * 128, own0 + g * NT:own0 + (g + 1) * NT], yb[cc][:],
                      [pfx + f"yb{cc}"], [])
        P.flush()


def sb_phase(P, C, tag, T, own0, has_prev):
    nc = P.nc
    pfx = tag + "_"
    ps = C.ps
    yT = T["yT"]
    NB = 64 if has_prev else 32
    LB0 = NB - 32
    base = own0 - (NLOC if has_prev else 0)
    NK = NB * 128
    with ExitStack() as es:
        sb = lambda n, shp, dt: es.enter_context(nc.sbuf_tensor(pfx + n, shp, dt))
        VA = sb("VA", [128, NB, 512], BF16)
        KT = [sb(f"KT{i}", [64, NK], BF16) for i in range(2)]
        qs = [sb(f"qs{i}", [64, NLOC], BF16) for i in range(2)]
        msk = sb("msk", [128, 4, NT], BF16)
        ebuf = [sb(f"e{i}", [128, NT], F32) for i in range(2)]
        spb = [sb(f"sp{i}", [128, NT], BF16) for i in range(3)]
        wb = [sb(f"w{i}", [128, NT], BF16) for i in range(3)]
        sacc = [sb(f"sacc{i}", [128, NT], BF16) for i in range(3)]
        ob = [sb(f"ob{i}", [64, NT], BF16) for i in range(2)]

        P.dma(msk[:], C.sbmask, [], [pfx + "tab"])
        for b0 in range(0, NB, 32):
            P.dma(VA[:, b0:b0 + 32, :],
                  T["vtok"][base + b0 * 128:base + (b0 + 32) * 128, :].rearrange("(b p) c -> p b c", p=128),
                  [], [pfx + "VA"])

        def load_head(hh):
            b = hh % 2
            hrows = slice(hh * 64, (hh + 1) * 64)
            P.dma(KT[b][:], T["kT"][hrows, base:base + NK], [], [pfx + f"KT{b}"])
            P.dma(qs[b][:], T["qsT"][hrows, own0:own0 + NLOC], [], [pfx + f"qs{b}"])

        tuples = []
        for hh in range(8):
            for g in range(NG):
                blks = [LB0 + 4 * g + 3 - i for i in range(4 * g + 4)] + [LB0 - 1 - i for i in range(LB0)]
                for i, kb in enumerate(blks):
                    tuples.append((hh, g, kb, i == 0, i == len(blks) - 1))

        def stageA(idx):
            hh, g, kb, first, last = tuples[idx]
            b = hh % 2
            if g == 0 and first:
                if hh == 0:
                    load_head(0)
                if hh + 1 < 8:
                    load_head(hh + 1)
            i2, i3 = idx % 2, idx % 3
            q = qs[b][:, g * NT:(g + 1) * NT]
            kt = KT[b][:, kb * 128:(kb + 1) * 128]
            bA, kA = ps[i2], f"ps{i2}"
            P.mm(bA[:], kt, q, True, True, [pfx + f"KT{b}", pfx + f"qs{b}"], [kA])
            P.act(ebuf[i2][:], bA[:], AF.Exp, [kA], [pfx + f"e{i2}"])
            P.act(spb[i3][:], ebuf[i2][:], AF.Ln, [pfx + f"e{i2}"], [pfx + f"sp{i3}"], bias=1.0)
            dc = kb - (LB0 + 4 * g)
            if dc >= 0:
                P.tt(spb[i3][:], spb[i3][:], msk[:, dc, :], ALU.mult, [pfx + f"sp{i3}", pfx + "tab"],
                     [pfx + f"sp{i3}"])
            if not last:
                if first:
                    P.op("dve", lambda e: e.tensor_copy(out=sacc[i3][:], in_=spb[i3][:]),
                         [pfx + f"sp{i3}"], [pfx + f"sacc{i3}"])
                else:
                    p3 = (idx - 1) % 3
                    P.tt(sacc[i3][:], sacc[p3][:], spb[i3][:], ALU.add, [pfx + f"sacc{p3}", pfx + f"sp{i3}"],
                         [pfx + f"sacc{i3}"])

        def stageB(idx):
            hh, g, kb, first, last = tuples[idx]
            b = hh % 2
            i2, i3 = idx % 2, idx % 3
            q = qs[b][:, g * NT:(g + 1) * NT]
            kt = KT[b][:, kb * 128:(kb + 1) * 128]
            bB, kB = ps[2 + i2], f"ps{2 + i2}"
            P.mm(bB[:], C.trineg[:], spb[i3][:], True, False, [pfx + f"sp{i3}", "const"], [kB])
            if not first:
                p3 = (idx - 1) % 3
                P.mm(bB[:], C.negones[:], sacc[p3][:], False, False, [pfx + f"sacc{p3}", "const"], [kB])
            P.mm(bB[:], kt, q, False, True, [pfx + f"KT{b}", pfx + f"qs{b}"], [kB])
            P.act(wb[i3][:], bB[:], AF.Exp, [kB], [pfx + f"w{i3}"])
            dc = kb - (LB0 + 4 * g)
            if dc >= 0:
                P.tt(wb[i3][:], wb[i3][:], msk[:, dc, :], ALU.mult, [pfx + f"w{i3}", pfx + "tab"], [pfx + f"w{i3}"])

        def stageC(idx):
            hh, g, kb, first, last = tuples[idx]
            i3 = idx % 3
            gi = (hh * NG + g) % 2
            bO, kO = ps[4 + gi], f"ps{4 + gi}"
            P.mm(bO[0:64, :], VA[:, kb, hh * 64:(hh + 1) * 64], wb[i3][:], first, last,
                 [pfx + "VA", pfx + f"w{i3}"], [kO])
            if last:
                P.op("dve", lambda e: e.tensor_copy(out=ob[gi][:], in_=bO[0:64, :]), [kO], [pfx + f"ob{gi}"])
                P.dma(yT[256 + hh * 64:256 + (hh + 1) * 64, own0 + g * NT:own0 + (g + 1) * NT], ob[gi][:],
                      [pfx + f"ob{gi}"], [])

        n = len(tuples)
        for i in range(n + 2):
            if i < n:
                stageA(i)
            if 1 <= i <= n:
                stageB(i - 1)
            if i >= 2:
                stageC(i - 2)
        P.flush()


def outproj_phase(P, C, tag, w_o, T, xsrc, xdst, groups):
    nc = P.nc
    pfx = tag + "_"
    ps = C.ps
    xs = xsrc.rearrange("(c p) t -> p c t", p=128)
    xd = xdst.rearrange("(c p) t -> p c t", p=128)
    yr = T["yT"].rearrange("(c p) t -> p c t", p=128)
    with ExitStack() as es:
        sb = lambda n, shp, dt: es.enter_context(nc.sbuf_tensor(pfx + n, shp, dt))
        wo = sb("wo", [128, 8, D], BF16)
        yin = [sb(f"yin{i}", [128, 8, NT], BF16) for i in range(2)]
        xin = [sb(f"xin{i}", [128, 8, NT], F32) for i in range(2)]
        xo = [sb(f"xo{i}", [128, 8, NT], F32) for i in range(2)]
        P.dma(wo[:], w_o.rearrange("(c p) m -> p c m", p=128), [], [pfx + "wo"], q="pool")
        for g in groups:
            cols = slice(g * NT, (g + 1) * NT)
            b = g % 2
            P.dma(yin[b][:], yr[:, :, cols], [], [pfx + f"yin{b}"])
            P.dma(xin[b][:], xs[:, :, cols], [], [pfx + f"xin{b}"])
            for m in range(8):
                bO, kO = ps[m % 4], f"ps{m % 4}"
                for c in range(8):
                    P.mm(bO[:], wo[:, c, m * 128:(m + 1) * 128], yin[b][:, c, :], c == 0, c == 7,
                         [pfx + "wo", pfx + f"yin{b}"], [kO])
                P.tt(xo[b][:, m, :], bO[:], xin[b][:, m, :], ALU.add, [kO, pfx + f"xin{b}"], [pfx + f"xo{b}"])
            P.dma(xd[:, :, cols], xo[b][:], [pfx + f"xo{b}"], [])
        P.flush()


def final_phase(P, C, tag, gam_d, xsrc, out_d):
    nc = P.nc
    pfx = tag + "_"
    xs = xsrc.rearrange("(c p) t -> p c t", p=128)
    od = out_d.rearrange("(c p) t -> p c t", p=128)
    with ExitStack() as es:
        sb = lambda n, shp, dt: es.enter_context(nc.sbuf_tensor(pfx + n, shp, dt))
        gam = sb("gam", [128, 8], F32)
        xin = [sb(f"xin{i}", [128, 8, NT], F32) for i in range(2)]
        sqc = [sb(f"sqc{i}", [128, NT], BF16) for i in range(2)]
        rstd = sb("rstd", [128, NT], F32)
        yo = [sb(f"yo{i}", [128, 8, NT], F32) for i in range(2)]
        P.dma(gam[:], gam_d, [], [pfx + "gam"])
        for g in range(NG):
            cols = slice(NLOC + g * NT, NLOC + (g + 1) * NT)
            ocols = slice(g * NT, (g + 1) * NT)
            b = g % 2
            P.dma(xin[b][:], xs[:, :, cols], [], [pfx + f"xin{b}"])
            for c in range(8):
                P.act(sqc[c % 2][:], xin[b][:, c, :], AF.Square, [pfx + f"xin{b}"], [pfx + f"sqc{c % 2}"])
                P.mm(C.ps[6][:], C.ones_bf[:], sqc[c % 2][:], c == 0, c == 7, [pfx + f"sqc{c % 2}", "const"], ["ps6"])
            P.ts(rstd[:], C.ps[6][:], 1.0 / D, EPS, ALU.mult, ALU.add, ["ps6"], [pfx + "rstd"])
            rsqrt_inplace(P, rstd[:], pfx + "rstd")
            for c in range(8):
                P.stt(yo[b][:, c, :], xin[b][:, c, :], gam[:, c:c + 1], rstd[:], ALU.mult, ALU.mult,
                      [pfx + f"xin{b}", pfx + "rstd", pfx + "gam"], [pfx + f"yo{b}"])
            P.dma(od[:, :, ocols], yo[b][:], [pfx + f"yo{b}"], [], is_out=True)


MIXT = {
    "vconv": ([256, NSLOT], BF16), "qsT": ([512, NSLOT], BF16), "kT": ([512, NSLOT], BF16),
    "vtok": ([NSLOT, 512], BF16), "vr": ([NSLOT, 256], BF16), "qrT": ([256, NSLOT], BF16),
    "qdT": ([256, NSLOT], BF16), "krT": ([256, NSLOT], BF16), "kdtok": ([NSLOT, 256], BF16),
    "gsil": ([256, NSLOT], F32), "yT": ([D, NSLOT], BF16),
}
CONSTS = {"ones_bf": ([128, 128], BF16), "ones_f": ([128, 128], F32), "ident": ([128, 128], BF16),
          "trineg": ([128, 128], BF16), "negones": ([128, 128], BF16)}
TABLES = {"rope_cos": [128, NSLOT], "rope_sin": [128, NSLOT], "decq": [128, 2, NT], "deck": [128, 256],
          "g64": [64, 256], "dintra": [64, 4, NT], "flag": [128, 1]}
ALLG = list(range(2 * NG))
OWNG = list(range(NG, 2 * NG))


def build_program():
    nc = bass.Bass("TRN2", target_bir_lowering=False)
    dt = lambda name, shape, dtype, kind: nc.dram_tensor(name, shape, dtype, kind=kind).ap()
    C = Ctx()
    W = {}
    for l in range(DEPTH):
        for f in ("1", "2"):
            W[f"ffn{f}_w_in{l}"] = dt(f"ffn{f}_w_in{l}", [D, 2 * DFF], F32, "ExternalInput")
            W[f"ffn{f}_w_out{l}"] = dt(f"ffn{f}_w_out{l}", [DFF, D], F32, "ExternalInput")
            W[f"ffn{f}_norm{l}"] = dt(f"ffn{f}_norm{l}", [128, 8], F32, "ExternalInput")
        W[f"mix_w_in{l}"] = dt(f"mix_w_in{l}", [D, 3072], F32, "ExternalInput")
        W[f"mix_norm{l}"] = dt(f"mix_norm{l}", [128, 8], F32, "ExternalInput")
        W[f"mix_w_out{l}"] = dt(f"mix_w_out{l}", [D, D], F32, "ExternalInput")
        W[f"conv_w{l}"] = dt(f"conv_w{l}", [128, 2, 31], F32, "ExternalInput")
        for nm in ("conv_b", "conv_ln_g", "conv_ln_b"):
            W[f"{nm}{l}"] = dt(f"{nm}{l}", [128, 2], F32, "ExternalInput")
        W[f"ret_norm_g{l}"] = dt(f"ret_norm_g{l}", [64, 4], F32, "ExternalInput")
    W["final_norm"] = dt("final_norm", [128, 8], F32, "ExternalInput")
    cd = {k: dt("c_" + k, s, d, "ExternalInput") for k, (s, d) in CONSTS.items()}
    for k, s in TABLES.items():
        setattr(C, k, dt("t_" + k, s, F32, "ExternalInput"))
    C.sbmask = dt("t_sbmask", [128, 4, NT], BF16, "ExternalInput")
    x_in = dt("x_in", [D, NSLOT], F32, "ExternalInput")
    xa = dt("xa", [D, NSLOT], F32, "Internal")
    xb = dt("xb", [D, NSLOT], F32, "Internal")
    T = {k: dt("m_" + k, s, d, "Internal") for k, (s, d) in MIXT.items()}
    out = dt("out", [D, NLOC], F32, "ExternalOutput")

    with ExitStack() as es:
        P = Prog(nc, es)
        C.ps = [es.enter_context(nc.psum_tensor(f"ps{i}", [128, NT], F32)) for i in range(7)]
        C.ps.append(C.ps[6])
        pst = es.enter_context(nc.psum_tensor("pst", [128, 2 * NT], BF16))
        C.pst = pst
        C.ps[7] = pst.bitcast(F32)
        for k, (s, d) in CONSTS.items():
            t = es.enter_context(nc.sbuf_tensor("k_" + k, s, d))
            setattr(C, k, t)
            P.dma(t[:], cd[k], [], ["const"])
        P.flush()
        mixw = lambda l: (W[f"conv_w{l}"], W[f"conv_b{l}"], W[f"conv_ln_g{l}"], W[f"conv_ln_b{l}"],
                          W[f"ret_norm_g{l}"])
        ffn_phase(P, C, "f1a", W["ffn1_w_in0"], W["ffn1_w_out0"], W["ffn1_norm0"], x_in, xa, ALLG)
        inproj_phase(P, C, "ipa", W["mix_w_in0"], W["mix_norm0"], xa, T, ALLG)
        retconv_phase(P, C, "rc0a", T, *mixw(0), own0=0, has_prev=False)
        sb_phase(P, C, "sb0a", T, own0=0, has_prev=False)
        retconv_phase(P, C, "rc0b", T, *mixw(0), own0=NLOC, has_prev=True)
        sb_phase(P, C, "sb0b", T, own0=NLOC, has_prev=True)
        outproj_phase(P, C, "opa", W["mix_w_out0"], T, xa, xb, ALLG)
        ffn_phase(P, C, "f2a", W["ffn2_w_in0"], W["ffn2_w_out0"], W["ffn2_norm0"], xb, xa, ALLG, flag_prefix=True)
        ffn_phase(P, C, "f1b", W["ffn1_w_in1"], W["ffn1_w_out1"], W["ffn1_norm1"], xa, xb, ALLG)
        inproj_phase(P, C, "ipb", W["mix_w_in1"], W["mix_norm1"], xb, T, ALLG, kv_only=tuple(range(NG)))
        retconv_phase(P, C, "rc1", T, *mixw(1), own0=NLOC, has_prev=True)
        sb_phase(P, C, "sb1", T, own0=NLOC, has_prev=True)
        outproj_phase(P, C, "opb", W["mix_w_out1"], T, xb, xa, OWNG)
        ffn_phase(P, C, "f2b", W["ffn2_w_in1"], W["ffn2_w_out1"], W["ffn2_norm1"], xa, xb, OWNG)
        final_phase(P, C, "fin", W["final_norm"], xb, out)
        P.flush(final=True)
    return nc


def _consts():
    bf = ml_dtypes.bfloat16
    j = np.arange(128)
    c = {
        "c_ones_bf": np.ones((128, 128), bf), "c_ones_f": np.ones((128, 128), np.float32),
        "c_ident": np.eye(128).astype(bf),
        "c_trineg": (-(j[:, None] >= j[None, :]).astype(np.float32)).astype(bf),
        "c_negones": (-np.ones((128, 128), np.float32)).astype(bf),
    }
    gam = 1.0 - np.exp2(-5.0 - np.arange(4, dtype=np.float64))
    i512 = np.arange(NT)
    p = np.arange(128)
    decq = np.zeros((128, 2, NT))
    for cc in range(2):
        hh = 2 * cc + p // 64
        decq[:, cc, :] = 0.125 * gam[hh][:, None] ** ((i512 % 64) + 1.0)[None, :]
    col = np.arange(256)
    deck = gam[col // 64][None, :] ** (63.0 - (p % 64))[:, None]
    g64 = np.broadcast_to((gam[col // 64] ** 64.0)[None, :], (64, 256))
    jj = np.arange(64)
    dintra = np.zeros((64, 4, NT))
    for hh in range(4):
        dintra[:, hh, :] = gam[hh] ** np.abs(jj[:, None] - (i512 % 64)[None, :])
    sbmask = np.zeros((128, 4, NT), np.float32)
    for cc in range(4):
        sbmask[:, cc, :] = ((cc * 128 + p)[:, None] < i512[None, :])
    c.update({"t_decq": decq.astype(np.float32), "t_deck": deck.astype(np.float32),
              "t_g64": np.ascontiguousarray(g64).astype(np.float32), "t_dintra": dintra.astype(np.float32),
              "t_sbmask": sbmask.astype(bf)})
    return c


def _rope(half):
    slot = np.arange(NSLOT)
    pos = (slot if half == 1 else slot % NLOC).astype(np.float32)
    inv = (1.0 / (10000.0 ** (np.arange(32, dtype=np.float32) / 32))).astype(np.float32)
    p = np.arange(128)
    ang = pos[None, :] * inv[(p % 64) % 32][:, None]
    sign = np.where((p % 64) < 32, -1.0, 1.0)[:, None]
    return np.cos(ang).astype(np.float32), (sign * np.sin(ang)).astype(np.float32)


def _vec8(v):
    return np.ascontiguousarray(v.reshape(8, 128).T)


def _vec2(v):
    return np.ascontiguousarray(v.reshape(2, 128).T)


def kernel(**inp):
    ncores = 8
    x = inp["x"]
    w = dict(_consts())
    for l in range(DEPTH):
        for f in ("1", "2"):
            w[f"ffn{f}_w_in{l}"] = np.ascontiguousarray(inp[f"ffn{f}_w_in"][l])
            w[f"ffn{f}_w_out{l}"] = np.ascontiguousarray(inp[f"ffn{f}_w_out"][l])
            w[f"ffn{f}_norm{l}"] = _vec8(inp[f"ffn{f}_norm"][l])
        w[f"mix_w_in{l}"] = np.ascontiguousarray(inp["mix_w_in"][l])
        w[f"mix_norm{l}"] = _vec8(inp["mix_norm"][l])
        w[f"mix_w_out{l}"] = np.ascontiguousarray(inp["mix_w_out"][l])
        w[f"conv_w{l}"] = np.ascontiguousarray(inp["conv_w"][l].T.reshape(2, 128, 31).transpose(1, 0, 2))
        for nm in ("conv_b", "conv_ln_g", "conv_ln_b"):
            w[f"{nm}{l}"] = _vec2(inp[nm][l])
        w[f"ret_norm_g{l}"] = np.ascontiguousarray(inp["ret_norm_g"][l].reshape(4, 64).T)
    w["final_norm"] = _vec8(inp["final_norm"])
    ropes = [_rope(0), _rope(1)]
    maps = []
    for core in range(ncores):
        b, half = core // 2, core % 2
        m = dict(w)
        m["t_rope_cos"], m["t_rope_sin"] = ropes[half]
        m["t_flag"] = np.full((128, 1), float(half), np.float32)
        xi = np.zeros((D, NSLOT), np.float32)
        if half == 1:
            xi[:, :] = x[b].T
        else:
            xi[:, NLOC:] = x[b, :NLOC, :].T
        m["x_in"] = xi
        maps.append(m)
    res = run_bass_kernel_spmd(build_program(), maps, core_ids=list(range(ncores))).results
    out = np.empty(x.shape, np.float32)
    for c in range(ncores):
        out[c // 2, (c % 2) * NLOC:(c % 2 + 1) * NLOC, :] = res[c]["out"].T
    return out
```

```python
import numpy as np
import ml_dtypes
from contextlib import ExitStack
import concourse.bass as bass
import concourse.mybir as mybir
from concourse.bass_utils import run_bass_kernel_spmd

F32 = mybir.dt.float32
BF16 = mybir.dt.bfloat16
AF = mybir.ActivationFunctionType
ALU = mybir.AluOpType

D = 1024
DFF = 2816
NLOC = 4096
NSLOT = 8192
NT = 512
NG = NLOC // NT
DEPTH = 2
EPS = 1e-6
NDS = 8


class Op:
    __slots__ = ("stream", "fn", "deps", "dma", "token", "needed", "phase", "wkeys")


class Prog:
    STREAMS = ["pe", "act", "dve", "pool", "sp"]

    def __init__(self, nc, es):
        self.nc = nc
        self.csem = {s: es.enter_context(nc.semaphore("c_" + s)) for s in ["pe", "act", "dve", "pool"]}
        self.dsem = {s: [es.enter_context(nc.semaphore(f"d_{s}{i}")) for i in range(NDS)]
                     for s in ["sp", "pool"]}
        self.ccnt = {s: 0 for s in self.csem}
        self.dcnt = {s: [0] * NDS for s in self.dsem}
        self.dnum = {s: 0 for s in self.dsem}
        self.lastw = {}
        self.readers = {}
        self.ops = []
        self.phase = 0
        self.waited = {s: {} for s in self.STREAMS}
        self.pending = {s: {} for s in self.STREAMS}
        self.out_tokens = []

    def op(self, stream, fn, reads=(), writes=(), dma=False, is_out=False):
        o = Op()
        o.stream, o.fn, o.dma, o.needed, o.phase, o.token = stream, fn, dma, False, self.phase, None
        o.wkeys = set(writes)
        deps = {}
        for k in reads:
            w = self.lastw.get(k)
            if w is not None:
                deps[id(w)] = (w, True)
        for k in writes:
            w = self.lastw.get(k)
            if w is not None and id(w) not in deps:
                deps[id(w)] = (w, False)
            for r in self.readers.get(k, {}).values():
                if id(r) not in deps:
                    deps[id(r)] = (r, False)
        o.deps = []
        for d, raw in deps.values():
            if d.phase != self.phase or d is o:
                continue
            if d.stream == stream and not d.dma and not dma:
                if stream == "pe" or not raw:
                    continue
            o.deps.append(d)
        if dma:
            i = self.dnum[stream] % NDS
            self.dnum[stream] += 1
            self.dcnt[stream][i] += 16
            o.token = (self.dsem[stream][i], self.dcnt[stream][i])
            rkey = (stream, i)
            if is_out:
                self.out_tokens.append(o.token)
        else:
            rkey = (stream, -1)
        for k in reads:
            self.readers.setdefault(k, {})[rkey] = o
        for k in writes:
            self.lastw[k] = o
            self.readers[k] = {}
        self.ops.append(o)
        return o

    def flush(self, final=False):
        ops = self.ops
        for o in ops:
            for d in o.deps:
                d.needed = True
        last = {}
        for o in ops:
            if not o.dma:
                last[o.stream] = o
        for o in last.values():
            o.needed = True
        for o in ops:
            if not o.dma and o.needed:
                self.ccnt[o.stream] += 1
                o.token = (self.csem[o.stream], self.ccnt[o.stream])
        by_stream = {s: [o for o in ops if o.stream == s] for s in self.STREAMS}
        snapshot = {}
        for s in self.csem:
            if self.ccnt[s]:
                snapshot[id(self.csem[s])] = (self.csem[s], self.ccnt[s])
        for s in self.dsem:
            for i in range(NDS):
                if self.dcnt[s][i]:
                    snapshot[id(self.dsem[s][i])] = (self.dsem[s][i], self.dcnt[s][i])

        def emit(stream, eng):
            waited = self.waited[stream]
            for o in by_stream[stream]:
                w = dict(self.pending[stream])
                self.pending[stream] = {}
                for d in o.deps:
                    sem, val = d.token
                    if id(sem) not in w or w[id(sem)][1] < val:
                        w[id(sem)] = (sem, val)
                for sid, (sem, val) in w.items():
                    if waited.get(sid, 0) < val:
                        eng.wait_ge(sem, val)
                        waited[sid] = val
                ins = o.fn(eng)
                if o.token is not None:
                    ins.then_inc(o.token[0], 16 if o.dma else 1)
            if final and stream == "sp":
                for sid, (sem, val) in snapshot.items():
                    if waited.get(sid, 0) < val:
                        eng.wait_ge(sem, val)
                        waited[sid] = val

        with self.nc.Block() as blk:
            if by_stream["pe"]:
                blk.tensor(lambda e: emit("pe", e))
            if by_stream["act"]:
                blk.scalar(lambda e: emit("act", e))
            if by_stream["dve"]:
                blk.vector(lambda e: emit("dve", e))
            if by_stream["pool"]:
                blk.gpsimd(lambda e: emit("pool", e))
            if by_stream["sp"] or final:
                blk.sync(lambda e: emit("sp", e))
        for s in self.STREAMS:
            p = self.pending[s]
            for sid, (sem, val) in snapshot.items():
                if sid not in p or p[sid][1] < val:
                    p[sid] = (sem, val)
        self.ops = []
        self.phase += 1

    def dma(self, out, in_, reads, writes, q="sp", is_out=False):
        return self.op(q, lambda e: e.dma_start(out=out, in_=in_), reads, writes, dma=True, is_out=is_out)

    def mm(self, out, lhsT, rhs, start, stop, reads, writes):
        return self.op("pe", lambda e: e.matmul(out, lhsT, rhs, start=start, stop=stop), reads, writes)

    def act(self, out, in_, func, reads, writes, bias=None, scale=None):
        kw = {}
        if bias is not None:
            kw["bias"] = bias
        if scale is not None:
            kw["scale"] = scale
        return self.op("act", lambda e: e.activation(out=out, in_=in_, func=func, **kw), reads, writes)

    def tt(self, out, in0, in1, op, reads, writes, eng="dve"):
        return self.op(eng, lambda e: e.tensor_tensor(out=out, in0=in0, in1=in1, op=op), reads, writes)

    def ts(self, out, in0, s1, s2, op0, op1, reads, writes, eng="dve"):
        if op1 is None:
            return self.op(eng, lambda e: e.tensor_scalar(out=out, in0=in0, scalar1=s1, scalar2=None, op0=op0),
                           reads, writes)
        return self.op(eng, lambda e: e.tensor_scalar(out=out, in0=in0, scalar1=s1, scalar2=s2, op0=op0, op1=op1),
                       reads, writes)

    def stt(self, out, in0, scalar, in1, op0, op1, reads, writes, eng="dve"):
        return self.op(eng, lambda e: e.scalar_tensor_tensor(out=out, in0=in0, scalar=scalar, in1=in1,
                                                             op0=op0, op1=op1), reads, writes)


class Ctx:
    pass


def rsqrt_inplace(P, t, key):
    P.act(t, t, AF.Sqrt, [key], [key])
    P.op("dve", lambda e: e.reciprocal(out=t, in_=t), [key], [key])


def rms_norm_group(P, C, pfx, xin, gam, h, sqc, rstd, g):
    ps = C.ps
    for c in range(8):
        P.act(sqc[c % 2][:], xin[:, c, :], AF.Square, [pfx + "xin"], [pfx + f"sqc{c % 2}"])
        P.mm(ps[6][:], C.ones_bf[:], sqc[c % 2][:], c == 0, c == 7, [pfx + f"sqc{c % 2}", "const"], ["ps6"])
    P.ts(rstd[:], ps[6][:], 1.0 / D, EPS, ALU.mult, ALU.add, ["ps6"], [pfx + "rstd"])
    rsqrt_inplace(P, rstd[:], pfx + "rstd")
    for c in range(8):
        P.stt(h[:, c, :], xin[:, c, :], gam[:, c:c + 1], rstd[:], ALU.mult, ALU.mult,
              [pfx + "xin", pfx + "rstd", pfx + "gam"], [pfx + "h"])


def ffn_phase(P, C, tag, w_in, w_out, gam_d, xsrc, xdst, groups, flag_prefix=False):
    nc = P.nc
    pfx = tag + "_"
    ps = C.ps
    xs = xsrc.rearrange("(c p) t -> p c t", p=128)
    xd = xdst.rearrange("(c p) t -> p c t", p=128)
    with ExitStack() as es:
        sb = lambda n, shp, dt: es.enter_context(nc.sbuf_tensor(pfx + n, shp, dt))
        win = sb("win", [128, 8, 2 * DFF], BF16)
        wout = sb("wout", [128, 22, D], BF16)
        gam = sb("gam", [128, 8], F32)
        xin = sb("xin", [128, 8, NT], F32)
        h = sb("h", [128, 8, NT], BF16)
        sqc = [sb(f"sqc{i}", [128, NT], BF16) for i in range(2)]
        rstd = sb("rstd", [128, NT], F32)
        a = sb("a", [128, 22, NT], BF16)
        sg = [sb(f"sg{i}", [128, NT], F32) for i in range(2)]
        xr = [sb(f"xr{i}", [128, NT], F32) for i in range(2)]
        xo = [sb(f"xo{i}", [128, NT], F32) for i in range(2)]
        flg = sb("flg", [128, 1], F32)

        P.dma(gam[:], gam_d, [], [pfx + "gam"])
        P.dma(flg[:], C.flag, [], [pfx + "flg"])
        g0 = groups[0]
        P.dma(xin[:], xs[:, :, g0 * NT:(g0 + 1) * NT], [], [pfx + "xin"])
        for c in range(8):
            P.dma(win[:, c, :], w_in[c * 128:(c + 1) * 128, :], [], [pfx + f"win{c}"], q="pool")
        wo_r = w_out.rearrange("(j p) m -> p j m", p=128)
        for j0 in range(0, 22, 6):
            j1 = min(22, j0 + 6)
            P.dma(wout[:, j0:j1, :], wo_r[:, j0:j1, :], [], [pfx + f"wout{jj}" for jj in range(j0, j1)], q="pool")
        WIN = [pfx + f"win{c}" for c in range(8)]
        WOUT = [pfx + f"wout{j}" for j in range(22)]

        rms_norm_group(P, C, pfx, xin, gam, h, sqc, rstd, 0)
        for gi_, g in enumerate(groups):
            cols = slice(g * NT, (g + 1) * NT)
            gn = groups[gi_ + 1] if gi_ + 1 < len(groups) else None
            if gn is not None:
                P.dma(xin[:], xs[:, :, gn * NT:(gn + 1) * NT], [], [pfx + "xin"])
            for j in range(22):
                bG, bU = ps[(j % 2) * 2], ps[(j % 2) * 2 + 1]
                kG, kU = f"ps{(j % 2) * 2}", f"ps{(j % 2) * 2 + 1}"
                for c in range(8):
                    P.mm(bG[:], win[:, c, j * 128:(j + 1) * 128], h[:, c, :], c == 0, c == 7,
                         [pfx + "h", WIN[c]], [kG])
                for c in range(8):
                    P.mm(bU[:], win[:, c, DFF + j * 128:DFF + (j + 1) * 128], h[:, c, :], c == 0, c == 7,
                         [pfx + "h", WIN[c]], [kU])
                P.act(sg[j % 2][:], bG[:], AF.Silu, [kG], [pfx + f"sg{j % 2}"])
                P.tt(a[:, j, :], sg[j % 2][:], bU[:], ALU.mult, [pfx + f"sg{j % 2}", kU], [pfx + f"a{j}"])
            if gn is not None:
                rms_norm_group(P, C, pfx, xin, gam, h, sqc, rstd, gn)
            for m in range(8):
                P.dma(xr[m % 2][:], xs[:, m, cols], [], [pfx + f"xr{m % 2}"])
                bO, kO = ps[4 + m % 2], f"ps{4 + m % 2}"
                for j in range(22):
                    P.mm(bO[:], wout[:, j, m * 128:(m + 1) * 128], a[:, j, :], j == 0, j == 21,
                         [pfx + f"a{j}", WOUT[j]], [kO])
                P.stt(xo[m % 2][:], bO[:], 0.5, xr[m % 2][:], ALU.mult, ALU.add,
                      [kO, pfx + f"xr{m % 2}"], [pfx + f"xo{m % 2}"])
                if flag_prefix and g < NG:
                    P.ts(xo[m % 2][:], xo[m % 2][:], flg[:, 0:1], None, ALU.mult, None,
                         [pfx + f"xo{m % 2}", pfx + "flg"], [pfx + f"xo{m % 2}"])
                P.dma(xd[:, m, cols], xo[m % 2][:], [pfx + f"xo{m % 2}"], [])
        P.flush()


def inproj_phase(P, C, tag, w_mix, gam_d, xsrc, T, groups, kv_only=()):
    nc = P.nc
    pfx = tag + "_"
    ps = C.ps
    xs = xsrc.rearrange("(c p) t -> p c t", p=128)
    with ExitStack() as es:
        sb = lambda n, shp, dt: es.enter_context(nc.sbuf_tensor(pfx + n, shp, dt))
        wm = sb("wm", [128, 8, 3072], BF16)
        wsw = sb("wsw", [128, 8, 512], BF16)
        gam = sb("gam", [128, 8], F32)
        xin = sb("xin", [128, 8, NT], F32)
        h = sb("h", [128, 8, NT], BF16)
        sqc = [sb(f"sqc{i}", [128, NT], BF16) for i in range(2)]
        rstd = sb("rstd", [128, NT], F32)
        cosb = [sb(f"cosb{i}", [128, NT], F32) for i in range(2)]
        sinb = [sb(f"sinb{i}", [128, NT], F32) for i in range(2)]
        decq = sb("decq", [128, 2, NT], F32)
        deck = sb("deck", [128, 256], F32)
        t1 = [sb(f"t1{i}", [128, NT], F32) for i in range(2)]
        t2 = [sb(f"t2{i}", [128, NT], F32) for i in range(2)]
        ob = [sb(f"ob{i}", [128, NT], BF16) for i in range(4)]
        of = [sb(f"of{i}", [128, NT], F32) for i in range(2)]
        krb = [sb(f"krb{i}", [128, NT], BF16) for i in range(2)]
        kdb = [sb(f"kdb{i}", [128, 256], BF16) for i in range(2)]

        P.dma(gam[:], gam_d, [], [pfx + "gam"])
        P.dma(xin[:], xs[:, :, groups[0] * NT:(groups[0] + 1) * NT], [], [pfx + "xin"])
        P.dma(decq[:], C.decq, [], [pfx + "tab"])
        P.dma(deck[:], C.deck, [], [pfx + "tab"])
        for c in range(8):
            P.dma(wm[:, c, :], w_mix[c * 128:(c + 1) * 128, :], [], [pfx + "wm"], q="pool")
            src = w_mix[c * 128:(c + 1) * 128, 2048:2560].rearrange("p (h two d) -> p h two d", two=2, d=32)
            dst = wsw[:, c, :].rearrange("p (h two d) -> p h two d", two=2, d=32)
            P.dma(dst[:, :, 0, :], src[:, :, 1, :], [], [pfx + "wm"], q="pool")
            P.dma(dst[:, :, 1, :], src[:, :, 0, :], [], [pfx + "wm"], q="pool")

        cnt = {"b": 0, "ob": 0, "of": 0, "t": 0, "kr": 0, "kd": 0}

        def bank():
            i = cnt["b"] % 6
            cnt["b"] += 1
            return ps[i], f"ps{i}"

        def proj(bk, kk, wt, c0, ncols):
            for c in range(8):
                P.mm(bk[:, 0:NT] if ncols == 128 else bk[:], wt[:, c, c0:c0 + 128], h[:, c, :], c == 0, c == 7,
                     [pfx + "h", pfx + "wm"], [kk])

        for gi_, g in enumerate(groups):
            cols = slice(g * NT, (g + 1) * NT)
            kvo = g in kv_only
            rb = gi_ % 2
            cosT, sinT = cosb[rb], sinb[rb]
            P.dma(cosT[:], C.rope_cos[:, cols], [], [pfx + f"rope{rb}"])
            P.dma(sinT[:], C.rope_sin[:, cols], [], [pfx + f"rope{rb}"])
            rms_norm_group(P, C, pfx, xin, gam, h, sqc, rstd, g)
            if gi_ + 1 < len(groups):
                gn = groups[gi_ + 1]
                P.dma(xin[:], xs[:, :, gn * NT:(gn + 1) * NT], [], [pfx + "xin"])
            for cc in range(2):
                if kvo and g != NG - 1:
                    continue
                bA, kA = bank()
                proj(bA, kA, wm, cc * 128, 128)
                bB, kB = bank()
                proj(bB, kB, wm, 256 + cc * 128, 128)
                i = cnt["t"] % 2
                cnt["t"] += 1
                P.act(t1[i][:], bB[:], AF.Sigmoid, [kB], [pfx + f"t1{i}"])
                o = cnt["ob"] % 4
                cnt["ob"] += 1
                P.tt(ob[o][:], bA[:], t1[i][:], ALU.mult, [kA, pfx + f"t1{i}"], [pfx + f"ob{o}"])
                P.dma(T["vconv"][cc * 128:(cc + 1) * 128, cols], ob[o][:], [pfx + f"ob{o}"], [])
            for sec, dname, scl in ((512, "qsT", 0.125), (1024, "kT", 1.0)):
                if kvo and dname == "qsT":
                    continue
                for cc in range(4):
                    bk, kk = bank()
                    proj(bk, kk, wm, sec + cc * 128, 128)
                    o = cnt["ob"] % 4
                    cnt["ob"] += 1
                    P.ts(ob[o][:], bk[:], scl, None, ALU.mult, None, [kk], [pfx + f"ob{o}"])
                    P.dma(T[dname][cc * 128:(cc + 1) * 128, cols], ob[o][:], [pfx + f"ob{o}"], [])
            for tb in range(4):
                tsl = slice(tb * 128, (tb + 1) * 128)
                rows = slice(g * NT + tb * 128, g * NT + (tb + 1) * 128)
                bk, kk = bank()
                for c in range(8):
                    P.mm(bk[:], h[:, c, tsl], wm[:, c, 1536:2048], c == 0, c == 7, [pfx + "h", pfx + "wm"], [kk])
                o = cnt["ob"] % 4
                cnt["ob"] += 1
                P.act(ob[o][:], bk[:], AF.Copy, [kk], [pfx + f"ob{o}"])
                P.dma(T["vtok"][rows, :], ob[o][:], [pfx + f"ob{o}"], [])
                bk, kk = bank()
                for c in range(8):
                    P.mm(bk[:, 0:256], h[:, c, tsl], wm[:, c, 2560:2816], c == 0, c == 7,
                         [pfx + "h", pfx + "wm"], [kk])
                o = cnt["ob"] % 4
                cnt["ob"] += 1
                P.act(ob[o][:, 0:256], bk[:, 0:256], AF.Copy, [kk], [pfx + f"ob{o}"])
                P.dma(T["vr"][rows, :], ob[o][:, 0:256], [pfx + f"ob{o}"], [])
            for isk, sec, swc in ((0, 2048, 0), (1, 2304, 256)):
                if kvo and not isk:
                    continue
                for cc in range(2):
                    bX, kX = bank()
                    proj(bX, kX, wm, sec + cc * 128, 128)
                    bS, kS = bank()
                    proj(bS, kS, wsw, swc + cc * 128, 128)
                    i = cnt["t"] % 2
                    cnt["t"] += 1
                    P.tt(t1[i][:], bX[:], cosT[:], ALU.mult, [kX, pfx + f"rope{rb}"], [pfx + f"t1{i}"])
                    P.tt(t2[i][:], bS[:], sinT[:], ALU.mult, [kS, pfx + f"rope{rb}"], [pfx + f"t2{i}"])
                    P.tt(t1[i][:], t1[i][:], t2[i][:], ALU.add, [pfx + f"t1{i}", pfx + f"t2{i}"], [pfx + f"t1{i}"],
                         eng="pool")
                    frows = slice(cc * 128, (cc + 1) * 128)
                    if not isk:
                        o = cnt["ob"] % 4
                        cnt["ob"] += 1
                        P.ts(ob[o][:], t1[i][:], 0.125, None, ALU.mult, None, [pfx + f"t1{i}"], [pfx + f"ob{o}"],
                             eng="pool")
                        P.dma(T["qrT"][frows, cols], ob[o][:], [pfx + f"ob{o}"], [])
                        o = cnt["ob"] % 4
                        cnt["ob"] += 1
                        P.tt(ob[o][:], t1[i][:], decq[:, cc, :], ALU.mult, [pfx + f"t1{i}", pfx + "tab"],
                             [pfx + f"ob{o}"], eng="pool")
                        P.dma(T["qdT"][frows, cols], ob[o][:], [pfx + f"ob{o}"], [])
                    else:
                        r = cnt["kr"] % 2
                        cnt["kr"] += 1
                        P.act(krb[r][:], t1[i][:], AF.Copy, [pfx + f"t1{i}"], [pfx + f"krb{r}"])
                        P.dma(T["krT"][frows, cols], krb[r][:], [pfx + f"krb{r}"], [])
                        for tb in range(4):
                            rows = slice(g * NT + tb * 128, g * NT + (tb + 1) * 128)
                            P.op("pe", lambda e, r=r, tb=tb: e.transpose(C.pst[:, 0:128],
                                                                          krb[r][:, tb * 128:(tb + 1) * 128],
                                                                          C.ident[:]),
                                 [pfx + f"krb{r}", "const"], ["pst"])
                            d = cnt["kd"] % 2
                            cnt["kd"] += 1
                            P.tt(kdb[d][:, 0:128], C.pst[:, 0:128], deck[:, cc * 128:(cc + 1) * 128], ALU.mult,
                                 ["pst", pfx + "tab"], [pfx + f"kdb{d}"])
                            P.dma(T["kdtok"][rows, cc * 128:(cc + 1) * 128], kdb[d][:, 0:128],
                                  [pfx + f"kdb{d}"], [])
            for cc in range(2):
                if kvo:
                    continue
                bk, kk = bank()
                proj(bk, kk, wm, 2816 + cc * 128, 128)
                o = cnt["of"] % 2
                cnt["of"] += 1
                P.act(of[o][:], bk[:], AF.Silu, [kk], [pfx + f"of{o}"])
                P.dma(T["gsil"][cc * 128:(cc + 1) * 128, cols], of[o][:], [pfx + f"of{o}"], [])
        P.flush()


def retconv_phase(P, C, tag, T, cw_d, cb_d, lg_d, lb_d, rg_d, own0, has_prev):
    nc = P.nc
    pfx = tag + "_"
    ps = C.ps
    yT = T["yT"]
    with ExitStack() as es:
        sb = lambda n, shp, dt: es.enter_context(nc.sbuf_tensor(pfx + n, shp, dt))
        vpad = sb("vpad", [128, 2, 32 + NLOC], BF16)
        dg = sb("dg", [128, 62, 128], BF16)
        yc = [sb(f"yc{i}", [128, NT], F32) for i in range(2)]
        cw = sb("cw", [128, 2, 31], F32)
        cb = sb("cb", [128, 2], F32)
        lg = sb("lg", [128, 2], F32)
        lb = sb("lb", [128, 2], F32)
        rg = sb("rg", [64, 4], F32)
        kdq = sb("kdq", [64, 4 * NLOC], BF16)
        kd64 = kdq[:, :].rearrange("p (n c) -> p n c", c=256)
        vr64 = sb("vr64", [64, 64, 256], BF16)
        sball = sb("sball", [64, 64, 256], BF16)
        S = sb("S", [64, 256], F32)
        g64 = sb("g64", [64, 256], F32)
        dintra = sb("dintra", [64, 4, NT], F32)
        qr = [kdq[:, 0:NLOC]] * 2
        kr = [kdq[:, NLOC:2 * NLOC]] * 2
        qd = [kdq[:, 2 * NLOC:3 * NLOC]] * 2
        gs = [sb(f"gs{i}", [64, NT], F32) for i in range(2)]
        stb = [sb(f"stb{i}", [64, NT], BF16) for i in range(2)]
        osb = sb("osb", [64, NT], F32)
        osq = sb("osq", [64, NT], F32)
        mean = sb("mean", [128, NT], F32)
        msq = sb("msq", [128, NT], F32)
        var = sb("var", [128, NT], F32)
        tn = [sb(f"tn{i}", [128, NT], F32) for i in range(2)]
        yb = [sb(f"yb{i}", [128, NT], BF16) for i in range(2)]
        sqy = [sb(f"sqy{i}", [128, NT], F32) for i in range(2)]

        for t, d in ((cw, cw_d), (cb, cb_d), (lg, lg_d), (lb, lb_d), (rg, rg_d), (g64, C.g64), (dintra, C.dintra)):
            P.dma(t[:], d, [], [pfx + "tab"])
        for cc in range(2):
            if has_prev:
                P.dma(vpad[:, cc, 0:32], T["vconv"][cc * 128:(cc + 1) * 128, own0 - 32:own0], [], [pfx + "vpad"])
            else:
                P.op("dve", lambda e, cc=cc: e.memset(vpad[:, cc, 0:32], 0.0), [], [pfx + "vpad"])
            P.dma(vpad[:, cc, 32:], T["vconv"][cc * 128:(cc + 1) * 128, own0:own0 + NLOC], [], [pfx + "vpad"])
            for k in range(31):
                P.ts(dg[:, cc * 31 + k, :], C.ident[:], cw[:, cc, k:k + 1], None, ALU.mult, None,
                     ["const", pfx + "tab"], [pfx + "dg"])
        P.op("dve", lambda e: e.memset(S[:], 0.0), [], [pfx + "S"])
        for half in range(2):
            if half == 0 and not has_prev:
                continue
            r0 = own0 - NLOC if half == 0 else own0
            ksrc = T["kdtok"][r0:r0 + NLOC, :]
            vsrc = T["vr"][r0:r0 + NLOC, :]
            P.dma(kd64, ksrc.rearrange("(n p) c -> p n c", p=64), [], [pfx + "kd64"])
            P.dma(vr64[:], vsrc.rearrange("(n p) c -> p n c", p=64), [], [pfx + "vr64"])
            for n in range(64):
                if half == 1:
                    P.act(sball[:, n, :], S[:], AF.Copy, [pfx + "S"], [pfx + f"sball{n}"])
                for hh in range(4):
                    hs = slice(hh * 64, (hh + 1) * 64)
                    P.mm(ps[7][0:64, hs], kd64[:, n, hs], vr64[:, n, hs], True, True,
                         [pfx + "kd64", pfx + "vr64"], ["ps7"])
                P.tt(S[:], S[:], g64[:], ALU.mult, [pfx + "S", pfx + "tab"], [pfx + "S"])
                P.tt(S[:], S[:], ps[7][0:64, 0:256], ALU.add, [pfx + "S", "ps7"], [pfx + "S"])
        for hh in range(4):
            b = 0
            hrows = slice(hh * 64, (hh + 1) * 64)
            hs = slice(hh * 64, (hh + 1) * 64)
            P.dma(qr[b], T["qrT"][hrows, own0:own0 + NLOC], [], [pfx + f"qr{b}", pfx + "kd64"])
            P.dma(kr[b], T["krT"][hrows, own0:own0 + NLOC], [], [pfx + f"kr{b}", pfx + "kd64"])
            P.dma(qd[b], T["qdT"][hrows, own0:own0 + NLOC], [], [pfx + f"qd{b}", pfx + "kd64"])
            for g in range(NG):
                cols = slice(g * NT, (g + 1) * NT)
                gi = g % 2
                ocols = slice(own0 + g * NT, own0 + (g + 1) * NT)
                P.dma(gs[gi][:], T["gsil"][hrows, ocols], [], [pfx + f"gs{gi}"])
                bS, kS = ps[gi], f"ps{gi}"
                bO, kO = ps[2 + gi], f"ps{2 + gi}"
                for c in range(8):
                    n = g * 8 + c
                    tc_ = slice(n * 64, (n + 1) * 64)
                    P.mm(bS[0:64, c * 64:(c + 1) * 64], kr[b][:, tc_], qr[b][:, tc_], True, True,
                         [pfx + f"kr{b}", pfx + f"qr{b}"], [kS])
                P.tt(stb[gi][:], bS[0:64, :], dintra[:, hh, :], ALU.mult, [kS, pfx + "tab"], [pfx + f"stb{gi}"])
                for c in range(8):
                    n = g * 8 + c
                    tc_ = slice(n * 64, (n + 1) * 64)
                    cs = slice(c * 64, (c + 1) * 64)
                    P.mm(bO[0:64, cs], vr64[:, n, hs], stb[gi][:, cs], True, False,
                         [pfx + "vr64", pfx + f"stb{gi}"], [kO])
                    P.mm(bO[0:64, cs], sball[:, n, hs], qd[b][:, tc_], False, True,
                         [pfx + f"sball{n}", pfx + f"qd{b}"], [kO])
                P.act(osb[:], bO[0:64, :], AF.Copy, [kO], [pfx + "osb"])
                P.act(osq[:], bO[0:64, :], AF.Square, [kO], [pfx + "osq"])
                P.mm(ps[4][0:64, :], C.ones_f[0:64, 0:64], osb[:], True, True, [pfx + "osb", "const"], ["ps4"])
                P.mm(ps[5][0:64, :], C.ones_f[0:64, 0:64], osq[:], True, True, [pfx + "osq", "const"], ["ps5"])
                m_, q_, v_ = mean[0:64, :], msq[0:64, :], var[0:64, :]
                P.ts(m_, ps[4][0:64, :], 1.0 / 64, None, ALU.mult, None, ["ps4"], [pfx + "mean"])
                P.tt(q_, m_, m_, ALU.mult, [pfx + "mean"], [pfx + "msq"])
                P.stt(v_, ps[5][0:64, :], 1.0 / 64, q_, ALU.mult, ALU.subtract, ["ps5", pfx + "msq"], [pfx + "var"])
                P.ts(v_, v_, EPS, None, ALU.add, None, [pfx + "var"], [pfx + "var"])
                rsqrt_inplace(P, v_, pfx + "var")
                t_ = tn[gi][0:64, :]
                P.tt(t_, osb[:], m_, ALU.subtract, [pfx + "osb", pfx + "mean"], [pfx + f"tn{gi}"])
                P.tt(t_, t_, v_, ALU.mult, [pfx + f"tn{gi}", pfx + "var"], [pfx + f"tn{gi}"])
                P.stt(yb[gi][0:64, :], t_, rg[:, hh:hh + 1], gs[gi][:], ALU.mult, ALU.mult,
                      [pfx + f"tn{gi}", pfx + "tab", pfx + f"gs{gi}"], [pfx + f"yb{gi}"])
                P.dma(yT[768 + hh * 64:768 + (hh + 1) * 64, ocols], yb[gi][0:64, :], [pfx + f"yb{gi}"], [])
        for g in range(NG):
            cols = slice(g * NT, (g + 1) * NT)
            for cc in range(2):
                for k in range(31):
                    P.mm(ps[cc][:], dg[:, cc * 31 + k, :], vpad[:, cc, 2 + k + g * NT:2 + k + (g + 1) * NT],
                         k == 0, k == 30, [pfx + "dg", pfx + "vpad"], [f"ps{cc}"])
                P.ts(yc[cc][:], ps[cc][:], cb[:, cc:cc + 1], None, ALU.add, None, [f"ps{cc}", pfx + "tab"],
                     [pfx + f"yc{cc}"])
                P.act(sqy[cc][:], yc[cc][:], AF.Square, [pfx + f"yc{cc}"], [pfx + f"sqy{cc}"])
            for cc in range(2):
                P.mm(ps[4][:], C.ones_f[:], yc[cc][:], cc == 0, cc == 1, [pfx + f"yc{cc}", "const"], ["ps4"])
            for cc in range(2):
                P.mm(ps[5][:], C.ones_f[:], sqy[cc][:], cc == 0, cc == 1, [pfx + f"sqy{cc}", "const"], ["ps5"])
            P.ts(mean[:], ps[4][:], 1.0 / 256, None, ALU.mult, None, ["ps4"], [pfx + "mean"])
            P.tt(msq[:], mean[:], mean[:], ALU.mult, [pfx + "mean"], [pfx + "msq"])
            P.stt(var[:], ps[5][:], 1.0 / 256, msq[:], ALU.mult, ALU.subtract, ["ps5", pfx + "msq"], [pfx + "var"])
            P.ts(var[:], var[:], EPS, None, ALU.add, None, [pfx + "var"], [pfx + "var"])
            rsqrt_inplace(P, var[:], pfx + "var")
            for cc in range(2):
                P.tt(tn[cc][:], yc[cc][:], mean[:], ALU.subtract, [pfx + f"yc{cc}", pfx + "mean"],
                     [pfx + f"tn{cc}"])
                P.tt(tn[cc][:], tn[cc][:], var[:], ALU.mult, [pfx + f"tn{cc}", pfx + "var"], [pfx + f"tn{cc}"])
                P.act(yb[cc][:], tn[cc][:], AF.Silu, [pfx + f"tn{cc}", pfx + "tab"], [pfx + f"yb{cc}"],
                      bias=lb[:, cc:cc + 1], scale=lg[:, cc:cc + 1])
                P.dma(yT[cc * 128:(cc + 1) * 128, own0 + g * NT:own0 + (g + 1) * NT], yb[cc][:],
                      [pfx + f"yb{cc}"], [])
        P.flush()


def sb_phase(P, C, tag, T, own0, has_prev):
    nc = P.nc
    pfx = tag + "_"
    ps = C.ps
    yT = T["yT"]
    NB = 64 if has_prev else 32
    LB0 = NB - 32
    base = own0 - (NLOC if has_prev else 0)
    NK = NB * 128
    QG = 256
    NQG = NLOC // QG
    with ExitStack() as es:
        sb = lambda n, shp, dt: es.enter_context(nc.sbuf_tensor(pfx + n, shp, dt))
        VA = sb("VA", [128, NB, 512], BF16)
        KT = [sb(f"KT{i}", [128, NK], BF16) for i in range(2)]
        QB = [sb(f"QB{i}", [128, NQG, 2 * QG], BF16) for i in range(2)]
        msk = sb("msk", [128, 2, NT], BF16)
        ebuf = [sb(f"e{i}", [128, NT], F32) for i in range(2)]
        spb = [sb(f"sp{i}", [128, NT], BF16) for i in range(3)]
        wb = [sb(f"w{i}", [128, NT], BF16) for i in range(3)]
        sacc = [sb(f"sacc{i}", [128, NT], BF16) for i in range(3)]
        ob = [sb(f"ob{i}", [128, QG], BF16) for i in range(2)]

        P.dma(msk[:], C.sbmask2, [], [pfx + "tab"])
        for b in range(2):
            P.op("dve", lambda e, b=b: e.memset(QB[b][0:64, :, QG:2 * QG], 0.0), [], [pfx + f"QB{b}"])
            P.op("dve", lambda e, b=b: e.memset(QB[b][64:128, :, 0:QG], 0.0), [], [pfx + f"QB{b}"])
        for b0 in range(0, NB, 32):
            P.dma(VA[:, b0:b0 + 32, :],
                  T["vtok"][base + b0 * 128:base + (b0 + 32) * 128, :].rearrange("(b p) c -> p b c", p=128),
                  [], [pfx + "VA"])

        def load_pair(hp):
            b = hp % 2
            r0 = hp * 128
            P.dma(KT[b][:], T["kT"][r0:r0 + 128, base:base + NK], [], [pfx + f"KT{b}"])
            P.dma(QB[b][0:64, :, 0:QG],
                  T["qsT"][r0:r0 + 64, own0:own0 + NLOC].rearrange("p (g q) -> p g q", q=QG), [], [pfx + f"QB{b}"])
            P.dma(QB[b][64:128, :, QG:2 * QG],
                  T["qsT"][r0 + 64:r0 + 128, own0:own0 + NLOC].rearrange("p (g q) -> p g q", q=QG),
                  [], [pfx + f"QB{b}"])

        tuples = []
        for hp in range(4):
            for g in range(NQG):
                blks = [LB0 + 2 * g + 1 - i for i in range(2 * g + 2)] + [LB0 - 1 - i for i in range(LB0)]
                for i, kb in enumerate(blks):
                    tuples.append((hp, g, kb, i == 0, i == len(blks) - 1))

        def stageA(idx):
            hp, g, kb, first, last = tuples[idx]
            b = hp % 2
            if g == 0 and first:
                if hp == 0:
                    load_pair(0)
                if hp + 1 < 4:
                    load_pair(hp + 1)
            i2, i3 = idx % 2, idx % 3
            q = QB[b][:, g, :]
            kt = KT[b][:, kb * 128:(kb + 1) * 128]
            bA, kA = ps[i2], f"ps{i2}"
            P.mm(bA[:], kt, q, True, True, [pfx + f"KT{b}", pfx + f"QB{b}"], [kA])
            P.act(ebuf[i2][:], bA[:], AF.Exp, [kA], [pfx + f"e{i2}"])
            P.act(spb[i3][:], ebuf[i2][:], AF.Ln, [pfx + f"e{i2}"], [pfx + f"sp{i3}"], bias=1.0)
            dc = kb - (LB0 + 2 * g)
            if dc >= 0:
                P.tt(spb[i3][:], spb[i3][:], msk[:, dc, :], ALU.mult, [pfx + f"sp{i3}", pfx + "tab"],
                     [pfx + f"sp{i3}"])
            if not last:
                if first:
                    P.op("dve", lambda e: e.tensor_copy(out=sacc[i3][:], in_=spb[i3][:]),
                         [pfx + f"sp{i3}"], [pfx + f"sacc{i3}"])
                else:
                    p3 = (idx - 1) % 3
                    P.tt(sacc[i3][:], sacc[p3][:], spb[i3][:], ALU.add, [pfx + f"sacc{p3}", pfx + f"sp{i3}"],
                         [pfx + f"sacc{i3}"])

        def stageB(idx):
            hp, g, kb, first, last = tuples[idx]
            b = hp % 2
            i2, i3 = idx % 2, idx % 3
            q = QB[b][:, g, :]
            kt = KT[b][:, kb * 128:(kb + 1) * 128]
            bB, kB = ps[2 + i2], f"ps{2 + i2}"
            P.mm(bB[:], C.trineg[:], spb[i3][:], True, False, [pfx + f"sp{i3}", "const"], [kB])
            if not first:
                p3 = (idx - 1) % 3
                P.mm(bB[:], C.negones[:], sacc[p3][:], False, False, [pfx + f"sacc{p3}", "const"], [kB])
            P.mm(bB[:], kt, q, False, True, [pfx + f"KT{b}", pfx + f"QB{b}"], [kB])
            P.act(wb[i3][:], bB[:], AF.Exp, [kB], [pfx + f"w{i3}"])
            dc = kb - (LB0 + 2 * g)
            if dc >= 0:
                P.tt(wb[i3][:], wb[i3][:], msk[:, dc, :], ALU.mult, [pfx + f"w{i3}", pfx + "tab"], [pfx + f"w{i3}"])

        def stageC(idx):
            hp, g, kb, first, last = tuples[idx]
            i3 = idx % 3
            gi = (hp * NQG + g) % 2
            bO, kO = ps[4 + gi], f"ps{4 + gi}"
            P.mm(bO[:], VA[:, kb, hp * 128:(hp + 1) * 128], wb[i3][:], first, last,
                 [pfx + "VA", pfx + f"w{i3}"], [kO])
            if last:
                P.op("dve", lambda e: e.tensor_copy(out=ob[gi][0:64, :], in_=bO[0:64, 0:QG]), [kO], [pfx + f"ob{gi}"])
                P.op("dve", lambda e: e.tensor_copy(out=ob[gi][64:128, :], in_=bO[64:128, QG:2 * QG]), [kO],
                     [pfx + f"ob{gi}"])
                P.dma(yT[256 + hp * 128:256 + (hp + 1) * 128, own0 + g * QG:own0 + (g + 1) * QG], ob[gi][:],
                      [pfx + f"ob{gi}"], [])

        n = len(tuples)
        for i in range(n + 2):
            if i < n:
                stageA(i)
            if 1 <= i <= n:
                stageB(i - 1)
            if i >= 2:
                stageC(i - 2)
        P.flush()


def outproj_phase(P, C, tag, w_o, T, xsrc, xdst, groups):
    nc = P.nc
    pfx = tag + "_"
    ps = C.ps
    xs = xsrc.rearrange("(c p) t -> p c t", p=128)
    xd = xdst.rearrange("(c p) t -> p c t", p=128)
    yr = T["yT"].rearrange("(c p) t -> p c t", p=128)
    with ExitStack() as es:
        sb = lambda n, shp, dt: es.enter_context(nc.sbuf_tensor(pfx + n, shp, dt))
        wo = sb("wo", [128, 8, D], BF16)
        yin = [sb(f"yin{i}", [128, 8, NT], BF16) for i in range(2)]
        xin = [sb(f"xin{i}", [128, 8, NT], F32) for i in range(2)]
        xo = [sb(f"xo{i}", [128, 8, NT], F32) for i in range(2)]
        P.dma(wo[:], w_o.rearrange("(c p) m -> p c m", p=128), [], [pfx + "wo"], q="pool")
        for g in groups:
            cols = slice(g * NT, (g + 1) * NT)
            b = g % 2
            P.dma(yin[b][:], yr[:, :, cols], [], [pfx + f"yin{b}"])
            P.dma(xin[b][:], xs[:, :, cols], [], [pfx + f"xin{b}"])
            for m in range(8):
                bO, kO = ps[m % 4], f"ps{m % 4}"
                for c in range(8):
                    P.mm(bO[:], wo[:, c, m * 128:(m + 1) * 128], yin[b][:, c, :], c == 0, c == 7,
                         [pfx + "wo", pfx + f"yin{b}"], [kO])
                P.tt(xo[b][:, m, :], bO[:], xin[b][:, m, :], ALU.add, [kO, pfx + f"xin{b}"], [pfx + f"xo{b}"])
            P.dma(xd[:, :, cols], xo[b][:], [pfx + f"xo{b}"], [])
        P.flush()


def final_phase(P, C, tag, gam_d, xsrc, out_d):
    nc = P.nc
    pfx = tag + "_"
    xs = xsrc.rearrange("(c p) t -> p c t", p=128)
    od = out_d.rearrange("(c p) t -> p c t", p=128)
    with ExitStack() as es:
        sb = lambda n, shp, dt: es.enter_context(nc.sbuf_tensor(pfx + n, shp, dt))
        gam = sb("gam", [128, 8], F32)
        xin = [sb(f"xin{i}", [128, 8, NT], F32) for i in range(2)]
        sqc = [sb(f"sqc{i}", [128, NT], BF16) for i in range(2)]
        rstd = sb("rstd", [128, NT], F32)
        yo = [sb(f"yo{i}", [128, 8, NT], F32) for i in range(2)]
        P.dma(gam[:], gam_d, [], [pfx + "gam"])
        for g in range(NG):
            cols = slice(NLOC + g * NT, NLOC + (g + 1) * NT)
            ocols = slice(g * NT, (g + 1) * NT)
            b = g % 2
            P.dma(xin[b][:], xs[:, :, cols], [], [pfx + f"xin{b}"])
            for c in range(8):
                P.act(sqc[c % 2][:], xin[b][:, c, :], AF.Square, [pfx + f"xin{b}"], [pfx + f"sqc{c % 2}"])
                P.mm(C.ps[6][:], C.ones_bf[:], sqc[c % 2][:], c == 0, c == 7, [pfx + f"sqc{c % 2}", "const"], ["ps6"])
            P.ts(rstd[:], C.ps[6][:], 1.0 / D, EPS, ALU.mult, ALU.add, ["ps6"], [pfx + "rstd"])
            rsqrt_inplace(P, rstd[:], pfx + "rstd")
            for c in range(8):
                P.stt(yo[b][:, c, :], xin[b][:, c, :], gam[:, c:c + 1], rstd[:], ALU.mult, ALU.mult,
                      [pfx + f"xin{b}", pfx + "rstd", pfx + "gam"], [pfx + f"yo{b}"])
            P.dma(od[:, :, ocols], yo[b][:], [pfx + f"yo{b}"], [], is_out=True)


MIXT = {
    "vconv": ([256, NSLOT], BF16), "qsT": ([512, NSLOT], BF16), "kT": ([512, NSLOT], BF16),
    "vtok": ([NSLOT, 512], BF16), "vr": ([NSLOT, 256], BF16), "qrT": ([256, NSLOT], BF16),
    "qdT": ([256, NSLOT], BF16), "krT": ([256, NSLOT], BF16), "kdtok": ([NSLOT, 256], BF16),
    "gsil": ([256, NSLOT], F32), "yT": ([D, NSLOT], BF16),
}
CONSTS = {"ones_bf": ([128, 128], BF16), "ones_f": ([128, 128], F32), "ident": ([128, 128], BF16),
          "trineg": ([128, 128], BF16), "negones": ([128, 128], BF16)}
TABLES = {"rope_cos": [128, NSLOT], "rope_sin": [128, NSLOT], "decq": [128, 2, NT], "deck": [128, 256],
          "g64": [64, 256], "dintra": [64, 4, NT], "flag": [128, 1]}
ALLG = list(range(2 * NG))
OWNG = list(range(NG, 2 * NG))


def build_program():
    nc = bass.Bass("TRN2", target_bir_lowering=False)
    dt = lambda name, shape, dtype, kind: nc.dram_tensor(name, shape, dtype, kind=kind).ap()
    C = Ctx()
    W = {}
    for l in range(DEPTH):
        for f in ("1", "2"):
            W[f"ffn{f}_w_in{l}"] = dt(f"ffn{f}_w_in{l}", [D, 2 * DFF], F32, "ExternalInput")
            W[f"ffn{f}_w_out{l}"] = dt(f"ffn{f}_w_out{l}", [DFF, D], F32, "ExternalInput")
            W[f"ffn{f}_norm{l}"] = dt(f"ffn{f}_norm{l}", [128, 8], F32, "ExternalInput")
        W[f"mix_w_in{l}"] = dt(f"mix_w_in{l}", [D, 3072], F32, "ExternalInput")
        W[f"mix_norm{l}"] = dt(f"mix_norm{l}", [128, 8], F32, "ExternalInput")
        W[f"mix_w_out{l}"] = dt(f"mix_w_out{l}", [D, D], F32, "ExternalInput")
        W[f"conv_w{l}"] = dt(f"conv_w{l}", [128, 2, 31], F32, "ExternalInput")
        for nm in ("conv_b", "conv_ln_g", "conv_ln_b"):
            W[f"{nm}{l}"] = dt(f"{nm}{l}", [128, 2], F32, "ExternalInput")
        W[f"ret_norm_g{l}"] = dt(f"ret_norm_g{l}", [64, 4], F32, "ExternalInput")
    W["final_norm"] = dt("final_norm", [128, 8], F32, "ExternalInput")
    cd = {k: dt("c_" + k, s, d, "ExternalInput") for k, (s, d) in CONSTS.items()}
    for k, s in TABLES.items():
        setattr(C, k, dt("t_" + k, s, F32, "ExternalInput"))
    C.sbmask2 = dt("t_sbmask2", [128, 2, NT], BF16, "ExternalInput")
    x_in = dt("x_in", [D, NSLOT], F32, "ExternalInput")
    xa = dt("xa", [D, NSLOT], F32, "Internal")
    xb = dt("xb", [D, NSLOT], F32, "Internal")
    T = {k: dt("m_" + k, s, d, "Internal") for k, (s, d) in MIXT.items()}
    out = dt("out", [D, NLOC], F32, "ExternalOutput")

    with ExitStack() as es:
        P = Prog(nc, es)
        C.ps = [es.enter_context(nc.psum_tensor(f"ps{i}", [128, NT], F32)) for i in range(7)]
        C.ps.append(C.ps[6])
        pst = es.enter_context(nc.psum_tensor("pst", [128, 2 * NT], BF16))
        C.pst = pst
        C.ps[7] = pst.bitcast(F32)
        for k, (s, d) in CONSTS.items():
            t = es.enter_context(nc.sbuf_tensor("k_" + k, s, d))
            setattr(C, k, t)
            P.dma(t[:], cd[k], [], ["const"])
        P.flush()
        mixw = lambda l: (W[f"conv_w{l}"], W[f"conv_b{l}"], W[f"conv_ln_g{l}"], W[f"conv_ln_b{l}"],
                          W[f"ret_norm_g{l}"])
        ffn_phase(P, C, "f1a", W["ffn1_w_in0"], W["ffn1_w_out0"], W["ffn1_norm0"], x_in, xa, ALLG)
        inproj_phase(P, C, "ipa", W["mix_w_in0"], W["mix_norm0"], xa, T, ALLG)
        retconv_phase(P, C, "rc0a", T, *mixw(0), own0=0, has_prev=False)
        sb_phase(P, C, "sb0a", T, own0=0, has_prev=False)
        retconv_phase(P, C, "rc0b", T, *mixw(0), own0=NLOC, has_prev=True)
        sb_phase(P, C, "sb0b", T, own0=NLOC, has_prev=True)
        outproj_phase(P, C, "opa", W["mix_w_out0"], T, xa, xb, ALLG)
        ffn_phase(P, C, "f2a", W["ffn2_w_in0"], W["ffn2_w_out0"], W["ffn2_norm0"], xb, xa, ALLG, flag_prefix=True)
        ffn_phase(P, C, "f1b", W["ffn1_w_in1"], W["ffn1_w_out1"], W["ffn1_norm1"], xa, xb, ALLG)
        inproj_phase(P, C, "ipb", W["mix_w_in1"], W["mix_norm1"], xb, T, ALLG, kv_only=tuple(range(NG)))
        retconv_phase(P, C, "rc1", T, *mixw(1), own0=NLOC, has_prev=True)
        sb_phase(P, C, "sb1", T, own0=NLOC, has_prev=True)
        outproj_phase(P, C, "opb", W["mix_w_out1"], T, xb, xa, OWNG)
        ffn_phase(P, C, "f2b", W["ffn2_w_in1"], W["ffn2_w_out1"], W["ffn2_norm1"], xa, xb, OWNG)
        final_phase(P, C, "fin", W["final_norm"], xb, out)
        P.flush(final=True)
    return nc


def _consts():
    bf = ml_dtypes.bfloat16
    j = np.arange(128)
    c = {
        "c_ones_bf": np.ones((128, 128), bf), "c_ones_f": np.ones((128, 128), np.float32),
        "c_ident": np.eye(128).astype(bf),
        "c_trineg": (-(j[:, None] >= j[None, :]).astype(np.float32)).astype(bf),
        "c_negones": (-np.ones((128, 128), np.float32)).astype(bf),
    }
    gam = 1.0 - np.exp2(-5.0 - np.arange(4, dtype=np.float64))
    i512 = np.arange(NT)
    p = np.arange(128)
    decq = np.zeros((128, 2, NT))
    for cc in range(2):
        hh = 2 * cc + p // 64
        decq[:, cc, :] = 0.125 * gam[hh][:, None] ** ((i512 % 64) + 1.0)[None, :]
    col = np.arange(256)
    deck = gam[col // 64][None, :] ** (63.0 - (p % 64))[:, None]
    g64 = np.broadcast_to((gam[col // 64] ** 64.0)[None, :], (64, 256))
    jj = np.arange(64)
    dintra = np.zeros((64, 4, NT))
    for hh in range(4):
        dintra[:, hh, :] = gam[hh] ** np.abs(jj[:, None] - (i512 % 64)[None, :])
    sbmask = np.zeros((128, 2, NT), np.float32)
    for cc in range(2):
        sbmask[:, cc, :] = ((cc * 128 + p)[:, None] < (i512 % 256)[None, :])
    c.update({"t_decq": decq.astype(np.float32), "t_deck": deck.astype(np.float32),
              "t_g64": np.ascontiguousarray(g64).astype(np.float32), "t_dintra": dintra.astype(np.float32),
              "t_sbmask2": sbmask.astype(bf)})
    return c


def _rope(half):
    slot = np.arange(NSLOT)
    pos = (slot if half == 1 else slot % NLOC).astype(np.float32)
    inv = (1.0 / (10000.0 ** (np.arange(32, dtype=np.float32) / 32))).astype(np.float32)
    p = np.arange(128)
    ang = pos[None, :] * inv[(p % 64) % 32][:, None]
    sign = np.where((p % 64) < 32, -1.0, 1.0)[:, None]
    return np.cos(ang).astype(np.float32), (sign * np.sin(ang)).astype(np.float32)


def _vec8(v):
    return np.ascontiguousarray(v.reshape(8, 128).T)


def _vec2(v):
    return np.ascontiguousarray(v.reshape(2, 128).T)


def kernel(**inp):
    ncores = 8
    x = inp["x"]
    w = dict(_consts())
    for l in range(DEPTH):
        for f in ("1", "2"):
            w[f"ffn{f}_w_in{l}"] = np.ascontiguousarray(inp[f"ffn{f}_w_in"][l])
            w[f"ffn{f}_w_out{l}"] = np.ascontiguousarray(inp[f"ffn{f}_w_out"][l])
            w[f"ffn{f}_norm{l}"] = _vec8(inp[f"ffn{f}_norm"][l])
        w[f"mix_w_in{l}"] = np.ascontiguousarray(inp["mix_w_in"][l])
        w[f"mix_norm{l}"] = _vec8(inp["mix_norm"][l])
        w[f"mix_w_out{l}"] = np.ascontiguousarray(inp["mix_w_out"][l])
        w[f"conv_w{l}"] = np.ascontiguousarray(inp["conv_w"][l].T.reshape(2, 128, 31).transpose(1, 0, 2))
        for nm in ("conv_b", "conv_ln_g", "conv_ln_b"):
            w[f"{nm}{l}"] = _vec2(inp[nm][l])
        w[f"ret_norm_g{l}"] = np.ascontiguousarray(inp["ret_norm_g"][l].reshape(4, 64).T)
    w["final_norm"] = _vec8(inp["final_norm"])
    ropes = [_rope(0), _rope(1)]
    maps = []
    for core in range(ncores):
        b, half = core // 2, core % 2
        m = dict(w)
        m["t_rope_cos"], m["t_rope_sin"] = ropes[half]
        m["t_flag"] = np.full((128, 1), float(half), np.float32)
        xi = np.zeros((D, NSLOT), np.float32)
        if half == 1:
            xi[:, :] = x[b].T
        else:
            xi[:, NLOC:] = x[b, :NLOC, :].T
        m["x_in"] = xi
        maps.append(m)
    res = run_bass_kernel_spmd(build_program(), maps, core_ids=list(range(ncores))).results
    out = np.empty(x.shape, np.float32)
    for c in range(ncores):
        out[c // 2, (c % 2) * NLOC:(c % 2 + 1) * NLOC, :] = res[c]["out"].T
    return out
```

```python
import numpy as np
import ml_dtypes
from contextlib import ExitStack
import concourse.bass as bass
import concourse.mybir as mybir
from concourse.bass_utils import run_bass_kernel_spmd

F32 = mybir.dt.float32
BF16 = mybir.dt.bfloat16
AF = mybir.ActivationFunctionType
ALU = mybir.AluOpType

D = 1024
DFF = 2816
NLOC = 4096
NSLOT = 8192
NT = 512
NG = NLOC // NT
DEPTH = 2
EPS = 1e-6
NDS = 8


class Op:
    __slots__ = ("stream", "fn", "deps", "dma", "token", "needed", "phase", "wkeys")


class Prog:
    STREAMS = ["pe", "act", "dve", "pool", "sp"]

    def __init__(self, nc, es):
        self.nc = nc
        self.csem = {s: es.enter_context(nc.semaphore("c_" + s)) for s in ["pe", "act", "dve", "pool"]}
        self.dsem = {s: [es.enter_context(nc.semaphore(f"d_{s}{i}")) for i in range(NDS)]
                     for s in ["sp", "pool"]}
        self.ccnt = {s: 0 for s in self.csem}
        self.dcnt = {s: [0] * NDS for s in self.dsem}
        self.dnum = {s: 0 for s in self.dsem}
        self.lastw = {}
        self.readers = {}
        self.ops = []
        self.phase = 0
        self.waited = {s: {} for s in self.STREAMS}
        self.pending = {s: {} for s in self.STREAMS}
        self.out_tokens = []

    def op(self, stream, fn, reads=(), writes=(), dma=False, is_out=False):
        o = Op()
        o.stream, o.fn, o.dma, o.needed, o.phase, o.token = stream, fn, dma, False, self.phase, None
        o.wkeys = set(writes)
        deps = {}
        for k in reads:
            w = self.lastw.get(k)
            if w is not None:
                deps[id(w)] = (w, True)
        for k in writes:
            w = self.lastw.get(k)
            if w is not None and id(w) not in deps:
                deps[id(w)] = (w, False)
            for r in self.readers.get(k, {}).values():
                if id(r) not in deps:
                    deps[id(r)] = (r, False)
        o.deps = []
        for d, raw in deps.values():
            if d.phase != self.phase or d is o:
                continue
            if d.stream == stream and not d.dma and not dma:
                if stream == "pe" or not raw:
                    continue
            o.deps.append(d)
        if dma:
            i = self.dnum[stream] % NDS
            self.dnum[stream] += 1
            self.dcnt[stream][i] += 16
            o.token = (self.dsem[stream][i], self.dcnt[stream][i])
            rkey = (stream, i)
            if is_out:
                self.out_tokens.append(o.token)
        else:
            rkey = (stream, -1)
        for k in reads:
            self.readers.setdefault(k, {})[rkey] = o
        for k in writes:
            self.lastw[k] = o
            self.readers[k] = {}
        self.ops.append(o)
        return o

    def flush(self, final=False):
        ops = self.ops
        for o in ops:
            for d in o.deps:
                d.needed = True
        last = {}
        for o in ops:
            if not o.dma:
                last[o.stream] = o
        for o in last.values():
            o.needed = True
        for o in ops:
            if not o.dma and o.needed:
                self.ccnt[o.stream] += 1
                o.token = (self.csem[o.stream], self.ccnt[o.stream])
        by_stream = {s: [o for o in ops if o.stream == s] for s in self.STREAMS}
        snapshot = {}
        for s in self.csem:
            if self.ccnt[s]:
                snapshot[id(self.csem[s])] = (self.csem[s], self.ccnt[s])
        for s in self.dsem:
            for i in range(NDS):
                if self.dcnt[s][i]:
                    snapshot[id(self.dsem[s][i])] = (self.dsem[s][i], self.dcnt[s][i])

        def emit(stream, eng):
            waited = self.waited[stream]
            for o in by_stream[stream]:
                w = dict(self.pending[stream])
                self.pending[stream] = {}
                for d in o.deps:
                    sem, val = d.token
                    if id(sem) not in w or w[id(sem)][1] < val:
                        w[id(sem)] = (sem, val)
                for sid, (sem, val) in w.items():
                    if waited.get(sid, 0) < val:
                        eng.wait_ge(sem, val)
                        waited[sid] = val
                ins = o.fn(eng)
                if o.token is not None:
                    ins.then_inc(o.token[0], 16 if o.dma else 1)
            if final and stream == "sp":
                for sid, (sem, val) in snapshot.items():
                    if waited.get(sid, 0) < val:
                        eng.wait_ge(sem, val)
                        waited[sid] = val

        with self.nc.Block() as blk:
            if by_stream["pe"]:
                blk.tensor(lambda e: emit("pe", e))
            if by_stream["act"]:
                blk.scalar(lambda e: emit("act", e))
            if by_stream["dve"]:
                blk.vector(lambda e: emit("dve", e))
            if by_stream["pool"]:
                blk.gpsimd(lambda e: emit("pool", e))
            if by_stream["sp"] or final:
                blk.sync(lambda e: emit("sp", e))
        for s in self.STREAMS:
            p = self.pending[s]
            for sid, (sem, val) in snapshot.items():
                if sid not in p or p[sid][1] < val:
                    p[sid] = (sem, val)
        self.ops = []
        self.phase += 1

    def dma(self, out, in_, reads, writes, q="sp", is_out=False):
        return self.op(q, lambda e: e.dma_start(out=out, in_=in_), reads, writes, dma=True, is_out=is_out)

    def mm(self, out, lhsT, rhs, start, stop, reads, writes):
        return self.op("pe", lambda e: e.matmul(out, lhsT, rhs, start=start, stop=stop), reads, writes)

    def act(self, out, in_, func, reads, writes, bias=None, scale=None):
        kw = {}
        if bias is not None:
            kw["bias"] = bias
        if scale is not None:
            kw["scale"] = scale
        return self.op("act", lambda e: e.activation(out=out, in_=in_, func=func, **kw), reads, writes)

    def tt(self, out, in0, in1, op, reads, writes, eng="dve"):
        return self.op(eng, lambda e: e.tensor_tensor(out=out, in0=in0, in1=in1, op=op), reads, writes)

    def ts(self, out, in0, s1, s2, op0, op1, reads, writes, eng="dve"):
        if op1 is None:
            return self.op(eng, lambda e: e.tensor_scalar(out=out, in0=in0, scalar1=s1, scalar2=None, op0=op0),
                           reads, writes)
        return self.op(eng, lambda e: e.tensor_scalar(out=out, in0=in0, scalar1=s1, scalar2=s2, op0=op0, op1=op1),
                       reads, writes)

    def stt(self, out, in0, scalar, in1, op0, op1, reads, writes, eng="dve"):
        return self.op(eng, lambda e: e.scalar_tensor_tensor(out=out, in0=in0, scalar=scalar, in1=in1,
                                                             op0=op0, op1=op1), reads, writes)


class Ctx:
    pass


def rsqrt_inplace(P, t, key):
    P.act(t, t, AF.Sqrt, [key], [key])
    P.op("dve", lambda e: e.reciprocal(out=t, in_=t), [key], [key])


def rms_norm_group(P, C, pfx, xin, gam, h, sqc, rstd, g):
    ps = C.ps
    for c in range(8):
        P.act(sqc[c % 2][:], xin[:, c, :], AF.Square, [pfx + "xin"], [pfx + f"sqc{c % 2}"])
        P.mm(ps[6][:], C.ones_bf[:], sqc[c % 2][:], c == 0, c == 7, [pfx + f"sqc{c % 2}", "const"], ["ps6"])
    P.ts(rstd[:], ps[6][:], 1.0 / D, EPS, ALU.mult, ALU.add, ["ps6"], [pfx + "rstd"])
    rsqrt_inplace(P, rstd[:], pfx + "rstd")
    for c in range(8):
        P.stt(h[:, c, :], xin[:, c, :], gam[:, c:c + 1], rstd[:], ALU.mult, ALU.mult,
              [pfx + "xin", pfx + "rstd", pfx + "gam"], [pfx + "h"])


def ffn_phase(P, C, tag, w_in, w_out, gam_d, xsrc, xdst, groups, flag_prefix=False):
    nc = P.nc
    pfx = tag + "_"
    ps = C.ps
    xs = xsrc.rearrange("(c p) t -> p c t", p=128)
    xd = xdst.rearrange("(c p) t -> p c t", p=128)
    with ExitStack() as es:
        sb = lambda n, shp, dt: es.enter_context(nc.sbuf_tensor(pfx + n, shp, dt))
        win = sb("win", [128, 8, 2 * DFF], BF16)
        wout = sb("wout", [128, 22, D], BF16)
        gam = sb("gam", [128, 8], F32)
        xin = sb("xin", [128, 8, NT], F32)
        h = sb("h", [128, 8, NT], BF16)
        sqc = [sb(f"sqc{i}", [128, NT], BF16) for i in range(2)]
        rstd = sb("rstd", [128, NT], F32)
        a = sb("a", [128, 22, NT], BF16)
        sg = [sb(f"sg{i}", [128, NT], F32) for i in range(2)]
        xr = [sb(f"xr{i}", [128, NT], F32) for i in range(2)]
        xo = [sb(f"xo{i}", [128, NT], F32) for i in range(2)]
        flg = sb("flg", [128, 1], F32)

        P.dma(gam[:], gam_d, [], [pfx + "gam"])
        P.dma(flg[:], C.flag, [], [pfx + "flg"])
        g0 = groups[0]
        P.dma(xin[:], xs[:, :, g0 * NT:(g0 + 1) * NT], [], [pfx + "xin"])
        for c in range(8):
            P.dma(win[:, c, :], w_in[c * 128:(c + 1) * 128, :], [], [pfx + f"win{c}"], q="pool")
        wo_r = w_out.rearrange("(j p) m -> p j m", p=128)
        for j0 in range(0, 22, 6):
            j1 = min(22, j0 + 6)
            P.dma(wout[:, j0:j1, :], wo_r[:, j0:j1, :], [], [pfx + f"wout{jj}" for jj in range(j0, j1)], q="pool")
        WIN = [pfx + f"win{c}" for c in range(8)]
        WOUT = [pfx + f"wout{j}" for j in range(22)]

        rms_norm_group(P, C, pfx, xin, gam, h, sqc, rstd, 0)
        for gi_, g in enumerate(groups):
            cols = slice(g * NT, (g + 1) * NT)
            gn = groups[gi_ + 1] if gi_ + 1 < len(groups) else None
            if gn is not None:
                P.dma(xin[:], xs[:, :, gn * NT:(gn + 1) * NT], [], [pfx + "xin"])
            for j in range(22):
                bG, bU = ps[(j % 2) * 2], ps[(j % 2) * 2 + 1]
                kG, kU = f"ps{(j % 2) * 2}", f"ps{(j % 2) * 2 + 1}"
                for c in range(8):
                    P.mm(bG[:], win[:, c, j * 128:(j + 1) * 128], h[:, c, :], c == 0, c == 7,
                         [pfx + "h", WIN[c]], [kG])
                for c in range(8):
                    P.mm(bU[:], win[:, c, DFF + j * 128:DFF + (j + 1) * 128], h[:, c, :], c == 0, c == 7,
                         [pfx + "h", WIN[c]], [kU])
                P.act(sg[j % 2][:], bG[:], AF.Silu, [kG], [pfx + f"sg{j % 2}"])
                P.tt(a[:, j, :], sg[j % 2][:], bU[:], ALU.mult, [pfx + f"sg{j % 2}", kU], [pfx + f"a{j}"])
            if gn is not None:
                rms_norm_group(P, C, pfx, xin, gam, h, sqc, rstd, gn)
            for m in range(8):
                P.dma(xr[m % 2][:], xs[:, m, cols], [], [pfx + f"xr{m % 2}"])
                bO, kO = ps[4 + m % 2], f"ps{4 + m % 2}"
                for j in range(22):
                    P.mm(bO[:], wout[:, j, m * 128:(m + 1) * 128], a[:, j, :], j == 0, j == 21,
                         [pfx + f"a{j}", WOUT[j]], [kO])
                P.stt(xo[m % 2][:], bO[:], 0.5, xr[m % 2][:], ALU.mult, ALU.add,
                      [kO, pfx + f"xr{m % 2}"], [pfx + f"xo{m % 2}"])
                if flag_prefix and g < NG:
                    P.ts(xo[m % 2][:], xo[m % 2][:], flg[:, 0:1], None, ALU.mult, None,
                         [pfx + f"xo{m % 2}", pfx + "flg"], [pfx + f"xo{m % 2}"])
                P.dma(xd[:, m, cols], xo[m % 2][:], [pfx + f"xo{m % 2}"], [])
        P.flush()


def inproj_phase(P, C, tag, w_mix, gam_d, xsrc, T, groups, kv_only=()):
    nc = P.nc
    pfx = tag + "_"
    ps = C.ps
    xs = xsrc.rearrange("(c p) t -> p c t", p=128)
    with ExitStack() as es:
        sb = lambda n, shp, dt: es.enter_context(nc.sbuf_tensor(pfx + n, shp, dt))
        wm = sb("wm", [128, 8, 3072], BF16)
        wsw = sb("wsw", [128, 8, 512], BF16)
        gam = sb("gam", [128, 8], F32)
        xin = sb("xin", [128, 8, NT], F32)
        h = sb("h", [128, 8, NT], BF16)
        sqc = [sb(f"sqc{i}", [128, NT], BF16) for i in range(2)]
        rstd = sb("rstd", [128, NT], F32)
        cosb = [sb(f"cosb{i}", [128, NT], F32) for i in range(2)]
        sinb = [sb(f"sinb{i}", [128, NT], F32) for i in range(2)]
        decq = sb("decq", [128, 2, NT], F32)
        deck = sb("deck", [128, 256], F32)
        t1 = [sb(f"t1{i}", [128, NT], F32) for i in range(2)]
        t2 = [sb(f"t2{i}", [128, NT], F32) for i in range(2)]
        ob = [sb(f"ob{i}", [128, NT], BF16) for i in range(4)]
        of = [sb(f"of{i}", [128, NT], F32) for i in range(2)]
        krb = [sb(f"krb{i}", [128, NT], BF16) for i in range(2)]
        kdb = [sb(f"kdb{i}", [128, 256], BF16) for i in range(2)]

        P.dma(gam[:], gam_d, [], [pfx + "gam"])
        P.dma(xin[:], xs[:, :, groups[0] * NT:(groups[0] + 1) * NT], [], [pfx + "xin"])
        P.dma(decq[:], C.decq, [], [pfx + "tab"])
        P.dma(deck[:], C.deck, [], [pfx + "tab"])
        for c in range(8):
            P.dma(wm[:, c, :], w_mix[c * 128:(c + 1) * 128, :], [], [pfx + "wm"], q="pool")
            src = w_mix[c * 128:(c + 1) * 128, 2048:2560].rearrange("p (h two d) -> p h two d", two=2, d=32)
            dst = wsw[:, c, :].rearrange("p (h two d) -> p h two d", two=2, d=32)
            P.dma(dst[:, :, 0, :], src[:, :, 1, :], [], [pfx + "wm"], q="pool")
            P.dma(dst[:, :, 1, :], src[:, :, 0, :], [], [pfx + "wm"], q="pool")

        cnt = {"b": 0, "ob": 0, "of": 0, "t": 0, "kr": 0, "kd": 0}

        def bank():
            i = cnt["b"] % 6
            cnt["b"] += 1
            return ps[i], f"ps{i}"

        def proj(bk, kk, wt, c0, ncols):
            for c in range(8):
                P.mm(bk[:, 0:NT] if ncols == 128 else bk[:], wt[:, c, c0:c0 + 128], h[:, c, :], c == 0, c == 7,
                     [pfx + "h", pfx + "wm"], [kk])

        for gi_, g in enumerate(groups):
            cols = slice(g * NT, (g + 1) * NT)
            kvo = g in kv_only
            rb = gi_ % 2
            cosT, sinT = cosb[rb], sinb[rb]
            P.dma(cosT[:], C.rope_cos[:, cols], [], [pfx + f"rope{rb}"])
            P.dma(sinT[:], C.rope_sin[:, cols], [], [pfx + f"rope{rb}"])
            rms_norm_group(P, C, pfx, xin, gam, h, sqc, rstd, g)
            if gi_ + 1 < len(groups):
                gn = groups[gi_ + 1]
                P.dma(xin[:], xs[:, :, gn * NT:(gn + 1) * NT], [], [pfx + "xin"])
            for cc in range(2):
                if kvo and g != NG - 1:
                    continue
                bA, kA = bank()
                proj(bA, kA, wm, cc * 128, 128)
                bB, kB = bank()
                proj(bB, kB, wm, 256 + cc * 128, 128)
                i = cnt["t"] % 2
                cnt["t"] += 1
                P.act(t1[i][:], bB[:], AF.Sigmoid, [kB], [pfx + f"t1{i}"])
                o = cnt["ob"] % 4
                cnt["ob"] += 1
                P.tt(ob[o][:], bA[:], t1[i][:], ALU.mult, [kA, pfx + f"t1{i}"], [pfx + f"ob{o}"])
                P.dma(T["vconv"][cc * 128:(cc + 1) * 128, cols], ob[o][:], [pfx + f"ob{o}"], [])
            for sec, dname, scl in ((512, "qsT", 0.125), (1024, "kT", 1.0)):
                if kvo and dname == "qsT":
                    continue
                for cc in range(4):
                    bk, kk = bank()
                    proj(bk, kk, wm, sec + cc * 128, 128)
                    o = cnt["ob"] % 4
                    cnt["ob"] += 1
                    P.ts(ob[o][:], bk[:], scl, None, ALU.mult, None, [kk], [pfx + f"ob{o}"])
                    P.dma(T[dname][cc * 128:(cc + 1) * 128, cols], ob[o][:], [pfx + f"ob{o}"], [])
            for tb in range(4):
                tsl = slice(tb * 128, (tb + 1) * 128)
                rows = slice(g * NT + tb * 128, g * NT + (tb + 1) * 128)
                bk, kk = bank()
                for c in range(8):
                    P.mm(bk[:], h[:, c, tsl], wm[:, c, 1536:2048], c == 0, c == 7, [pfx + "h", pfx + "wm"], [kk])
                o = cnt["ob"] % 4
                cnt["ob"] += 1
                P.act(ob[o][:], bk[:], AF.Copy, [kk], [pfx + f"ob{o}"])
                P.dma(T["vtok"][rows, :], ob[o][:], [pfx + f"ob{o}"], [])
                bk, kk = bank()
                for c in range(8):
                    P.mm(bk[:, 0:256], h[:, c, tsl], wm[:, c, 2560:2816], c == 0, c == 7,
                         [pfx + "h", pfx + "wm"], [kk])
                o = cnt["ob"] % 4
                cnt["ob"] += 1
                P.act(ob[o][:, 0:256], bk[:, 0:256], AF.Copy, [kk], [pfx + f"ob{o}"])
                P.dma(T["vr"][rows, :], ob[o][:, 0:256], [pfx + f"ob{o}"], [])
            for isk, sec, swc in ((0, 2048, 0), (1, 2304, 256)):
                if kvo and not isk:
                    continue
                for cc in range(2):
                    bX, kX = bank()
                    proj(bX, kX, wm, sec + cc * 128, 128)
                    bS, kS = bank()
                    proj(bS, kS, wsw, swc + cc * 128, 128)
                    i = cnt["t"] % 2
                    cnt["t"] += 1
                    P.tt(t1[i][:], bX[:], cosT[:], ALU.mult, [kX, pfx + f"rope{rb}"], [pfx + f"t1{i}"])
                    P.tt(t2[i][:], bS[:], sinT[:], ALU.mult, [kS, pfx + f"rope{rb}"], [pfx + f"t2{i}"])
                    P.tt(t1[i][:], t1[i][:], t2[i][:], ALU.add, [pfx + f"t1{i}", pfx + f"t2{i}"], [pfx + f"t1{i}"],
                         eng="pool")
                    frows = slice(cc * 128, (cc + 1) * 128)
                    if not isk:
                        o = cnt["ob"] % 4
                        cnt["ob"] += 1
                        P.ts(ob[o][:], t1[i][:], 0.125, None, ALU.mult, None, [pfx + f"t1{i}"], [pfx + f"ob{o}"],
                             eng="pool")
                        P.dma(T["qrT"][frows, cols], ob[o][:], [pfx + f"ob{o}"], [])
                        o = cnt["ob"] % 4
                        cnt["ob"] += 1
                        P.tt(ob[o][:], t1[i][:], decq[:, cc, :], ALU.mult, [pfx + f"t1{i}", pfx + "tab"],
                             [pfx + f"ob{o}"], eng="pool")
                        P.dma(T["qdT"][frows, cols], ob[o][:], [pfx + f"ob{o}"], [])
                    else:
                        r = cnt["kr"] % 2
                        cnt["kr"] += 1
                        P.act(krb[r][:], t1[i][:], AF.Copy, [pfx + f"t1{i}"], [pfx + f"krb{r}"])
                        P.dma(T["krT"][frows, cols], krb[r][:], [pfx + f"krb{r}"], [])
                        for tb in range(4):
                            rows = slice(g * NT + tb * 128, g * NT + (tb + 1) * 128)
                            P.op("pe", lambda e, r=r, tb=tb: e.transpose(C.pst[:, 0:128],
                                                                          krb[r][:, tb * 128:(tb + 1) * 128],
                                                                          C.ident[:]),
                                 [pfx + f"krb{r}", "const"], ["pst"])
                            d = cnt["kd"] % 2
                            cnt["kd"] += 1
                            P.tt(kdb[d][:, 0:128], C.pst[:, 0:128], deck[:, cc * 128:(cc + 1) * 128], ALU.mult,
                                 ["pst", pfx + "tab"], [pfx + f"kdb{d}"])
                            P.dma(T["kdtok"][rows, cc * 128:(cc + 1) * 128], kdb[d][:, 0:128],
                                  [pfx + f"kdb{d}"], [])
            for cc in range(2):
                if kvo:
                    continue
                bk, kk = bank()
                proj(bk, kk, wm, 2816 + cc * 128, 128)
                o = cnt["of"] % 2
                cnt["of"] += 1
                P.act(of[o][:], bk[:], AF.Silu, [kk], [pfx + f"of{o}"])
                P.dma(T["gsil"][cc * 128:(cc + 1) * 128, cols], of[o][:], [pfx + f"of{o}"], [])
        P.flush()


def retconv_phase(P, C, tag, T, cw_d, cb_d, lg_d, lb_d, rg_d, own0, has_prev):
    nc = P.nc
    pfx = tag + "_"
    ps = C.ps
    yT = T["yT"]
    with ExitStack() as es:
        sb = lambda n, shp, dt: es.enter_context(nc.sbuf_tensor(pfx + n, shp, dt))
        vpad = sb("vpad", [128, 2, 32 + NLOC], BF16)
        dg = sb("dg", [128, 62, 128], BF16)
        yc = [sb(f"yc{i}", [128, NT], F32) for i in range(2)]
        cw = sb("cw", [128, 2, 31], F32)
        cb = sb("cb", [128, 2], F32)
        lg = sb("lg", [128, 2], F32)
        lb = sb("lb", [128, 2], F32)
        rg = sb("rg", [64, 4], F32)
        kdq = sb("kdq", [64, 4 * NLOC], BF16)
        kd64 = kdq[:, :].rearrange("p (n c) -> p n c", c=256)
        vr64 = sb("vr64", [64, 64, 256], BF16)
        sball = sb("sball", [64, 64, 256], BF16)
        S = sb("S", [64, 256], F32)
        g64 = sb("g64", [64, 256], F32)
        dintra = sb("dintra", [64, 4, NT], F32)
        qr = [kdq[:, 0:NLOC]] * 2
        kr = [kdq[:, NLOC:2 * NLOC]] * 2
        qd = [kdq[:, 2 * NLOC:3 * NLOC]] * 2
        gs = [sb(f"gs{i}", [64, NT], F32) for i in range(2)]
        stb = [sb(f"stb{i}", [64, NT], BF16) for i in range(2)]
        osb = sb("osb", [64, NT], F32)
        osq = sb("osq", [64, NT], F32)
        mean = sb("mean", [128, NT], F32)
        msq = sb("msq", [128, NT], F32)
        var = sb("var", [128, NT], F32)
        tn = [sb(f"tn{i}", [128, NT], F32) for i in range(2)]
        yb = [sb(f"yb{i}", [128, NT], BF16) for i in range(2)]
        sqy = [sb(f"sqy{i}", [128, NT], F32) for i in range(2)]

        for t, d in ((cw, cw_d), (cb, cb_d), (lg, lg_d), (lb, lb_d), (rg, rg_d), (g64, C.g64), (dintra, C.dintra)):
            P.dma(t[:], d, [], [pfx + "tab"])
        for cc in range(2):
            if has_prev:
                P.dma(vpad[:, cc, 0:32], T["vconv"][cc * 128:(cc + 1) * 128, own0 - 32:own0], [], [pfx + "vpad"])
            else:
                P.op("dve", lambda e, cc=cc: e.memset(vpad[:, cc, 0:32], 0.0), [], [pfx + "vpad"])
            P.dma(vpad[:, cc, 32:], T["vconv"][cc * 128:(cc + 1) * 128, own0:own0 + NLOC], [], [pfx + "vpad"])
            for k in range(31):
                P.ts(dg[:, cc * 31 + k, :], C.ident[:], cw[:, cc, k:k + 1], None, ALU.mult, None,
                     ["const", pfx + "tab"], [pfx + "dg"])
        P.op("dve", lambda e: e.memset(S[:], 0.0), [], [pfx + "S"])
        for half in range(2):
            if half == 0 and not has_prev:
                continue
            r0 = own0 - NLOC if half == 0 else own0
            ksrc = T["kdtok"][r0:r0 + NLOC, :]
            vsrc = T["vr"][r0:r0 + NLOC, :]
            P.dma(kd64, ksrc.rearrange("(n p) c -> p n c", p=64), [], [pfx + "kd64"])
            P.dma(vr64[:], vsrc.rearrange("(n p) c -> p n c", p=64), [], [pfx + "vr64"])
            for n in range(64):
                if half == 1:
                    P.act(sball[:, n, :], S[:], AF.Copy, [pfx + "S"], [pfx + f"sball{n}"])
                for hh in range(4):
                    hs = slice(hh * 64, (hh + 1) * 64)
                    P.mm(ps[7][0:64, hs], kd64[:, n, hs], vr64[:, n, hs], True, True,
                         [pfx + "kd64", pfx + "vr64"], ["ps7"])
                P.tt(S[:], S[:], g64[:], ALU.mult, [pfx + "S", pfx + "tab"], [pfx + "S"])
                P.tt(S[:], S[:], ps[7][0:64, 0:256], ALU.add, [pfx + "S", "ps7"], [pfx + "S"])
        for hh in range(4):
            b = 0
            hrows = slice(hh * 64, (hh + 1) * 64)
            hs = slice(hh * 64, (hh + 1) * 64)
            P.dma(qr[b], T["qrT"][hrows, own0:own0 + NLOC], [], [pfx + f"qr{b}", pfx + "kd64"])
            P.dma(kr[b], T["krT"][hrows, own0:own0 + NLOC], [], [pfx + f"kr{b}", pfx + "kd64"])
            P.dma(qd[b], T["qdT"][hrows, own0:own0 + NLOC], [], [pfx + f"qd{b}", pfx + "kd64"])
            for g in range(NG):
                cols = slice(g * NT, (g + 1) * NT)
                gi = g % 2
                ocols = slice(own0 + g * NT, own0 + (g + 1) * NT)
                P.dma(gs[gi][:], T["gsil"][hrows, ocols], [], [pfx + f"gs{gi}"])
                bS, kS = ps[gi], f"ps{gi}"
                bO, kO = ps[2 + gi], f"ps{2 + gi}"
                for c in range(8):
                    n = g * 8 + c
                    tc_ = slice(n * 64, (n + 1) * 64)
                    P.mm(bS[0:64, c * 64:(c + 1) * 64], kr[b][:, tc_], qr[b][:, tc_], True, True,
                         [pfx + f"kr{b}", pfx + f"qr{b}"], [kS])
                P.tt(stb[gi][:], bS[0:64, :], dintra[:, hh, :], ALU.mult, [kS, pfx + "tab"], [pfx + f"stb{gi}"])
                for c in range(8):
                    n = g * 8 + c
                    tc_ = slice(n * 64, (n + 1) * 64)
                    cs = slice(c * 64, (c + 1) * 64)
                    P.mm(bO[0:64, cs], vr64[:, n, hs], stb[gi][:, cs], True, False,
                         [pfx + "vr64", pfx + f"stb{gi}"], [kO])
                    P.mm(bO[0:64, cs], sball[:, n, hs], qd[b][:, tc_], False, True,
                         [pfx + f"sball{n}", pfx + f"qd{b}"], [kO])
                P.act(osb[:], bO[0:64, :], AF.Copy, [kO], [pfx + "osb"])
                P.act(osq[:], bO[0:64, :], AF.Square, [kO], [pfx + "osq"])
                P.mm(ps[4][0:64, :], C.ones_f[0:64, 0:64], osb[:], True, True, [pfx + "osb", "const"], ["ps4"])
                P.mm(ps[5][0:64, :], C.ones_f[0:64, 0:64], osq[:], True, True, [pfx + "osq", "const"], ["ps5"])
                m_, q_, v_ = mean[0:64, :], msq[0:64, :], var[0:64, :]
                P.ts(m_, ps[4][0:64, :], 1.0 / 64, None, ALU.mult, None, ["ps4"], [pfx + "mean"])
                P.tt(q_, m_, m_, ALU.mult, [pfx + "mean"], [pfx + "msq"])
                P.stt(v_, ps[5][0:64, :], 1.0 / 64, q_, ALU.mult, ALU.subtract, ["ps5", pfx + "msq"], [pfx + "var"])
                P.ts(v_, v_, EPS, None, ALU.add, None, [pfx + "var"], [pfx + "var"])
                rsqrt_inplace(P, v_, pfx + "var")
                t_ = tn[gi][0:64, :]
                P.tt(t_, osb[:], m_, ALU.subtract, [pfx + "osb", pfx + "mean"], [pfx + f"tn{gi}"])
                P.tt(t_, t_, v_, ALU.mult, [pfx + f"tn{gi}", pfx + "var"], [pfx + f"tn{gi}"])
                P.stt(yb[gi][0:64, :], t_, rg[:, hh:hh + 1], gs[gi][:], ALU.mult, ALU.mult,
                      [pfx + f"tn{gi}", pfx + "tab", pfx + f"gs{gi}"], [pfx + f"yb{gi}"])
                P.dma(yT[768 + hh * 64:768 + (hh + 1) * 64, ocols], yb[gi][0:64, :], [pfx + f"yb{gi}"], [])
        for g in range(NG):
            cols = slice(g * NT, (g + 1) * NT)
            for cc in range(2):
                for k in range(31):
                    P.mm(ps[cc][:], dg[:, cc * 31 + k, :], vpad[:, cc, 2 + k + g * NT:2 + k + (g + 1) * NT],
                         k == 0, k == 30, [pfx + "dg", pfx + "vpad"], [f"ps{cc}"])
                P.ts(yc[cc][:], ps[cc][:], cb[:, cc:cc + 1], None, ALU.add, None, [f"ps{cc}", pfx + "tab"],
                     [pfx + f"yc{cc}"])
                P.act(sqy[cc][:], yc[cc][:], AF.Square, [pfx + f"yc{cc}"], [pfx + f"sqy{cc}"])
            for cc in range(2):
                P.mm(ps[4][:], C.ones_f[:], yc[cc][:], cc == 0, cc == 1, [pfx + f"yc{cc}", "const"], ["ps4"])
            for cc in range(2):
                P.mm(ps[5][:], C.ones_f[:], sqy[cc][:], cc == 0, cc == 1, [pfx + f"sqy{cc}", "const"], ["ps5"])
            P.ts(mean[:], ps[4][:], 1.0 / 256, None, ALU.mult, None, ["ps4"], [pfx + "mean"])
            P.tt(msq[:], mean[:], mean[:], ALU.mult, [pfx + "mean"], [pfx + "msq"])
            P.stt(var[:], ps[5][:], 1.0 / 256, msq[:], ALU.mult, ALU.subtract, ["ps5", pfx + "msq"], [pfx + "var"])
            P.ts(var[:], var[:], EPS, None, ALU.add, None, [pfx + "var"], [pfx + "var"])
            rsqrt_inplace(P, var[:], pfx + "var")
            for cc in range(2):
                P.tt(tn[cc][:], yc[cc][:], mean[:], ALU.subtract, [pfx + f"yc{cc}", pfx + "mean"],
                     [pfx + f"tn{cc}"])
                P.tt(tn[cc][:], tn[cc][:], var[:], ALU.mult, [pfx + f"tn{cc}", pfx + "var"], [pfx + f"tn{cc}"])
                P.act(yb[cc][:], tn[cc][:], AF.Silu, [pfx + f"tn{cc}", pfx + "tab"], [pfx + f"yb{cc}"],
                      bias=lb[:, cc:cc + 1], scale=lg[:, cc:cc + 1])
                P.dma(yT[cc * 128:(cc + 1) * 128, own0 + g * NT:own0 + (g + 1) * NT], yb[cc][:],
                      [pfx + f"yb{cc}"], [])
        P.flush()


def sb_phase(P, C, tag, T, own0, has_prev):
    nc = P.nc
    pfx = tag + "_"
    ps = C.ps
    yT = T["yT"]
    NB = 64 if has_prev else 32
    LB0 = NB - 32
    base = own0 - (NLOC if has_prev else 0)
    NK = NB * 128
    QG = 256
    NQG = NLOC // QG
    with ExitStack() as es:
        sb = lambda n, shp, dt: es.enter_context(nc.sbuf_tensor(pfx + n, shp, dt))
        VA = sb("VA", [128, NB, 512], BF16)
        KT = [sb(f"KT{i}", [128, NK], BF16) for i in range(2)]
        QB = [sb(f"QB{i}", [128, NQG, 2 * QG], BF16) for i in range(2)]
        msk = sb("msk", [128, 2 * NT], BF16)
        ebuf = [sb(f"e{i}", [128, 2 * NT], F32) for i in range(2)]
        spb = [sb(f"sp{i}", [128, 2 * NT], BF16) for i in range(3)]
        wb = [sb(f"w{i}", [128, 2 * NT], BF16) for i in range(3)]
        sA = [sb(f"sA{i}", [128, NT], BF16) for i in range(3)]
        sB = [sb(f"sB{i}", [128, NT], BF16) for i in range(3)]
        ob = [sb(f"ob{i}", [128, QG], BF16) for i in range(2)]

        P.dma(msk[:], C.sbmask2, [], [pfx + "tab"])
        for b in range(2):
            P.op("dve", lambda e, b=b: e.memset(QB[b][0:64, :, QG:2 * QG], 0.0), [], [pfx + f"QB{b}"])
            P.op("dve", lambda e, b=b: e.memset(QB[b][64:128, :, 0:QG], 0.0), [], [pfx + f"QB{b}"])
        for b0 in range(0, NB, 32):
            P.dma(VA[:, b0:b0 + 32, :],
                  T["vtok"][base + b0 * 128:base + (b0 + 32) * 128, :].rearrange("(b p) c -> p b c", p=128),
                  [], [pfx + "VA"])

        def load_pair(hp):
            b = hp % 2
            r0 = hp * 128
            P.dma(KT[b][:], T["kT"][r0:r0 + 128, base:base + NK], [], [pfx + f"KT{b}"])
            P.dma(QB[b][0:64, :, 0:QG],
                  T["qsT"][r0:r0 + 64, own0:own0 + NLOC].rearrange("p (g q) -> p g q", q=QG), [], [pfx + f"QB{b}"])
            P.dma(QB[b][64:128, :, QG:2 * QG],
                  T["qsT"][r0 + 64:r0 + 128, own0:own0 + NLOC].rearrange("p (g q) -> p g q", q=QG),
                  [], [pfx + f"QB{b}"])

        items = []
        for hp in range(4):
            for g in range(NQG):
                blks = [LB0 + 2 * g + 1 - i for i in range(2 * g + 2)] + [LB0 - 1 - i for i in range(LB0)]
                npair = len(blks) // 2
                for i in range(npair):
                    items.append((hp, g, blks[2 * i], blks[2 * i + 1], i == 0, i == npair - 1))
        big = C.psbig
        H0, H1 = slice(0, NT), slice(NT, 2 * NT)

        def stageA(idx):
            hp, g, kb0, kb1, first, last = items[idx]
            b = hp % 2
            if g == 0 and first:
                if hp == 0:
                    load_pair(0)
                if hp + 1 < 4:
                    load_pair(hp + 1)
            i2, i3 = idx % 2, idx % 3
            q = QB[b][:, g, :]
            bA, kA = big[i2], f"psb{i2}"
            for kb, hsl in ((kb0, H0), (kb1, H1)):
                P.mm(bA[:, hsl], KT[b][:, kb * 128:(kb + 1) * 128], q, True, True,
                     [pfx + f"KT{b}", pfx + f"QB{b}"], [kA])
            P.act(ebuf[i2][:], bA[:], AF.Exp, [kA], [pfx + f"e{i2}"])
            P.act(spb[i3][:], ebuf[i2][:], AF.Ln, [pfx + f"e{i2}"], [pfx + f"sp{i3}"], bias=1.0)
            if first:
                P.tt(spb[i3][:], spb[i3][:], msk[:], ALU.mult, [pfx + f"sp{i3}", pfx + "tab"], [pfx + f"sp{i3}"])
                P.op("dve", lambda e: e.tensor_copy(out=sA[i3][:], in_=spb[i3][:, H0]),
                     [pfx + f"sp{i3}"], [pfx + f"sA{i3}"])
            else:
                p3 = (idx - 1) % 3
                P.tt(sA[i3][:], sB[p3][:], spb[i3][:, H0], ALU.add, [pfx + f"sB{p3}", pfx + f"sp{i3}"],
                     [pfx + f"sA{i3}"])
            if not last:
                P.tt(sB[i3][:], sA[i3][:], spb[i3][:, H1], ALU.add, [pfx + f"sA{i3}", pfx + f"sp{i3}"],
                     [pfx + f"sB{i3}"])

        def stageB(idx):
            hp, g, kb0, kb1, first, last = items[idx]
            b = hp % 2
            i3 = idx % 3
            q = QB[b][:, g, :]
            bB, kB = big[2], "psb2"
            for kb, hsl, prev in ((kb0, H0, None if first else sB[(idx - 1) % 3]), (kb1, H1, sA[i3])):
                P.mm(bB[:, hsl], C.trineg[:], spb[i3][:, hsl], True, False, [pfx + f"sp{i3}", "const"], [kB])
                if prev is not None:
                    pk = pfx + (f"sA{i3}" if prev is sA[i3] else f"sB{(idx - 1) % 3}")
                    P.mm(bB[:, hsl], C.negones[:], prev[:], False, False, [pk, "const"], [kB])
                P.mm(bB[:, hsl], KT[b][:, kb * 128:(kb + 1) * 128], q, False, True,
                     [pfx + f"KT{b}", pfx + f"QB{b}"], [kB])
            P.act(wb[i3][:], bB[:], AF.Exp, [kB], [pfx + f"w{i3}"])
            if first:
                P.tt(wb[i3][:], wb[i3][:], msk[:], ALU.mult, [pfx + f"w{i3}", pfx + "tab"], [pfx + f"w{i3}"])

        def stageC(idx):
            hp, g, kb0, kb1, first, last = items[idx]
            i3 = idx % 3
            gi = (hp * NQG + g) % 2
            bO, kO = ps[6 + gi], f"psO{gi}"
            P.mm(bO[:], VA[:, kb0, hp * 128:(hp + 1) * 128], wb[i3][:, H0], first, False,
                 [pfx + "VA", pfx + f"w{i3}"], [kO])
            P.mm(bO[:], VA[:, kb1, hp * 128:(hp + 1) * 128], wb[i3][:, H1], False, last,
                 [pfx + "VA", pfx + f"w{i3}"], [kO])
            if last:
                P.op("dve", lambda e: e.tensor_copy(out=ob[gi][0:64, :], in_=bO[0:64, 0:QG]), [kO], [pfx + f"ob{gi}"])
                P.op("dve", lambda e: e.tensor_copy(out=ob[gi][64:128, :], in_=bO[64:128, QG:2 * QG]), [kO],
                     [pfx + f"ob{gi}"])
                P.dma(yT[256 + hp * 128:256 + (hp + 1) * 128, own0 + g * QG:own0 + (g + 1) * QG], ob[gi][:],
                      [pfx + f"ob{gi}"], [])

        n = len(items)
        for i in range(n + 2):
            if i < n:
                stageA(i)
            if 1 <= i <= n:
                stageB(i - 1)
            if i >= 2:
                stageC(i - 2)
        P.flush()


def outproj_phase(P, C, tag, w_o, T, xsrc, xdst, groups):
    nc = P.nc
    pfx = tag + "_"
    ps = C.ps
    xs = xsrc.rearrange("(c p) t -> p c t", p=128)
    xd = xdst.rearrange("(c p) t -> p c t", p=128)
    yr = T["yT"].rearrange("(c p) t -> p c t", p=128)
    with ExitStack() as es:
        sb = lambda n, shp, dt: es.enter_context(nc.sbuf_tensor(pfx + n, shp, dt))
        wo = sb("wo", [128, 8, D], BF16)
        yin = [sb(f"yin{i}", [128, 8, NT], BF16) for i in range(2)]
        xin = [sb(f"xin{i}", [128, 8, NT], F32) for i in range(2)]
        xo = [sb(f"xo{i}", [128, 8, NT], F32) for i in range(2)]
        P.dma(wo[:], w_o.rearrange("(c p) m -> p c m", p=128), [], [pfx + "wo"], q="pool")
        for g in groups:
            cols = slice(g * NT, (g + 1) * NT)
            b = g % 2
            P.dma(yin[b][:], yr[:, :, cols], [], [pfx + f"yin{b}"])
            P.dma(xin[b][:], xs[:, :, cols], [], [pfx + f"xin{b}"])
            for m in range(8):
                bO, kO = ps[m % 4], f"ps{m % 4}"
                for c in range(8):
                    P.mm(bO[:], wo[:, c, m * 128:(m + 1) * 128], yin[b][:, c, :], c == 0, c == 7,
                         [pfx + "wo", pfx + f"yin{b}"], [kO])
                P.tt(xo[b][:, m, :], bO[:], xin[b][:, m, :], ALU.add, [kO, pfx + f"xin{b}"], [pfx + f"xo{b}"])
            P.dma(xd[:, :, cols], xo[b][:], [pfx + f"xo{b}"], [])
        P.flush()


def final_phase(P, C, tag, gam_d, xsrc, out_d):
    nc = P.nc
    pfx = tag + "_"
    xs = xsrc.rearrange("(c p) t -> p c t", p=128)
    od = out_d.rearrange("(c p) t -> p c t", p=128)
    with ExitStack() as es:
        sb = lambda n, shp, dt: es.enter_context(nc.sbuf_tensor(pfx + n, shp, dt))
        gam = sb("gam", [128, 8], F32)
        xin = [sb(f"xin{i}", [128, 8, NT], F32) for i in range(2)]
        sqc = [sb(f"sqc{i}", [128, NT], BF16) for i in range(2)]
        rstd = sb("rstd", [128, NT], F32)
        yo = [sb(f"yo{i}", [128, 8, NT], F32) for i in range(2)]
        P.dma(gam[:], gam_d, [], [pfx + "gam"])
        for g in range(NG):
            cols = slice(NLOC + g * NT, NLOC + (g + 1) * NT)
            ocols = slice(g * NT, (g + 1) * NT)
            b = g % 2
            P.dma(xin[b][:], xs[:, :, cols], [], [pfx + f"xin{b}"])
            for c in range(8):
                P.act(sqc[c % 2][:], xin[b][:, c, :], AF.Square, [pfx + f"xin{b}"], [pfx + f"sqc{c % 2}"])
                P.mm(C.ps[6][:], C.ones_bf[:], sqc[c % 2][:], c == 0, c == 7, [pfx + f"sqc{c % 2}", "const"], ["ps6"])
            P.ts(rstd[:], C.ps[6][:], 1.0 / D, EPS, ALU.mult, ALU.add, ["ps6"], [pfx + "rstd"])
            rsqrt_inplace(P, rstd[:], pfx + "rstd")
            for c in range(8):
                P.stt(yo[b][:, c, :], xin[b][:, c, :], gam[:, c:c + 1], rstd[:], ALU.mult, ALU.mult,
                      [pfx + f"xin{b}", pfx + "rstd", pfx + "gam"], [pfx + f"yo{b}"])
            P.dma(od[:, :, ocols], yo[b][:], [pfx + f"yo{b}"], [], is_out=True)


MIXT = {
    "vconv": ([256, NSLOT], BF16), "qsT": ([512, NSLOT], BF16), "kT": ([512, NSLOT], BF16),
    "vtok": ([NSLOT, 512], BF16), "vr": ([NSLOT, 256], BF16), "qrT": ([256, NSLOT], BF16),
    "qdT": ([256, NSLOT], BF16), "krT": ([256, NSLOT], BF16), "kdtok": ([NSLOT, 256], BF16),
    "gsil": ([256, NSLOT], F32), "yT": ([D, NSLOT], BF16),
}
CONSTS = {"ones_bf": ([128, 128], BF16), "ones_f": ([128, 128], F32), "ident": ([128, 128], BF16),
          "trineg": ([128, 128], BF16), "negones": ([128, 128], BF16)}
TABLES = {"rope_cos": [128, NSLOT], "rope_sin": [128, NSLOT], "decq": [128, 2, NT], "deck": [128, 256],
          "g64": [64, 256], "dintra": [64, 4, NT], "flag": [128, 1]}
ALLG = list(range(2 * NG))
OWNG = list(range(NG, 2 * NG))


def build_program():
    nc = bass.Bass("TRN2", target_bir_lowering=False)
    dt = lambda name, shape, dtype, kind: nc.dram_tensor(name, shape, dtype, kind=kind).ap()
    C = Ctx()
    W = {}
    for l in range(DEPTH):
        for f in ("1", "2"):
            W[f"ffn{f}_w_in{l}"] = dt(f"ffn{f}_w_in{l}", [D, 2 * DFF], F32, "ExternalInput")
            W[f"ffn{f}_w_out{l}"] = dt(f"ffn{f}_w_out{l}", [DFF, D], F32, "ExternalInput")
            W[f"ffn{f}_norm{l}"] = dt(f"ffn{f}_norm{l}", [128, 8], F32, "ExternalInput")
        W[f"mix_w_in{l}"] = dt(f"mix_w_in{l}", [D, 3072], F32, "ExternalInput")
        W[f"mix_norm{l}"] = dt(f"mix_norm{l}", [128, 8], F32, "ExternalInput")
        W[f"mix_w_out{l}"] = dt(f"mix_w_out{l}", [D, D], F32, "ExternalInput")
        W[f"conv_w{l}"] = dt(f"conv_w{l}", [128, 2, 31], F32, "ExternalInput")
        for nm in ("conv_b", "conv_ln_g", "conv_ln_b"):
            W[f"{nm}{l}"] = dt(f"{nm}{l}", [128, 2], F32, "ExternalInput")
        W[f"ret_norm_g{l}"] = dt(f"ret_norm_g{l}", [64, 4], F32, "ExternalInput")
    W["final_norm"] = dt("final_norm", [128, 8], F32, "ExternalInput")
    cd = {k: dt("c_" + k, s, d, "ExternalInput") for k, (s, d) in CONSTS.items()}
    for k, s in TABLES.items():
        setattr(C, k, dt("t_" + k, s, F32, "ExternalInput"))
    C.sbmask2 = dt("t_sbmask2", [128, 2 * NT], BF16, "ExternalInput")
    x_in = dt("x_in", [D, NSLOT], F32, "ExternalInput")
    xa = dt("xa", [D, NSLOT], F32, "Internal")
    xb = dt("xb", [D, NSLOT], F32, "Internal")
    T = {k: dt("m_" + k, s, d, "Internal") for k, (s, d) in MIXT.items()}
    out = dt("out", [D, NLOC], F32, "ExternalOutput")

    with ExitStack() as es:
        P = Prog(nc, es)
        C.psbig = [es.enter_context(nc.psum_tensor(f"psb{i}", [128, 2 * NT], F32)) for i in range(4)]
        C.ps = [C.psbig[i // 2][:, (i % 2) * NT:(i % 2 + 1) * NT] for i in range(8)]
        C.pst = C.psbig[3].bitcast(BF16)[:, 2 * NT:4 * NT]
        for k, (s, d) in CONSTS.items():
            t = es.enter_context(nc.sbuf_tensor("k_" + k, s, d))
            setattr(C, k, t)
            P.dma(t[:], cd[k], [], ["const"])
        P.flush()
        mixw = lambda l: (W[f"conv_w{l}"], W[f"conv_b{l}"], W[f"conv_ln_g{l}"], W[f"conv_ln_b{l}"],
                          W[f"ret_norm_g{l}"])
        ffn_phase(P, C, "f1a", W["ffn1_w_in0"], W["ffn1_w_out0"], W["ffn1_norm0"], x_in, xa, ALLG)
        inproj_phase(P, C, "ipa", W["mix_w_in0"], W["mix_norm0"], xa, T, ALLG)
        retconv_phase(P, C, "rc0a", T, *mixw(0), own0=0, has_prev=False)
        sb_phase(P, C, "sb0a", T, own0=0, has_prev=False)
        retconv_phase(P, C, "rc0b", T, *mixw(0), own0=NLOC, has_prev=True)
        sb_phase(P, C, "sb0b", T, own0=NLOC, has_prev=True)
        outproj_phase(P, C, "opa", W["mix_w_out0"], T, xa, xb, ALLG)
        ffn_phase(P, C, "f2a", W["ffn2_w_in0"], W["ffn2_w_out0"], W["ffn2_norm0"], xb, xa, ALLG, flag_prefix=True)
        ffn_phase(P, C, "f1b", W["ffn1_w_in1"], W["ffn1_w_out1"], W["ffn1_norm1"], xa, xb, ALLG)
        inproj_phase(P, C, "ipb", W["mix_w_in1"], W["mix_norm1"], xb, T, ALLG, kv_only=tuple(range(NG)))
        retconv_phase(P, C, "rc1", T, *mixw(1), own0=NLOC, has_prev=True)
        sb_phase(P, C, "sb1", T, own0=NLOC, has_prev=True)
        outproj_phase(P, C, "opb", W["mix_w_out1"], T, xb, xa, OWNG)
        ffn_phase(P, C, "f2b", W["ffn2_w_in1"], W["ffn2_w_out1"], W["ffn2_norm1"], xa, xb, OWNG)
        final_phase(P, C, "fin", W["final_norm"], xb, out)
        P.flush(final=True)
    return nc


def _consts():
    bf = ml_dtypes.bfloat16
    j = np.arange(128)
    c = {
        "c_ones_bf": np.ones((128, 128), bf), "c_ones_f": np.ones((128, 128), np.float32),
        "c_ident": np.eye(128).astype(bf),
        "c_trineg": (-(j[:, None] >= j[None, :]).astype(np.float32)).astype(bf),
        "c_negones": (-np.ones((128, 128), np.float32)).astype(bf),
    }
    gam = 1.0 - np.exp2(-5.0 - np.arange(4, dtype=np.float64))
    i512 = np.arange(NT)
    p = np.arange(128)
    decq = np.zeros((128, 2, NT))
    for cc in range(2):
        hh = 2 * cc + p // 64
        decq[:, cc, :] = 0.125 * gam[hh][:, None] ** ((i512 % 64) + 1.0)[None, :]
    col = np.arange(256)
    deck = gam[col // 64][None, :] ** (63.0 - (p % 64))[:, None]
    g64 = np.broadcast_to((gam[col // 64] ** 64.0)[None, :], (64, 256))
    jj = np.arange(64)
    dintra = np.zeros((64, 4, NT))
    for hh in range(4):
        dintra[:, hh, :] = gam[hh] ** np.abs(jj[:, None] - (i512 % 64)[None, :])
    sbmask = np.zeros((128, 2 * NT), np.float32)
    for slot, cc in enumerate((1, 0)):
        sbmask[:, slot * NT:(slot + 1) * NT] = ((cc * 128 + p)[:, None] < (i512 % 256)[None, :])
    c.update({"t_decq": decq.astype(np.float32), "t_deck": deck.astype(np.float32),
              "t_g64": np.ascontiguousarray(g64).astype(np.float32), "t_dintra": dintra.astype(np.float32),
              "t_sbmask2": sbmask.astype(bf)})
    return c


def _rope(half):
    slot = np.arange(NSLOT)
    pos = (slot if half == 1 else slot % NLOC).astype(np.float32)
    inv = (1.0 / (10000.0 ** (np.arange(32, dtype=np.float32) / 32))).astype(np.float32)
    p = np.arange(128)
    ang = pos[None, :] * inv[(p % 64) % 32][:, None]
    sign = np.where((p % 64) < 32, -1.0, 1.0)[:, None]
    return np.cos(ang).astype(np.float32), (sign * np.sin(ang)).astype(np.float32)


def _vec8(v):
    return np.ascontiguousarray(v.reshape(8, 128).T)


def _vec2(v):
    return np.ascontiguousarray(v.reshape(2, 128).T)


def kernel(**inp):
    ncores = 8
    x = inp["x"]
    w = dict(_consts())
    for l in range(DEPTH):
        for f in ("1", "2"):
            w[f"ffn{f}_w_in{l}"] = np.ascontiguousarray(inp[f"ffn{f}_w_in"][l])
            w[f"ffn{f}_w_out{l}"] = np.ascontiguousarray(inp[f"ffn{f}_w_out"][l])
            w[f"ffn{f}_norm{l}"] = _vec8(inp[f"ffn{f}_norm"][l])
        w[f"mix_w_in{l}"] = np.ascontiguousarray(inp["mix_w_in"][l])
        w[f"mix_norm{l}"] = _vec8(inp["mix_norm"][l])
        w[f"mix_w_out{l}"] = np.ascontiguousarray(inp["mix_w_out"][l])
        w[f"conv_w{l}"] = np.ascontiguousarray(inp["conv_w"][l].T.reshape(2, 128, 31).transpose(1, 0, 2))
        for nm in ("conv_b", "conv_ln_g", "conv_ln_b"):
            w[f"{nm}{l}"] = _vec2(inp[nm][l])
        w[f"ret_norm_g{l}"] = np.ascontiguousarray(inp["ret_norm_g"][l].reshape(4, 64).T)
    w["final_norm"] = _vec8(inp["final_norm"])
    ropes = [_rope(0), _rope(1)]
    maps = []
    for core in range(ncores):
        b, half = core // 2, core % 2
        m = dict(w)
        m["t_rope_cos"], m["t_rope_sin"] = ropes[half]
        m["t_flag"] = np.full((128, 1), float(half), np.float32)
        xi = np.zeros((D, NSLOT), np.float32)
        if half == 1:
            xi[:, :] = x[b].T
        else:
            xi[:, NLOC:] = x[b, :NLOC, :].T
        m["x_in"] = xi
        maps.append(m)
    res = run_bass_kernel_spmd(build_program(), maps, core_ids=list(range(ncores))).results
    out = np.empty(x.shape, np.float32)
    for c in range(ncores):
        out[c // 2, (c % 2) * NLOC:(c % 2 + 1) * NLOC, :] = res[c]["out"].T
    return out
```

```python
import numpy as np
import ml_dtypes
from contextlib import ExitStack
import concourse.bass as bass
import concourse.mybir as mybir
from concourse.bass_utils import run_bass_kernel_spmd

F32 = mybir.dt.float32
BF16 = mybir.dt.bfloat16
AF = mybir.ActivationFunctionType
ALU = mybir.AluOpType

D = 1024
DFF = 2816
NLOC = 4096
NSLOT = 8192
NT = 512
NG = NLOC // NT
DEPTH = 2
EPS = 1e-6
NDS = 8


class Op:
    __slots__ = ("stream", "fn", "deps", "dma", "token", "needed", "phase", "wkeys")


class Prog:
    STREAMS = ["pe", "act", "dve", "pool", "sp"]

    def __init__(self, nc, es):
        self.nc = nc
        self.csem = {s: es.enter_context(nc.semaphore("c_" + s)) for s in ["pe", "act", "dve", "pool"]}
        self.dsem = {s: [es.enter_context(nc.semaphore(f"d_{s}{i}")) for i in range(NDS)]
                     for s in ["sp", "pool"]}
        self.ccnt = {s: 0 for s in self.csem}
        self.dcnt = {s: [0] * NDS for s in self.dsem}
        self.dnum = {s: 0 for s in self.dsem}
        self.lastw = {}
        self.readers = {}
        self.ops = []
        self.phase = 0
        self.waited = {s: {} for s in self.STREAMS}
        self.pending = {s: {} for s in self.STREAMS}
        self.out_tokens = []

    def op(self, stream, fn, reads=(), writes=(), dma=False, is_out=False):
        o = Op()
        o.stream, o.fn, o.dma, o.needed, o.phase, o.token = stream, fn, dma, False, self.phase, None
        o.wkeys = set(writes)
        deps = {}
        for k in reads:
            w = self.lastw.get(k)
            if w is not None:
                deps[id(w)] = (w, True)
        for k in writes:
            w = self.lastw.get(k)
            if w is not None and id(w) not in deps:
                deps[id(w)] = (w, False)
            for r in self.readers.get(k, {}).values():
                if id(r) not in deps:
                    deps[id(r)] = (r, False)
        o.deps = []
        for d, raw in deps.values():
            if d.phase != self.phase or d is o:
                continue
            if d.stream == stream and not d.dma and not dma:
                if stream == "pe" or not raw:
                    continue
            o.deps.append(d)
        if dma:
            i = self.dnum[stream] % NDS
            self.dnum[stream] += 1
            self.dcnt[stream][i] += 16
            o.token = (self.dsem[stream][i], self.dcnt[stream][i])
            rkey = (stream, i)
            if is_out:
                self.out_tokens.append(o.token)
        else:
            rkey = (stream, -1)
        for k in reads:
            self.readers.setdefault(k, {})[rkey] = o
        for k in writes:
            self.lastw[k] = o
            self.readers[k] = {}
        self.ops.append(o)
        return o

    def flush(self, final=False):
        ops = self.ops
        for o in ops:
            for d in o.deps:
                d.needed = True
        last = {}
        for o in ops:
            if not o.dma:
                last[o.stream] = o
        for o in last.values():
            o.needed = True
        for o in ops:
            if not o.dma and o.needed:
                self.ccnt[o.stream] += 1
                o.token = (self.csem[o.stream], self.ccnt[o.stream])
        by_stream = {s: [o for o in ops if o.stream == s] for s in self.STREAMS}
        snapshot = {}
        for s in self.csem:
            if self.ccnt[s]:
                snapshot[id(self.csem[s])] = (self.csem[s], self.ccnt[s])
        for s in self.dsem:
            for i in range(NDS):
                if self.dcnt[s][i]:
                    snapshot[id(self.dsem[s][i])] = (self.dsem[s][i], self.dcnt[s][i])

        def emit(stream, eng):
            waited = self.waited[stream]
            for o in by_stream[stream]:
                w = dict(self.pending[stream])
                self.pending[stream] = {}
                for d in o.deps:
                    sem, val = d.token
                    if id(sem) not in w or w[id(sem)][1] < val:
                        w[id(sem)] = (sem, val)
                for sid, (sem, val) in w.items():
                    if waited.get(sid, 0) < val:
                        eng.wait_ge(sem, val)
                        waited[sid] = val
                ins = o.fn(eng)
                if o.token is not None:
                    ins.then_inc(o.token[0], 16 if o.dma else 1)
            if final and stream == "sp":
                for sid, (sem, val) in snapshot.items():
                    if waited.get(sid, 0) < val:
                        eng.wait_ge(sem, val)
                        waited[sid] = val

        with self.nc.Block() as blk:
            if by_stream["pe"]:
                blk.tensor(lambda e: emit("pe", e))
            if by_stream["act"]:
                blk.scalar(lambda e: emit("act", e))
            if by_stream["dve"]:
                blk.vector(lambda e: emit("dve", e))
            if by_stream["pool"]:
                blk.gpsimd(lambda e: emit("pool", e))
            if by_stream["sp"] or final:
                blk.sync(lambda e: emit("sp", e))
        for s in self.STREAMS:
            p = self.pending[s]
            for sid, (sem, val) in snapshot.items():
                if sid not in p or p[sid][1] < val:
                    p[sid] = (sem, val)
        self.ops = []
        self.phase += 1

    def dma(self, out, in_, reads, writes, q="sp", is_out=False):
        return self.op(q, lambda e: e.dma_start(out=out, in_=in_), reads, writes, dma=True, is_out=is_out)

    def mm(self, out, lhsT, rhs, start, stop, reads, writes):
        return self.op("pe", lambda e: e.matmul(out, lhsT, rhs, start=start, stop=stop), reads, writes)

    def act(self, out, in_, func, reads, writes, bias=None, scale=None):
        kw = {}
        if bias is not None:
            kw["bias"] = bias
        if scale is not None:
            kw["scale"] = scale
        return self.op("act", lambda e: e.activation(out=out, in_=in_, func=func, **kw), reads, writes)

    def tt(self, out, in0, in1, op, reads, writes, eng="dve"):
        return self.op(eng, lambda e: e.tensor_tensor(out=out, in0=in0, in1=in1, op=op), reads, writes)

    def ts(self, out, in0, s1, s2, op0, op1, reads, writes, eng="dve"):
        if op1 is None:
            return self.op(eng, lambda e: e.tensor_scalar(out=out, in0=in0, scalar1=s1, scalar2=None, op0=op0),
                           reads, writes)
        return self.op(eng, lambda e: e.tensor_scalar(out=out, in0=in0, scalar1=s1, scalar2=s2, op0=op0, op1=op1),
                       reads, writes)

    def stt(self, out, in0, scalar, in1, op0, op1, reads, writes, eng="dve"):
        return self.op(eng, lambda e: e.scalar_tensor_tensor(out=out, in0=in0, scalar=scalar, in1=in1,
                                                             op0=op0, op1=op1), reads, writes)


class Ctx:
    pass


def rsqrt_inplace(P, t, key):
    P.act(t, t, AF.Sqrt, [key], [key])
    P.op("dve", lambda e: e.reciprocal(out=t, in_=t), [key], [key])


def rms_norm_group(P, C, pfx, xin, gam, h, sqc, rstd, g, xk="xin", hk="h"):
    ps = C.ps
    for c in range(8):
        P.act(sqc[c % 2][:], xin[:, c, :], AF.Square, [pfx + xk], [pfx + f"sqc{c % 2}"])
        P.mm(ps[6][:], C.ones_bf[:], sqc[c % 2][:], c == 0, c == 7, [pfx + f"sqc{c % 2}", "const"], ["ps6"])
    P.ts(rstd[:], ps[6][:], 1.0 / D, EPS, ALU.mult, ALU.add, ["ps6"], [pfx + "rstd"])
    rsqrt_inplace(P, rstd[:], pfx + "rstd")
    for c in range(8):
        P.stt(h[:, c, :], xin[:, c, :], gam[:, c:c + 1], rstd[:], ALU.mult, ALU.mult,
              [pfx + xk, pfx + "rstd", pfx + "gam"], [pfx + hk])


def ffn_phase(P, C, tag, w_in, w_out, gam_d, xsrc, xdst, groups, flag_prefix=False):
    nc = P.nc
    pfx = tag + "_"
    ps = C.ps
    xs = xsrc.rearrange("(c p) t -> p c t", p=128)
    xd = xdst.rearrange("(c p) t -> p c t", p=128)
    with ExitStack() as es:
        sb = lambda n, shp, dt: es.enter_context(nc.sbuf_tensor(pfx + n, shp, dt))
        win = sb("win", [128, 8, 2 * DFF], BF16)
        wout = sb("wout", [128, 22, D], BF16)
        gam = sb("gam", [128, 8], F32)
        xin = sb("xin", [128, 8, NT], F32)
        h = sb("h", [128, 8, NT], BF16)
        sqc = [sb(f"sqc{i}", [128, NT], BF16) for i in range(2)]
        rstd = sb("rstd", [128, NT], F32)
        a = sb("a", [128, 22, NT], BF16)
        sg = [sb(f"sg{i}", [128, NT], F32) for i in range(2)]
        xr = [sb(f"xr{i}", [128, NT], F32) for i in range(2)]
        xo = [sb(f"xo{i}", [128, NT], F32) for i in range(2)]
        flg = sb("flg", [128, 1], F32)

        P.dma(gam[:], gam_d, [], [pfx + "gam"])
        P.dma(flg[:], C.flag, [], [pfx + "flg"])
        g0 = groups[0]
        P.dma(xin[:], xs[:, :, g0 * NT:(g0 + 1) * NT], [], [pfx + "xin"])
        for c in range(8):
            P.dma(win[:, c, :], w_in[c * 128:(c + 1) * 128, :], [], [pfx + f"win{c}"], q="pool")
        wo_r = w_out.rearrange("(j p) m -> p j m", p=128)
        for j0 in range(0, 22, 6):
            j1 = min(22, j0 + 6)
            P.dma(wout[:, j0:j1, :], wo_r[:, j0:j1, :], [], [pfx + f"wout{jj}" for jj in range(j0, j1)], q="pool")
        WIN = [pfx + f"win{c}" for c in range(8)]
        WOUT = [pfx + f"wout{j}" for j in range(22)]

        rms_norm_group(P, C, pfx, xin, gam, h, sqc, rstd, 0)
        for gi_, g in enumerate(groups):
            cols = slice(g * NT, (g + 1) * NT)
            gn = groups[gi_ + 1] if gi_ + 1 < len(groups) else None
            if gn is not None:
                P.dma(xin[:], xs[:, :, gn * NT:(gn + 1) * NT], [], [pfx + "xin"])
            for j in range(22):
                bG, bU = ps[(j % 2) * 2], ps[(j % 2) * 2 + 1]
                kG, kU = f"ps{(j % 2) * 2}", f"ps{(j % 2) * 2 + 1}"
                for c in range(8):
                    P.mm(bG[:], win[:, c, j * 128:(j + 1) * 128], h[:, c, :], c == 0, c == 7,
                         [pfx + "h", WIN[c]], [kG])
                for c in range(8):
                    P.mm(bU[:], win[:, c, DFF + j * 128:DFF + (j + 1) * 128], h[:, c, :], c == 0, c == 7,
                         [pfx + "h", WIN[c]], [kU])
                P.act(sg[j % 2][:], bG[:], AF.Silu, [kG], [pfx + f"sg{j % 2}"])
                P.tt(a[:, j, :], sg[j % 2][:], bU[:], ALU.mult, [pfx + f"sg{j % 2}", kU], [pfx + f"a{j}"])
            if gn is not None:
                rms_norm_group(P, C, pfx, xin, gam, h, sqc, rstd, gn)
            for m in range(8):
                P.dma(xr[m % 2][:], xs[:, m, cols], [], [pfx + f"xr{m % 2}"])
                bO, kO = ps[4 + m % 2], f"ps{4 + m % 2}"
                for j in range(22):
                    P.mm(bO[:], wout[:, j, m * 128:(m + 1) * 128], a[:, j, :], j == 0, j == 21,
                         [pfx + f"a{j}", WOUT[j]], [kO])
                P.stt(xo[m % 2][:], bO[:], 0.5, xr[m % 2][:], ALU.mult, ALU.add,
                      [kO, pfx + f"xr{m % 2}"], [pfx + f"xo{m % 2}"])
                if flag_prefix and g < NG:
                    P.ts(xo[m % 2][:], xo[m % 2][:], flg[:, 0:1], None, ALU.mult, None,
                         [pfx + f"xo{m % 2}", pfx + "flg"], [pfx + f"xo{m % 2}"])
                P.dma(xd[:, m, cols], xo[m % 2][:], [pfx + f"xo{m % 2}"], [])
        P.flush()


def inproj_phase(P, C, tag, w_mix, gam_d, xsrc, T, groups, kv_only=()):
    nc = P.nc
    pfx = tag + "_"
    ps = C.ps
    xs = xsrc.rearrange("(c p) t -> p c t", p=128)
    with ExitStack() as es:
        sb = lambda n, shp, dt: es.enter_context(nc.sbuf_tensor(pfx + n, shp, dt))
        wm = sb("wm", [128, 8, 3072], BF16)
        wsw = sb("wsw", [128, 8, 512], BF16)
        gam = sb("gam", [128, 8], F32)
        xin2 = [sb(f"xin{i}", [128, 8, NT], F32) for i in range(2)]
        h2 = [sb(f"h{i}", [128, 8, NT], BF16) for i in range(2)]
        sqc = [sb(f"sqc{i}", [128, NT], BF16) for i in range(2)]
        rstd = sb("rstd", [128, NT], F32)
        cosb = [sb(f"cosb{i}", [128, NT], F32) for i in range(2)]
        sinb = [sb(f"sinb{i}", [128, NT], F32) for i in range(2)]
        decq = sb("decq", [128, 2, NT], F32)
        deck = sb("deck", [128, 256], F32)
        t1 = [sb(f"t1{i}", [128, NT], F32) for i in range(2)]
        t2 = [sb(f"t2{i}", [128, NT], F32) for i in range(2)]
        ob = [sb(f"ob{i}", [128, NT], BF16) for i in range(4)]
        of = [sb(f"of{i}", [128, NT], F32) for i in range(2)]
        krb = [sb(f"krb{i}", [128, NT], BF16) for i in range(2)]
        kdb = [sb(f"kdb{i}", [128, 256], BF16) for i in range(2)]

        P.dma(gam[:], gam_d, [], [pfx + "gam"])
        for i_ in range(min(2, len(groups))):
            P.dma(xin2[i_][:], xs[:, :, groups[i_] * NT:(groups[i_] + 1) * NT], [], [pfx + f"xin{i_}"])
        P.dma(decq[:], C.decq, [], [pfx + "tab"])
        P.dma(deck[:], C.deck, [], [pfx + "tab"])
        for c in range(8):
            P.dma(wm[:, c, :], w_mix[c * 128:(c + 1) * 128, :], [], [pfx + "wm"], q="pool")
            src = w_mix[c * 128:(c + 1) * 128, 2048:2560].rearrange("p (h two d) -> p h two d", two=2, d=32)
            dst = wsw[:, c, :].rearrange("p (h two d) -> p h two d", two=2, d=32)
            P.dma(dst[:, :, 0, :], src[:, :, 1, :], [], [pfx + "wm"], q="pool")
            P.dma(dst[:, :, 1, :], src[:, :, 0, :], [], [pfx + "wm"], q="pool")

        cnt = {"b": 0, "ob": 0, "of": 0, "t": 0, "kr": 0, "kd": 0}

        def bank():
            i = cnt["b"] % 6
            cnt["b"] += 1
            return ps[i], f"ps{i}"

        cur = {}

        def proj(bk, kk, wt, c0, ncols):
            for c in range(8):
                P.mm(bk[:, 0:NT] if ncols == 128 else bk[:], wt[:, c, c0:c0 + 128], cur["h"][:, c, :], c == 0, c == 7,
                     [cur["hk"], pfx + "wm"], [kk])

        def norm_next(gi_):
            if gi_ + 1 < len(groups):
                nb_ = (gi_ + 1) % 2
                rms_norm_group(P, C, pfx, xin2[nb_], gam, h2[nb_], sqc, rstd, 0, xk=f"xin{nb_}", hk=f"h{nb_}")
            if gi_ + 2 < len(groups):
                g2 = groups[gi_ + 2]
                P.dma(xin2[gi_ % 2][:], xs[:, :, g2 * NT:(g2 + 1) * NT], [], [pfx + f"xin{gi_ % 2}"])

        rms_norm_group(P, C, pfx, xin2[0], gam, h2[0], sqc, rstd, 0, xk="xin0", hk="h0")
        for gi_, g in enumerate(groups):
            cols = slice(g * NT, (g + 1) * NT)
            kvo = g in kv_only
            rb = gi_ % 2
            h = h2[rb]
            cur["h"], cur["hk"] = h, pfx + f"h{rb}"
            cosT, sinT = cosb[rb], sinb[rb]
            P.dma(cosT[:], C.rope_cos[:, cols], [], [pfx + f"rope{rb}"])
            P.dma(sinT[:], C.rope_sin[:, cols], [], [pfx + f"rope{rb}"])
            for cc in range(2):
                if kvo and g != NG - 1:
                    continue
                bA, kA = bank()
                proj(bA, kA, wm, cc * 128, 128)
                bB, kB = bank()
                proj(bB, kB, wm, 256 + cc * 128, 128)
                i = cnt["t"] % 2
                cnt["t"] += 1
                P.act(t1[i][:], bB[:], AF.Sigmoid, [kB], [pfx + f"t1{i}"])
                o = cnt["ob"] % 4
                cnt["ob"] += 1
                P.tt(ob[o][:], bA[:], t1[i][:], ALU.mult, [kA, pfx + f"t1{i}"], [pfx + f"ob{o}"])
                P.dma(T["vconv"][cc * 128:(cc + 1) * 128, cols], ob[o][:], [pfx + f"ob{o}"], [])
            for sec, dname, scl in ((512, "qsT", 0.125), (1024, "kT", 1.0)):
                if kvo and dname == "qsT":
                    continue
                for cc in range(4):
                    bk, kk = bank()
                    proj(bk, kk, wm, sec + cc * 128, 128)
                    o = cnt["ob"] % 4
                    cnt["ob"] += 1
                    P.ts(ob[o][:], bk[:], scl, None, ALU.mult, None, [kk], [pfx + f"ob{o}"])
                    P.dma(T[dname][cc * 128:(cc + 1) * 128, cols], ob[o][:], [pfx + f"ob{o}"], [])
            norm_next(gi_)
            for tb in range(4):
                tsl = slice(tb * 128, (tb + 1) * 128)
                rows = slice(g * NT + tb * 128, g * NT + (tb + 1) * 128)
                bk, kk = bank()
                for c in range(8):
                    P.mm(bk[:], h[:, c, tsl], wm[:, c, 1536:2048], c == 0, c == 7, [cur["hk"], pfx + "wm"], [kk])
                o = cnt["ob"] % 4
                cnt["ob"] += 1
                P.act(ob[o][:], bk[:], AF.Copy, [kk], [pfx + f"ob{o}"])
                P.dma(T["vtok"][rows, :], ob[o][:], [pfx + f"ob{o}"], [])
                bk, kk = bank()
                for c in range(8):
                    P.mm(bk[:, 0:256], h[:, c, tsl], wm[:, c, 2560:2816], c == 0, c == 7,
                         [cur["hk"], pfx + "wm"], [kk])
                o = cnt["ob"] % 4
                cnt["ob"] += 1
                P.act(ob[o][:, 0:256], bk[:, 0:256], AF.Copy, [kk], [pfx + f"ob{o}"])
                P.dma(T["vr"][rows, :], ob[o][:, 0:256], [pfx + f"ob{o}"], [])
            for isk, sec, swc in ((0, 2048, 0), (1, 2304, 256)):
                if kvo and not isk:
                    continue
                for cc in range(2):
                    bX, kX = bank()
                    proj(bX, kX, wm, sec + cc * 128, 128)
                    bS, kS = bank()
                    proj(bS, kS, wsw, swc + cc * 128, 128)
                    i = cnt["t"] % 2
                    cnt["t"] += 1
                    P.tt(t1[i][:], bX[:], cosT[:], ALU.mult, [kX, pfx + f"rope{rb}"], [pfx + f"t1{i}"])
                    P.tt(t2[i][:], bS[:], sinT[:], ALU.mult, [kS, pfx + f"rope{rb}"], [pfx + f"t2{i}"])
                    P.tt(t1[i][:], t1[i][:], t2[i][:], ALU.add, [pfx + f"t1{i}", pfx + f"t2{i}"], [pfx + f"t1{i}"],
                         eng="pool")
                    frows = slice(cc * 128, (cc + 1) * 128)
                    if not isk:
                        o = cnt["ob"] % 4
                        cnt["ob"] += 1
                        P.ts(ob[o][:], t1[i][:], 0.125, None, ALU.mult, None, [pfx + f"t1{i}"], [pfx + f"ob{o}"],
                             eng="pool")
                        P.dma(T["qrT"][frows, cols], ob[o][:], [pfx + f"ob{o}"], [])
                        o = cnt["ob"] % 4
                        cnt["ob"] += 1
                        P.tt(ob[o][:], t1[i][:], decq[:, cc, :], ALU.mult, [pfx + f"t1{i}", pfx + "tab"],
                             [pfx + f"ob{o}"], eng="pool")
                        P.dma(T["qdT"][frows, cols], ob[o][:], [pfx + f"ob{o}"], [])
                    else:
                        r = cnt["kr"] % 2
                        cnt["kr"] += 1
                        P.act(krb[r][:], t1[i][:], AF.Copy, [pfx + f"t1{i}"], [pfx + f"krb{r}"])
                        P.dma(T["krT"][frows, cols], krb[r][:], [pfx + f"krb{r}"], [])
                        for tb in range(4):
                            rows = slice(g * NT + tb * 128, g * NT + (tb + 1) * 128)
                            P.op("pe", lambda e, r=r, tb=tb: e.transpose(C.pst[:, 0:128],
                                                                          krb[r][:, tb * 128:(tb + 1) * 128],
                                                                          C.ident[:]),
                                 [pfx + f"krb{r}", "const"], ["pst"])
                            d = cnt["kd"] % 2
                            cnt["kd"] += 1
                            P.tt(kdb[d][:, 0:128], C.pst[:, 0:128], deck[:, cc * 128:(cc + 1) * 128], ALU.mult,
                                 ["pst", pfx + "tab"], [pfx + f"kdb{d}"])
                            P.dma(T["kdtok"][rows, cc * 128:(cc + 1) * 128], kdb[d][:, 0:128],
                                  [pfx + f"kdb{d}"], [])
            for cc in range(2):
                if kvo:
                    continue
                bk, kk = bank()
                proj(bk, kk, wm, 2816 + cc * 128, 128)
                o = cnt["of"] % 2
                cnt["of"] += 1
                P.act(of[o][:], bk[:], AF.Silu, [kk], [pfx + f"of{o}"])
                P.dma(T["gsil"][cc * 128:(cc + 1) * 128, cols], of[o][:], [pfx + f"of{o}"], [])
        P.flush()


def retconv_phase(P, C, tag, T, cw_d, cb_d, lg_d, lb_d, rg_d, own0, has_prev):
    nc = P.nc
    pfx = tag + "_"
    ps = C.ps
    yT = T["yT"]
    with ExitStack() as es:
        sb = lambda n, shp, dt: es.enter_context(nc.sbuf_tensor(pfx + n, shp, dt))
        vpad = sb("vpad", [128, 2, 32 + NLOC], BF16)
        dg = sb("dg", [128, 62, 128], BF16)
        yc = [sb(f"yc{i}", [128, NT], F32) for i in range(2)]
        cw = sb("cw", [128, 2, 31], F32)
        cb = sb("cb", [128, 2], F32)
        lg = sb("lg", [128, 2], F32)
        lb = sb("lb", [128, 2], F32)
        rg = sb("rg", [64, 4], F32)
        kdq = sb("kdq", [64, 4 * NLOC], BF16)
        kd64 = kdq[:, :].rearrange("p (n c) -> p n c", c=256)
        vr64 = sb("vr64", [64, 64, 256], BF16)
        sball = sb("sball", [64, 64, 256], BF16)
        S = sb("S", [64, 256], F32)
        g64 = sb("g64", [64, 256], F32)
        dintra = sb("dintra", [64, 4, NT], F32)
        qr = [kdq[:, 0:NLOC]] * 2
        kr = [kdq[:, NLOC:2 * NLOC]] * 2
        qd = [kdq[:, 2 * NLOC:3 * NLOC]] * 2
        gs = [sb(f"gs{i}", [64, NT], F32) for i in range(2)]
        stb = [sb(f"stb{i}", [64, NT], BF16) for i in range(2)]
        osb = sb("osb", [64, NT], F32)
        osq = sb("osq", [64, NT], F32)
        mean = sb("mean", [128, NT], F32)
        msq = sb("msq", [128, NT], F32)
        var = sb("var", [128, NT], F32)
        tn = [sb(f"tn{i}", [128, NT], F32) for i in range(2)]
        yb = [sb(f"yb{i}", [128, NT], BF16) for i in range(2)]
        sqy = [sb(f"sqy{i}", [128, NT], F32) for i in range(2)]

        for t, d in ((cw, cw_d), (cb, cb_d), (lg, lg_d), (lb, lb_d), (rg, rg_d), (g64, C.g64), (dintra, C.dintra)):
            P.dma(t[:], d, [], [pfx + "tab"])
        for cc in range(2):
            if has_prev:
                P.dma(vpad[:, cc, 0:32], T["vconv"][cc * 128:(cc + 1) * 128, own0 - 32:own0], [], [pfx + "vpad"])
            else:
                P.op("dve", lambda e, cc=cc: e.memset(vpad[:, cc, 0:32], 0.0), [], [pfx + "vpad"])
            P.dma(vpad[:, cc, 32:], T["vconv"][cc * 128:(cc + 1) * 128, own0:own0 + NLOC], [], [pfx + "vpad"])
            for k in range(31):
                P.ts(dg[:, cc * 31 + k, :], C.ident[:], cw[:, cc, k:k + 1], None, ALU.mult, None,
                     ["const", pfx + "tab"], [pfx + "dg"])
        P.op("dve", lambda e: e.memset(S[:], 0.0), [], [pfx + "S"])
        for half in range(2):
            if half == 0 and not has_prev:
                continue
            r0 = own0 - NLOC if half == 0 else own0
            ksrc = T["kdtok"][r0:r0 + NLOC, :]
            vsrc = T["vr"][r0:r0 + NLOC, :]
            P.dma(kd64, ksrc.rearrange("(n p) c -> p n c", p=64), [], [pfx + "kd64"])
            P.dma(vr64[:], vsrc.rearrange("(n p) c -> p n c", p=64), [], [pfx + "vr64"])
            for n in range(64):
                if half == 1:
                    P.act(sball[:, n, :], S[:], AF.Copy, [pfx + "S"], [pfx + f"sball{n}"])
                for hh in range(4):
                    hs = slice(hh * 64, (hh + 1) * 64)
                    P.mm(ps[7][0:64, hs], kd64[:, n, hs], vr64[:, n, hs], True, True,
                         [pfx + "kd64", pfx + "vr64"], ["ps7"])
                P.tt(S[:], S[:], g64[:], ALU.mult, [pfx + "S", pfx + "tab"], [pfx + "S"])
                P.tt(S[:], S[:], ps[7][0:64, 0:256], ALU.add, [pfx + "S", "ps7"], [pfx + "S"])
        for hh in range(4):
            b = 0
            hrows = slice(hh * 64, (hh + 1) * 64)
            hs = slice(hh * 64, (hh + 1) * 64)
            P.dma(qr[b], T["qrT"][hrows, own0:own0 + NLOC], [], [pfx + f"qr{b}", pfx + "kd64"])
            P.dma(kr[b], T["krT"][hrows, own0:own0 + NLOC], [], [pfx + f"kr{b}", pfx + "kd64"])
            P.dma(qd[b], T["qdT"][hrows, own0:own0 + NLOC], [], [pfx + f"qd{b}", pfx + "kd64"])
            for g in range(NG):
                cols = slice(g * NT, (g + 1) * NT)
                gi = g % 2
                ocols = slice(own0 + g * NT, own0 + (g + 1) * NT)
                P.dma(gs[gi][:], T["gsil"][hrows, ocols], [], [pfx + f"gs{gi}"])
                bS, kS = ps[gi], f"ps{gi}"
                bO, kO = ps[2 + gi], f"ps{2 + gi}"
                for c in range(8):
                    n = g * 8 + c
                    tc_ = slice(n * 64, (n + 1) * 64)
                    P.mm(bS[0:64, c * 64:(c + 1) * 64], kr[b][:, tc_], qr[b][:, tc_], True, True,
                         [pfx + f"kr{b}", pfx + f"qr{b}"], [kS])
                P.tt(stb[gi][:], bS[0:64, :], dintra[:, hh, :], ALU.mult, [kS, pfx + "tab"], [pfx + f"stb{gi}"])
                for c in range(8):
                    n = g * 8 + c
                    tc_ = slice(n * 64, (n + 1) * 64)
                    cs = slice(c * 64, (c + 1) * 64)
                    P.mm(bO[0:64, cs], vr64[:, n, hs], stb[gi][:, cs], True, False,
                         [pfx + "vr64", pfx + f"stb{gi}"], [kO])
                    P.mm(bO[0:64, cs], sball[:, n, hs], qd[b][:, tc_], False, True,
                         [pfx + f"sball{n}", pfx + f"qd{b}"], [kO])
                P.act(osb[:], bO[0:64, :], AF.Copy, [kO], [pfx + "osb"])
                P.act(osq[:], bO[0:64, :], AF.Square, [kO], [pfx + "osq"])
                P.mm(ps[4][0:64, :], C.ones_f[0:64, 0:64], osb[:], True, True, [pfx + "osb", "const"], ["ps4"])
                P.mm(ps[5][0:64, :], C.ones_f[0:64, 0:64], osq[:], True, True, [pfx + "osq", "const"], ["ps5"])
                m_, q_, v_ = mean[0:64, :], msq[0:64, :], var[0:64, :]
                P.ts(m_, ps[4][0:64, :], 1.0 / 64, None, ALU.mult, None, ["ps4"], [pfx + "mean"])
                P.tt(q_, m_, m_, ALU.mult, [pfx + "mean"], [pfx + "msq"])
                P.stt(v_, ps[5][0:64, :], 1.0 / 64, q_, ALU.mult, ALU.subtract, ["ps5", pfx + "msq"], [pfx + "var"])
                P.ts(v_, v_, EPS, None, ALU.add, None, [pfx + "var"], [pfx + "var"])
                rsqrt_inplace(P, v_, pfx + "var")
                t_ = tn[gi][0:64, :]
                P.tt(t_, osb[:], m_, ALU.subtract, [pfx + "osb", pfx + "mean"], [pfx + f"tn{gi}"])
                P.tt(t_, t_, v_, ALU.mult, [pfx + f"tn{gi}", pfx + "var"], [pfx + f"tn{gi}"])
                P.stt(yb[gi][0:64, :], t_, rg[:, hh:hh + 1], gs[gi][:], ALU.mult, ALU.mult,
                      [pfx + f"tn{gi}", pfx + "tab", pfx + f"gs{gi}"], [pfx + f"yb{gi}"])
                P.dma(yT[768 + hh * 64:768 + (hh + 1) * 64, ocols], yb[gi][0:64, :], [pfx + f"yb{gi}"], [])
        for g in range(NG):
            cols = slice(g * NT, (g + 1) * NT)
            for cc in range(2):
                for k in range(31):
                    P.mm(ps[cc][:], dg[:, cc * 31 + k, :], vpad[:, cc, 2 + k + g * NT:2 + k + (g + 1) * NT],
                         k == 0, k == 30, [pfx + "dg", pfx + "vpad"], [f"ps{cc}"])
                P.ts(yc[cc][:], ps[cc][:], cb[:, cc:cc + 1], None, ALU.add, None, [f"ps{cc}", pfx + "tab"],
                     [pfx + f"yc{cc}"])
                P.act(sqy[cc][:], yc[cc][:], AF.Square, [pfx + f"yc{cc}"], [pfx + f"sqy{cc}"])
            for cc in range(2):
                P.mm(ps[4][:], C.ones_f[:], yc[cc][:], cc == 0, cc == 1, [pfx + f"yc{cc}", "const"], ["ps4"])
            for cc in range(2):
                P.mm(ps[5][:], C.ones_f[:], sqy[cc][:], cc == 0, cc == 1, [pfx + f"sqy{cc}", "const"], ["ps5"])
            P.ts(mean[:], ps[4][:], 1.0 / 256, None, ALU.mult, None, ["ps4"], [pfx + "mean"])
            P.tt(msq[:], mean[:], mean[:], ALU.mult, [pfx + "mean"], [pfx + "msq"])
            P.stt(var[:], ps[5][:], 1.0 / 256, msq[:], ALU.mult, ALU.subtract, ["ps5", pfx + "msq"], [pfx + "var"])
            P.ts(var[:], var[:], EPS, None, ALU.add, None, [pfx + "var"], [pfx + "var"])
            rsqrt_inplace(P, var[:], pfx + "var")
            for cc in range(2):
                P.tt(tn[cc][:], yc[cc][:], mean[:], ALU.subtract, [pfx + f"yc{cc}", pfx + "mean"],
                     [pfx + f"tn{cc}"])
                P.tt(tn[cc][:], tn[cc][:], var[:], ALU.mult, [pfx + f"tn{cc}", pfx + "var"], [pfx + f"tn{cc}"])
                P.act(yb[cc][:], tn[cc][:], AF.Silu, [pfx + f"tn{cc}", pfx + "tab"], [pfx + f"yb{cc}"],
                      bias=lb[:, cc:cc + 1], scale=lg[:, cc:cc + 1])
                P.dma(yT[cc * 128:(cc + 1) * 128, own0 + g * NT:own0 + (g + 1) * NT], yb[cc][:],
                      [pfx + f"yb{cc}"], [])
        P.flush()


def sb_phase(P, C, tag, T, own0, has_prev):
    nc = P.nc
    pfx = tag + "_"
    ps = C.ps
    yT = T["yT"]
    NB = 64 if has_prev else 32
    LB0 = NB - 32
    base = own0 - (NLOC if has_prev else 0)
    NK = NB * 128
    QG = 256
    NQG = NLOC // QG
    with ExitStack() as es:
        sb = lambda n, shp, dt: es.enter_context(nc.sbuf_tensor(pfx + n, shp, dt))
        VA = sb("VA", [128, NB, 512], BF16)
        KT = [sb(f"KT{i}", [128, NK], BF16) for i in range(2)]
        QB = [sb(f"QB{i}", [128, NQG, 2 * QG], BF16) for i in range(2)]
        msk = sb("msk", [128, 2 * NT], BF16)
        ebuf = [sb(f"e{i}", [128, 2 * NT], F32) for i in range(2)]
        spb = [sb(f"sp{i}", [128, 2 * NT], BF16) for i in range(3)]
        wb = [sb(f"w{i}", [128, 2 * NT], BF16) for i in range(3)]
        sA = [sb(f"sA{i}", [128, NT], BF16) for i in range(3)]
        sB = [sb(f"sB{i}", [128, NT], BF16) for i in range(3)]
        ob = [sb(f"ob{i}", [128, QG], BF16) for i in range(2)]

        P.dma(msk[:], C.sbmask2, [], [pfx + "tab"])
        for b in range(2):
            P.op("dve", lambda e, b=b: e.memset(QB[b][0:64, :, QG:2 * QG], 0.0), [], [pfx + f"QB{b}"])
            P.op("dve", lambda e, b=b: e.memset(QB[b][64:128, :, 0:QG], 0.0), [], [pfx + f"QB{b}"])
        for b0 in range(0, NB, 32):
            P.dma(VA[:, b0:b0 + 32, :],
                  T["vtok"][base + b0 * 128:base + (b0 + 32) * 128, :].rearrange("(b p) c -> p b c", p=128),
                  [], [pfx + "VA"])

        def load_pair(hp):
            b = hp % 2
            r0 = hp * 128
            P.dma(KT[b][:], T["kT"][r0:r0 + 128, base:base + NK], [], [pfx + f"KT{b}"])
            P.dma(QB[b][0:64, :, 0:QG],
                  T["qsT"][r0:r0 + 64, own0:own0 + NLOC].rearrange("p (g q) -> p g q", q=QG), [], [pfx + f"QB{b}"])
            P.dma(QB[b][64:128, :, QG:2 * QG],
                  T["qsT"][r0 + 64:r0 + 128, own0:own0 + NLOC].rearrange("p (g q) -> p g q", q=QG),
                  [], [pfx + f"QB{b}"])

        items = []
        for hp in range(4):
            for g in range(NQG):
                blks = [LB0 + 2 * g + 1 - i for i in range(2 * g + 2)] + [LB0 - 1 - i for i in range(LB0)]
                npair = len(blks) // 2
                for i in range(npair):
                    items.append((hp, g, blks[2 * i], blks[2 * i + 1], i == 0, i == npair - 1))
        big = C.psbig
        H0, H1 = slice(0, NT), slice(NT, 2 * NT)

        def stageA(idx):
            hp, g, kb0, kb1, first, last = items[idx]
            b = hp % 2
            if g == 0 and first:
                if hp == 0:
                    load_pair(0)
                if hp + 1 < 4:
                    load_pair(hp + 1)
            i2, i3 = idx % 2, idx % 3
            q = QB[b][:, g, :]
            bA, kA = big[i2], f"psb{i2}"
            for kb, hsl in ((kb0, H0), (kb1, H1)):
                P.mm(bA[:, hsl], KT[b][:, kb * 128:(kb + 1) * 128], q, True, True,
                     [pfx + f"KT{b}", pfx + f"QB{b}"], [kA])
            P.act(ebuf[i2][:], bA[:], AF.Exp, [kA], [pfx + f"e{i2}"])
            P.act(spb[i3][:], ebuf[i2][:], AF.Ln, [pfx + f"e{i2}"], [pfx + f"sp{i3}"], bias=1.0)
            if first:
                P.tt(spb[i3][:], spb[i3][:], msk[:], ALU.mult, [pfx + f"sp{i3}", pfx + "tab"], [pfx + f"sp{i3}"])
                P.op("dve", lambda e: e.tensor_copy(out=sA[i3][:], in_=spb[i3][:, H0]),
                     [pfx + f"sp{i3}"], [pfx + f"sA{i3}"])
            else:
                p3 = (idx - 1) % 3
                P.tt(sA[i3][:], sB[p3][:], spb[i3][:, H0], ALU.add, [pfx + f"sB{p3}", pfx + f"sp{i3}"],
                     [pfx + f"sA{i3}"])
            if not last:
                P.tt(sB[i3][:], sA[i3][:], spb[i3][:, H1], ALU.add, [pfx + f"sA{i3}", pfx + f"sp{i3}"],
                     [pfx + f"sB{i3}"])

        def stageB(idx):
            hp, g, kb0, kb1, first, last = items[idx]
            b = hp % 2
            i3 = idx % 3
            q = QB[b][:, g, :]
            bB, kB = big[2], "psb2"
            for kb, hsl, prev in ((kb0, H0, None if first else sB[(idx - 1) % 3]), (kb1, H1, sA[i3])):
                P.mm(bB[:, hsl], C.trineg[:], spb[i3][:, hsl], True, False, [pfx + f"sp{i3}", "const"], [kB])
                if prev is not None:
                    pk = pfx + (f"sA{i3}" if prev is sA[i3] else f"sB{(idx - 1) % 3}")
                    P.mm(bB[:, hsl], C.negones[:], prev[:], False, False, [pk, "const"], [kB])
                P.mm(bB[:, hsl], KT[b][:, kb * 128:(kb + 1) * 128], q, False, True,
                     [pfx + f"KT{b}", pfx + f"QB{b}"], [kB])
            P.act(wb[i3][:], bB[:], AF.Exp, [kB], [pfx + f"w{i3}"])
            if first:
                P.tt(wb[i3][:], wb[i3][:], msk[:], ALU.mult, [pfx + f"w{i3}", pfx + "tab"], [pfx + f"w{i3}"])

        def stageC(idx):
            hp, g, kb0, kb1, first, last = items[idx]
            i3 = idx % 3
            gi = (hp * NQG + g) % 2
            bO, kO = ps[6 + gi], f"psO{gi}"
            P.mm(bO[:], VA[:, kb0, hp * 128:(hp + 1) * 128], wb[i3][:, H0], first, False,
                 [pfx + "VA", pfx + f"w{i3}"], [kO])
            P.mm(bO[:], VA[:, kb1, hp * 128:(hp + 1) * 128], wb[i3][:, H1], False, last,
                 [pfx + "VA", pfx + f"w{i3}"], [kO])
            if last:
                P.op("dve", lambda e: e.tensor_copy(out=ob[gi][0:64, :], in_=bO[0:64, 0:QG]), [kO], [pfx + f"ob{gi}"])
                P.op("dve", lambda e: e.tensor_copy(out=ob[gi][64:128, :], in_=bO[64:128, QG:2 * QG]), [kO],
                     [pfx + f"ob{gi}"])
                P.dma(yT[256 + hp * 128:256 + (hp + 1) * 128, own0 + g * QG:own0 + (g + 1) * QG], ob[gi][:],
                      [pfx + f"ob{gi}"], [])

        n = len(items)
        for i in range(n + 2):
            if i < n:
                stageA(i)
            if 1 <= i <= n:
                stageB(i - 1)
            if i >= 2:
                stageC(i - 2)
        P.flush()


def outproj_phase(P, C, tag, w_o, T, xsrc, xdst, groups):
    nc = P.nc
    pfx = tag + "_"
    ps = C.ps
    xs = xsrc.rearrange("(c p) t -> p c t", p=128)
    xd = xdst.rearrange("(c p) t -> p c t", p=128)
    yr = T["yT"].rearrange("(c p) t -> p c t", p=128)
    with ExitStack() as es:
        sb = lambda n, shp, dt: es.enter_context(nc.sbuf_tensor(pfx + n, shp, dt))
        wo = sb("wo", [128, 8, D], BF16)
        yin = [sb(f"yin{i}", [128, 8, NT], BF16) for i in range(2)]
        xin = [sb(f"xin{i}", [128, 8, NT], F32) for i in range(2)]
        xo = [sb(f"xo{i}", [128, 8, NT], F32) for i in range(2)]
        P.dma(wo[:], w_o.rearrange("(c p) m -> p c m", p=128), [], [pfx + "wo"], q="pool")
        for g in groups:
            cols = slice(g * NT, (g + 1) * NT)
            b = g % 2
            P.dma(yin[b][:], yr[:, :, cols], [], [pfx + f"yin{b}"])
            P.dma(xin[b][:], xs[:, :, cols], [], [pfx + f"xin{b}"])
            for m in range(8):
                bO, kO = ps[m % 4], f"ps{m % 4}"
                for c in range(8):
                    P.mm(bO[:], wo[:, c, m * 128:(m + 1) * 128], yin[b][:, c, :], c == 0, c == 7,
                         [pfx + "wo", pfx + f"yin{b}"], [kO])
                P.tt(xo[b][:, m, :], bO[:], xin[b][:, m, :], ALU.add, [kO, pfx + f"xin{b}"], [pfx + f"xo{b}"])
            P.dma(xd[:, :, cols], xo[b][:], [pfx + f"xo{b}"], [])
        P.flush()


def final_phase(P, C, tag, gam_d, xsrc, out_d):
    nc = P.nc
    pfx = tag + "_"
    xs = xsrc.rearrange("(c p) t -> p c t", p=128)
    od = out_d.rearrange("(c p) t -> p c t", p=128)
    with ExitStack() as es:
        sb = lambda n, shp, dt: es.enter_context(nc.sbuf_tensor(pfx + n, shp, dt))
        gam = sb("gam", [128, 8], F32)
        xin = [sb(f"xin{i}", [128, 8, NT], F32) for i in range(2)]
        sqc = [sb(f"sqc{i}", [128, NT], BF16) for i in range(2)]
        rstd = sb("rstd", [128, NT], F32)
        yo = [sb(f"yo{i}", [128, 8, NT], F32) for i in range(2)]
        P.dma(gam[:], gam_d, [], [pfx + "gam"])
        for g in range(NG):
            cols = slice(NLOC + g * NT, NLOC + (g + 1) * NT)
            ocols = slice(g * NT, (g + 1) * NT)
            b = g % 2
            P.dma(xin[b][:], xs[:, :, cols], [], [pfx + f"xin{b}"])
            for c in range(8):
                P.act(sqc[c % 2][:], xin[b][:, c, :], AF.Square, [pfx + f"xin{b}"], [pfx + f"sqc{c % 2}"])
                P.mm(C.ps[6][:], C.ones_bf[:], sqc[c % 2][:], c == 0, c == 7, [pfx + f"sqc{c % 2}", "const"], ["ps6"])
            P.ts(rstd[:], C.ps[6][:], 1.0 / D, EPS, ALU.mult, ALU.add, ["ps6"], [pfx + "rstd"])
            rsqrt_inplace(P, rstd[:], pfx + "rstd")
            for c in range(8):
                P.stt(yo[b][:, c, :], xin[b][:, c, :], gam[:, c:c + 1], rstd[:], ALU.mult, ALU.mult,
                      [pfx + f"xin{b}", pfx + "rstd", pfx + "gam"], [pfx + f"yo{b}"])
            P.dma(od[:, :, ocols], yo[b][:], [pfx + f"yo{b}"], [], is_out=True)


MIXT = {
    "vconv": ([256, NSLOT], BF16), "qsT": ([512, NSLOT], BF16), "kT": ([512, NSLOT], BF16),
    "vtok": ([NSLOT, 512], BF16), "vr": ([NSLOT, 256], BF16), "qrT": ([256, NSLOT], BF16),
    "qdT": ([256, NSLOT], BF16), "krT": ([256, NSLOT], BF16), "kdtok": ([NSLOT, 256], BF16),
    "gsil": ([256, NSLOT], F32), "yT": ([D, NSLOT], BF16),
}
CONSTS = {"ones_bf": ([128, 128], BF16), "ones_f": ([128, 128], F32), "ident": ([128, 128], BF16),
          "trineg": ([128, 128], BF16), "negones": ([128, 128], BF16)}
TABLES = {"rope_cos": [128, NSLOT], "rope_sin": [128, NSLOT], "decq": [128, 2, NT], "deck": [128, 256],
          "g64": [64, 256], "dintra": [64, 4, NT], "flag": [128, 1]}
ALLG = list(range(2 * NG))
OWNG = list(range(NG, 2 * NG))


def build_program():
    nc = bass.Bass("TRN2", target_bir_lowering=False)
    dt = lambda name, shape, dtype, kind: nc.dram_tensor(name, shape, dtype, kind=kind).ap()
    C = Ctx()
    W = {}
    for l in range(DEPTH):
        for f in ("1", "2"):
            W[f"ffn{f}_w_in{l}"] = dt(f"ffn{f}_w_in{l}", [D, 2 * DFF], F32, "ExternalInput")
            W[f"ffn{f}_w_out{l}"] = dt(f"ffn{f}_w_out{l}", [DFF, D], F32, "ExternalInput")
            W[f"ffn{f}_norm{l}"] = dt(f"ffn{f}_norm{l}", [128, 8], F32, "ExternalInput")
        W[f"mix_w_in{l}"] = dt(f"mix_w_in{l}", [D, 3072], F32, "ExternalInput")
        W[f"mix_norm{l}"] = dt(f"mix_norm{l}", [128, 8], F32, "ExternalInput")
        W[f"mix_w_out{l}"] = dt(f"mix_w_out{l}", [D, D], F32, "ExternalInput")
        W[f"conv_w{l}"] = dt(f"conv_w{l}", [128, 2, 31], F32, "ExternalInput")
        for nm in ("conv_b", "conv_ln_g", "conv_ln_b"):
            W[f"{nm}{l}"] = dt(f"{nm}{l}", [128, 2], F32, "ExternalInput")
        W[f"ret_norm_g{l}"] = dt(f"ret_norm_g{l}", [64, 4], F32, "ExternalInput")
    W["final_norm"] = dt("final_norm", [128, 8], F32, "ExternalInput")
    cd = {k: dt("c_" + k, s, d, "ExternalInput") for k, (s, d) in CONSTS.items()}
    for k, s in TABLES.items():
        setattr(C, k, dt("t_" + k, s, F32, "ExternalInput"))
    C.sbmask2 = dt("t_sbmask2", [128, 2 * NT], BF16, "ExternalInput")
    x_in = dt("x_in", [D, NSLOT], F32, "ExternalInput")
    xa = dt("xa", [D, NSLOT], F32, "Internal")
    xb = dt("xb", [D, NSLOT], F32, "Internal")
    T = {k: dt("m_" + k, s, d, "Internal") for k, (s, d) in MIXT.items()}
    out = dt("out", [D, NLOC], F32, "ExternalOutput")

    with ExitStack() as es:
        P = Prog(nc, es)
        C.psbig = [es.enter_context(nc.psum_tensor(f"psb{i}", [128, 2 * NT], F32)) for i in range(4)]
        C.ps = [C.psbig[i // 2][:, (i % 2) * NT:(i % 2 + 1) * NT] for i in range(8)]
        C.pst = C.psbig[3].bitcast(BF16)[:, 2 * NT:4 * NT]
        for k, (s, d) in CONSTS.items():
            t = es.enter_context(nc.sbuf_tensor("k_" + k, s, d))
            setattr(C, k, t)
            P.dma(t[:], cd[k], [], ["const"])
        P.flush()
        mixw = lambda l: (W[f"conv_w{l}"], W[f"conv_b{l}"], W[f"conv_ln_g{l}"], W[f"conv_ln_b{l}"],
                          W[f"ret_norm_g{l}"])
        ffn_phase(P, C, "f1a", W["ffn1_w_in0"], W["ffn1_w_out0"], W["ffn1_norm0"], x_in, xa, ALLG)
        inproj_phase(P, C, "ipa", W["mix_w_in0"], W["mix_norm0"], xa, T, ALLG)
        retconv_phase(P, C, "rc0a", T, *mixw(0), own0=0, has_prev=False)
        sb_phase(P, C, "sb0a", T, own0=0, has_prev=False)
        retconv_phase(P, C, "rc0b", T, *mixw(0), own0=NLOC, has_prev=True)
        sb_phase(P, C, "sb0b", T, own0=NLOC, has_prev=True)
        outproj_phase(P, C, "opa", W["mix_w_out0"], T, xa, xb, ALLG)
        ffn_phase(P, C, "f2a", W["ffn2_w_in0"], W["ffn2_w_out0"], W["ffn2_norm0"], xb, xa, ALLG, flag_prefix=True)
        ffn_phase(P, C, "f1b", W["ffn1_w_in1"], W["ffn1_w_out1"], W["ffn1_norm1"], xa, xb, ALLG)
        inproj_phase(P, C, "ipb", W["mix_w_in1"], W["mix_norm1"], xb, T, ALLG, kv_only=tuple(range(NG)))
        retconv_phase(P, C, "rc1", T, *mixw(1), own0=NLOC, has_prev=True)
        sb_phase(P, C, "sb1", T, own0=NLOC, has_prev=True)
        outproj_phase(P, C, "opb", W["mix_w_out1"], T, xb, xa, OWNG)
        ffn_phase(P, C, "f2b", W["ffn2_w_in1"], W["ffn2_w_out1"], W["ffn2_norm1"], xa, xb, OWNG)
        final_phase(P, C, "fin", W["final_norm"], xb, out)
        P.flush(final=True)
    return nc


def _consts():
    bf = ml_dtypes.bfloat16
    j = np.arange(128)
    c = {
        "c_ones_bf": np.ones((128, 128), bf), "c_ones_f": np.ones((128, 128), np.float32),
        "c_ident": np.eye(128).astype(bf),
        "c_trineg": (-(j[:, None] >= j[None, :]).astype(np.float32)).astype(bf),
        "c_negones": (-np.ones((128, 128), np.float32)).astype(bf),
    }
    gam = 1.0 - np.exp2(-5.0 - np.arange(4, dtype=np.float64))
    i512 = np.arange(NT)
    p = np.arange(128)
    decq = np.zeros((128, 2, NT))
    for cc in range(2):
        hh = 2 * cc + p // 64
        decq[:, cc, :] = 0.125 * gam[hh][:, None] ** ((i512 % 64) + 1.0)[None, :]
    col = np.arange(256)
    deck = gam[col // 64][None, :] ** (63.0 - (p % 64))[:, None]
    g64 = np.broadcast_to((gam[col // 64] ** 64.0)[None, :], (64, 256))
    jj = np.arange(64)
    dintra = np.zeros((64, 4, NT))
    for hh in range(4):
        dintra[:, hh, :] = gam[hh] ** np.abs(jj[:, None] - (i512 % 64)[None, :])
    sbmask = np.zeros((128, 2 * NT), np.float32)
    for slot, cc in enumerate((1, 0)):
        sbmask[:, slot * NT:(slot + 1) * NT] = ((cc * 128 + p)[:, None] < (i512 % 256)[None, :])
    c.update({"t_decq": decq.astype(np.float32), "t_deck": deck.astype(np.float32),
              "t_g64": np.ascontiguousarray(g64).astype(np.float32), "t_dintra": dintra.astype(np.float32),
              "t_sbmask2": sbmask.astype(bf)})
    return c


def _rope(half):
    slot = np.arange(NSLOT)
    pos = (slot if half == 1 else slot % NLOC).astype(np.float32)
    inv = (1.0 / (10000.0 ** (np.arange(32, dtype=np.float32) / 32))).astype(np.float32)
    p = np.arange(128)
    ang = pos[None, :] * inv[(p % 64) % 32][:, None]
    sign = np.where((p % 64) < 32, -1.0, 1.0)[:, None]
    return np.cos(ang).astype(np.float32), (sign * np.sin(ang)).astype(np.float32)


def _vec8(v):
    return np.ascontiguousarray(v.reshape(8, 128).T)


def _vec2(v):
    return np.ascontiguousarray(v.reshape(2, 128).T)


def kernel(**inp):
    ncores = 8
    x = inp["x"]
    w = dict(_consts())
    for l in range(DEPTH):
        for f in ("1", "2"):
            w[f"ffn{f}_w_in{l}"] = np.ascontiguousarray(inp[f"ffn{f}_w_in"][l])
            w[f"ffn{f}_w_out{l}"] = np.ascontiguousarray(inp[f"ffn{f}_w_out"][l])
            w[f"ffn{f}_norm{l}"] = _vec8(inp[f"ffn{f}_norm"][l])
        w[f"mix_w_in{l}"] = np.ascontiguousarray(inp["mix_w_in"][l])
        w[f"mix_norm{l}"] = _vec8(inp["mix_norm"][l])
        w[f"mix_w_out{l}"] = np.ascontiguousarray(inp["mix_w_out"][l])
        w[f"conv_w{l}"] = np.ascontiguousarray(inp["conv_w"][l].T.reshape(2, 128, 31).transpose(1, 0, 2))
        for nm in ("conv_b", "conv_ln_g", "conv_ln_b"):
            w[f"{nm}{l}"] = _vec2(inp[nm][l])
        w[f"ret_norm_g{l}"] = np.ascontiguousarray(inp["ret_norm_g"][l].reshape(4, 64).T)
    w["final_norm"] = _vec8(inp["final_norm"])
    ropes = [_rope(0), _rope(1)]
    maps = []
    for core in range(ncores):
        b, half = core // 2, core % 2
        m = dict(w)
        m["t_rope_cos"], m["t_rope_sin"] = ropes[half]
        m["t_flag"] = np.full((128, 1), float(half), np.float32)
        xi = np.zeros((D, NSLOT), np.float32)
        if half == 1:
            xi[:, :] = x[b].T
        else:
            xi[:, NLOC:] = x[b, :NLOC, :].T
        m["x_in"] = xi
        maps.append(m)
    res = run_bass_kernel_spmd(build_program(), maps, core_ids=list(range(ncores))).results
    out = np.empty(x.shape, np.float32)
    for c in range(ncores):
        out[c // 2, (c % 2) * NLOC:(c % 2 + 1) * NLOC, :] = res[c]["out"].T
    return out
```

```python
import numpy as np
import ml_dtypes
from contextlib import ExitStack
import concourse.bass as bass
import concourse.mybir as mybir
from concourse.bass_utils import run_bass_kernel_spmd

F32 = mybir.dt.float32
BF16 = mybir.dt.bfloat16
AF = mybir.ActivationFunctionType
ALU = mybir.AluOpType

D = 1024
DFF = 2816
NLOC = 4096
NSLOT = 8192
NT = 512
NG = NLOC // NT
DEPTH = 2
EPS = 1e-6
NDS = 8


class Op:
    __slots__ = ("stream", "fn", "deps", "dma", "token", "needed", "phase", "wkeys")


class Prog:
    STREAMS = ["pe", "act", "dve", "pool", "sp"]

    def __init__(self, nc, es):
        self.nc = nc
        self.csem = {s: es.enter_context(nc.semaphore("c_" + s)) for s in ["pe", "act", "dve", "pool"]}
        self.dsem = {s: [es.enter_context(nc.semaphore(f"d_{s}{i}")) for i in range(NDS)]
                     for s in ["sp", "pool"]}
        self.ccnt = {s: 0 for s in self.csem}
        self.dcnt = {s: [0] * NDS for s in self.dsem}
        self.dnum = {s: 0 for s in self.dsem}
        self.lastw = {}
        self.readers = {}
        self.ops = []
        self.phase = 0
        self.waited = {s: {} for s in self.STREAMS}
        self.pending = {s: {} for s in self.STREAMS}
        self.out_tokens = []

    def capture(self, f):
        self._cap = []
        f()
        cap, self._cap = self._cap, None
        return cap

    def interleave(self, *caps):
        n = max(len(c) for c in caps)
        for i in range(n):
            for c in caps:
                if i < len(c):
                    self.op(*c[i])

    def op(self, stream, fn, reads=(), writes=(), dma=False, is_out=False):
        if getattr(self, "_cap", None) is not None:
            self._cap.append((stream, fn, tuple(reads), tuple(writes), dma, is_out))
            return None
        o = Op()
        o.stream, o.fn, o.dma, o.needed, o.phase, o.token = stream, fn, dma, False, self.phase, None
        o.wkeys = set(writes)
        deps = {}
        for k in reads:
            w = self.lastw.get(k)
            if w is not None:
                deps[id(w)] = (w, True)
        for k in writes:
            w = self.lastw.get(k)
            if w is not None and id(w) not in deps:
                deps[id(w)] = (w, False)
            for r in self.readers.get(k, {}).values():
                if id(r) not in deps:
                    deps[id(r)] = (r, False)
        o.deps = []
        for d, raw in deps.values():
            if d.phase != self.phase or d is o:
                continue
            if d.stream == stream and not d.dma and not dma:
                if stream == "pe" or not raw:
                    continue
            o.deps.append(d)
        if dma:
            i = self.dnum[stream] % NDS
            self.dnum[stream] += 1
            self.dcnt[stream][i] += 16
            o.token = (self.dsem[stream][i], self.dcnt[stream][i])
            rkey = (stream, i)
            if is_out:
                self.out_tokens.append(o.token)
        else:
            rkey = (stream, -1)
        for k in reads:
            self.readers.setdefault(k, {})[rkey] = o
        for k in writes:
            self.lastw[k] = o
            self.readers[k] = {}
        self.ops.append(o)
        return o

    def flush(self, final=False):
        ops = self.ops
        for o in ops:
            for d in o.deps:
                d.needed = True
        last = {}
        for o in ops:
            if not o.dma:
                last[o.stream] = o
        for o in last.values():
            o.needed = True
        for o in ops:
            if not o.dma and o.needed:
                self.ccnt[o.stream] += 1
                o.token = (self.csem[o.stream], self.ccnt[o.stream])
        by_stream = {s: [o for o in ops if o.stream == s] for s in self.STREAMS}
        snapshot = {}
        for s in self.csem:
            if self.ccnt[s]:
                snapshot[id(self.csem[s])] = (self.csem[s], self.ccnt[s])
        for s in self.dsem:
            for i in range(NDS):
                if self.dcnt[s][i]:
                    snapshot[id(self.dsem[s][i])] = (self.dsem[s][i], self.dcnt[s][i])

        def emit(stream, eng):
            waited = self.waited[stream]
            for o in by_stream[stream]:
                w = dict(self.pending[stream])
                self.pending[stream] = {}
                for d in o.deps:
                    sem, val = d.token
                    if id(sem) not in w or w[id(sem)][1] < val:
                        w[id(sem)] = (sem, val)
                for sid, (sem, val) in w.items():
                    if waited.get(sid, 0) < val:
                        eng.wait_ge(sem, val)
                        waited[sid] = val
                ins = o.fn(eng)
                if o.token is not None:
                    ins.then_inc(o.token[0], 16 if o.dma else 1)
            if final and stream == "sp":
                for sid, (sem, val) in snapshot.items():
                    if waited.get(sid, 0) < val:
                        eng.wait_ge(sem, val)
                        waited[sid] = val

        with self.nc.Block() as blk:
            if by_stream["pe"]:
                blk.tensor(lambda e: emit("pe", e))
            if by_stream["act"]:
                blk.scalar(lambda e: emit("act", e))
            if by_stream["dve"]:
                blk.vector(lambda e: emit("dve", e))
            if by_stream["pool"]:
                blk.gpsimd(lambda e: emit("pool", e))
            if by_stream["sp"] or final:
                blk.sync(lambda e: emit("sp", e))
        for s in self.STREAMS:
            p = self.pending[s]
            for sid, (sem, val) in snapshot.items():
                if sid not in p or p[sid][1] < val:
                    p[sid] = (sem, val)
        self.ops = []
        self.phase += 1

    def dma(self, out, in_, reads, writes, q="sp", is_out=False):
        return self.op(q, lambda e: e.dma_start(out=out, in_=in_), reads, writes, dma=True, is_out=is_out)

    def mm(self, out, lhsT, rhs, start, stop, reads, writes):
        return self.op("pe", lambda e: e.matmul(out, lhsT, rhs, start=start, stop=stop), reads, writes)

    def act(self, out, in_, func, reads, writes, bias=None, scale=None):
        kw = {}
        if bias is not None:
            kw["bias"] = bias
        if scale is not None:
            kw["scale"] = scale
        return self.op("act", lambda e: e.activation(out=out, in_=in_, func=func, **kw), reads, writes)

    def tt(self, out, in0, in1, op, reads, writes, eng="dve"):
        return self.op(eng, lambda e: e.tensor_tensor(out=out, in0=in0, in1=in1, op=op), reads, writes)

    def ts(self, out, in0, s1, s2, op0, op1, reads, writes, eng="dve"):
        if op1 is None:
            return self.op(eng, lambda e: e.tensor_scalar(out=out, in0=in0, scalar1=s1, scalar2=None, op0=op0),
                           reads, writes)
        return self.op(eng, lambda e: e.tensor_scalar(out=out, in0=in0, scalar1=s1, scalar2=s2, op0=op0, op1=op1),
                       reads, writes)

    def stt(self, out, in0, scalar, in1, op0, op1, reads, writes, eng="dve"):
        return self.op(eng, lambda e: e.scalar_tensor_tensor(out=out, in0=in0, scalar=scalar, in1=in1,
                                                             op0=op0, op1=op1), reads, writes)


class Ctx:
    pass


def rsqrt_inplace(P, t, key):
    P.act(t, t, AF.Sqrt, [key], [key])
    P.op("dve", lambda e: e.reciprocal(out=t, in_=t), [key], [key])


def rms_norm_group(P, C, pfx, xin, gam, h, sqc, rstd, g, xk="xin", hk="h"):
    ps = C.ps
    for c in range(8):
        P.act(sqc[c % 2][:], xin[:, c, :], AF.Square, [pfx + xk], [pfx + f"sqc{c % 2}"])
        P.mm(ps[6][:], C.ones_bf[:], sqc[c % 2][:], c == 0, c == 7, [pfx + f"sqc{c % 2}", "const"], ["ps6"])
    P.ts(rstd[:], ps[6][:], 1.0 / D, EPS, ALU.mult, ALU.add, ["ps6"], [pfx + "rstd"])
    rsqrt_inplace(P, rstd[:], pfx + "rstd")
    for c in range(8):
        P.stt(h[:, c, :], xin[:, c, :], gam[:, c:c + 1], rstd[:], ALU.mult, ALU.mult,
              [pfx + xk, pfx + "rstd", pfx + "gam"], [pfx + hk])


def ffn_phase(P, C, tag, w_in, w_out, gam_d, xsrc, xdst, groups, flag_prefix=False):
    nc = P.nc
    pfx = tag + "_"
    ps = C.ps
    xs = xsrc.rearrange("(c p) t -> p c t", p=128)
    xd = xdst.rearrange("(c p) t -> p c t", p=128)
    with ExitStack() as es:
        sb = lambda n, shp, dt: es.enter_context(nc.sbuf_tensor(pfx + n, shp, dt))
        win = sb("win", [128, 8, 2 * DFF], BF16)
        wout = sb("wout", [128, 22, D], BF16)
        gam = sb("gam", [128, 8], F32)
        xin = sb("xin", [128, 8, NT], F32)
        h = sb("h", [128, 8, NT], BF16)
        sqc = [sb(f"sqc{i}", [128, NT], BF16) for i in range(2)]
        rstd = sb("rstd", [128, NT], F32)
        a = sb("a", [128, 22, NT], BF16)
        sg = [sb(f"sg{i}", [128, NT], F32) for i in range(2)]
        xr = [sb(f"xr{i}", [128, NT], F32) for i in range(2)]
        xo = [sb(f"xo{i}", [128, NT], F32) for i in range(2)]
        flg = sb("flg", [128, 1], F32)

        P.dma(gam[:], gam_d, [], [pfx + "gam"])
        P.dma(flg[:], C.flag, [], [pfx + "flg"])
        g0 = groups[0]
        P.dma(xin[:], xs[:, :, g0 * NT:(g0 + 1) * NT], [], [pfx + "xin"])
        for c in range(8):
            P.dma(win[:, c, :], w_in[c * 128:(c + 1) * 128, :], [], [pfx + f"win{c}"], q="pool")
        wo_r = w_out.rearrange("(j p) m -> p j m", p=128)
        for j0 in range(0, 22, 6):
            j1 = min(22, j0 + 6)
            P.dma(wout[:, j0:j1, :], wo_r[:, j0:j1, :], [], [pfx + f"wout{jj}" for jj in range(j0, j1)], q="pool")
        WIN = [pfx + f"win{c}" for c in range(8)]
        WOUT = [pfx + f"wout{j}" for j in range(22)]

        rms_norm_group(P, C, pfx, xin, gam, h, sqc, rstd, 0)
        for gi_, g in enumerate(groups):
            cols = slice(g * NT, (g + 1) * NT)
            gn = groups[gi_ + 1] if gi_ + 1 < len(groups) else None
            if gn is not None:
                P.dma(xin[:], xs[:, :, gn * NT:(gn + 1) * NT], [], [pfx + "xin"])
            for j in range(22):
                bG, bU = ps[(j % 2) * 2], ps[(j % 2) * 2 + 1]
                kG, kU = f"ps{(j % 2) * 2}", f"ps{(j % 2) * 2 + 1}"
                for c in range(8):
                    P.mm(bG[:], win[:, c, j * 128:(j + 1) * 128], h[:, c, :], c == 0, c == 7,
                         [pfx + "h", WIN[c]], [kG])
                for c in range(8):
                    P.mm(bU[:], win[:, c, DFF + j * 128:DFF + (j + 1) * 128], h[:, c, :], c == 0, c == 7,
                         [pfx + "h", WIN[c]], [kU])
                P.act(sg[j % 2][:], bG[:], AF.Silu, [kG], [pfx + f"sg{j % 2}"])
                P.tt(a[:, j, :], sg[j % 2][:], bU[:], ALU.mult, [pfx + f"sg{j % 2}", kU], [pfx + f"a{j}"])
            if gn is not None:
                rms_norm_group(P, C, pfx, xin, gam, h, sqc, rstd, gn)
            for m in range(8):
                P.dma(xr[m % 2][:], xs[:, m, cols], [], [pfx + f"xr{m % 2}"])
                bO, kO = ps[4 + m % 2], f"ps{4 + m % 2}"
                for j in range(22):
                    P.mm(bO[:], wout[:, j, m * 128:(m + 1) * 128], a[:, j, :], j == 0, j == 21,
                         [pfx + f"a{j}", WOUT[j]], [kO])
                P.stt(xo[m % 2][:], bO[:], 0.5, xr[m % 2][:], ALU.mult, ALU.add,
                      [kO, pfx + f"xr{m % 2}"], [pfx + f"xo{m % 2}"])
                if flag_prefix and g < NG:
                    P.ts(xo[m % 2][:], xo[m % 2][:], flg[:, 0:1], None, ALU.mult, None,
                         [pfx + f"xo{m % 2}", pfx + "flg"], [pfx + f"xo{m % 2}"])
                P.dma(xd[:, m, cols], xo[m % 2][:], [pfx + f"xo{m % 2}"], [])
        P.flush()


def inproj_phase(P, C, tag, w_mix, gam_d, xsrc, T, groups, kv_only=()):
    nc = P.nc
    pfx = tag + "_"
    ps = C.ps
    xs = xsrc.rearrange("(c p) t -> p c t", p=128)
    with ExitStack() as es:
        sb = lambda n, shp, dt: es.enter_context(nc.sbuf_tensor(pfx + n, shp, dt))
        wm = sb("wm", [128, 8, 3072], BF16)
        wsw = sb("wsw", [128, 8, 512], BF16)
        gam = sb("gam", [128, 8], F32)
        xin2 = [sb(f"xin{i}", [128, 8, NT], F32) for i in range(2)]
        h2 = [sb(f"h{i}", [128, 8, NT], BF16) for i in range(2)]
        sqc = [sb(f"sqc{i}", [128, NT], BF16) for i in range(2)]
        rstd = sb("rstd", [128, NT], F32)
        cosb = [sb(f"cosb{i}", [128, NT], F32) for i in range(2)]
        sinb = [sb(f"sinb{i}", [128, NT], F32) for i in range(2)]
        decq = sb("decq", [128, 2, NT], F32)
        deck = sb("deck", [128, 256], F32)
        t1 = [sb(f"t1{i}", [128, NT], F32) for i in range(2)]
        t2 = [sb(f"t2{i}", [128, NT], F32) for i in range(2)]
        ob = [sb(f"ob{i}", [128, NT], BF16) for i in range(4)]
        of = [sb(f"of{i}", [128, NT], F32) for i in range(2)]
        krb = [sb(f"krb{i}", [128, NT], BF16) for i in range(2)]
        kdb = [sb(f"kdb{i}", [128, 256], BF16) for i in range(2)]

        P.dma(gam[:], gam_d, [], [pfx + "gam"])
        for i_ in range(min(2, len(groups))):
            P.dma(xin2[i_][:], xs[:, :, groups[i_] * NT:(groups[i_] + 1) * NT], [], [pfx + f"xin{i_}"])
        P.dma(decq[:], C.decq, [], [pfx + "tab"])
        P.dma(deck[:], C.deck, [], [pfx + "tab"])
        for c in range(8):
            P.dma(wm[:, c, :], w_mix[c * 128:(c + 1) * 128, :], [], [pfx + "wm"], q="pool")
            src = w_mix[c * 128:(c + 1) * 128, 2048:2560].rearrange("p (h two d) -> p h two d", two=2, d=32)
            dst = wsw[:, c, :].rearrange("p (h two d) -> p h two d", two=2, d=32)
            P.dma(dst[:, :, 0, :], src[:, :, 1, :], [], [pfx + "wm"], q="pool")
            P.dma(dst[:, :, 1, :], src[:, :, 0, :], [], [pfx + "wm"], q="pool")

        cnt = {"b": 0, "ob": 0, "of": 0, "t": 0, "kr": 0, "kd": 0}

        def bank():
            i = cnt["b"] % 6
            cnt["b"] += 1
            return ps[i], f"ps{i}"

        cur = {}

        def proj(bk, kk, wt, c0, ncols):
            for c in range(8):
                P.mm(bk[:, 0:NT] if ncols == 128 else bk[:], wt[:, c, c0:c0 + 128], cur["h"][:, c, :], c == 0, c == 7,
                     [cur["hk"], pfx + "wm"], [kk])

        def norm_next(gi_):
            if gi_ + 1 < len(groups):
                nb_ = (gi_ + 1) % 2
                rms_norm_group(P, C, pfx, xin2[nb_], gam, h2[nb_], sqc, rstd, 0, xk=f"xin{nb_}", hk=f"h{nb_}")
            if gi_ + 2 < len(groups):
                g2 = groups[gi_ + 2]
                P.dma(xin2[gi_ % 2][:], xs[:, :, g2 * NT:(g2 + 1) * NT], [], [pfx + f"xin{gi_ % 2}"])

        rms_norm_group(P, C, pfx, xin2[0], gam, h2[0], sqc, rstd, 0, xk="xin0", hk="h0")
        for gi_, g in enumerate(groups):
            cols = slice(g * NT, (g + 1) * NT)
            kvo = g in kv_only
            rb = gi_ % 2
            h = h2[rb]
            cur["h"], cur["hk"] = h, pfx + f"h{rb}"
            cosT, sinT = cosb[rb], sinb[rb]
            P.dma(cosT[:], C.rope_cos[:, cols], [], [pfx + f"rope{rb}"])
            P.dma(sinT[:], C.rope_sin[:, cols], [], [pfx + f"rope{rb}"])
            for cc in range(2):
                if kvo and g != NG - 1:
                    continue
                bA, kA = bank()
                proj(bA, kA, wm, cc * 128, 128)
                bB, kB = bank()
                proj(bB, kB, wm, 256 + cc * 128, 128)
                i = cnt["t"] % 2
                cnt["t"] += 1
                P.act(t1[i][:], bB[:], AF.Sigmoid, [kB], [pfx + f"t1{i}"])
                o = cnt["ob"] % 4
                cnt["ob"] += 1
                P.tt(ob[o][:], bA[:], t1[i][:], ALU.mult, [kA, pfx + f"t1{i}"], [pfx + f"ob{o}"])
                P.dma(T["vconv"][cc * 128:(cc + 1) * 128, cols], ob[o][:], [pfx + f"ob{o}"], [])
            for sec, dname, scl in ((512, "qsT", 0.125), (1024, "kT", 1.0)):
                if kvo and dname == "qsT":
                    continue
                for cc in range(4):
                    bk, kk = bank()
                    proj(bk, kk, wm, sec + cc * 128, 128)
                    o = cnt["ob"] % 4
                    cnt["ob"] += 1
                    P.ts(ob[o][:], bk[:], scl, None, ALU.mult, None, [kk], [pfx + f"ob{o}"])
                    P.dma(T[dname][cc * 128:(cc + 1) * 128, cols], ob[o][:], [pfx + f"ob{o}"], [])
            norm_next(gi_)
            for tb in range(4):
                tsl = slice(tb * 128, (tb + 1) * 128)
                rows = slice(g * NT + tb * 128, g * NT + (tb + 1) * 128)
                bk, kk = bank()
                for c in range(8):
                    P.mm(bk[:], h[:, c, tsl], wm[:, c, 1536:2048], c == 0, c == 7, [cur["hk"], pfx + "wm"], [kk])
                o = cnt["ob"] % 4
                cnt["ob"] += 1
                P.act(ob[o][:], bk[:], AF.Copy, [kk], [pfx + f"ob{o}"])
                P.dma(T["vtok"][rows, :], ob[o][:], [pfx + f"ob{o}"], [])
                bk, kk = bank()
                for c in range(8):
                    P.mm(bk[:, 0:256], h[:, c, tsl], wm[:, c, 2560:2816], c == 0, c == 7,
                         [cur["hk"], pfx + "wm"], [kk])
                o = cnt["ob"] % 4
                cnt["ob"] += 1
                P.act(ob[o][:, 0:256], bk[:, 0:256], AF.Copy, [kk], [pfx + f"ob{o}"])
                P.dma(T["vr"][rows, :], ob[o][:, 0:256], [pfx + f"ob{o}"], [])
            for isk, sec, swc in ((0, 2048, 0), (1, 2304, 256)):
                if kvo and not isk:
                    continue
                for cc in range(2):
                    bX, kX = bank()
                    proj(bX, kX, wm, sec + cc * 128, 128)
                    bS, kS = bank()
                    proj(bS, kS, wsw, swc + cc * 128, 128)
                    i = cnt["t"] % 2
                    cnt["t"] += 1
                    P.tt(t1[i][:], bX[:], cosT[:], ALU.mult, [kX, pfx + f"rope{rb}"], [pfx + f"t1{i}"])
                    P.tt(t2[i][:], bS[:], sinT[:], ALU.mult, [kS, pfx + f"rope{rb}"], [pfx + f"t2{i}"])
                    P.tt(t1[i][:], t1[i][:], t2[i][:], ALU.add, [pfx + f"t1{i}", pfx + f"t2{i}"], [pfx + f"t1{i}"],
                         eng="pool")
                    frows = slice(cc * 128, (cc + 1) * 128)
                    if not isk:
                        o = cnt["ob"] % 4
                        cnt["ob"] += 1
                        P.ts(ob[o][:], t1[i][:], 0.125, None, ALU.mult, None, [pfx + f"t1{i}"], [pfx + f"ob{o}"],
                             eng="pool")
                        P.dma(T["qrT"][frows, cols], ob[o][:], [pfx + f"ob{o}"], [])
                        o = cnt["ob"] % 4
                        cnt["ob"] += 1
                        P.tt(ob[o][:], t1[i][:], decq[:, cc, :], ALU.mult, [pfx + f"t1{i}", pfx + "tab"],
                             [pfx + f"ob{o}"], eng="pool")
                        P.dma(T["qdT"][frows, cols], ob[o][:], [pfx + f"ob{o}"], [])
                    else:
                        r = cnt["kr"] % 2
                        cnt["kr"] += 1
                        P.act(krb[r][:], t1[i][:], AF.Copy, [pfx + f"t1{i}"], [pfx + f"krb{r}"])
                        P.dma(T["krT"][frows, cols], krb[r][:], [pfx + f"krb{r}"], [])
                        for tb in range(4):
                            rows = slice(g * NT + tb * 128, g * NT + (tb + 1) * 128)
                            P.op("pe", lambda e, r=r, tb=tb: e.transpose(C.pst[:, 0:128],
                                                                          krb[r][:, tb * 128:(tb + 1) * 128],
                                                                          C.ident[:]),
                                 [pfx + f"krb{r}", "const"], ["pst"])
                            d = cnt["kd"] % 2
                            cnt["kd"] += 1
                            P.tt(kdb[d][:, 0:128], C.pst[:, 0:128], deck[:, cc * 128:(cc + 1) * 128], ALU.mult,
                                 ["pst", pfx + "tab"], [pfx + f"kdb{d}"])
                            P.dma(T["kdtok"][rows, cc * 128:(cc + 1) * 128], kdb[d][:, 0:128],
                                  [pfx + f"kdb{d}"], [])
            for cc in range(2):
                if kvo:
                    continue
                bk, kk = bank()
                proj(bk, kk, wm, 2816 + cc * 128, 128)
                o = cnt["of"] % 2
                cnt["of"] += 1
                P.act(of[o][:], bk[:], AF.Silu, [kk], [pfx + f"of{o}"])
                P.dma(T["gsil"][cc * 128:(cc + 1) * 128, cols], of[o][:], [pfx + f"of{o}"], [])
        P.flush()


def retconv_phase(P, C, tag, T, cw_d, cb_d, lg_d, lb_d, rg_d, own0, has_prev):
    nc = P.nc
    pfx = tag + "_"
    ps = C.ps
    yT = T["yT"]
    with ExitStack() as es:
        sb = lambda n, shp, dt: es.enter_context(nc.sbuf_tensor(pfx + n, shp, dt))
        vpad = sb("vpad", [128, 2, 32 + NLOC], BF16)
        dg = sb("dg", [128, 62, 128], BF16)
        yc = [sb(f"yc{i}", [128, NT], F32) for i in range(2)]
        cw = sb("cw", [128, 2, 31], F32)
        cb = sb("cb", [128, 2], F32)
        lg = sb("lg", [128, 2], F32)
        lb = sb("lb", [128, 2], F32)
        rg = sb("rg", [64, 4], F32)
        kdq = sb("kdq", [64, 4 * NLOC], BF16)
        kd64 = kdq[:, :].rearrange("p (n c) -> p n c", c=256)
        vr64 = sb("vr64", [64, 64, 256], BF16)
        sball = sb("sball", [64, 64, 256], BF16)
        S = sb("S", [64, 256], F32)
        g64 = sb("g64", [64, 256], F32)
        dintra = sb("dintra", [64, 4, NT], F32)
        qr = [kdq[:, 0:NLOC]] * 2
        kr = [kdq[:, NLOC:2 * NLOC]] * 2
        qd = [kdq[:, 2 * NLOC:3 * NLOC]] * 2
        gs = [sb(f"gs{i}", [64, NT], F32) for i in range(2)]
        stb = [sb(f"stb{i}", [64, NT], BF16) for i in range(2)]
        osb2 = [sb(f"osb{i}", [64, NT], F32) for i in range(2)]
        osq2 = [sb(f"osq{i}", [64, NT], F32) for i in range(2)]
        mean2 = [sb(f"mean{i}", [128, NT], F32) for i in range(2)]
        msq2 = [sb(f"msq{i}", [128, NT], F32) for i in range(2)]
        var2 = [sb(f"var{i}", [128, NT], F32) for i in range(2)]
        mean, msq, var = mean2[0], msq2[0], var2[0]
        tn = [sb(f"tn{i}", [128, NT], F32) for i in range(2)]
        yb = [sb(f"yb{i}", [128, NT], BF16) for i in range(2)]
        sqy = [sb(f"sqy{i}", [128, NT], F32) for i in range(2)]

        for t, d in ((cw, cw_d), (cb, cb_d), (lg, lg_d), (lb, lb_d), (rg, rg_d), (g64, C.g64), (dintra, C.dintra)):
            P.dma(t[:], d, [], [pfx + "tab"])
        for cc in range(2):
            if has_prev:
                P.dma(vpad[:, cc, 0:32], T["vconv"][cc * 128:(cc + 1) * 128, own0 - 32:own0], [], [pfx + "vpad"])
            else:
                P.op("dve", lambda e, cc=cc: e.memset(vpad[:, cc, 0:32], 0.0), [], [pfx + "vpad"])
            P.dma(vpad[:, cc, 32:], T["vconv"][cc * 128:(cc + 1) * 128, own0:own0 + NLOC], [], [pfx + "vpad"])
            for k in range(31):
                P.ts(dg[:, cc * 31 + k, :], C.ident[:], cw[:, cc, k:k + 1], None, ALU.mult, None,
                     ["const", pfx + "tab"], [pfx + "dg"])
        P.op("dve", lambda e: e.memset(S[:], 0.0), [], [pfx + "S"])
        for half in range(2):
            if half == 0 and not has_prev:
                continue
            r0 = own0 - NLOC if half == 0 else own0
            ksrc = T["kdtok"][r0:r0 + NLOC, :]
            vsrc = T["vr"][r0:r0 + NLOC, :]
            P.dma(kd64, ksrc.rearrange("(n p) c -> p n c", p=64), [], [pfx + "kd64"])
            P.dma(vr64[:], vsrc.rearrange("(n p) c -> p n c", p=64), [], [pfx + "vr64"])
            for n in range(64):
                if half == 1:
                    P.act(sball[:, n, :], S[:], AF.Copy, [pfx + "S"], [pfx + f"sball{n}"])
                kvb, kvk = ps[6 + n % 2], f"ps{6 + n % 2}"
                for hh in range(4):
                    hs = slice(hh * 64, (hh + 1) * 64)
                    P.mm(kvb[0:64, hs], kd64[:, n, hs], vr64[:, n, hs], True, True,
                         [pfx + "kd64", pfx + "vr64"], [kvk])
                P.tt(S[:], S[:], g64[:], ALU.mult, [pfx + "S", pfx + "tab"], [pfx + "S"])
                P.tt(S[:], S[:], kvb[0:64, 0:256], ALU.add, [pfx + "S", kvk], [pfx + "S"])
        for hh in range(4):
            b = 0
            hrows = slice(hh * 64, (hh + 1) * 64)
            hs = slice(hh * 64, (hh + 1) * 64)
            P.dma(qr[b], T["qrT"][hrows, own0:own0 + NLOC], [], [pfx + f"qr{b}", pfx + "kd64"])
            P.dma(kr[b], T["krT"][hrows, own0:own0 + NLOC], [], [pfx + f"kr{b}", pfx + "kd64"])
            P.dma(qd[b], T["qdT"][hrows, own0:own0 + NLOC], [], [pfx + f"qd{b}", pfx + "kd64"])
            def item(g, k, hh=hh, hrows=hrows, hs=hs, b=b):
                ocols = slice(own0 + g * NT, own0 + (g + 1) * NT)
                P.dma(gs[k][:], T["gsil"][hrows, ocols], [], [pfx + f"gs{k}"])
                bS, kS = ps[k], f"ps{k}"
                bO, kO = ps[2 + k], f"ps{2 + k}"
                bM, kM = ps[4 + k], f"ps{4 + k}"
                bQ, kQ = ps[6 + k], f"ps{6 + k}"
                for c in range(8):
                    n = g * 8 + c
                    tc_ = slice(n * 64, (n + 1) * 64)
                    P.mm(bS[0:64, c * 64:(c + 1) * 64], kr[b][:, tc_], qr[b][:, tc_], True, True,
                         [pfx + f"kr{b}", pfx + f"qr{b}"], [kS])
                P.tt(stb[k][:], bS[0:64, :], dintra[:, hh, :], ALU.mult, [kS, pfx + "tab"], [pfx + f"stb{k}"])
                for c in range(8):
                    n = g * 8 + c
                    tc_ = slice(n * 64, (n + 1) * 64)
                    cs = slice(c * 64, (c + 1) * 64)
                    P.mm(bO[0:64, cs], vr64[:, n, hs], stb[k][:, cs], True, False,
                         [pfx + "vr64", pfx + f"stb{k}"], [kO])
                    P.mm(bO[0:64, cs], sball[:, n, hs], qd[b][:, tc_], False, True,
                         [pfx + f"sball{n}", pfx + f"qd{b}"], [kO])
                osb_, osq_ = osb2[k], osq2[k]
                P.act(osb_[:], bO[0:64, :], AF.Copy, [kO], [pfx + f"osb{k}"])
                P.act(osq_[:], bO[0:64, :], AF.Square, [kO], [pfx + f"osq{k}"])
                P.mm(bM[0:64, :], C.ones_f[0:64, 0:64], osb_[:], True, True, [pfx + f"osb{k}", "const"], [kM])
                P.mm(bQ[0:64, :], C.ones_f[0:64, 0:64], osq_[:], True, True, [pfx + f"osq{k}", "const"], [kQ])
                m_, q_, v_ = mean2[k][0:64, :], msq2[k][0:64, :], var2[k][0:64, :]
                mk, qk, vk = pfx + f"mean{k}", pfx + f"msq{k}", pfx + f"var{k}"
                P.ts(m_, bM[0:64, :], 1.0 / 64, None, ALU.mult, None, [kM], [mk])
                P.tt(q_, m_, m_, ALU.mult, [mk], [qk])
                P.stt(v_, bQ[0:64, :], 1.0 / 64, q_, ALU.mult, ALU.subtract, [kQ, qk], [vk])
                P.ts(v_, v_, EPS, None, ALU.add, None, [vk], [vk])
                rsqrt_inplace(P, v_, vk)
                t_ = tn[k][0:64, :]
                P.tt(t_, osb_[:], m_, ALU.subtract, [pfx + f"osb{k}", mk], [pfx + f"tn{k}"])
                P.tt(t_, t_, v_, ALU.mult, [pfx + f"tn{k}", vk], [pfx + f"tn{k}"])
                P.stt(yb[k][0:64, :], t_, rg[:, hh:hh + 1], gs[k][:], ALU.mult, ALU.mult,
                      [pfx + f"tn{k}", pfx + "tab", pfx + f"gs{k}"], [pfx + f"yb{k}"])
                P.dma(yT[768 + hh * 64:768 + (hh + 1) * 64, ocols], yb[k][0:64, :], [pfx + f"yb{k}"], [])

            for g in range(0, NG, 2):
                P.interleave(P.capture(lambda: item(g, 0)), P.capture(lambda: item(g + 1, 1)))
        for g in range(NG):
            cols = slice(g * NT, (g + 1) * NT)
            for cc in range(2):
                for k in range(31):
                    P.mm(ps[cc][:], dg[:, cc * 31 + k, :], vpad[:, cc, 2 + k + g * NT:2 + k + (g + 1) * NT],
                         k == 0, k == 30, [pfx + "dg", pfx + "vpad"], [f"ps{cc}"])
                P.ts(yc[cc][:], ps[cc][:], cb[:, cc:cc + 1], None, ALU.add, None, [f"ps{cc}", pfx + "tab"],
                     [pfx + f"yc{cc}"])
                P.act(sqy[cc][:], yc[cc][:], AF.Square, [pfx + f"yc{cc}"], [pfx + f"sqy{cc}"])
            for cc in range(2):
                P.mm(ps[4][:], C.ones_f[:], yc[cc][:], cc == 0, cc == 1, [pfx + f"yc{cc}", "const"], ["ps4"])
            for cc in range(2):
                P.mm(ps[5][:], C.ones_f[:], sqy[cc][:], cc == 0, cc == 1, [pfx + f"sqy{cc}", "const"], ["ps5"])
            P.ts(mean[:], ps[4][:], 1.0 / 256, None, ALU.mult, None, ["ps4"], [pfx + "mean0"])
            P.tt(msq[:], mean[:], mean[:], ALU.mult, [pfx + "mean0"], [pfx + "msq0"])
            P.stt(var[:], ps[5][:], 1.0 / 256, msq[:], ALU.mult, ALU.subtract, ["ps5", pfx + "msq0"], [pfx + "var0"])
            P.ts(var[:], var[:], EPS, None, ALU.add, None, [pfx + "var0"], [pfx + "var0"])
            rsqrt_inplace(P, var[:], pfx + "var0")
            for cc in range(2):
                P.tt(tn[cc][:], yc[cc][:], mean[:], ALU.subtract, [pfx + f"yc{cc}", pfx + "mean0"],
                     [pfx + f"tn{cc}"])
                P.tt(tn[cc][:], tn[cc][:], var[:], ALU.mult, [pfx + f"tn{cc}", pfx + "var0"], [pfx + f"tn{cc}"])
                P.act(yb[cc][:], tn[cc][:], AF.Silu, [pfx + f"tn{cc}", pfx + "tab"], [pfx + f"yb{cc}"],
                      bias=lb[:, cc:cc + 1], scale=lg[:, cc:cc + 1])
                P.dma(yT[cc * 128:(cc + 1) * 128, own0 + g * NT:own0 + (g + 1) * NT], yb[cc][:],
                      [pfx + f"yb{cc}"], [])
        P.flush()


def sb_phase(P, C, tag, T, own0, has_prev):
    nc = P.nc
    pfx = tag + "_"
    ps = C.ps
    yT = T["yT"]
    NB = 64 if has_prev else 32
    LB0 = NB - 32
    base = own0 - (NLOC if has_prev else 0)
    NK = NB * 128
    QG = 256
    NQG = NLOC // QG
    with ExitStack() as es:
        sb = lambda n, shp, dt: es.enter_context(nc.sbuf_tensor(pfx + n, shp, dt))
        VA = sb("VA", [128, NB, 512], BF16)
        KT = [sb(f"KT{i}", [128, NK], BF16) for i in range(2)]
        QB = [sb(f"QB{i}", [128, NQG, 2 * QG], BF16) for i in range(2)]
        msk = sb("msk", [128, 2 * NT], BF16)
        ebuf = [sb(f"e{i}", [128, 2 * NT], F32) for i in range(2)]
        spb = [sb(f"sp{i}", [128, 2 * NT], BF16) for i in range(3)]
        wb = [sb(f"w{i}", [128, 2 * NT], BF16) for i in range(3)]
        sA = [sb(f"sA{i}", [128, NT], BF16) for i in range(3)]
        sB = [sb(f"sB{i}", [128, NT], BF16) for i in range(3)]
        ob = [sb(f"ob{i}", [128, QG], BF16) for i in range(2)]

        P.dma(msk[:], C.sbmask2, [], [pfx + "tab"])
        for b in range(2):
            P.op("dve", lambda e, b=b: e.memset(QB[b][0:64, :, QG:2 * QG], 0.0), [], [pfx + f"QB{b}"])
            P.op("dve", lambda e, b=b: e.memset(QB[b][64:128, :, 0:QG], 0.0), [], [pfx + f"QB{b}"])
        for b0 in range(0, NB, 32):
            P.dma(VA[:, b0:b0 + 32, :],
                  T["vtok"][base + b0 * 128:base + (b0 + 32) * 128, :].rearrange("(b p) c -> p b c", p=128),
                  [], [pfx + "VA"])

        def load_pair(hp):
            b = hp % 2
            r0 = hp * 128
            P.dma(KT[b][:], T["kT"][r0:r0 + 128, base:base + NK], [], [pfx + f"KT{b}"])
            P.dma(QB[b][0:64, :, 0:QG],
                  T["qsT"][r0:r0 + 64, own0:own0 + NLOC].rearrange("p (g q) -> p g q", q=QG), [], [pfx + f"QB{b}"])
            P.dma(QB[b][64:128, :, QG:2 * QG],
                  T["qsT"][r0 + 64:r0 + 128, own0:own0 + NLOC].rearrange("p (g q) -> p g q", q=QG),
                  [], [pfx + f"QB{b}"])

        items = []
        for hp in range(4):
            for g in range(NQG):
                blks = [LB0 + 2 * g + 1 - i for i in range(2 * g + 2)] + [LB0 - 1 - i for i in range(LB0)]
                npair = len(blks) // 2
                for i in range(npair):
                    items.append((hp, g, blks[2 * i], blks[2 * i + 1], i == 0, i == npair - 1))
        big = C.psbig
        H0, H1 = slice(0, NT), slice(NT, 2 * NT)

        def stageA(idx):
            hp, g, kb0, kb1, first, last = items[idx]
            b = hp % 2
            if g == 0 and first:
                if hp == 0:
                    load_pair(0)
                if hp + 1 < 4:
                    load_pair(hp + 1)
            i2, i3 = idx % 2, idx % 3
            q = QB[b][:, g, :]
            bA, kA = big[i2], f"psb{i2}"
            for kb, hsl in ((kb0, H0), (kb1, H1)):
                P.mm(bA[:, hsl], KT[b][:, kb * 128:(kb + 1) * 128], q, True, True,
                     [pfx + f"KT{b}", pfx + f"QB{b}"], [kA])
            P.act(ebuf[i2][:], bA[:], AF.Exp, [kA], [pfx + f"e{i2}"])
            P.act(spb[i3][:], ebuf[i2][:], AF.Ln, [pfx + f"e{i2}"], [pfx + f"sp{i3}"], bias=1.0)
            if first:
                P.tt(spb[i3][:], spb[i3][:], msk[:], ALU.mult, [pfx + f"sp{i3}", pfx + "tab"], [pfx + f"sp{i3}"])
                P.op("dve", lambda e: e.tensor_copy(out=sA[i3][:], in_=spb[i3][:, H0]),
                     [pfx + f"sp{i3}"], [pfx + f"sA{i3}"])
            else:
                p3 = (idx - 1) % 3
                P.tt(sA[i3][:], sB[p3][:], spb[i3][:, H0], ALU.add, [pfx + f"sB{p3}", pfx + f"sp{i3}"],
                     [pfx + f"sA{i3}"])
            if not last:
                P.tt(sB[i3][:], sA[i3][:], spb[i3][:, H1], ALU.add, [pfx + f"sA{i3}", pfx + f"sp{i3}"],
                     [pfx + f"sB{i3}"])

        def stageB(idx):
            hp, g, kb0, kb1, first, last = items[idx]
            b = hp % 2
            i3 = idx % 3
            q = QB[b][:, g, :]
            bB, kB = big[2], "psb2"
            for kb, hsl, prev in ((kb0, H0, None if first else sB[(idx - 1) % 3]), (kb1, H1, sA[i3])):
                P.mm(bB[:, hsl], C.trineg[:], spb[i3][:, hsl], True, False, [pfx + f"sp{i3}", "const"], [kB])
                if prev is not None:
                    pk = pfx + (f"sA{i3}" if prev is sA[i3] else f"sB{(idx - 1) % 3}")
                    P.mm(bB[:, hsl], C.negones[:], prev[:], False, False, [pk, "const"], [kB])
                P.mm(bB[:, hsl], KT[b][:, kb * 128:(kb + 1) * 128], q, False, True,
                     [pfx + f"KT{b}", pfx + f"QB{b}"], [kB])
            P.act(wb[i3][:], bB[:], AF.Exp, [kB], [pfx + f"w{i3}"])
            if first:
                P.tt(wb[i3][:], wb[i3][:], msk[:], ALU.mult, [pfx + f"w{i3}", pfx + "tab"], [pfx + f"w{i3}"])

        def stageC(idx):
            hp, g, kb0, kb1, first, last = items[idx]
            i3 = idx % 3
            gi = (hp * NQG + g) % 2
            bO, kO = ps[6 + gi], f"psO{gi}"
            P.mm(bO[:], VA[:, kb0, hp * 128:(hp + 1) * 128], wb[i3][:, H0], first, False,
                 [pfx + "VA", pfx + f"w{i3}"], [kO])
            P.mm(bO[:], VA[:, kb1, hp * 128:(hp + 1) * 128], wb[i3][:, H1], False, last,
                 [pfx + "VA", pfx + f"w{i3}"], [kO])
            if last:
                P.op("dve", lambda e: e.tensor_copy(out=ob[gi][0:64, :], in_=bO[0:64, 0:QG]), [kO], [pfx + f"ob{gi}"])
                P.op("dve", lambda e: e.tensor_copy(out=ob[gi][64:128, :], in_=bO[64:128, QG:2 * QG]), [kO],
                     [pfx + f"ob{gi}"])
                P.dma(yT[256 + hp * 128:256 + (hp + 1) * 128, own0 + g * QG:own0 + (g + 1) * QG], ob[gi][:],
                      [pfx + f"ob{gi}"], [])

        n = len(items)
        for i in range(n + 2):
            if i < n:
                stageA(i)
            if 1 <= i <= n:
                stageB(i - 1)
            if i >= 2:
                stageC(i - 2)
        P.flush()


def outproj_phase(P, C, tag, w_o, T, xsrc, xdst, groups):
    nc = P.nc
    pfx = tag + "_"
    ps = C.ps
    xs = xsrc.rearrange("(c p) t -> p c t", p=128)
    xd = xdst.rearrange("(c p) t -> p c t", p=128)
    yr = T["yT"].rearrange("(c p) t -> p c t", p=128)
    with ExitStack() as es:
        sb = lambda n, shp, dt: es.enter_context(nc.sbuf_tensor(pfx + n, shp, dt))
        wo = sb("wo", [128, 8, D], BF16)
        yin = [sb(f"yin{i}", [128, 8, NT], BF16) for i in range(2)]
        xin = [sb(f"xin{i}", [128, 8, NT], F32) for i in range(2)]
        xo = [sb(f"xo{i}", [128, 8, NT], F32) for i in range(2)]
        P.dma(wo[:], w_o.rearrange("(c p) m -> p c m", p=128), [], [pfx + "wo"], q="pool")
        for g in groups:
            cols = slice(g * NT, (g + 1) * NT)
            b = g % 2
            P.dma(yin[b][:], yr[:, :, cols], [], [pfx + f"yin{b}"])
            P.dma(xin[b][:], xs[:, :, cols], [], [pfx + f"xin{b}"])
            for m in range(8):
                bO, kO = ps[m % 4], f"ps{m % 4}"
                for c in range(8):
                    P.mm(bO[:], wo[:, c, m * 128:(m + 1) * 128], yin[b][:, c, :], c == 0, c == 7,
                         [pfx + "wo", pfx + f"yin{b}"], [kO])
                P.tt(xo[b][:, m, :], bO[:], xin[b][:, m, :], ALU.add, [kO, pfx + f"xin{b}"], [pfx + f"xo{b}"])
            P.dma(xd[:, :, cols], xo[b][:], [pfx + f"xo{b}"], [])
        P.flush()


def final_phase(P, C, tag, gam_d, xsrc, out_d):
    nc = P.nc
    pfx = tag + "_"
    xs = xsrc.rearrange("(c p) t -> p c t", p=128)
    od = out_d.rearrange("(c p) t -> p c t", p=128)
    with ExitStack() as es:
        sb = lambda n, shp, dt: es.enter_context(nc.sbuf_tensor(pfx + n, shp, dt))
        gam = sb("gam", [128, 8], F32)
        xin = [sb(f"xin{i}", [128, 8, NT], F32) for i in range(2)]
        sqc = [sb(f"sqc{i}", [128, NT], BF16) for i in range(2)]
        rstd = sb("rstd", [128, NT], F32)
        yo = [sb(f"yo{i}", [128, 8, NT], F32) for i in range(2)]
        P.dma(gam[:], gam_d, [], [pfx + "gam"])
        for g in range(NG):
            cols = slice(NLOC + g * NT, NLOC + (g + 1) * NT)
            ocols = slice(g * NT, (g + 1) * NT)
            b = g % 2
            P.dma(xin[b][:], xs[:, :, cols], [], [pfx + f"xin{b}"])
            for c in range(8):
                P.act(sqc[c % 2][:], xin[b][:, c, :], AF.Square, [pfx + f"xin{b}"], [pfx + f"sqc{c % 2}"])
                P.mm(C.ps[6][:], C.ones_bf[:], sqc[c % 2][:], c == 0, c == 7, [pfx + f"sqc{c % 2}", "const"], ["ps6"])
            P.ts(rstd[:], C.ps[6][:], 1.0 / D, EPS, ALU.mult, ALU.add, ["ps6"], [pfx + "rstd"])
            rsqrt_inplace(P, rstd[:], pfx + "rstd")
            for c in range(8):
                P.stt(yo[b][:, c, :], xin[b][:, c, :], gam[:, c:c + 1], rstd[:], ALU.mult, ALU.mult,
                      [pfx + f"xin{b}", pfx + "rstd", pfx + "gam"], [pfx + f"yo{b}"])
            P.dma(od[:, :, ocols], yo[b][:], [pfx + f"yo{b}"], [], is_out=True)


MIXT = {
    "vconv": ([256, NSLOT], BF16), "qsT": ([512, NSLOT], BF16), "kT": ([512, NSLOT], BF16),
    "vtok": ([NSLOT, 512], BF16), "vr": ([NSLOT, 256], BF16), "qrT": ([256, NSLOT], BF16),
    "qdT": ([256, NSLOT], BF16), "krT": ([256, NSLOT], BF16), "kdtok": ([NSLOT, 256], BF16),
    "gsil": ([256, NSLOT], F32), "yT": ([D, NSLOT], BF16),
}
CONSTS = {"ones_bf": ([128, 128], BF16), "ones_f": ([128, 128], F32), "ident": ([128, 128], BF16),
          "trineg": ([128, 128], BF16), "negones": ([128, 128], BF16)}
TABLES = {"rope_cos": [128, NSLOT], "rope_sin": [128, NSLOT], "decq": [128, 2, NT], "deck": [128, 256],
          "g64": [64, 256], "dintra": [64, 4, NT], "flag": [128, 1]}
ALLG = list(range(2 * NG))
OWNG = list(range(NG, 2 * NG))


def build_program():
    nc = bass.Bass("TRN2", target_bir_lowering=False)
    dt = lambda name, shape, dtype, kind: nc.dram_tensor(name, shape, dtype, kind=kind).ap()
    C = Ctx()
    W = {}
    for l in range(DEPTH):
        for f in ("1", "2"):
            W[f"ffn{f}_w_in{l}"] = dt(f"ffn{f}_w_in{l}", [D, 2 * DFF], F32, "ExternalInput")
            W[f"ffn{f}_w_out{l}"] = dt(f"ffn{f}_w_out{l}", [DFF, D], F32, "ExternalInput")
            W[f"ffn{f}_norm{l}"] = dt(f"ffn{f}_norm{l}", [128, 8], F32, "ExternalInput")
        W[f"mix_w_in{l}"] = dt(f"mix_w_in{l}", [D, 3072], F32, "ExternalInput")
        W[f"mix_norm{l}"] = dt(f"mix_norm{l}", [128, 8], F32, "ExternalInput")
        W[f"mix_w_out{l}"] = dt(f"mix_w_out{l}", [D, D], F32, "ExternalInput")
        W[f"conv_w{l}"] = dt(f"conv_w{l}", [128, 2, 31], F32, "ExternalInput")
        for nm in ("conv_b", "conv_ln_g", "conv_ln_b"):
            W[f"{nm}{l}"] = dt(f"{nm}{l}", [128, 2], F32, "ExternalInput")
        W[f"ret_norm_g{l}"] = dt(f"ret_norm_g{l}", [64, 4], F32, "ExternalInput")
    W["final_norm"] = dt("final_norm", [128, 8], F32, "ExternalInput")
    cd = {k: dt("c_" + k, s, d, "ExternalInput") for k, (s, d) in CONSTS.items()}
    for k, s in TABLES.items():
        setattr(C, k, dt("t_" + k, s, F32, "ExternalInput"))
    C.sbmask2 = dt("t_sbmask2", [128, 2 * NT], BF16, "ExternalInput")
    x_in = dt("x_in", [D, NSLOT], F32, "ExternalInput")
    xa = dt("xa", [D, NSLOT], F32, "Internal")
    xb = dt("xb", [D, NSLOT], F32, "Internal")
    T = {k: dt("m_" + k, s, d, "Internal") for k, (s, d) in MIXT.items()}
    out = dt("out", [D, NLOC], F32, "ExternalOutput")

    with ExitStack() as es:
        P = Prog(nc, es)
        C.psbig = [es.enter_context(nc.psum_tensor(f"psb{i}", [128, 2 * NT], F32)) for i in range(4)]
        C.ps = [C.psbig[i // 2][:, (i % 2) * NT:(i % 2 + 1) * NT] for i in range(8)]
        C.pst = C.psbig[3].bitcast(BF16)[:, 2 * NT:4 * NT]
        for k, (s, d) in CONSTS.items():
            t = es.enter_context(nc.sbuf_tensor("k_" + k, s, d))
            setattr(C, k, t)
            P.dma(t[:], cd[k], [], ["const"])
        P.flush()
        mixw = lambda l: (W[f"conv_w{l}"], W[f"conv_b{l}"], W[f"conv_ln_g{l}"], W[f"conv_ln_b{l}"],
                          W[f"ret_norm_g{l}"])
        ffn_phase(P, C, "f1a", W["ffn1_w_in0"], W["ffn1_w_out0"], W["ffn1_norm0"], x_in, xa, ALLG)
        inproj_phase(P, C, "ipa", W["mix_w_in0"], W["mix_norm0"], xa, T, ALLG)
        retconv_phase(P, C, "rc0a", T, *mixw(0), own0=0, has_prev=False)
        sb_phase(P, C, "sb0a", T, own0=0, has_prev=False)
        retconv_phase(P, C, "rc0b", T, *mixw(0), own0=NLOC, has_prev=True)
        sb_phase(P, C, "sb0b", T, own0=NLOC, has_prev=True)
        outproj_phase(P, C, "opa", W["mix_w_out0"], T, xa, xb, ALLG)
        ffn_phase(P, C, "f2a", W["ffn2_w_in0"], W["ffn2_w_out0"], W["ffn2_norm0"], xb, xa, ALLG, flag_prefix=True)
        ffn_phase(P, C, "f1b", W["ffn1_w_in1"], W["ffn1_w_out1"], W["ffn1_norm1"], xa, xb, ALLG)
        inproj_phase(P, C, "ipb", W["mix_w_in1"], W["mix_norm1"], xb, T, ALLG, kv_only=tuple(range(NG)))
        retconv_phase(P, C, "rc1", T, *mixw(1), own0=NLOC, has_prev=True)
        sb_phase(P, C, "sb1", T, own0=NLOC, has_prev=True)
        outproj_phase(P, C, "opb", W["mix_w_out1"], T, xb, xa, OWNG)
        ffn_phase(P, C, "f2b", W["ffn2_w_in1"], W["ffn2_w_out1"], W["ffn2_norm1"], xa, xb, OWNG)
        final_phase(P, C, "fin", W["final_norm"], xb, out)
        P.flush(final=True)
    return nc


def _consts():
    bf = ml_dtypes.bfloat16
    j = np.arange(128)
    c = {
        "c_ones_bf": np.ones((128, 128), bf), "c_ones_f": np.ones((128, 128), np.float32),
        "c_ident": np.eye(128).astype(bf),
        "c_trineg": (-(j[:, None] >= j[None, :]).astype(np.float32)).astype(bf),
        "c_negones": (-np.ones((128, 128), np.float32)).astype(bf),
    }
    gam = 1.0 - np.exp2(-5.0 - np.arange(4, dtype=np.float64))
    i512 = np.arange(NT)
    p = np.arange(128)
    decq = np.zeros((128, 2, NT))
    for cc in range(2):
        hh = 2 * cc + p // 64
        decq[:, cc, :] = 0.125 * gam[hh][:, None] ** ((i512 % 64) + 1.0)[None, :]
    col = np.arange(256)
    deck = gam[col // 64][None, :] ** (63.0 - (p % 64))[:, None]
    g64 = np.broadcast_to((gam[col // 64] ** 64.0)[None, :], (64, 256))
    jj = np.arange(64)
    dintra = np.zeros((64, 4, NT))
    for hh in range(4):
        dintra[:, hh, :] = gam[hh] ** np.abs(jj[:, None] - (i512 % 64)[None, :])
    sbmask = np.zeros((128, 2 * NT), np.float32)
    for slot, cc in enumerate((1, 0)):
        sbmask[:, slot * NT:(slot + 1) * NT] = ((cc * 128 + p)[:, None] < (i512 % 256)[None, :])
    c.update({"t_decq": decq.astype(np.float32), "t_deck": deck.astype(np.float32),
              "t_g64": np.ascontiguousarray(g64).astype(np.float32), "t_dintra": dintra.astype(np.float32),
              "t_sbmask2": sbmask.astype(bf)})
    return c


def _rope(half):
    slot = np.arange(NSLOT)
    pos = (slot if half == 1 else slot % NLOC).astype(np.float32)
    inv = (1.0 / (10000.0 ** (np.arange(32, dtype=np.float32) / 32))).astype(np.float32)
    p = np.arange(128)
    ang = pos[None, :] * inv[(p % 64) % 32][:, None]
    sign = np.where((p % 64) < 32, -1.0, 1.0)[:, None]
    return np.cos(ang).astype(np.float32), (sign * np.sin(ang)).astype(np.float32)


def _vec8(v):
    return np.ascontiguousarray(v.reshape(8, 128).T)


def _vec2(v):
    return np.ascontiguousarray(v.reshape(2, 128).T)


def kernel(**inp):
    ncores = 8
    x = inp["x"]
    w = dict(_consts())
    for l in range(DEPTH):
        for f in ("1", "2"):
            w[f"ffn{f}_w_in{l}"] = np.ascontiguousarray(inp[f"ffn{f}_w_in"][l])
            w[f"ffn{f}_w_out{l}"] = np.ascontiguousarray(inp[f"ffn{f}_w_out"][l])
            w[f"ffn{f}_norm{l}"] = _vec8(inp[f"ffn{f}_norm"][l])
        w[f"mix_w_in{l}"] = np.ascontiguousarray(inp["mix_w_in"][l])
        w[f"mix_norm{l}"] = _vec8(inp["mix_norm"][l])
        w[f"mix_w_out{l}"] = np.ascontiguousarray(inp["mix_w_out"][l])
        w[f"conv_w{l}"] = np.ascontiguousarray(inp["conv_w"][l].T.reshape(2, 128, 31).transpose(1, 0, 2))
        for nm in ("conv_b", "conv_ln_g", "conv_ln_b"):
            w[f"{nm}{l}"] = _vec2(inp[nm][l])
        w[f"ret_norm_g{l}"] = np.ascontiguousarray(inp["ret_norm_g"][l].reshape(4, 64).T)
    w["final_norm"] = _vec8(inp["final_norm"])
    ropes = [_rope(0), _rope(1)]
    maps = []
    for core in range(ncores):
        b, half = core // 2, core % 2
        m = dict(w)
        m["t_rope_cos"], m["t_rope_sin"] = ropes[half]
        m["t_flag"] = np.full((128, 1), float(half), np.float32)
        xi = np.zeros((D, NSLOT), np.float32)
        if half == 1:
            xi[:, :] = x[b].T
        else:
            xi[:, NLOC:] = x[b, :NLOC, :].T
        m["x_in"] = xi
        maps.append(m)
    res = run_bass_kernel_spmd(build_program(), maps, core_ids=list(range(ncores))).results
    out = np.empty(x.shape, np.float32)
    for c in range(ncores):
        out[c // 2, (c % 2) * NLOC:(c % 2 + 1) * NLOC, :] = res[c]["out"].T
    return out
```

```python
import numpy as np
import ml_dtypes
from contextlib import ExitStack
import concourse.bass as bass
import concourse.mybir as mybir
from concourse.bass_utils import run_bass_kernel_spmd

F32 = mybir.dt.float32
BF16 = mybir.dt.bfloat16
AF = mybir.ActivationFunctionType
ALU = mybir.AluOpType

D = 1024
DFF = 2816
NLOC = 4096
NT = 512
NG = NLOC // NT
DEPTH = 2
EPS = 1e-6
NDS = 8


class Op:
    __slots__ = ("stream", "fn", "deps", "dma", "token", "needed", "phase", "wkeys", "cc")


class Prog:
    STREAMS = ["pe", "act", "dve", "pool", "sp"]

    def __init__(self, nc, es):
        self.nc = nc
        self.es = es
        self.ccsnap = {}
        self.csem = {s: es.enter_context(nc.semaphore("c_" + s)) for s in ["pe", "act", "dve", "pool"]}
        self.dsem = {s: [es.enter_context(nc.semaphore(f"d_{s}{i}")) for i in range(NDS)]
                     for s in ["sp", "pool"]}
        self.ccnt = {s: 0 for s in self.csem}
        self.dcnt = {s: [0] * NDS for s in self.dsem}
        self.dnum = {s: 0 for s in self.dsem}
        self.lastw = {}
        self.readers = {}
        self.ops = []
        self.phase = 0
        self.waited = {s: {} for s in self.STREAMS}
        self.pending = {s: {} for s in self.STREAMS}
        self.out_tokens = []

    def capture(self, f):
        self._cap = []
        f()
        cap, self._cap = self._cap, None
        return cap

    def interleave(self, *caps):
        n = max(len(c) for c in caps)
        for i in range(n):
            for c in caps:
                if i < len(c):
                    self.op(*c[i])

    def cc(self, in_ap, out_ap, reads, writes):
        sem = self.es.enter_context(self.nc.semaphore(f"cc{len(self.ccsnap)}"))
        o = self.op("pool", lambda e: e.collective_compute("AllGather", ALU.bypass, replica_groups=[list(range(8))],
                                                            ins=[in_ap], outs=[out_ap]),
                    reads, writes, dma=True, ccsem=sem)
        return o

    def op(self, stream, fn, reads=(), writes=(), dma=False, is_out=False, ccsem=None):
        if getattr(self, "_cap", None) is not None:
            self._cap.append((stream, fn, tuple(reads), tuple(writes), dma, is_out))
            return None
        o = Op()
        o.stream, o.fn, o.dma, o.needed, o.phase, o.token = stream, fn, dma, False, self.phase, None
        o.wkeys = set(writes)
        deps = {}
        for k in reads:
            w = self.lastw.get(k)
            if w is not None:
                deps[id(w)] = (w, True)
        for k in writes:
            w = self.lastw.get(k)
            if w is not None and id(w) not in deps:
                deps[id(w)] = (w, False)
            for r in self.readers.get(k, {}).values():
                if id(r) not in deps:
                    deps[id(r)] = (r, False)
        o.deps = []
        for d, raw in deps.values():
            if d.phase != self.phase or d is o:
                continue
            if d.stream == stream and not d.dma and not dma:
                if stream == "pe" or not raw:
                    continue
            o.deps.append(d)
        o.cc = ccsem is not None
        if o.cc:
            o.token = (ccsem, 1)
            self.ccsnap[id(ccsem)] = (ccsem, 1)
            rkey = (stream, "cc%d" % len(self.ccsnap))
        elif dma:
            i = self.dnum[stream] % NDS
            self.dnum[stream] += 1
            self.dcnt[stream][i] += 16
            o.token = (self.dsem[stream][i], self.dcnt[stream][i])
            rkey = (stream, i)
            if is_out:
                self.out_tokens.append(o.token)
        else:
            rkey = (stream, -1)
        for k in reads:
            self.readers.setdefault(k, {})[rkey] = o
        for k in writes:
            self.lastw[k] = o
            self.readers[k] = {}
        self.ops.append(o)
        return o

    def flush(self, final=False):
        ops = self.ops
        for o in ops:
            for d in o.deps:
                d.needed = True
        last = {}
        for o in ops:
            if not o.dma:
                last[o.stream] = o
        for o in last.values():
            o.needed = True
        for o in ops:
            if not o.dma and o.needed:
                self.ccnt[o.stream] += 1
                o.token = (self.csem[o.stream], self.ccnt[o.stream])
        by_stream = {s: [o for o in ops if o.stream == s] for s in self.STREAMS}
        snapshot = {}
        for s in self.csem:
            if self.ccnt[s]:
                snapshot[id(self.csem[s])] = (self.csem[s], self.ccnt[s])
        for s in self.dsem:
            for i in range(NDS):
                if self.dcnt[s][i]:
                    snapshot[id(self.dsem[s][i])] = (self.dsem[s][i], self.dcnt[s][i])
        snapshot.update(self.ccsnap)

        def emit(stream, eng):
            waited = self.waited[stream]
            for o in by_stream[stream]:
                w = dict(self.pending[stream])
                self.pending[stream] = {}
                for d in o.deps:
                    sem, val = d.token
                    if id(sem) not in w or w[id(sem)][1] < val:
                        w[id(sem)] = (sem, val)
                for sid, (sem, val) in w.items():
                    if waited.get(sid, 0) < val:
                        eng.wait_ge(sem, val)
                        waited[sid] = val
                ins = o.fn(eng)
                if o.cc:
                    ins.then_inc(o.token[0])
                elif o.token is not None:
                    ins.then_inc(o.token[0], 16 if o.dma else 1)
            if final and stream == "sp":
                for sid, (sem, val) in snapshot.items():
                    if waited.get(sid, 0) < val:
                        eng.wait_ge(sem, val)
                        waited[sid] = val

        with self.nc.Block() as blk:
            if by_stream["pe"]:
                blk.tensor(lambda e: emit("pe", e))
            if by_stream["act"]:
                blk.scalar(lambda e: emit("act", e))
            if by_stream["dve"]:
                blk.vector(lambda e: emit("dve", e))
            if by_stream["pool"]:
                blk.gpsimd(lambda e: emit("pool", e))
            if by_stream["sp"] or final:
                blk.sync(lambda e: emit("sp", e))
        for s in self.STREAMS:
            p = self.pending[s]
            for sid, (sem, val) in snapshot.items():
                if sid not in p or p[sid][1] < val:
                    p[sid] = (sem, val)
        self.ops = []
        self.phase += 1

    def dma(self, out, in_, reads, writes, q="sp", is_out=False):
        return self.op(q, lambda e: e.dma_start(out=out, in_=in_), reads, writes, dma=True, is_out=is_out)

    def mm(self, out, lhsT, rhs, start, stop, reads, writes):
        return self.op("pe", lambda e: e.matmul(out, lhsT, rhs, start=start, stop=stop), reads, writes)

    def act(self, out, in_, func, reads, writes, bias=None, scale=None):
        kw = {}
        if bias is not None:
            kw["bias"] = bias
        if scale is not None:
            kw["scale"] = scale
        return self.op("act", lambda e: e.activation(out=out, in_=in_, func=func, **kw), reads, writes)

    def tt(self, out, in0, in1, op, reads, writes, eng="dve"):
        return self.op(eng, lambda e: e.tensor_tensor(out=out, in0=in0, in1=in1, op=op), reads, writes)

    def ts(self, out, in0, s1, s2, op0, op1, reads, writes, eng="dve"):
        if op1 is None:
            return self.op(eng, lambda e: e.tensor_scalar(out=out, in0=in0, scalar1=s1, scalar2=None, op0=op0),
                           reads, writes)
        return self.op(eng, lambda e: e.tensor_scalar(out=out, in0=in0, scalar1=s1, scalar2=s2, op0=op0, op1=op1),
                       reads, writes)

    def stt(self, out, in0, scalar, in1, op0, op1, reads, writes, eng="dve"):
        return self.op(eng, lambda e: e.scalar_tensor_tensor(out=out, in0=in0, scalar=scalar, in1=in1,
                                                             op0=op0, op1=op1), reads, writes)


class Ctx:
    pass


def rsqrt_inplace(P, t, key):
    P.act(t, t, AF.Sqrt, [key], [key])
    P.op("dve", lambda e: e.reciprocal(out=t, in_=t), [key], [key])


def rms_norm_group(P, C, pfx, xin, gam, h, sqc, rstd, g, xk="xin", hk="h"):
    ps = C.ps
    for c in range(8):
        P.act(sqc[c % 2][:], xin[:, c, :], AF.Square, [pfx + xk], [pfx + f"sqc{c % 2}"])
        P.mm(ps[6][:], C.ones_bf[:], sqc[c % 2][:], c == 0, c == 7, [pfx + f"sqc{c % 2}", "const"], ["ps6"])
    P.ts(rstd[:], ps[6][:], 1.0 / D, EPS, ALU.mult, ALU.add, ["ps6"], [pfx + "rstd"])
    rsqrt_inplace(P, rstd[:], pfx + "rstd")
    for c in range(8):
        P.stt(h[:, c, :], xin[:, c, :], gam[:, c:c + 1], rstd[:], ALU.mult, ALU.mult,
              [pfx + xk, pfx + "rstd", pfx + "gam"], [pfx + hk])


def ffn_phase(P, C, tag, w_in, w_out, gam_d, xsrc, xdst, groups, flag_prefix=False):
    nc = P.nc
    pfx = tag + "_"
    ps = C.ps
    xs = xsrc.rearrange("(c p) t -> p c t", p=128)
    xd = xdst.rearrange("(c p) t -> p c t", p=128)
    with ExitStack() as es:
        sb = lambda n, shp, dt: es.enter_context(nc.sbuf_tensor(pfx + n, shp, dt))
        win = sb("win", [128, 8, 2 * DFF], BF16)
        wout = sb("wout", [128, 22, D], BF16)
        gam = sb("gam", [128, 8], F32)
        xin = sb("xin", [128, 8, NT], F32)
        h = sb("h", [128, 8, NT], BF16)
        sqc = [sb(f"sqc{i}", [128, NT], BF16) for i in range(2)]
        rstd = sb("rstd", [128, NT], F32)
        a = sb("a", [128, 22, NT], BF16)
        sg = [sb(f"sg{i}", [128, NT], F32) for i in range(2)]
        xr = [sb(f"xr{i}", [128, NT], F32) for i in range(2)]
        xo = [sb(f"xo{i}", [128, NT], F32) for i in range(2)]

        P.dma(gam[:], gam_d, [], [pfx + "gam"])
        g0 = groups[0]
        P.dma(xin[:], xs[:, :, g0 * NT:(g0 + 1) * NT], [], [pfx + "xin"])
        for c in range(8):
            P.dma(win[:, c, :], w_in[c * 128:(c + 1) * 128, :], [], [pfx + f"win{c}"], q="pool")
        wo_r = w_out.rearrange("(j p) m -> p j m", p=128)
        for j0 in range(0, 22, 6):
            j1 = min(22, j0 + 6)
            P.dma(wout[:, j0:j1, :], wo_r[:, j0:j1, :], [], [pfx + f"wout{jj}" for jj in range(j0, j1)], q="pool")
        WIN = [pfx + f"win{c}" for c in range(8)]
        WOUT = [pfx + f"wout{j}" for j in range(22)]

        rms_norm_group(P, C, pfx, xin, gam, h, sqc, rstd, 0)
        for gi_, g in enumerate(groups):
            cols = slice(g * NT, (g + 1) * NT)
            gn = groups[gi_ + 1] if gi_ + 1 < len(groups) else None
            if gn is not None:
                P.dma(xin[:], xs[:, :, gn * NT:(gn + 1) * NT], [], [pfx + "xin"])
            for j in range(22):
                bG, bU = ps[(j % 2) * 2], ps[(j % 2) * 2 + 1]
                kG, kU = f"ps{(j % 2) * 2}", f"ps{(j % 2) * 2 + 1}"
                for c in range(8):
                    P.mm(bG[:], win[:, c, j * 128:(j + 1) * 128], h[:, c, :], c == 0, c == 7,
                         [pfx + "h", WIN[c]], [kG])
                for c in range(8):
                    P.mm(bU[:], win[:, c, DFF + j * 128:DFF + (j + 1) * 128], h[:, c, :], c == 0, c == 7,
                         [pfx + "h", WIN[c]], [kU])
                P.act(sg[j % 2][:], bG[:], AF.Silu, [kG], [pfx + f"sg{j % 2}"])
                P.tt(a[:, j, :], sg[j % 2][:], bU[:], ALU.mult, [pfx + f"sg{j % 2}", kU], [pfx + f"a{j}"])
            if gn is not None:
                rms_norm_group(P, C, pfx, xin, gam, h, sqc, rstd, gn)
            for m in range(8):
                P.dma(xr[m % 2][:], xs[:, m, cols], [], [pfx + f"xr{m % 2}"])
                bO, kO = ps[4 + m % 2], f"ps{4 + m % 2}"
                for j in range(22):
                    P.mm(bO[:], wout[:, j, m * 128:(m + 1) * 128], a[:, j, :], j == 0, j == 21,
                         [pfx + f"a{j}", WOUT[j]], [kO])
                P.stt(xo[m % 2][:], bO[:], 0.5, xr[m % 2][:], ALU.mult, ALU.add,
                      [kO, pfx + f"xr{m % 2}"], [pfx + f"xo{m % 2}"])
                P.dma(xd[:, m, cols], xo[m % 2][:], [pfx + f"xo{m % 2}"], [])
        P.flush()


def inproj_phase(P, C, tag, w_mix, gam_d, xsrc, T, groups, kv_only=()):
    nc = P.nc
    pfx = tag + "_"
    ps = C.ps
    xs = xsrc.rearrange("(c p) t -> p c t", p=128)
    with ExitStack() as es:
        sb = lambda n, shp, dt: es.enter_context(nc.sbuf_tensor(pfx + n, shp, dt))
        wm = sb("wm", [128, 8, 3072], BF16)
        wsw = sb("wsw", [128, 8, 512], BF16)
        gam = sb("gam", [128, 8], F32)
        xin2 = [sb(f"xin{i}", [128, 8, NT], F32) for i in range(2)]
        h2 = [sb(f"h{i}", [128, 8, NT], BF16) for i in range(2)]
        sqc = [sb(f"sqc{i}", [128, NT], BF16) for i in range(2)]
        rstd = sb("rstd", [128, NT], F32)
        cosb = [sb(f"cosb{i}", [128, NT], F32) for i in range(2)]
        sinb = [sb(f"sinb{i}", [128, NT], F32) for i in range(2)]
        decq = sb("decq", [128, 2, NT], F32)
        deck = sb("deck", [128, 256], F32)
        t1 = [sb(f"t1{i}", [128, NT], F32) for i in range(2)]
        t2 = [sb(f"t2{i}", [128, NT], F32) for i in range(2)]
        ob = [sb(f"ob{i}", [128, NT], BF16) for i in range(4)]
        of = [sb(f"of{i}", [128, NT], F32) for i in range(2)]
        krb = [sb(f"krb{i}", [128, NT], BF16) for i in range(2)]
        kdb = [sb(f"kdb{i}", [128, 256], BF16) for i in range(2)]

        P.dma(gam[:], gam_d, [], [pfx + "gam"])
        for i_ in range(min(2, len(groups))):
            P.dma(xin2[i_][:], xs[:, :, groups[i_] * NT:(groups[i_] + 1) * NT], [], [pfx + f"xin{i_}"])
        P.dma(decq[:], C.decq, [], [pfx + "tab"])
        P.dma(deck[:], C.deck, [], [pfx + "tab"])
        for c in range(8):
            P.dma(wm[:, c, :], w_mix[c * 128:(c + 1) * 128, :], [], [pfx + "wm"], q="pool")
            src = w_mix[c * 128:(c + 1) * 128, 2048:2560].rearrange("p (h two d) -> p h two d", two=2, d=32)
            dst = wsw[:, c, :].rearrange("p (h two d) -> p h two d", two=2, d=32)
            P.dma(dst[:, :, 0, :], src[:, :, 1, :], [], [pfx + "wm"], q="pool")
            P.dma(dst[:, :, 1, :], src[:, :, 0, :], [], [pfx + "wm"], q="pool")

        cnt = {"b": 0, "ob": 0, "of": 0, "t": 0, "kr": 0, "kd": 0}

        def bank():
            i = cnt["b"] % 6
            cnt["b"] += 1
            return ps[i], f"ps{i}"

        cur = {}

        def proj(bk, kk, wt, c0, ncols):
            for c in range(8):
                P.mm(bk[:, 0:NT] if ncols == 128 else bk[:], wt[:, c, c0:c0 + 128], cur["h"][:, c, :], c == 0, c == 7,
                     [cur["hk"], pfx + "wm"], [kk])

        def norm_next(gi_):
            if gi_ + 1 < len(groups):
                nb_ = (gi_ + 1) % 2
                rms_norm_group(P, C, pfx, xin2[nb_], gam, h2[nb_], sqc, rstd, 0, xk=f"xin{nb_}", hk=f"h{nb_}")
            if gi_ + 2 < len(groups):
                g2 = groups[gi_ + 2]
                P.dma(xin2[gi_ % 2][:], xs[:, :, g2 * NT:(g2 + 1) * NT], [], [pfx + f"xin{gi_ % 2}"])

        rms_norm_group(P, C, pfx, xin2[0], gam, h2[0], sqc, rstd, 0, xk="xin0", hk="h0")
        for gi_, g in enumerate(groups):
            cols = slice(g * NT, (g + 1) * NT)
            kvo = g in kv_only
            rb = gi_ % 2
            h = h2[rb]
            cur["h"], cur["hk"] = h, pfx + f"h{rb}"
            cosT, sinT = cosb[rb], sinb[rb]
            P.dma(cosT[:], C.rope_cos[:, cols], [], [pfx + f"rope{rb}"])
            P.dma(sinT[:], C.rope_sin[:, cols], [], [pfx + f"rope{rb}"])
            for cc in range(2):
                if kvo and g != NG - 1:
                    continue
                bA, kA = bank()
                proj(bA, kA, wm, cc * 128, 128)
                bB, kB = bank()
                proj(bB, kB, wm, 256 + cc * 128, 128)
                i = cnt["t"] % 2
                cnt["t"] += 1
                P.act(t1[i][:], bB[:], AF.Sigmoid, [kB], [pfx + f"t1{i}"])
                o = cnt["ob"] % 4
                cnt["ob"] += 1
                P.tt(ob[o][:], bA[:], t1[i][:], ALU.mult, [kA, pfx + f"t1{i}"], [pfx + f"ob{o}"])
                P.dma(T["vconv"][cc * 128:(cc + 1) * 128, cols], ob[o][:], [pfx + f"ob{o}"], [])
            for sec, dname, scl in ((512, "qsT", 0.125), (1024, "kT", 1.0)):
                if kvo and dname == "qsT":
                    continue
                for cc in range(4):
                    bk, kk = bank()
                    proj(bk, kk, wm, sec + cc * 128, 128)
                    o = cnt["ob"] % 4
                    cnt["ob"] += 1
                    P.ts(ob[o][:], bk[:], scl, None, ALU.mult, None, [kk], [pfx + f"ob{o}"])
                    P.dma(T[dname][cc * 128:(cc + 1) * 128, cols], ob[o][:], [pfx + f"ob{o}"], [])
            norm_next(gi_)
            for tb in range(4):
                tsl = slice(tb * 128, (tb + 1) * 128)
                rows = slice(g * NT + tb * 128, g * NT + (tb + 1) * 128)
                bk, kk = bank()
                for c in range(8):
                    P.mm(bk[:], h[:, c, tsl], wm[:, c, 1536:2048], c == 0, c == 7, [cur["hk"], pfx + "wm"], [kk])
                o = cnt["ob"] % 4
                cnt["ob"] += 1
                P.act(ob[o][:], bk[:], AF.Copy, [kk], [pfx + f"ob{o}"])
                P.dma(T["vtok"][rows, :], ob[o][:], [pfx + f"ob{o}"], [])
                bk, kk = bank()
                for c in range(8):
                    P.mm(bk[:, 0:256], h[:, c, tsl], wm[:, c, 2560:2816], c == 0, c == 7,
                         [cur["hk"], pfx + "wm"], [kk])
                o = cnt["ob"] % 4
                cnt["ob"] += 1
                P.act(ob[o][:, 0:256], bk[:, 0:256], AF.Copy, [kk], [pfx + f"ob{o}"])
                P.dma(T["vr"][rows, :], ob[o][:, 0:256], [pfx + f"ob{o}"], [])
            for isk, sec, swc in ((0, 2048, 0), (1, 2304, 256)):
                if kvo and not isk:
                    continue
                for cc in range(2):
                    bX, kX = bank()
                    proj(bX, kX, wm, sec + cc * 128, 128)
                    bS, kS = bank()
                    proj(bS, kS, wsw, swc + cc * 128, 128)
                    i = cnt["t"] % 2
                    cnt["t"] += 1
                    P.tt(t1[i][:], bX[:], cosT[:], ALU.mult, [kX, pfx + f"rope{rb}"], [pfx + f"t1{i}"])
                    P.tt(t2[i][:], bS[:], sinT[:], ALU.mult, [kS, pfx + f"rope{rb}"], [pfx + f"t2{i}"])
                    P.tt(t1[i][:], t1[i][:], t2[i][:], ALU.add, [pfx + f"t1{i}", pfx + f"t2{i}"], [pfx + f"t1{i}"],
                         eng="pool")
                    frows = slice(cc * 128, (cc + 1) * 128)
                    if not isk:
                        o = cnt["ob"] % 4
                        cnt["ob"] += 1
                        P.act(ob[o][:], t1[i][:], AF.Copy, [pfx + f"t1{i}"], [pfx + f"ob{o}"], scale=0.125)
                        P.dma(T["qrT"][frows, cols], ob[o][:], [pfx + f"ob{o}"], [])
                        o = cnt["ob"] % 4
                        cnt["ob"] += 1
                        P.tt(ob[o][:], t1[i][:], decq[:, cc, :], ALU.mult, [pfx + f"t1{i}", pfx + "tab"],
                             [pfx + f"ob{o}"], eng="pool")
                        P.dma(T["qdT"][frows, cols], ob[o][:], [pfx + f"ob{o}"], [])
                    else:
                        r = cnt["kr"] % 2
                        cnt["kr"] += 1
                        P.act(krb[r][:], t1[i][:], AF.Copy, [pfx + f"t1{i}"], [pfx + f"krb{r}"])
                        P.dma(T["krT"][frows, cols], krb[r][:], [pfx + f"krb{r}"], [])
                        for tb in range(4):
                            rows = slice(g * NT + tb * 128, g * NT + (tb + 1) * 128)
                            P.op("pe", lambda e, r=r, tb=tb: e.transpose(C.pst[:, 0:128],
                                                                          krb[r][:, tb * 128:(tb + 1) * 128],
                                                                          C.ident[:]),
                                 [pfx + f"krb{r}", "const"], ["pst"])
                            d = cnt["kd"] % 2
                            cnt["kd"] += 1
                            P.tt(kdb[d][:, 0:128], C.pst[:, 0:128], deck[:, cc * 128:(cc + 1) * 128], ALU.mult,
                                 ["pst", pfx + "tab"], [pfx + f"kdb{d}"])
                            P.dma(T["kdtok"][rows, cc * 128:(cc + 1) * 128], kdb[d][:, 0:128],
                                  [pfx + f"kdb{d}"], [])
            for cc in range(2):
                if kvo:
                    continue
                bk, kk = bank()
                proj(bk, kk, wm, 2816 + cc * 128, 128)
                o = cnt["of"] % 2
                cnt["of"] += 1
                P.act(of[o][:], bk[:], AF.Silu, [kk], [pfx + f"of{o}"])
                P.dma(T["gsil"][cc * 128:(cc + 1) * 128, cols], of[o][:], [pfx + f"of{o}"], [])
        P.flush()


def exch_phase(P, C, tag, T, G):
    nc = P.nc
    pfx = tag + "_"
    ps = C.ps
    with ExitStack() as es:
        sb = lambda n, shp, dt: es.enter_context(nc.sbuf_tensor(pfx + n, shp, dt))
        kd64 = sb("kd64", [64, 64, 256], BF16)
        vr64 = sb("vr64", [64, 64, 256], BF16)
        S = sb("S", [64, 256], F32)
        g64 = sb("g64", [64, 256], F32)
        P.cc(T["kT"], G["kT"], [], [])
        P.cc(T["vtok"], G["v"], [], [])
        P.dma(G["halo_loc"], T["vconv"][:, NLOC - 32:NLOC], [], [pfx + "haloloc"])
        P.cc(G["halo_loc"], G["halo"], [pfx + "haloloc"], [])
        P.dma(g64[:], C.g64, [], [pfx + "tab"])
        P.dma(kd64[:], T["kdtok"].rearrange("(n p) c -> p n c", p=64), [], [pfx + "kd64"])
        P.dma(vr64[:], T["vr"].rearrange("(n p) c -> p n c", p=64), [], [pfx + "vr64"])
        P.op("dve", lambda e: e.memset(S[:], 0.0), [], [pfx + "S"])
        for n in range(64):
            kvb, kvk = ps[6 + n % 2], f"ps{6 + n % 2}"
            for hh in range(4):
                hs = slice(hh * 64, (hh + 1) * 64)
                P.mm(kvb[0:64, hs], kd64[:, n, hs], vr64[:, n, hs], True, True, [pfx + "kd64", pfx + "vr64"], [kvk])
            P.tt(S[:], S[:], g64[:], ALU.mult, [pfx + "S", pfx + "tab"], [pfx + "S"])
            P.tt(S[:], S[:], kvb[0:64, 0:256], ALU.add, [pfx + "S", kvk], [pfx + "S"])
        P.dma(G["S_loc"], S[:], [pfx + "S"], [pfx + "sloc"])
        P.cc(G["S_loc"], G["S"], [pfx + "sloc"], [])
        P.flush()


def retconv_phase(P, C, tag, T, G, cw_d, cb_d, lg_d, lb_d, rg_d):
    nc = P.nc
    pfx = tag + "_"
    ps = C.ps
    yT = T["yT"]
    with ExitStack() as es:
        sb = lambda n, shp, dt: es.enter_context(nc.sbuf_tensor(pfx + n, shp, dt))
        vpad = sb("vpad", [128, 2, 32 + NLOC], BF16)
        dg = sb("dg", [128, 62, 128], BF16)
        yc = [sb(f"yc{i}", [128, NT], F32) for i in range(2)]
        cw = sb("cw", [128, 2, 31], F32)
        cb = sb("cb", [128, 2], F32)
        lg = sb("lg", [128, 2], F32)
        lb = sb("lb", [128, 2], F32)
        rg = sb("rg", [64, 4], F32)
        kdq = sb("kdq", [64, 4 * NLOC], BF16)
        kd64 = kdq[:, :].rearrange("p (n c) -> p n c", c=256)
        vr64 = sb("vr64", [64, 64, 256], BF16)
        sball = sb("sball", [64, 64, 256], BF16)
        S = sb("S", [64, 256], F32)
        g64 = sb("g64", [64, 256], F32)
        dintra = sb("dintra", [64, 4, NT], F32)
        qr = [kdq[:, 0:NLOC]] * 2
        kr = [kdq[:, NLOC:2 * NLOC]] * 2
        qd = [kdq[:, 2 * NLOC:3 * NLOC]] * 2
        gs = [sb(f"gs{i}", [64, NT], F32) for i in range(2)]
        stb = [sb(f"stb{i}", [64, NT], BF16) for i in range(2)]
        osb2 = [sb(f"osb{i}", [64, NT], F32) for i in range(2)]
        osq2 = [sb(f"osq{i}", [64, NT], F32) for i in range(2)]
        mean2 = [sb(f"mean{i}", [128, NT], F32) for i in range(2)]
        msq2 = [sb(f"msq{i}", [128, NT], F32) for i in range(2)]
        var2 = [sb(f"var{i}", [128, NT], F32) for i in range(2)]
        mean, msq, var = mean2[0], msq2[0], var2[0]
        tn = [sb(f"tn{i}", [128, NT], F32) for i in range(2)]
        yb = [sb(f"yb{i}", [128, NT], BF16) for i in range(2)]
        sqy = [sb(f"sqy{i}", [128, NT], F32) for i in range(2)]

        own0 = 0
        sel = sb("sel", [128, 4], F32)
        hst = [sb(f"hst{i}", [128, 32], BF16) for i in range(2)]
        sst = [sb(f"sst{i}", [64, 256], F32) for i in range(2)]
        for t, d in ((cw, cw_d), (cb, cb_d), (lg, lg_d), (lb, lb_d), (rg, rg_d), (g64, C.g64), (dintra, C.dintra),
                     (sel, C.sel)):
            P.dma(t[:], d, [], [pfx + "tab"])
        for cc in range(2):
            for r in range(4):
                hb = (cc * 4 + r) % 2
                P.dma(hst[hb][:], G["halo"][(2 * r) * 256 + cc * 128:(2 * r) * 256 + (cc + 1) * 128, :], [],
                      [pfx + f"hst{hb}"])
                if r == 0:
                    P.ts(vpad[:, cc, 0:32], hst[hb][:], sel[:, 0:1], None, ALU.mult, None,
                         [pfx + f"hst{hb}", pfx + "tab"], [pfx + "vpad"])
                else:
                    P.stt(vpad[:, cc, 0:32], hst[hb][:], sel[:, r:r + 1], vpad[:, cc, 0:32], ALU.mult, ALU.add,
                          [pfx + f"hst{hb}", pfx + "tab", pfx + "vpad"], [pfx + "vpad"])
            P.dma(vpad[:, cc, 32:], T["vconv"][cc * 128:(cc + 1) * 128, own0:own0 + NLOC], [], [pfx + "vpad"])
            for k in range(31):
                P.ts(dg[:, cc * 31 + k, :], C.ident[:], cw[:, cc, k:k + 1], None, ALU.mult, None,
                     ["const", pfx + "tab"], [pfx + "dg"])
        for r in range(4):
            P.dma(sst[r % 2][:], G["S"][(2 * r) * 64:(2 * r + 1) * 64, :], [], [pfx + f"sst{r % 2}"])
            if r == 0:
                P.ts(S[:], sst[0][:], sel[0:64, 0:1], None, ALU.mult, None, [pfx + "sst0", pfx + "tab"], [pfx + "S"])
            else:
                P.stt(S[:], sst[r % 2][:], sel[0:64, r:r + 1], S[:], ALU.mult, ALU.add,
                      [pfx + f"sst{r % 2}", pfx + "tab", pfx + "S"], [pfx + "S"])
        for half in (1,):
            ksrc = T["kdtok"]
            vsrc = T["vr"]
            P.dma(kd64, ksrc.rearrange("(n p) c -> p n c", p=64), [], [pfx + "kd64"])
            P.dma(vr64[:], vsrc.rearrange("(n p) c -> p n c", p=64), [], [pfx + "vr64"])
            for n in range(64):
                if half == 1:
                    P.act(sball[:, n, :], S[:], AF.Copy, [pfx + "S"], [pfx + f"sball{n}"])
                kvb, kvk = ps[6 + n % 2], f"ps{6 + n % 2}"
                for hh in range(4):
                    hs = slice(hh * 64, (hh + 1) * 64)
                    P.mm(kvb[0:64, hs], kd64[:, n, hs], vr64[:, n, hs], True, True,
                         [pfx + "kd64", pfx + "vr64"], [kvk])
                P.tt(S[:], S[:], g64[:], ALU.mult, [pfx + "S", pfx + "tab"], [pfx + "S"])
                P.tt(S[:], S[:], kvb[0:64, 0:256], ALU.add, [pfx + "S", kvk], [pfx + "S"])
        for hh in range(4):
            b = 0
            hrows = slice(hh * 64, (hh + 1) * 64)
            hs = slice(hh * 64, (hh + 1) * 64)
            P.dma(qr[b], T["qrT"][hrows, own0:own0 + NLOC], [], [pfx + f"qr{b}", pfx + "kd64"])
            P.dma(kr[b], T["krT"][hrows, own0:own0 + NLOC], [], [pfx + f"kr{b}", pfx + "kd64"])
            P.dma(qd[b], T["qdT"][hrows, own0:own0 + NLOC], [], [pfx + f"qd{b}", pfx + "kd64"])
            def item(g, k, hh=hh, hrows=hrows, hs=hs, b=b):
                ocols = slice(own0 + g * NT, own0 + (g + 1) * NT)
                P.dma(gs[k][:], T["gsil"][hrows, ocols], [], [pfx + f"gs{k}"])
                bS, kS = ps[k], f"ps{k}"
                bO, kO = ps[2 + k], f"ps{2 + k}"
                bM, kM = ps[4 + k], f"ps{4 + k}"
                bQ, kQ = ps[6 + k], f"ps{6 + k}"
                for c in range(8):
                    n = g * 8 + c
                    tc_ = slice(n * 64, (n + 1) * 64)
                    P.mm(bS[0:64, c * 64:(c + 1) * 64], kr[b][:, tc_], qr[b][:, tc_], True, True,
                         [pfx + f"kr{b}", pfx + f"qr{b}"], [kS])
                P.tt(stb[k][:], bS[0:64, :], dintra[:, hh, :], ALU.mult, [kS, pfx + "tab"], [pfx + f"stb{k}"])
                for c in range(8):
                    n = g * 8 + c
                    tc_ = slice(n * 64, (n + 1) * 64)
                    cs = slice(c * 64, (c + 1) * 64)
                    P.mm(bO[0:64, cs], vr64[:, n, hs], stb[k][:, cs], True, False,
                         [pfx + "vr64", pfx + f"stb{k}"], [kO])
                    P.mm(bO[0:64, cs], sball[:, n, hs], qd[b][:, tc_], False, True,
                         [pfx + f"sball{n}", pfx + f"qd{b}"], [kO])
                osb_, osq_ = osb2[k], osq2[k]
                P.act(osb_[:], bO[0:64, :], AF.Copy, [kO], [pfx + f"osb{k}"])
                P.act(osq_[:], bO[0:64, :], AF.Square, [kO], [pfx + f"osq{k}"])
                P.mm(bM[0:64, :], C.ones_f[0:64, 0:64], osb_[:], True, True, [pfx + f"osb{k}", "const"], [kM])
                P.mm(bQ[0:64, :], C.ones_f[0:64, 0:64], osq_[:], True, True, [pfx + f"osq{k}", "const"], [kQ])
                m_, q_, v_ = mean2[k][0:64, :], msq2[k][0:64, :], var2[k][0:64, :]
                mk, qk, vk = pfx + f"mean{k}", pfx + f"msq{k}", pfx + f"var{k}"
                P.ts(m_, bM[0:64, :], 1.0 / 64, None, ALU.mult, None, [kM], [mk])
                P.tt(q_, m_, m_, ALU.mult, [mk], [qk])
                P.stt(v_, bQ[0:64, :], 1.0 / 64, q_, ALU.mult, ALU.subtract, [kQ, qk], [vk])
                P.ts(v_, v_, EPS, None, ALU.add, None, [vk], [vk])
                rsqrt_inplace(P, v_, vk)
                t_ = tn[k][0:64, :]
                P.tt(t_, osb_[:], m_, ALU.subtract, [pfx + f"osb{k}", mk], [pfx + f"tn{k}"])
                P.tt(t_, t_, v_, ALU.mult, [pfx + f"tn{k}", vk], [pfx + f"tn{k}"])
                P.stt(yb[k][0:64, :], t_, rg[:, hh:hh + 1], gs[k][:], ALU.mult, ALU.mult,
                      [pfx + f"tn{k}", pfx + "tab", pfx + f"gs{k}"], [pfx + f"yb{k}"])
                P.dma(yT[768 + hh * 64:768 + (hh + 1) * 64, ocols], yb[k][0:64, :], [pfx + f"yb{k}"], [])

            for g in range(0, NG, 2):
                P.interleave(P.capture(lambda: item(g, 0)), P.capture(lambda: item(g + 1, 1)))
        for g in range(NG):
            cols = slice(g * NT, (g + 1) * NT)
            for cc in range(2):
                for k in range(31):
                    P.mm(ps[cc][:], dg[:, cc * 31 + k, :], vpad[:, cc, 2 + k + g * NT:2 + k + (g + 1) * NT],
                         k == 0, k == 30, [pfx + "dg", pfx + "vpad"], [f"ps{cc}"])
                P.ts(yc[cc][:], ps[cc][:], cb[:, cc:cc + 1], None, ALU.add, None, [f"ps{cc}", pfx + "tab"],
                     [pfx + f"yc{cc}"])
                P.act(sqy[cc][:], yc[cc][:], AF.Square, [pfx + f"yc{cc}"], [pfx + f"sqy{cc}"])
            for cc in range(2):
                P.mm(ps[4][:], C.ones_f[:], yc[cc][:], cc == 0, cc == 1, [pfx + f"yc{cc}", "const"], ["ps4"])
            for cc in range(2):
                P.mm(ps[5][:], C.ones_f[:], sqy[cc][:], cc == 0, cc == 1, [pfx + f"sqy{cc}", "const"], ["ps5"])
            P.ts(mean[:], ps[4][:], 1.0 / 256, None, ALU.mult, None, ["ps4"], [pfx + "mean0"])
            P.tt(msq[:], mean[:], mean[:], ALU.mult, [pfx + "mean0"], [pfx + "msq0"])
            P.stt(var[:], ps[5][:], 1.0 / 256, msq[:], ALU.mult, ALU.subtract, ["ps5", pfx + "msq0"], [pfx + "var0"])
            P.ts(var[:], var[:], EPS, None, ALU.add, None, [pfx + "var0"], [pfx + "var0"])
            rsqrt_inplace(P, var[:], pfx + "var0")
            for cc in range(2):
                P.tt(tn[cc][:], yc[cc][:], mean[:], ALU.subtract, [pfx + f"yc{cc}", pfx + "mean0"],
                     [pfx + f"tn{cc}"])
                P.tt(tn[cc][:], tn[cc][:], var[:], ALU.mult, [pfx + f"tn{cc}", pfx + "var0"], [pfx + f"tn{cc}"])
                P.act(yb[cc][:], tn[cc][:], AF.Silu, [pfx + f"tn{cc}", pfx + "tab"], [pfx + f"yb{cc}"],
                      bias=lb[:, cc:cc + 1], scale=lg[:, cc:cc + 1])
                P.dma(yT[cc * 128:(cc + 1) * 128, own0 + g * NT:own0 + (g + 1) * NT], yb[cc][:],
                      [pfx + f"yb{cc}"], [])
        P.flush()


def sb_phase(P, C, tag, T, G):
    nc = P.nc
    pfx = tag + "_"
    ps = C.ps
    yT = T["yT"]
    NB = 64
    LB0 = 32
    own0 = 0
    NK = NB * 128
    QG = 256
    NQG = NLOC // QG
    with ExitStack() as es:
        sb = lambda n, shp, dt: es.enter_context(nc.sbuf_tensor(pfx + n, shp, dt))
        VA = sb("VA", [128, NB, 512], BF16)
        KT = [sb(f"KT{i}", [128, NK], BF16) for i in range(2)]
        QB = [sb(f"QB{i}", [128, NQG, 2 * QG], BF16) for i in range(2)]
        msk = sb("msk", [128, 2 * NT], BF16)
        ebuf = [sb(f"e{i}", [128, 2 * NT], F32) for i in range(2)]
        spb = [sb(f"sp{i}", [128, 2 * NT], BF16) for i in range(3)]
        wb = [sb(f"w{i}", [128, 2 * NT], BF16) for i in range(3)]
        sA = [sb(f"sA{i}", [128, NT], BF16) for i in range(3)]
        sB = [sb(f"sB{i}", [128, NT], BF16) for i in range(3)]
        ob = [sb(f"ob{i}", [128, QG], BF16) for i in range(2)]

        sel = sb("sel", [128, 4], F32)
        stg = [sb(f"stg{i}", [128, NLOC], BF16) for i in range(2)]
        P.dma(msk[:], C.sbmask2, [], [pfx + "tab"])
        P.dma(sel[:], C.sel, [], [pfx + "tab"])
        stn = {"i": 0}

        def select_into(dst, dkey, src_of_rank):
            for r in range(4):
                k = stn["i"] % 2
                stn["i"] += 1
                sv = stg[k][:] if len(dst.shape) == 2 else stg[k][:].rearrange("p (b c) -> p b c", c=512)
                P.dma(sv, src_of_rank(2 * r), [], [pfx + f"stg{k}"])
                if r == 0:
                    P.ts(dst, sv, sel[:, 0:1], None, ALU.mult, None, [pfx + f"stg{k}", pfx + "tab"], [dkey])
                else:
                    P.stt(dst, sv, sel[:, r:r + 1], dst, ALU.mult, ALU.add, [pfx + f"stg{k}", pfx + "tab", dkey], [dkey])

        for b in range(2):
            P.op("dve", lambda e, b=b: e.memset(QB[b][0:64, :, QG:2 * QG], 0.0), [], [pfx + f"QB{b}"])
            P.op("dve", lambda e, b=b: e.memset(QB[b][64:128, :, 0:QG], 0.0), [], [pfx + f"QB{b}"])
        P.dma(VA[:, 32:64, :], T["vtok"].rearrange("(b p) c -> p b c", p=128), [], [pfx + "VA"])
        for c8 in range(4):
            select_into(VA[:, c8 * 8:(c8 + 1) * 8, :], pfx + "VA",
                        lambda rk, c8=c8: G["v"][rk * NLOC + c8 * 1024:rk * NLOC + (c8 + 1) * 1024, :]
                        .rearrange("(b p) c -> p b c", p=128))

        def load_pair(hp):
            b = hp % 2
            r0 = hp * 128
            P.dma(KT[b][:, NLOC:], T["kT"][r0:r0 + 128, :], [], [pfx + f"KT{b}"])
            select_into(KT[b][:, 0:NLOC], pfx + f"KT{b}", lambda rk, r0=r0: G["kT"][rk * 512 + r0:rk * 512 + r0 + 128, :])
            P.dma(QB[b][0:64, :, 0:QG],
                  T["qsT"][r0:r0 + 64, own0:own0 + NLOC].rearrange("p (g q) -> p g q", q=QG), [], [pfx + f"QB{b}"])
            P.dma(QB[b][64:128, :, QG:2 * QG],
                  T["qsT"][r0 + 64:r0 + 128, own0:own0 + NLOC].rearrange("p (g q) -> p g q", q=QG),
                  [], [pfx + f"QB{b}"])

        items = []
        for hp in range(4):
            for g in range(NQG):
                blks = [LB0 + 2 * g + 1 - i for i in range(2 * g + 2)] + [LB0 - 1 - i for i in range(LB0)]
                npair = len(blks) // 2
                for i in range(npair):
                    items.append((hp, g, blks[2 * i], blks[2 * i + 1], i == 0, i == npair - 1))
        big = C.psbig
        H0, H1 = slice(0, NT), slice(NT, 2 * NT)

        def stageA(idx):
            hp, g, kb0, kb1, first, last = items[idx]
            b = hp % 2
            if g == 0 and first:
                if hp == 0:
                    load_pair(0)
                if hp + 1 < 4:
                    load_pair(hp + 1)
            i2, i3 = idx % 2, idx % 3
            q = QB[b][:, g, :]
            bA, kA = big[i2], f"psb{i2}"
            for kb, hsl in ((kb0, H0), (kb1, H1)):
                P.mm(bA[:, hsl], KT[b][:, kb * 128:(kb + 1) * 128], q, True, True,
                     [pfx + f"KT{b}", pfx + f"QB{b}"], [kA])
            P.act(ebuf[i2][:], bA[:], AF.Exp, [kA], [pfx + f"e{i2}"])
            P.act(spb[i3][:], ebuf[i2][:], AF.Ln, [pfx + f"e{i2}"], [pfx + f"sp{i3}"], bias=1.0)
            if first:
                P.tt(spb[i3][:], spb[i3][:], msk[:], ALU.mult, [pfx + f"sp{i3}", pfx + "tab"], [pfx + f"sp{i3}"])
                P.op("dve", lambda e: e.tensor_copy(out=sA[i3][:], in_=spb[i3][:, H0]),
                     [pfx + f"sp{i3}"], [pfx + f"sA{i3}"])
            else:
                p3 = (idx - 1) % 3
                P.tt(sA[i3][:], sB[p3][:], spb[i3][:, H0], ALU.add, [pfx + f"sB{p3}", pfx + f"sp{i3}"],
                     [pfx + f"sA{i3}"])
            if not last:
                P.tt(sB[i3][:], sA[i3][:], spb[i3][:, H1], ALU.add, [pfx + f"sA{i3}", pfx + f"sp{i3}"],
                     [pfx + f"sB{i3}"])

        def stageB(idx):
            hp, g, kb0, kb1, first, last = items[idx]
            b = hp % 2
            i3 = idx % 3
            q = QB[b][:, g, :]
            bB, kB = big[2], "psb2"
            for kb, hsl, prev in ((kb0, H0, None if first else sB[(idx - 1) % 3]), (kb1, H1, sA[i3])):
                P.mm(bB[:, hsl], C.trineg[:], spb[i3][:, hsl], True, False, [pfx + f"sp{i3}", "const"], [kB])
                if prev is not None:
                    pk = pfx + (f"sA{i3}" if prev is sA[i3] else f"sB{(idx - 1) % 3}")
                    P.mm(bB[:, hsl], C.negones[:], prev[:], False, False, [pk, "const"], [kB])
                P.mm(bB[:, hsl], KT[b][:, kb * 128:(kb + 1) * 128], q, False, True,
                     [pfx + f"KT{b}", pfx + f"QB{b}"], [kB])
            P.act(wb[i3][:], bB[:], AF.Exp, [kB], [pfx + f"w{i3}"])
            if first:
                P.tt(wb[i3][:], wb[i3][:], msk[:], ALU.mult, [pfx + f"w{i3}", pfx + "tab"], [pfx + f"w{i3}"])

        def stageC(idx):
            hp, g, kb0, kb1, first, last = items[idx]
            i3 = idx % 3
            gi = (hp * NQG + g) % 2
            bO, kO = ps[6 + gi], f"psO{gi}"
            P.mm(bO[:], VA[:, kb0, hp * 128:(hp + 1) * 128], wb[i3][:, H0], first, False,
                 [pfx + "VA", pfx + f"w{i3}"], [kO])
            P.mm(bO[:], VA[:, kb1, hp * 128:(hp + 1) * 128], wb[i3][:, H1], False, last,
                 [pfx + "VA", pfx + f"w{i3}"], [kO])
            if last:
                P.op("dve", lambda e: e.tensor_copy(out=ob[gi][0:64, :], in_=bO[0:64, 0:QG]), [kO], [pfx + f"ob{gi}"])
                P.op("dve", lambda e: e.tensor_copy(out=ob[gi][64:128, :], in_=bO[64:128, QG:2 * QG]), [kO],
                     [pfx + f"ob{gi}"])
                P.dma(yT[256 + hp * 128:256 + (hp + 1) * 128, own0 + g * QG:own0 + (g + 1) * QG], ob[gi][:],
                      [pfx + f"ob{gi}"], [])

        n = len(items)
        for i in range(n + 2):
            if i < n:
                stageA(i)
            if 1 <= i <= n:
                stageB(i - 1)
            if i >= 2:
                stageC(i - 2)
        P.flush()


def outproj_phase(P, C, tag, w_o, T, xsrc, xdst, groups):
    nc = P.nc
    pfx = tag + "_"
    ps = C.ps
    xs = xsrc.rearrange("(c p) t -> p c t", p=128)
    xd = xdst.rearrange("(c p) t -> p c t", p=128)
    yr = T["yT"].rearrange("(c p) t -> p c t", p=128)
    with ExitStack() as es:
        sb = lambda n, shp, dt: es.enter_context(nc.sbuf_tensor(pfx + n, shp, dt))
        wo = sb("wo", [128, 8, D], BF16)
        yin = [sb(f"yin{i}", [128, 8, NT], BF16) for i in range(2)]
        xin = [sb(f"xin{i}", [128, 8, NT], F32) for i in range(2)]
        xo = [sb(f"xo{i}", [128, 8, NT], F32) for i in range(2)]
        P.dma(wo[:], w_o.rearrange("(c p) m -> p c m", p=128), [], [pfx + "wo"], q="pool")
        def loads(g):
            cols = slice(g * NT, (g + 1) * NT)
            P.dma(yin[g % 2][:], yr[:, :, cols], [], [pfx + f"yin{g % 2}"])
            P.dma(xin[g % 2][:], xs[:, :, cols], [], [pfx + f"xin{g % 2}"])

        loads(groups[0])
        for gi_, g in enumerate(groups):
            cols = slice(g * NT, (g + 1) * NT)
            b = g % 2
            if gi_ + 1 < len(groups):
                loads(groups[gi_ + 1])
            for m in range(8):
                bO, kO = ps[m % 4], f"ps{m % 4}"
                for c in range(8):
                    P.mm(bO[:], wo[:, c, m * 128:(m + 1) * 128], yin[b][:, c, :], c == 0, c == 7,
                         [pfx + "wo", pfx + f"yin{b}"], [kO])
                P.tt(xo[b][:, m, :], bO[:], xin[b][:, m, :], ALU.add, [kO, pfx + f"xin{b}"], [pfx + f"xo{b}"])
            P.dma(xd[:, :, cols], xo[b][:], [pfx + f"xo{b}"], [])
        P.flush()


def final_phase(P, C, tag, gam_d, xsrc, out_d):
    nc = P.nc
    pfx = tag + "_"
    xs = xsrc.rearrange("(c p) t -> p c t", p=128)
    od = out_d.rearrange("(c p) t -> p c t", p=128)
    with ExitStack() as es:
        sb = lambda n, shp, dt: es.enter_context(nc.sbuf_tensor(pfx + n, shp, dt))
        gam = sb("gam", [128, 8], F32)
        xin = [sb(f"xin{i}", [128, 8, NT], F32) for i in range(2)]
        sqc = [sb(f"sqc{i}", [128, NT], BF16) for i in range(2)]
        rstd = sb("rstd", [128, NT], F32)
        yo = [sb(f"yo{i}", [128, 8, NT], F32) for i in range(2)]
        P.dma(gam[:], gam_d, [], [pfx + "gam"])
        for g in range(NG):
            cols = slice(g * NT, (g + 1) * NT)
            ocols = cols
            b = g % 2
            P.dma(xin[b][:], xs[:, :, cols], [], [pfx + f"xin{b}"])
            for c in range(8):
                P.act(sqc[c % 2][:], xin[b][:, c, :], AF.Square, [pfx + f"xin{b}"], [pfx + f"sqc{c % 2}"])
                P.mm(C.ps[6][:], C.ones_bf[:], sqc[c % 2][:], c == 0, c == 7, [pfx + f"sqc{c % 2}", "const"], ["ps6"])
            P.ts(rstd[:], C.ps[6][:], 1.0 / D, EPS, ALU.mult, ALU.add, ["ps6"], [pfx + "rstd"])
            rsqrt_inplace(P, rstd[:], pfx + "rstd")
            for c in range(8):
                P.stt(yo[b][:, c, :], xin[b][:, c, :], gam[:, c:c + 1], rstd[:], ALU.mult, ALU.mult,
                      [pfx + f"xin{b}", pfx + "rstd", pfx + "gam"], [pfx + f"yo{b}"])
            P.dma(od[:, :, ocols], yo[b][:], [pfx + f"yo{b}"], [], is_out=True)


MIXT = {
    "vconv": ([256, NLOC], BF16), "qsT": ([512, NLOC], BF16), "kT": ([512, NLOC], BF16),
    "vtok": ([NLOC, 512], BF16), "vr": ([NLOC, 256], BF16), "qrT": ([256, NLOC], BF16),
    "qdT": ([256, NLOC], BF16), "krT": ([256, NLOC], BF16), "kdtok": ([NLOC, 256], BF16),
    "gsil": ([256, NLOC], F32), "yT": ([D, NLOC], BF16),
}
GATH = {
    "kT": ([8 * 512, NLOC], BF16), "v": ([8 * NLOC, 512], BF16), "halo": ([8 * 256, 32], BF16),
    "S": ([8 * 64, 256], F32), "halo_loc": ([256, 32], BF16), "S_loc": ([64, 256], F32),
}
CONSTS = {"ones_bf": ([128, 128], BF16), "ones_f": ([128, 128], F32), "ident": ([128, 128], BF16),
          "trineg": ([128, 128], BF16), "negones": ([128, 128], BF16)}
TABLES = {"rope_cos": [128, NLOC], "rope_sin": [128, NLOC], "decq": [128, 2, NT], "deck": [128, 256],
          "g64": [64, 256], "dintra": [64, 4, NT], "sel": [128, 4]}
OWNG = list(range(NG))


def build_program():
    nc = bass.Bass("TRN2", target_bir_lowering=False)
    dt = lambda name, shape, dtype, kind: nc.dram_tensor(name, shape, dtype, kind=kind).ap()
    C = Ctx()
    W = {}
    for l in range(DEPTH):
        for f in ("1", "2"):
            W[f"ffn{f}_w_in{l}"] = dt(f"ffn{f}_w_in{l}", [D, 2 * DFF], F32, "ExternalInput")
            W[f"ffn{f}_w_out{l}"] = dt(f"ffn{f}_w_out{l}", [DFF, D], F32, "ExternalInput")
            W[f"ffn{f}_norm{l}"] = dt(f"ffn{f}_norm{l}", [128, 8], F32, "ExternalInput")
        W[f"mix_w_in{l}"] = dt(f"mix_w_in{l}", [D, 3072], F32, "ExternalInput")
        W[f"mix_norm{l}"] = dt(f"mix_norm{l}", [128, 8], F32, "ExternalInput")
        W[f"mix_w_out{l}"] = dt(f"mix_w_out{l}", [D, D], F32, "ExternalInput")
        W[f"conv_w{l}"] = dt(f"conv_w{l}", [128, 2, 31], F32, "ExternalInput")
        for nm in ("conv_b", "conv_ln_g", "conv_ln_b"):
            W[f"{nm}{l}"] = dt(f"{nm}{l}", [128, 2], F32, "ExternalInput")
        W[f"ret_norm_g{l}"] = dt(f"ret_norm_g{l}", [64, 4], F32, "ExternalInput")
    W["final_norm"] = dt("final_norm", [128, 8], F32, "ExternalInput")
    cd = {k: dt("c_" + k, s, d, "ExternalInput") for k, (s, d) in CONSTS.items()}
    for k, s in TABLES.items():
        setattr(C, k, dt("t_" + k, s, F32, "ExternalInput"))
    C.sbmask2 = dt("t_sbmask2", [128, 2 * NT], BF16, "ExternalInput")
    x_in = dt("x_in", [D, NLOC], F32, "ExternalInput")
    xa = dt("xa", [D, NLOC], F32, "Internal")
    xb = dt("xb", [D, NLOC], F32, "Internal")
    T = {k: dt("m_" + k, s, d, "Internal") for k, (s, d) in MIXT.items()}
    G = {k: dt("g_" + k, s, d, "Internal") for k, (s, d) in GATH.items()}
    out = dt("out", [D, NLOC], F32, "ExternalOutput")

    with ExitStack() as es:
        P = Prog(nc, es)
        C.psbig = [es.enter_context(nc.psum_tensor(f"psb{i}", [128, 2 * NT], F32)) for i in range(4)]
        C.ps = [C.psbig[i // 2][:, (i % 2) * NT:(i % 2 + 1) * NT] for i in range(8)]
        C.pst = C.psbig[3].bitcast(BF16)[:, 2 * NT:4 * NT]
        for k, (s, d) in CONSTS.items():
            t = es.enter_context(nc.sbuf_tensor("k_" + k, s, d))
            setattr(C, k, t)
            P.dma(t[:], cd[k], [], ["const"])
        P.flush()
        mixw = lambda l: (W[f"conv_w{l}"], W[f"conv_b{l}"], W[f"conv_ln_g{l}"], W[f"conv_ln_b{l}"],
                          W[f"ret_norm_g{l}"])
        src, b1, b2 = x_in, xa, xb
        for l in range(DEPTH):
            t = "ab"[l]
            ffn_phase(P, C, "f1" + t, W[f"ffn1_w_in{l}"], W[f"ffn1_w_out{l}"], W[f"ffn1_norm{l}"], src, b1, OWNG)
            inproj_phase(P, C, "ip" + t, W[f"mix_w_in{l}"], W[f"mix_norm{l}"], b1, T, OWNG)
            exch_phase(P, C, "ex" + t, T, G)
            retconv_phase(P, C, "rc" + t, T, G, *mixw(l))
            sb_phase(P, C, "sb" + t, T, G)
            outproj_phase(P, C, "op" + t, W[f"mix_w_out{l}"], T, b1, b2, OWNG)
            ffn_phase(P, C, "f2" + t, W[f"ffn2_w_in{l}"], W[f"ffn2_w_out{l}"], W[f"ffn2_norm{l}"], b2, b1, OWNG)
            src, b1, b2 = b1, b2, b1
        final_phase(P, C, "fin", W["final_norm"], src, out)
        P.flush(final=True)
    return nc


def _consts():
    bf = ml_dtypes.bfloat16
    j = np.arange(128)
    c = {
        "c_ones_bf": np.ones((128, 128), bf), "c_ones_f": np.ones((128, 128), np.float32),
        "c_ident": np.eye(128).astype(bf),
        "c_trineg": (-(j[:, None] >= j[None, :]).astype(np.float32)).astype(bf),
        "c_negones": (-np.ones((128, 128), np.float32)).astype(bf),
    }
    gam = 1.0 - np.exp2(-5.0 - np.arange(4, dtype=np.float64))
    i512 = np.arange(NT)
    p = np.arange(128)
    decq = np.zeros((128, 2, NT))
    for cc in range(2):
        hh = 2 * cc + p // 64
        decq[:, cc, :] = 0.125 * gam[hh][:, None] ** ((i512 % 64) + 1.0)[None, :]
    col = np.arange(256)
    deck = gam[col // 64][None, :] ** (63.0 - (p % 64))[:, None]
    g64 = np.broadcast_to((gam[col // 64] ** 64.0)[None, :], (64, 256))
    jj = np.arange(64)
    dintra = np.zeros((64, 4, NT))
    for hh in range(4):
        dintra[:, hh, :] = gam[hh] ** np.abs(jj[:, None] - (i512 % 64)[None, :])
    sbmask = np.zeros((128, 2 * NT), np.float32)
    for slot, cc in enumerate((1, 0)):
        sbmask[:, slot * NT:(slot + 1) * NT] = ((cc * 128 + p)[:, None] < (i512 % 256)[None, :])
    c.update({"t_decq": decq.astype(np.float32), "t_deck": deck.astype(np.float32),
              "t_g64": np.ascontiguousarray(g64).astype(np.float32), "t_dintra": dintra.astype(np.float32),
              "t_sbmask2": sbmask.astype(bf)})
    return c


def _rope(half):
    pos = (half * NLOC + np.arange(NLOC)).astype(np.float32)
    inv = (1.0 / (10000.0 ** (np.arange(32, dtype=np.float32) / 32))).astype(np.float32)
    p = np.arange(128)
    ang = pos[None, :] * inv[(p % 64) % 32][:, None]
    sign = np.where((p % 64) < 32, -1.0, 1.0)[:, None]
    return np.cos(ang).astype(np.float32), (sign * np.sin(ang)).astype(np.float32)


def _vec8(v):
    return np.ascontiguousarray(v.reshape(8, 128).T)


def _vec2(v):
    return np.ascontiguousarray(v.reshape(2, 128).T)


def kernel(**inp):
    ncores = 8
    x = inp["x"]
    w = dict(_consts())
    for l in range(DEPTH):
        for f in ("1", "2"):
            w[f"ffn{f}_w_in{l}"] = np.ascontiguousarray(inp[f"ffn{f}_w_in"][l])
            w[f"ffn{f}_w_out{l}"] = np.ascontiguousarray(inp[f"ffn{f}_w_out"][l])
            w[f"ffn{f}_norm{l}"] = _vec8(inp[f"ffn{f}_norm"][l])
        w[f"mix_w_in{l}"] = np.ascontiguousarray(inp["mix_w_in"][l])
        w[f"mix_norm{l}"] = _vec8(inp["mix_norm"][l])
        w[f"mix_w_out{l}"] = np.ascontiguousarray(inp["mix_w_out"][l])
        w[f"conv_w{l}"] = np.ascontiguousarray(inp["conv_w"][l].T.reshape(2, 128, 31).transpose(1, 0, 2))
        for nm in ("conv_b", "conv_ln_g", "conv_ln_b"):
            w[f"{nm}{l}"] = _vec2(inp[nm][l])
        w[f"ret_norm_g{l}"] = np.ascontiguousarray(inp["ret_norm_g"][l].reshape(4, 64).T)
    w["final_norm"] = _vec8(inp["final_norm"])
    ropes = [_rope(0), _rope(1)]
    maps = []
    for core in range(ncores):
        b, half = core // 2, core % 2
        m = dict(w)
        m["t_rope_cos"], m["t_rope_sin"] = ropes[half]
        sel = np.zeros((128, 4), np.float32)
        if half == 1:
            sel[:, b] = 1.0
        m["t_sel"] = sel
        m["x_in"] = np.ascontiguousarray(x[b, half * NLOC:(half + 1) * NLOC, :].T)
        maps.append(m)
    res = run_bass_kernel_spmd(build_program(), maps, core_ids=list(range(ncores))).results
    out = np.empty(x.shape, np.float32)
    for c in range(ncores):
        out[c // 2, (c % 2) * NLOC:(c % 2 + 1) * NLOC, :] = res[c]["out"].T
    return out
```
